# Optimizing a Trainium2 kernel written in Bass

```python
import math
import jax, jax.numpy as jnp
from jax import lax
import numpy as np

D_MODEL = 1024
BATCH = 2
SEQ = 8192
DEPTH = 1

MEM_LEN = 256
EPS = 1e-6
SB_HEADS = 8
SB_HEAD_DIM = 64
SB_WIDTH = SB_HEADS * SB_HEAD_DIM
CONV_CH = D_MODEL - SB_WIDTH
CONV_GROUPS = 8
CONV_K = 3
MIX_WIDTH = SB_WIDTH + CONV_CH
IN_COLS = 3 * SB_WIDTH + 3 * CONV_CH
Q_BLOCK = 128
MEM_HEADS = 4
MEM_HEAD_DIM = D_MODEL // MEM_HEADS
PEER_HEADS = 8
PEER_KEYS = 128
PEER_EXPERTS = PEER_KEYS * PEER_KEYS
PEER_QDIM = 256
PEER_HALF = PEER_QDIM // 2
PEER_TOPK = 16
PEER_TOK_BLOCK = 128

kernel_name = "hybrid_sb_attn_shortconv_peer"


def rmsnorm(x, g):
    xf = x.astype(jnp.float32)
    xf = xf * lax.rsqrt(jnp.mean(xf * xf, axis=-1, keepdims=True) + EPS)
    return xf.astype(x.dtype) * g


def stick_breaking_attention(q, k, v):
    b, h, s, dh = q.shape
    nblk = s // Q_BLOCK
    qb = q.reshape(b, h, nblk, Q_BLOCK, dh).transpose(2, 0, 1, 3, 4)
    key_pos = jnp.arange(s)
    scale = 1.0 / math.sqrt(dh)

    def block(args):
        qi, i = args
        z = jnp.einsum('bhqd,bhkd->bhqk', qi, k).astype(jnp.float32) * scale
        q_pos = i * Q_BLOCK + jnp.arange(Q_BLOCK)
        past = key_pos[None, :] < q_pos[:, None]
        log_beta = jax.nn.log_sigmoid(z)
        log_1m = jnp.where(past, jax.nn.log_sigmoid(-z), 0.0)
        tail = lax.cumsum(log_1m, axis=3, reverse=True) - log_1m
        a = jnp.where(past, jnp.exp(log_beta + tail), 0.0)
        return jnp.einsum('bhqk,bhkd->bhqd', a.astype(v.dtype), v)

    out = lax.map(block, (qb, jnp.arange(nblk)))
    return out.transpose(1, 0, 3, 2, 4).reshape(b, s, h * dh)


def causal_short_conv(xc, w):
    c = xc.shape[-1]
    return lax.conv_general_dilated(
        xc, w[:, None, :].astype(xc.dtype), window_strides=(1,),
        padding=[(CONV_K - 1, 0)], dimension_numbers=('NWC', 'WIO', 'NWC'),
        feature_group_count=c)


def token_mixer(n, w_in, conv_w, g_sb_out, g_conv_out, w_out):
    b, s, _ = n.shape
    proj = n @ w_in
    cuts = [SB_WIDTH, 2 * SB_WIDTH, 3 * SB_WIDTH,
            3 * SB_WIDTH + CONV_CH, 3 * SB_WIDTH + 2 * CONV_CH]
    q, k, v, gate_b, gate_c, xin = jnp.split(proj, cuts, axis=-1)
    to_heads = lambda t: t.reshape(b, s, SB_HEADS, SB_HEAD_DIM).transpose(0, 2, 1, 3)
    sb = stick_breaking_attention(to_heads(q), to_heads(k), to_heads(v))
    conv = gate_b * causal_short_conv(gate_c * xin, conv_w)
    mixed = jnp.concatenate([rmsnorm(sb, g_sb_out), rmsnorm(conv, g_conv_out)], axis=-1)
    return mixed @ w_out


def memory_cross_attention(n, mem_n, w_q_mem, w_kv_mem, w_o_mem):
    b, s, _ = n.shape
    m = mem_n.shape[1]
    q = (n @ w_q_mem).reshape(b, s, MEM_HEADS, MEM_HEAD_DIM)
    k, v = jnp.split(mem_n @ w_kv_mem, 2, axis=-1)
    k = k.reshape(b, m, MEM_HEADS, MEM_HEAD_DIM)
    v = v.reshape(b, m, MEM_HEADS, MEM_HEAD_DIM)
    sc = jnp.einsum('bshd,bmhd->bhsm', q, k).astype(jnp.float32) / math.sqrt(MEM_HEAD_DIM)
    p = jax.nn.softmax(sc, axis=-1).astype(v.dtype)
    o = jnp.einsum('bhsm,bmhd->bshd', p, v).reshape(b, s, MEM_HEADS * MEM_HEAD_DIM)
    return o @ w_o_mem


def peer_layer(n, w_query, sub_keys, expert_u, expert_v):
    b, s, d = n.shape
    xt = n.reshape((b * s) // PEER_TOK_BLOCK, PEER_TOK_BLOCK, d)

    def block(xb):
        tb = xb.shape[0]
        q = (xb @ w_query).reshape(tb, PEER_HEADS, 2, PEER_HALF)
        half_s = jnp.einsum('thcd,hcnd->thcn', q, sub_keys).astype(jnp.float32)
        top_s, top_i = lax.top_k(half_s, PEER_TOPK)
        cand_s = top_s[:, :, 0, :, None] + top_s[:, :, 1, None, :]
        cand_i = top_i[:, :, 0, :, None] * PEER_KEYS + top_i[:, :, 1, None, :]
        cand_s = cand_s.reshape(tb, PEER_HEADS, PEER_TOPK * PEER_TOPK)
        cand_i = cand_i.reshape(tb, PEER_HEADS, PEER_TOPK * PEER_TOPK)
        best_s, best_j = lax.top_k(cand_s, PEER_TOPK)
        expert = jnp.take_along_axis(cand_i, best_j, axis=-1)
        gate = jax.nn.softmax(best_s, axis=-1)
        u = expert_u[expert]
        act = jax.nn.gelu(jnp.einsum('thkd,td->thk', u, xb).astype(jnp.float32), approximate=False)
        coef = (gate * act).astype(xb.dtype)
        vv = expert_v[expert]
        return jnp.einsum('thk,thkd->td', coef, vv)

    return lax.map(block, xt).reshape(b, s, d)


def setup_inputs(seed: int = 0) -> dict:
    key = jax.random.key(seed)
    ks = jax.random.split(key, 20)
    f32 = jnp.float32
    nrm = lambda k, shape, sc: jax.random.normal(k, shape, f32) * sc
    gain = lambda k, shape: 1.0 + 0.02 * jax.random.normal(k, shape, f32)
    L = DEPTH
    return {
        "x": jax.random.normal(ks[0], (BATCH, SEQ, D_MODEL), f32),
        "mem": jax.random.normal(ks[1], (BATCH, MEM_LEN, D_MODEL), f32),
        "g_mix": gain(ks[2], (L, D_MODEL)),
        "w_in": nrm(ks[3], (L, D_MODEL, IN_COLS), D_MODEL ** -0.5),
        "conv_w": nrm(ks[4], (L, CONV_K, CONV_CH), CONV_K ** -0.5),
        "g_sb_out": gain(ks[5], (L, SB_WIDTH)),
        "g_conv_out": gain(ks[6], (L, CONV_CH)),
        "w_out": nrm(ks[7], (L, MIX_WIDTH, D_MODEL), MIX_WIDTH ** -0.5),
        "g_xattn": gain(ks[8], (L, D_MODEL)),
        "g_mem": gain(ks[9], (L, D_MODEL)),
        "w_q_mem": nrm(ks[10], (L, D_MODEL, MEM_HEADS * MEM_HEAD_DIM), D_MODEL ** -0.5),
        "w_kv_mem": nrm(ks[11], (L, D_MODEL, 2 * MEM_HEADS * MEM_HEAD_DIM), D_MODEL ** -0.5),
        "w_o_mem": nrm(ks[12], (L, MEM_HEADS * MEM_HEAD_DIM, D_MODEL), D_MODEL ** -0.5),
        "g_ffn": gain(ks[13], (L, D_MODEL)),
        "w_query": nrm(ks[14], (L, D_MODEL, PEER_HEADS * PEER_QDIM), D_MODEL ** -0.5),
        "sub_keys": nrm(ks[15], (L, PEER_HEADS, 2, PEER_KEYS, PEER_HALF), PEER_HALF ** -0.5),
        "expert_u": nrm(ks[16], (L, PEER_EXPERTS, D_MODEL), D_MODEL ** -0.5),
        "expert_v": nrm(ks[17], (L, PEER_EXPERTS, D_MODEL), (PEER_HEADS * PEER_TOPK) ** -0.5),
        "g_final": gain(ks[18], (D_MODEL,)),
    }


def reference(x, mem, g_mix, w_in, conv_w, g_sb_out, g_conv_out, w_out,
              g_xattn, g_mem, w_q_mem, w_kv_mem, w_o_mem,
              g_ffn, w_query, sub_keys, expert_u, expert_v, g_final):
    h = x
    for l in range(DEPTH):
        h = h + token_mixer(rmsnorm(h, g_mix[l]), w_in[l], conv_w[l],
                            g_sb_out[l], g_conv_out[l], w_out[l])
        h = h + memory_cross_attention(rmsnorm(h, g_xattn[l]), rmsnorm(mem, g_mem[l]),
                                       w_q_mem[l], w_kv_mem[l], w_o_mem[l])
        h = h + peer_layer(rmsnorm(h, g_ffn[l]), w_query[l], sub_keys[l],
                           expert_u[l], expert_v[l])
    return rmsnorm(h, g_final)
```

```python
import contextlib
import numpy as np
import ml_dtypes
import concourse.bass as bass
import concourse.mybir as mybir
from concourse.alu_op_type import AluOpType as ALU
from concourse.bass_utils import run_bass_kernel_spmd

AF = mybir.ActivationFunctionType
F32 = mybir.dt.float32
BF16 = mybir.dt.bfloat16
U32 = mybir.dt.uint32
AX = mybir.AxisListType

COMPUTE = ("pe", "act", "dve", "pool")
ALLENG = ("pe", "act", "dve", "pool", "sp")
NDSEM = 40
EPS = 1e-6


class T:
    __slots__ = ("name", "w", "r", "dsem")

    def __init__(self, name, dsem=None):
        self.name = name
        self.w = None
        self.r = []
        self.dsem = dsem


class Prog:
    def __init__(self, nc, stack, same_engine_sync=True):
        self.nc = nc
        self.q = {e: [] for e in ALLENG}
        self.cnt = {}
        self.sem = {}
        for e in COMPUTE:
            self.sem[e] = stack.enter_context(nc.semaphore("c_" + e))
            self.cnt[e] = 0
        for i in range(NDSEM):
            k = "d%d" % i
            self.sem[k] = stack.enter_context(nc.semaphore(k))
            self.cnt[k] = 0
        self.waited = {e: {} for e in ALLENG}
        self.same = same_engine_sync
        self._rr = 0
        self.nins = 0

    def _deps(self, reads, writes):
        deps = {}

        def add(d):
            if d is None:
                return
            k, v = d
            if deps.get(k, 0) < v:
                deps[k] = v
        for t in reads:
            add(t.w)
        for t in writes:
            add(t.w)
            for d in t.r:
                add(d)
        return deps

    def _emit_waits(self, eng, deps):
        for k, v in deps.items():
            if k == eng and (eng == "pe" or not self.same):
                continue
            if k.startswith("d"):
                v = self.cnt[k]
            if self.waited[eng].get(k, 0) >= v:
                continue
            self.waited[eng][k] = v
            sem = self.sem[k]
            self.q[eng].append(lambda e, sem=sem, v=v: e.wait_ge(sem, v))
            self.nins += 1

    def _mark(self, key, val, reads, writes):
        for t in reads:
            t.r.append((key, val))
            if len(t.r) > 64:
                d = {}
                for k, v in t.r:
                    if d.get(k, 0) < v:
                        d[k] = v
                t.r = list(d.items())
        for t in writes:
            t.w = (key, val)
            t.r = []

    def op(self, eng, fn, reads=(), writes=()):
        deps = self._deps(reads, writes)
        self._emit_waits(eng, deps)
        self.cnt[eng] += 1
        val = self.cnt[eng]
        sem = self.sem[eng]
        self.q[eng].append(lambda e, fn=fn, sem=sem: fn(e).then_inc(sem, 1))
        self.nins += 1
        self._mark(eng, val, reads, writes)

    def dma(self, out, in_, reads=(), writes=(), qeng="sp", dsem=None, **kw):
        deps = self._deps(reads, writes)
        self._emit_waits(qeng, deps)
        if dsem is None:
            for t in writes:
                if t.dsem is not None:
                    dsem = t.dsem
                    break
        if dsem is None:
            dsem = self._rr
            self._rr = (self._rr + 1) % NDSEM
            for t in writes:
                t.dsem = dsem
        k = "d%d" % dsem
        self.cnt[k] += 16
        val = self.cnt[k]
        sem = self.sem[k]
        self.q[qeng].append(
            lambda e, out=out, in_=in_, sem=sem, kw=kw: e.dma_start(out=out, in_=in_, **kw).then_inc(sem, 16))
        self.nins += 1
        self._mark(k, val, reads, writes)

    def barrier(self):
        for eng in ALLENG:
            for k, v in self.cnt.items():
                if v == 0 or k == eng:
                    continue
                if self.waited[eng].get(k, 0) >= v:
                    continue
                self.waited[eng][k] = v
                sem = self.sem[k]
                self.q[eng].append(lambda e, sem=sem, v=v: e.wait_ge(sem, v))

    def emit(self):
        nc = self.nc
        with nc.Block() as block:
            @block.tensor
            def _(e):
                for f in self.q["pe"]:
                    f(e)

            @block.scalar
            def _(e):
                for f in self.q["act"]:
                    f(e)

            @block.vector
            def _(e):
                for f in self.q["dve"]:
                    f(e)

            @block.gpsimd
            def _(e):
                for f in self.q["pool"]:
                    f(e)

            @block.sync
            def _(e):
                for f in self.q["sp"]:
                    f(e)


G_MIX, G_XATTN, G_MEM, G_FFN, G_SB, G_CONV, G_CW = 0, 8, 16, 24, 32, 36, 40
NGP = 52


def build(stage=99, dbg=()):
    nc = bass.Bass("TRN2", target_bir_lowering=False)

    def di(n, s, d=F32):
        return nc.dram_tensor(n, list(s), d, kind="ExternalInput").ap()

    xb = di("xb", [8192, 1024])
    xq = di("xq", [2048, 1024])
    xh = di("xh", [8, 1024])
    memb = di("memb", [256, 1024])
    maskd = di("mask", [128, 16, 512], BF16)
    gpack = di("gpack", [128, NGP])
    gfin = di("gfin", [128, 1024])
    w_in = di("w_in", [1024, 3072])
    w_out = di("w_out", [1024, 1024])
    w_q_mem = di("w_q_mem", [1024, 1024])
    w_kv_mem = di("w_kv_mem", [1024, 2048])
    w_o_mem = di("w_o_mem", [1024, 1024])
    w_query = di("w_query", [1024, 2048])
    skT = di("skT", [128, 16, 128])
    euT = di("euT", [1024, 16384])
    ev = di("ev", [16384, 1024])
    out = nc.dram_tensor("out", [2048, 1024], F32, kind="ExternalOutput").ap()
    dbg_out = {}
    for name, shape, dt in dbg:
        dbg_out[name] = nc.dram_tensor(name, list(shape), dt, kind="ExternalOutput").ap()
    kT_d = nc.dram_tensor("kT_d", [4, 128, 8192], BF16, kind="Internal").ap()
    v_d = nc.dram_tensor("v_d", [4, 128, 64, 128], BF16, kind="Internal").ap()
    h_d = nc.dram_tensor("h_d", [2048, 1024], F32, kind="Internal").ap()

    with contextlib.ExitStack() as st:
        P = Prog(nc, st)

        uid = [0]

        def alloc(stk, n, s, d):
            uid[0] += 1
            return stk.enter_context(nc.sbuf_tensor("%s_%d" % (n, uid[0]), list(s), d))

        def palloc(stk, n, s, d=F32):
            uid[0] += 1
            return stk.enter_context(nc.psum_tensor("%s_%d" % (n, uid[0]), list(s), d))

        ident_f = alloc(st, "ident_f", [128, 128], F32)
        ident_b = alloc(st, "ident_b", [128, 128], BF16)
        Uneg = alloc(st, "Uneg", [128, 128], BF16)
        Unegb = alloc(st, "Unegb", [128, 128], BF16)
        ones_f = alloc(st, "ones_f", [128, 1], F32)
        gp = alloc(st, "gp", [128, NGP], F32)
        Tc = T("consts")
        P.dma(gp[:], gpack[:, :], writes=[Tc])
        P.op("pool", lambda e: e.memset(ident_f[:], 1.0), writes=[Tc])
        P.op("pool", lambda e: e.affine_select(out=ident_f[:], in_=ident_f[:], pattern=[[1, 128]],
                                               compare_op=ALU.is_equal, fill=0.0, base=0, channel_multiplier=-1),
             reads=[Tc], writes=[Tc])
        P.op("pool", lambda e: e.tensor_copy(out=ident_b[:], in_=ident_f[:]), reads=[Tc], writes=[Tc])
        P.op("pool", lambda e: e.memset(Uneg[:], -1.0), writes=[Tc])
        P.op("pool", lambda e: e.affine_select(out=Uneg[:], in_=Uneg[:], pattern=[[-1, 128]],
                                               compare_op=ALU.is_gt, fill=0.0, base=0, channel_multiplier=1),
             reads=[Tc], writes=[Tc])
        P.op("pool", lambda e: e.memset(Unegb[:], -1.0), writes=[Tc])
        P.op("pool", lambda e: e.affine_select(out=Unegb[:], in_=Unegb[:], pattern=[[1, 128]],
                                               compare_op=ALU.is_ge, fill=0.0, base=0, channel_multiplier=-1),
             reads=[Tc], writes=[Tc])
        P.op("pool", lambda e: e.memset(ones_f[:], 1.0), writes=[Tc])

        class NormRes:
            pass

        def make_norm_res(stk, pT):
            R = NormRes()
            R.junk = alloc(stk, "n_junk", [128, 1024], BF16)
            R.ssq = alloc(stk, "n_ssq", [128, 4], F32)
            R.rstd = alloc(stk, "n_rstd", [128, 4], F32)
            R.xs = [alloc(stk, "n_xs%d" % i, [128, 1024], F32) for i in range(2)]
            R.Tjunk = T("n_junk")
            R.Tssq = [T("n_ssq%d" % i) for i in range(4)]
            R.Trstd = T("n_rstd")
            R.Txs = [T("n_xs0"), T("n_xs1")]
            R.pT = pT
            R.k = 0
            return R

        def norm_group(R, srcs, gcol, nT, TnT):
            n = len(srcs)
            for i, (xa, Tx) in enumerate(srcs):
                P.op("act", lambda e, xa=xa, i=i: e.activation(out=R.junk[:], in_=xa, func=AF.Square,
                                                                accum_out=R.ssq[:, i:i + 1]),
                     reads=[Tx], writes=[R.Tjunk, R.Tssq[i]])
            P.op("dve", lambda e: e.tensor_scalar(out=R.rstd[:, 0:n], in0=R.ssq[:, 0:n], scalar1=1.0 / 1024,
                                                  scalar2=EPS, op0=ALU.mult, op1=ALU.add),
                 reads=R.Tssq[0:n], writes=[R.Trstd])
            P.op("act", lambda e: e.activation(out=R.rstd[:, 0:n], in_=R.rstd[:, 0:n], func=AF.Sqrt),
                 reads=[R.Trstd], writes=[R.Trstd])
            P.op("dve", lambda e: e.reciprocal(out=R.rstd[:, 0:n], in_=R.rstd[:, 0:n]),
                 reads=[R.Trstd], writes=[R.Trstd])
            for i, (xa, Tx) in enumerate(srcs):
                xs = R.xs[i % 2]
                Txs = R.Txs[i % 2]
                P.op("act", lambda e, xa=xa, xs=xs, i=i: e.activation(out=xs[:], in_=xa, func=AF.Copy,
                                                                      scale=R.rstd[:, i:i + 1]),
                     reads=[Tx, R.Trstd], writes=[Txs])
                for half in range(2):
                    pt, Tp = R.pT[R.k % len(R.pT)]
                    R.k += 1
                    for c in range(4):
                        cc = half * 4 + c
                        P.op("pe", lambda e, pt=pt, c=c, cc=cc, xs=xs: e.transpose(
                            out=pt[:, c * 128:(c + 1) * 128], in_=xs[:, cc * 128:(cc + 1) * 128],
                            identity=ident_f[:]), reads=[Txs, Tc], writes=[Tp])
                    P.op("dve", lambda e, pt=pt, half=half, i=i: e.tensor_tensor(
                        out=nT[:, half * 4:(half + 1) * 4, i * 128:(i + 1) * 128],
                        in0=pt[:, :].rearrange("p (c n) -> p c n", c=4),
                        in1=gcol[:, half * 4:(half + 1) * 4].unsqueeze(2).broadcast_to([128, 4, 128]),
                        op=ALU.mult), reads=[Tp, Tc], writes=[TnT])

        with contextlib.ExitStack() as sAC:
            QT_sb = alloc(sAC, "QT_sb", [128, 8, 2048], BF16)
            convT_sb = alloc(sAC, "convT_sb", [128, 4, 2048], BF16)
            sbT_sb = alloc(sAC, "sbT_sb", [128, 4, 2048], BF16)
            ssq_c = alloc(sAC, "ssq_c", [128, 16], F32)
            ssq_s = alloc(sAC, "ssq_s", [128, 16], F32)
            TQT, TconvT, TsbT, Tssqc, Tssqs = T("QT"), T("convT"), T("sbT"), T("ssqc"), T("ssqs")
            TkTd, Tvd = T("kT_d"), T("v_d")

            with contextlib.ExitStack() as ph:
                pT = [(palloc(ph, "pT%d" % i, [128, 512]), T("pT%d" % i)) for i in range(4)]
                pK = [(palloc(ph, "pK%d" % i, [128, 512]), T("pK%d" % i)) for i in range(2)]
                pV = [(palloc(ph, "pV%d" % i, [128, 512]), T("pV%d" % i)) for i in range(2)]
                R = make_norm_res(ph, pT)
                wkv = alloc(ph, "wkv", [128, 8, 1024], BF16)
                Twkv = T("wkv")
                P.dma(wkv[:], w_in[:, 512:1536].rearrange("(c p) n -> p c n", p=128), writes=[Twkv], qeng="pool")
                NXB = 8
                xt = [alloc(ph, "xt%d" % i, [128, 1024], F32) for i in range(NXB)]
                Txt = [T("xt%d" % i) for i in range(NXB)]
                nTb = [alloc(ph, "nT%d" % i, [128, 8, 512], BF16) for i in range(2)]
                TnTb = [T("nT0"), T("nT1")]
                kst = [alloc(ph, "kst%d" % i, [128, 4, 512], BF16) for i in range(2)]
                vst = [alloc(ph, "vst%d" % i, [128, 4, 512], BF16) for i in range(2)]
                Tkst = [T("kst0"), T("kst1")]
                Tvst = [T("vst0"), T("vst1")]
                NG = 16 if stage >= 1 else 0

                def load_group(g):
                    for tt in range(4):
                        b = (g * 4 + tt) % NXB
                        r0 = (g * 4 + tt) * 128
                        P.dma(xt[b][:], xb[r0:r0 + 128, :], writes=[Txt[b]])
                if NG:
                    load_group(0)
                for g in range(NG):
                    if g + 1 < NG:
                        load_group(g + 1)
                    nT = nTb[g % 2]
                    TnT = TnTb[g % 2]
                    srcs = [(xt[(g * 4 + tt) % NXB][:], Txt[(g * 4 + tt) % NXB]) for tt in range(4)]
                    norm_group(R, srcs, gp[:, G_MIX:G_MIX + 8], nT, TnT)
                    ks, Tks = kst[g % 2], Tkst[g % 2]
                    vs, Tvs = vst[g % 2], Tvst[g % 2]
                    for hp in range(4):
                        pk, Tpk = pK[hp % 2]
                        for dc in range(8):
                            P.op("pe", lambda e, pk=pk, dc=dc, hp=hp, nT=nT: e.matmul(
                                pk[:], lhsT=wkv[:, dc, hp * 128:(hp + 1) * 128], rhs=nT[:, dc, :],
                                start=(dc == 0), stop=(dc == 7)), reads=[Twkv, TnT], writes=[Tpk])
                        P.op("act", lambda e, pk=pk, ks=ks, hp=hp: e.activation(out=ks[:, hp, :], in_=pk[:],
                                                                              func=AF.Copy),
                             reads=[Tpk], writes=[Tks])
                    P.dma(kT_d[:, :, g * 512:(g + 1) * 512].rearrange("h p n -> p h n"), ks[:],
                          reads=[Tks], writes=[TkTd], qeng="pool")
                    for tt in range(4):
                        pv, Tpv = pV[tt % 2]
                        for dc in range(8):
                            P.op("pe", lambda e, pv=pv, dc=dc, tt=tt, nT=nT: e.matmul(
                                pv[:], lhsT=nT[:, dc, tt * 128:(tt + 1) * 128], rhs=wkv[:, dc, 512:1024],
                                start=(dc == 0), stop=(dc == 7)), reads=[Twkv, TnT], writes=[Tpv])
                        P.op("dve", lambda e, pv=pv, vs=vs, tt=tt: e.tensor_copy(out=vs[:, tt, :], in_=pv[:]),
                             reads=[Tpv], writes=[Tvs])
                    for hp in range(4):
                        P.dma(v_d[hp, :, 4 * g:4 * g + 4, :], vs[:, :, hp * 128:(hp + 1) * 128],
                              reads=[Tvs], writes=[Tvd], qeng="pool")
                P.barrier()

            with contextlib.ExitStack() as ph:
                pT = [(palloc(ph, "pT%d" % i, [128, 512]), T("pT%d" % i)) for i in range(2)]
                pQ = [(palloc(ph, "pQ%d" % i, [128, 512]), T("pQ%d" % i)) for i in range(2)]
                pG3 = [(palloc(ph, "pG%d" % i, [128, 512]), T("pG%d" % i)) for i in range(3)]
                pss = palloc(ph, "pss", [128, 512])
                Tpss = T("pss")
                R = make_norm_res(ph, pT)
                wq = alloc(ph, "wq", [128, 8, 512], BF16)
                wg = alloc(ph, "wg", [128, 8, 1536], BF16)
                Twq, Twg = T("wq"), T("wg")
                xt = [alloc(ph, "xt%d" % i, [128, 1024], F32) for i in range(8)]
                Txt = [T("xt%d" % i) for i in range(8)]
                xht = alloc(ph, "xht", [128, 1024], F32)
                Txht = T("xht")
                nTb = [alloc(ph, "nT%d" % i, [128, 8, 512], BF16) for i in range(2)]
                TnTb = [T("nT0"), T("nT1")]
                nhT = alloc(ph, "nhT", [128, 8, 128], BF16)
                TnhT = T("nhT")
                uh = alloc(ph, "uh", [128, 4, 8], F32)
                Tuh = T("uh")
                gch = alloc(ph, "gch", [128, 8], F32)
                Tgch = T("gch")
                gc_sb = alloc(ph, "gc_sb", [128, 512], F32)
                u_sb = alloc(ph, "u_sb", [128, 514], F32)
                acc = alloc(ph, "acc", [128, 512], F32)
                conv = alloc(ph, "conv", [128, 512], F32)
                sqc = alloc(ph, "sqc", [128, 512], F32)
                Tgc, Tu, Tacc, Tconv, Tsqc = T("gc"), T("u"), T("acc"), T("conv"), T("sqc")
                if stage >= 2:
                    P.dma(wq[:], w_in[:, 0:512].rearrange("(c p) n -> p c n", p=128), writes=[Twq], qeng="pool")
                    P.dma(wg[:], w_in[:, 1536:3072].rearrange("(c p) n -> p c n", p=128), writes=[Twg], qeng="pool")
                    P.op("pool", lambda e: e.memset(QT_sb[:], 0.0), writes=[TQT])
                    P.op("pool", lambda e: e.memset(xht[:], 0.0), writes=[Txht])
                    P.dma(xht[0:8, :], xh[:, :], writes=[Txht])

                    def load_slot(s):
                        for tt in range(4):
                            b = (s * 4 + tt) % 8
                            r0 = (s * 4 + tt) * 128
                            P.dma(xt[b][:], xq[r0:r0 + 128, :], writes=[Txt[b]])
                    load_slot(0)
                    norm_group(R, [(xht[:], Txht)], gp[:, G_MIX:G_MIX + 8], nhT, TnhT)
                    for cc in range(4):
                        pa, Tpa = pG3[0]
                        pb, Tpb = pG3[1]
                        for dc in range(8):
                            P.op("pe", lambda e, pa=pa, dc=dc, cc=cc: e.matmul(
                                pa[:, 0:8], lhsT=wg[:, dc, 512 + cc * 128:512 + (cc + 1) * 128], rhs=nhT[:, dc, 0:8],
                                start=(dc == 0), stop=(dc == 7)), reads=[Twg, TnhT], writes=[Tpa])
                        for dc in range(8):
                            P.op("pe", lambda e, pb=pb, dc=dc, cc=cc: e.matmul(
                                pb[:, 0:8], lhsT=wg[:, dc, 1024 + cc * 128:1024 + (cc + 1) * 128], rhs=nhT[:, dc, 0:8],
                                start=(dc == 0), stop=(dc == 7)), reads=[Twg, TnhT], writes=[Tpb])
                        P.op("act", lambda e, pa=pa: e.activation(out=gch[:], in_=pa[:, 0:8], func=AF.Copy),
                             reads=[Tpa], writes=[Tgch])
                        P.op("dve", lambda e, pb=pb, cc=cc: e.tensor_tensor(out=uh[:, cc, :], in0=gch[:], in1=pb[:, 0:8],
                                                                           op=ALU.mult),
                             reads=[Tgch, Tpb], writes=[Tuh])
                    gi = 0
                    for s in range(4):
                        if s + 1 < 4:
                            load_slot(s + 1)
                        nT, TnT = nTb[s % 2], TnTb[s % 2]
                        srcs = [(xt[(s * 4 + tt) % 8][:], Txt[(s * 4 + tt) % 8]) for tt in range(4)]
                        norm_group(R, srcs, gp[:, G_MIX:G_MIX + 8], nT, TnT)
                        for hp in range(4):
                            pq, Tpq = pQ[hp % 2]
                            for dc in range(8):
                                P.op("pe", lambda e, pq=pq, dc=dc, hp=hp, nT=nT: e.matmul(
                                    pq[:], lhsT=wq[:, dc, hp * 128:(hp + 1) * 128], rhs=nT[:, dc, :],
                                    start=(dc == 0), stop=(dc == 7)), reads=[Twq, TnT], writes=[Tpq])
                            for hd in range(2):
                                r0, r1 = hd * 64, hd * 64 + 64
                                P.op("act", lambda e, pq=pq, hp=hp, s=s, hd=hd, r0=r0, r1=r1: e.activation(
                                    out=QT_sb[r0:r1, 2 * hp + hd, s * 512:(s + 1) * 512], in_=pq[r0:r1, :],
                                    func=AF.Copy, scale=0.125), reads=[Tpq], writes=[TQT])
                        for cc in range(4):
                            banks = []
                            for col0 in (512 + cc * 128, 1024 + cc * 128, cc * 128):
                                pg, Tpg = pG3[gi % 3]
                                gi += 1
                                for dc in range(8):
                                    P.op("pe", lambda e, pg=pg, dc=dc, col0=col0, nT=nT: e.matmul(
                                        pg[:], lhsT=wg[:, dc, col0:col0 + 128], rhs=nT[:, dc, :],
                                        start=(dc == 0), stop=(dc == 7)), reads=[Twg, TnT], writes=[Tpg])
                                banks.append((pg, Tpg))
                            (pgc, Tpgc), (pxi, Tpxi), (pgb, Tpgb) = banks
                            P.op("act", lambda e, pgc=pgc: e.activation(out=gc_sb[:], in_=pgc[:], func=AF.Copy),
                                 reads=[Tpgc], writes=[Tgc])
                            P.op("dve", lambda e, cc=cc, s=s: e.tensor_copy(out=u_sb[:, 0:2], in_=uh[:, cc, 2 * s:2 * s + 2]),
                                 reads=[Tuh], writes=[Tu])
                            P.op("dve", lambda e, pxi=pxi: e.tensor_tensor(out=u_sb[:, 2:514], in0=gc_sb[:], in1=pxi[:],
                                                                          op=ALU.mult),
                                 reads=[Tgc, Tpxi], writes=[Tu])
                            P.op("dve", lambda e, cc=cc: e.tensor_scalar(
                                out=acc[:], in0=u_sb[:, 2:514], scalar1=gp[:, G_CW + cc * 3 + 2:G_CW + cc * 3 + 3],
                                scalar2=None, op0=ALU.mult), reads=[Tu, Tc], writes=[Tacc])
                            P.op("dve", lambda e, cc=cc: e.scalar_tensor_tensor(
                                out=acc[:], in0=u_sb[:, 1:513], scalar=gp[:, G_CW + cc * 3 + 1:G_CW + cc * 3 + 2],
                                in1=acc[:], op0=ALU.mult, op1=ALU.add), reads=[Tu, Tc, Tacc], writes=[Tacc])
                            P.op("dve", lambda e, cc=cc: e.scalar_tensor_tensor(
                                out=acc[:], in0=u_sb[:, 0:512], scalar=gp[:, G_CW + cc * 3:G_CW + cc * 3 + 1],
                                in1=acc[:], op0=ALU.mult, op1=ALU.add), reads=[Tu, Tc, Tacc], writes=[Tacc])
                            P.op("dve", lambda e, pgb=pgb: e.tensor_tensor(out=conv[:], in0=acc[:], in1=pgb[:],
                                                                          op=ALU.mult),
                                 reads=[Tacc, Tpgb], writes=[Tconv])
                            P.op("act", lambda e: e.activation(out=sqc[:], in_=conv[:], func=AF.Square),
                                 reads=[Tconv], writes=[Tsqc])
                            P.op("act", lambda e, cc=cc, s=s: e.activation(
                                out=convT_sb[:, cc, s * 512:(s + 1) * 512], in_=conv[:], func=AF.Copy,
                                scale=gp[:, G_CONV + cc:G_CONV + cc + 1]), reads=[Tconv, Tc], writes=[TconvT])
                            for tt in range(4):
                                col = (s * 4 + tt) * 4 + cc
                                P.op("pe", lambda e, tt=tt, col=col: e.matmul(
                                    pss[:, col:col + 1], lhsT=sqc[:, tt * 128:(tt + 1) * 128], rhs=ones_f[:, 0:1],
                                    start=True, stop=True), reads=[Tsqc, Tc], writes=[Tpss])
                    P.op("dve", lambda e: e.tensor_reduce(out=ssq_c[:], in_=pss[:, 0:64].rearrange("p (t c) -> p t c", c=4),
                                                          axis=AX.X, op=ALU.add), reads=[Tpss], writes=[Tssqc])
                P.barrier()

            with contextlib.ExitStack() as ph:
                pz = [(palloc(ph, "pz%d" % i, [128, 512]), T("pz%d" % i)) for i in range(3)]
                pGc = [(palloc(ph, "pGc%d" % i, [128, 512]), T("pGc%d" % i)) for i in range(2)]
                pO = [(palloc(ph, "pO%d" % i, [128, 512]), T("pO%d" % i)) for i in range(2)]
                pss = palloc(ph, "pssb", [128, 512])
                Tpss = T("pssb")
                msk = alloc(ph, "msk", [128, 16, 512], BF16)
                Tmsk = T("msk")
                KT_sb = [alloc(ph, "KT%d" % i, [128, 8192], BF16) for i in range(2)]
                V_sb = [alloc(ph, "V%d" % i, [128, 64, 128], BF16) for i in range(2)]
                TKT = [T("KT0"), T("KT1")]
                TV = [T("V0"), T("V1")]
                NB = 3
                e1 = [alloc(ph, "e1_%d" % i, [128, 512], F32) for i in range(NB)]
                sp = [alloc(ph, "sp_%d" % i, [128, 512], F32) for i in range(NB)]
                Lb = [alloc(ph, "Lb_%d" % i, [128, 512], BF16) for i in range(NB)]
                t2 = [alloc(ph, "t2_%d" % i, [128, 512], F32) for i in range(NB)]
                Ab = [alloc(ph, "Ab_%d" % i, [128, 512], BF16) for i in range(NB)]
                tmpf = [alloc(ph, "tmpf_%d" % i, [128, 512], F32) for i in range(2)]
                Te1 = [T("e1") for _ in range(NB)]
                Tsp = [T("sp") for _ in range(NB)]
                TLb = [T("Lb") for _ in range(NB)]
                Tt2 = [T("t2") for _ in range(NB)]
                TAb = [T("Ab") for _ in range(NB)]
                Ttmpf = [T("tmpf0"), T("tmpf1")]
                sqs = alloc(ph, "sqs", [128, 512], F32)
                Tsqs = T("sqs")
                if stage >= 3:
                    P.dma(msk[:], maskd[:, :, :], writes=[Tmsk])
                    pairs = [(s, hp) for s in range(4) for hp in range(4)]

                    def load_kv(i):
                        s, hp = pairs[i]
                        b = i % 2
                        nk = (4 * s + 4) * 512
                        nblk = nk // 128
                        P.dma(KT_sb[b][:, 0:nk], kT_d[hp, :, 0:nk], reads=[TkTd], writes=[TKT[b]])
                        P.dma(V_sb[b][:, 0:nblk, :], v_d[hp, :, 0:nblk, :], reads=[Tvd], writes=[TV[b]])
                    load_kv(0)
                    step = 0
                    for i, (s, hp) in enumerate(pairs):
                        if i + 1 < len(pairs):
                            load_kv(i + 1)
                        b = i % 2
                        KT, V = KT_sb[b], V_sb[b]
                        nblk = (4 * s + 4) * 4
                        first = True
                        for blk in range(nblk - 1, -1, -1):
                            j, kb = blk // 4, blk % 4
                            masked = j >= 4 * s
                            mi = (j - 4 * s) * 4 + kb
                            last = blk == 0
                            for hd in range(2):
                                r0, r1 = hd * 64, hd * 64 + 64
                                z, Tz = pz[step % 3]
                                k = step % NB
                                step += 1
                                G, TG = pGc[hd]
                                O, TO = pO[hd]
                                P.op("pe", lambda e, z=z, KT=KT, blk=blk, hd=hd, hp=hp, s=s: e.matmul(
                                    z[:], lhsT=KT[:, blk * 128:(blk + 1) * 128],
                                    rhs=QT_sb[:, 2 * hp + hd, s * 512:(s + 1) * 512], start=True, stop=True),
                                    reads=[TKT[b], TQT], writes=[Tz])
                                P.op("act", lambda e, z=z, k=k: e.activation(out=e1[k][:], in_=z[:], func=AF.Exp,
                                                                            scale=-1.0),
                                     reads=[Tz], writes=[Te1[k]])
                                P.op("act", lambda e, k=k: e.activation(out=sp[k][:], in_=e1[k][:], func=AF.Ln,
                                                                       bias=1.0),
                                     reads=[Te1[k]], writes=[Tsp[k]])
                                if not masked:
                                    P.op("dve", lambda e, z=z, k=k: e.tensor_tensor(out=Lb[k][:], in0=z[:], in1=sp[k][:],
                                                                                   op=ALU.add),
                                         reads=[Tz, Tsp[k]], writes=[TLb[k]])
                                else:
                                    tf, Ttf = tmpf[hd], Ttmpf[hd]
                                    P.op("dve", lambda e, z=z, k=k, tf=tf: e.tensor_tensor(out=tf[:], in0=z[:],
                                                                                          in1=sp[k][:], op=ALU.add),
                                         reads=[Tz, Tsp[k]], writes=[Ttf])
                                    P.op("pool", lambda e, k=k, tf=tf, mi=mi: e.tensor_tensor(
                                        out=Lb[k][:], in0=tf[:], in1=msk[:, mi, :], op=ALU.mult),
                                        reads=[Ttf, Tmsk], writes=[TLb[k]])
                                P.op("pe", lambda e, G=G, k=k, first=first: e.matmul(
                                    G[:], lhsT=Uneg[:], rhs=Lb[k][:], start=first, stop=True),
                                    reads=[TLb[k], Tc], writes=[TG])
                                P.op("dve", lambda e, G=G, k=k: e.tensor_tensor(out=t2[k][:], in0=G[:], in1=sp[k][:],
                                                                               op=ALU.subtract),
                                     reads=[TG, Tsp[k]], writes=[Tt2[k]])
                                if not last:
                                    P.op("pe", lambda e, G=G, k=k: e.matmul(
                                        G[:], lhsT=Unegb[:], rhs=Lb[k][:], start=False, stop=True),
                                        reads=[TLb[k], Tc], writes=[TG])
                                P.op("act", lambda e, k=k: e.activation(out=Ab[k][:], in_=t2[k][:], func=AF.Exp),
                                     reads=[Tt2[k]], writes=[TAb[k]])
                                if masked:
                                    P.op("pool", lambda e, k=k, mi=mi: e.tensor_tensor(
                                        out=Ab[k][:], in0=Ab[k][:], in1=msk[:, mi, :], op=ALU.mult),
                                        reads=[TAb[k], Tmsk], writes=[TAb[k]])
                                P.op("pe", lambda e, O=O, V=V, blk=blk, k=k, first=first, last=last: e.matmul(
                                    O[:], lhsT=V[:, blk, :], rhs=Ab[k][:], start=first, stop=last),
                                    reads=[TV[b], TAb[k]], writes=[TO])
                            first = False
                        for hd in range(2):
                            r0, r1 = hd * 64, hd * 64 + 64
                            O, TO = pO[hd]
                            P.op("act", lambda e, O=O, r0=r0, r1=r1: e.activation(out=sqs[r0:r1, :], in_=O[r0:r1, :],
                                                                                func=AF.Square),
                                 reads=[], writes=[Tsqs, TO])
                            P.op("dve", lambda e, O=O, r0=r0, r1=r1, hp=hp, s=s: e.tensor_scalar(
                                out=sbT_sb[r0:r1, hp, s * 512:(s + 1) * 512], in0=O[r0:r1, :],
                                scalar1=gp[r0:r1, G_SB + hp:G_SB + hp + 1], scalar2=None, op0=ALU.mult),
                                reads=[Tc], writes=[TsbT, TO])
                        for tt in range(4):
                            col = (s * 4 + tt) * 4 + hp
                            P.op("pe", lambda e, tt=tt, col=col: e.matmul(
                                pss[:, col:col + 1], lhsT=sqs[:, tt * 128:(tt + 1) * 128], rhs=ones_f[:, 0:1],
                                start=True, stop=True), reads=[Tsqs, Tc], writes=[Tpss])
                    P.op("dve", lambda e: e.tensor_reduce(out=ssq_s[:], in_=pss[:, 0:64].rearrange("p (t c) -> p t c", c=4),
                                                          axis=AX.X, op=ALU.add), reads=[Tpss], writes=[Tssqs])
                P.barrier()

            if "d_sbT" in dbg_out:
                P.dma(dbg_out["d_sbT"].rearrange("c p n -> p c n"), sbT_sb[:], reads=[TsbT], writes=[T("x")])
                P.dma(dbg_out["d_convT"].rearrange("c p n -> p c n"), convT_sb[:], reads=[TconvT], writes=[T("x")])
                P.dma(dbg_out["d_ssq"][:, 0:16], ssq_s[:], reads=[Tssqs], writes=[T("x")])
                P.dma(dbg_out["d_ssq"][:, 16:32], ssq_c[:], reads=[Tssqc], writes=[T("x")])
                P.barrier()

            with contextlib.ExitStack() as ph:
                pP = [(palloc(ph, "pP%d" % i, [128, 512]), T("pP%d" % i)) for i in range(8)]
                wo = alloc(ph, "wo", [128, 8, 1024], BF16)
                Two = T("wo")
                rs = alloc(ph, "rs", [128, 32], F32)
                Trs = T("rs")
                xt = [alloc(ph, "xt%d" % i, [128, 1024], F32) for i in range(4)]
                Txt = [T("xt%d" % i) for i in range(4)]
                ht = [alloc(ph, "ht%d" % i, [128, 1024], F32) for i in range(2)]
                Tht = [T("ht0"), T("ht1")]
                Thd = T("h_d")
                if stage >= 4:
                    P.dma(wo[:], w_out[:, :].rearrange("(c p) n -> p c n", p=128), writes=[Two], qeng="pool")
                    P.op("dve", lambda e: e.tensor_scalar(out=rs[:, 0:16], in0=ssq_s[:], scalar1=1.0 / 512, scalar2=EPS,
                                                          op0=ALU.mult, op1=ALU.add), reads=[Tssqs], writes=[Trs])
                    P.op("dve", lambda e: e.tensor_scalar(out=rs[:, 16:32], in0=ssq_c[:], scalar1=1.0 / 512, scalar2=EPS,
                                                          op0=ALU.mult, op1=ALU.add), reads=[Tssqc, Trs], writes=[Trs])
                    P.op("act", lambda e: e.activation(out=rs[:], in_=rs[:], func=AF.Sqrt), reads=[Trs], writes=[Trs])
                    P.op("dve", lambda e: e.reciprocal(out=rs[:], in_=rs[:]), reads=[Trs], writes=[Trs])
                    for t in range(16):
                        xa, Txa = xt[t % 4], Txt[t % 4]
                        P.dma(xa[:], xq[t * 128:(t + 1) * 128, :], writes=[Txa])
                        h, Th = ht[t % 2], Tht[t % 2]
                        banks = [pP[(t % 2) * 4 + i] for i in range(4)]
                        for src, (Tsrc) in ((0, TsbT), (1, TconvT)):
                            srcT = sbT_sb if src == 0 else convT_sb
                            for half in range(2):
                                pb, Tpb = banks[src * 2 + half]
                                for c in range(4):
                                    P.op("pe", lambda e, pb=pb, srcT=srcT, c=c, t=t, src=src, half=half: e.matmul(
                                        pb[:], lhsT=srcT[:, c, t * 128:(t + 1) * 128],
                                        rhs=wo[:, src * 4 + c, half * 512:(half + 1) * 512],
                                        start=(c == 0), stop=(c == 3)), reads=[Tsrc, Two], writes=[Tpb])
                        for half in range(2):
                            pb, Tpb = banks[half]
                            P.op("dve", lambda e, pb=pb, h=h, xa=xa, half=half, t=t: e.scalar_tensor_tensor(
                                out=h[:, half * 512:(half + 1) * 512], in0=pb[:], scalar=rs[:, t:t + 1],
                                in1=xa[:, half * 512:(half + 1) * 512], op0=ALU.mult, op1=ALU.add),
                                reads=[Tpb, Trs, Txa], writes=[Th])
                        for half in range(2):
                            pb, Tpb = banks[2 + half]
                            P.op("dve", lambda e, pb=pb, h=h, half=half, t=t: e.scalar_tensor_tensor(
                                out=h[:, half * 512:(half + 1) * 512], in0=pb[:], scalar=rs[:, 16 + t:17 + t],
                                in1=h[:, half * 512:(half + 1) * 512], op0=ALU.mult, op1=ALU.add),
                                reads=[Tpb, Trs, Th], writes=[Th])
                        P.dma(h_d[t * 128:(t + 1) * 128, :], h[:], reads=[Th], writes=[Thd], qeng="pool")
                P.barrier()

        if "d_h1" in dbg_out:
            with contextlib.ExitStack() as ph:
                tmp = alloc(ph, "dbgtmp", [128, 16, 1024], F32)
                Tt = T("dbgtmp")
                P.dma(tmp[:], h_d.rearrange("(t p) n -> p t n", p=128), writes=[Tt])
                P.dma(dbg_out["d_h1"].rearrange("(t p) n -> p t n", p=128), tmp[:], reads=[Tt], writes=[T("x")])
                P.barrier()

        Thd = T("h_d")
        with contextlib.ExitStack() as ph:
            pT = [(palloc(ph, "pT%d" % i, [128, 512]), T("pT%d" % i)) for i in range(2)]
            pA = [(palloc(ph, "pA%d" % i, [128, 512]), T("pA%d" % i)) for i in range(2)]
            psc = palloc(ph, "psc", [128, 512])
            Tpsc = T("psc")
            pTp = palloc(ph, "pTp", [128, 1024], BF16)
            TpTp = T("pTp")
            poT = [(palloc(ph, "poT%d" % i, [128, 512]), T("poT%d" % i)) for i in range(2)]
            R = make_norm_res(ph, pT)
            wqm = alloc(ph, "wqm", [128, 8, 1024], BF16)
            wkvm = alloc(ph, "wkvm", [128, 8, 2048], BF16)
            wom = alloc(ph, "wom", [128, 8, 1024], BF16)
            Twqm, Twkvm, Twom = T("wqm"), T("wkvm"), T("wom")
            memt = [alloc(ph, "memt%d" % i, [128, 1024], F32) for i in range(2)]
            Tmemt = [T("memt0"), T("memt1")]
            memT = alloc(ph, "memT", [128, 8, 256], BF16)
            TmemT = T("memT")
            kTm = alloc(ph, "kTm", [128, 8, 256], BF16)
            vm = alloc(ph, "vm", [128, 2, 1024], BF16)
            TkTm, Tvm = T("kTm"), T("vm")
            ht = [alloc(ph, "ht%d" % i, [128, 1024], F32) for i in range(8)]
            Tht = [T("ht%d" % i) for i in range(8)]
            n2T = [alloc(ph, "n2T%d" % i, [128, 8, 512], BF16) for i in range(2)]
            Tn2T = [T("n2T0"), T("n2T1")]
            qTm = alloc(ph, "qTm", [128, 8, 512], BF16)
            TqTm = T("qTm")
            nmx = alloc(ph, "nmx", [128, 4], F32)
            rsum = alloc(ph, "rsum", [128, 4], F32)
            Tnmx = [T("nmx%d" % i) for i in range(4)]
            Trsum = [T("rsum%d" % i) for i in range(4)]
            pexp = [alloc(ph, "pexp%d" % i, [128, 256], F32) for i in range(2)]
            pn = [alloc(ph, "pn%d" % i, [128, 256], BF16) for i in range(2)]
            pTs = [alloc(ph, "pTs%d" % i, [128, 256], BF16) for i in range(2)]
            Tpexp = [T("pexp0"), T("pexp1")]
            Tpn = [T("pn0"), T("pn1")]
            TpTs = [T("pTs0"), T("pTs1")]
            oT_sb = alloc(ph, "oT_sb", [128, 8, 128], BF16)
            ToT = T("oT_sb")
            if stage >= 5:
                P.dma(wqm[:], w_q_mem[:, :].rearrange("(c p) n -> p c n", p=128), writes=[Twqm], qeng="pool")
                P.dma(wkvm[:], w_kv_mem[:, :].rearrange("(c p) n -> p c n", p=128), writes=[Twkvm], qeng="pool")
                P.dma(wom[:], w_o_mem[:, :].rearrange("(c p) n -> p c n", p=128), writes=[Twom], qeng="pool")
                for i in range(2):
                    P.dma(memt[i][:], memb[i * 128:(i + 1) * 128, :], writes=[Tmemt[i]])

                def load_hgroup(g):
                    for tt in range(4):
                        bb = (g * 4 + tt) % 8
                        r0 = (g * 4 + tt) * 128
                        P.dma(ht[bb][:], h_d[r0:r0 + 128, :], reads=[Thd], writes=[Tht[bb]])
                load_hgroup(0)
                norm_group(R, [(memt[0][:], Tmemt[0]), (memt[1][:], Tmemt[1])], gp[:, G_MEM:G_MEM + 8], memT, TmemT)
                for c in range(8):
                    pa, Tpa = pA[c % 2]
                    for dc in range(8):
                        P.op("pe", lambda e, pa=pa, dc=dc, c=c: e.matmul(
                            pa[:, 0:256], lhsT=wkvm[:, dc, c * 128:(c + 1) * 128], rhs=memT[:, dc, :],
                            start=(dc == 0), stop=(dc == 7)), reads=[Twkvm, TmemT], writes=[Tpa])
                    P.op("act", lambda e, pa=pa, c=c: e.activation(out=kTm[:, c, :], in_=pa[:, 0:256], func=AF.Copy),
                         reads=[Tpa], writes=[TkTm])
                for mc in range(2):
                    for half in range(2):
                        pa, Tpa = pA[half]
                        for dc in range(8):
                            P.op("pe", lambda e, pa=pa, dc=dc, mc=mc, half=half: e.matmul(
                                pa[:], lhsT=memT[:, dc, mc * 128:(mc + 1) * 128],
                                rhs=wkvm[:, dc, 1024 + half * 512:1024 + (half + 1) * 512],
                                start=(dc == 0), stop=(dc == 7)), reads=[Twkvm, TmemT], writes=[Tpa])
                        P.op("act", lambda e, pa=pa, mc=mc, half=half: e.activation(
                            out=vm[:, mc, half * 512:(half + 1) * 512], in_=pa[:], func=AF.Copy),
                            reads=[Tpa], writes=[Tvm])
                hk = 0
                for g in range(4):
                    if g + 1 < 4:
                        load_hgroup(g + 1)
                    nT, TnT = n2T[g % 2], Tn2T[g % 2]
                    srcs = [(ht[(g * 4 + tt) % 8][:], Tht[(g * 4 + tt) % 8]) for tt in range(4)]
                    norm_group(R, srcs, gp[:, G_XATTN:G_XATTN + 8], nT, TnT)
                    for c in range(8):
                        pa, Tpa = pA[c % 2]
                        for dc in range(8):
                            P.op("pe", lambda e, pa=pa, dc=dc, c=c, nT=nT: e.matmul(
                                pa[:], lhsT=wqm[:, dc, c * 128:(c + 1) * 128], rhs=nT[:, dc, :],
                                start=(dc == 0), stop=(dc == 7)), reads=[Twqm, TnT], writes=[Tpa])
                        P.op("act", lambda e, pa=pa, c=c: e.activation(out=qTm[:, c, :], in_=pa[:], func=AF.Copy,
                                                                      scale=1.0 / 16), reads=[Tpa], writes=[TqTm])
                    for tt in range(4):
                        bb = (g * 4 + tt) % 8
                        h, Th = ht[bb], Tht[bb]
                        for hd in range(4):
                            k2 = hk % 2
                            hk += 1
                            for c in range(2):
                                P.op("pe", lambda e, c=c, hd=hd, tt=tt: e.matmul(
                                    psc[:, 0:256], lhsT=qTm[:, 2 * hd + c, tt * 128:(tt + 1) * 128],
                                    rhs=kTm[:, 2 * hd + c, :], start=(c == 0), stop=(c == 1)),
                                    reads=[TqTm, TkTm], writes=[Tpsc])
                            P.op("dve", lambda e, hd=hd: e.tensor_reduce(out=nmx[:, hd:hd + 1], in_=psc[:, 0:256],
                                                                        axis=AX.X, op=ALU.max, negate=True),
                                 reads=[Tpsc], writes=[Tnmx[hd]])
                            P.op("act", lambda e, hd=hd, k2=k2: e.activation(
                                out=pexp[k2][:], in_=psc[:, 0:256], func=AF.Exp, bias=nmx[:, hd:hd + 1],
                                accum_out=rsum[:, hd:hd + 1]), reads=[Tnmx[hd]], writes=[Tpexp[k2], Trsum[hd], Tpsc])
                            P.op("dve", lambda e, hd=hd: e.reciprocal(out=rsum[:, hd:hd + 1], in_=rsum[:, hd:hd + 1]),
                                 reads=[Trsum[hd]], writes=[Trsum[hd]])
                            P.op("dve", lambda e, hd=hd, k2=k2: e.tensor_scalar(
                                out=pn[k2][:], in0=pexp[k2][:], scalar1=rsum[:, hd:hd + 1], scalar2=None, op0=ALU.mult),
                                reads=[Tpexp[k2], Trsum[hd]], writes=[Tpn[k2]])
                            for mc in range(2):
                                P.op("pe", lambda e, mc=mc, k2=k2: e.transpose(
                                    out=pTp[:, mc * 128:(mc + 1) * 128], in_=pn[k2][:, mc * 128:(mc + 1) * 128],
                                    identity=ident_b[:]), reads=[Tpn[k2], Tc], writes=[TpTp])
                            P.op("act", lambda e, k2=k2: e.activation(out=pTs[k2][:], in_=pTp[:, 0:256], func=AF.Copy),
                                 reads=[TpTp], writes=[TpTs[k2]])
                            for dch in range(2):
                                ch = 2 * hd + dch
                                po, Tpo = poT[ch // 4]
                                for mc in range(2):
                                    P.op("pe", lambda e, po=po, ch=ch, mc=mc, hd=hd, dch=dch, k2=k2: e.matmul(
                                        po[:, (ch % 4) * 128:(ch % 4 + 1) * 128],
                                        lhsT=vm[:, mc, hd * 256 + dch * 128:hd * 256 + (dch + 1) * 128],
                                        rhs=pTs[k2][:, mc * 128:(mc + 1) * 128], start=(mc == 0), stop=(mc == 1)),
                                        reads=[Tvm, TpTs[k2]], writes=[Tpo])
                        for i2 in range(2):
                            po, Tpo = poT[i2]
                            P.op("dve" if i2 == 0 else "act",
                                 (lambda e, po=po, i2=i2: e.tensor_copy(
                                     out=oT_sb[:, i2 * 4:(i2 + 1) * 4, :].rearrange("p c n -> p (c n)"), in_=po[:]))
                                 if i2 == 0 else
                                 (lambda e, po=po, i2=i2: e.activation(
                                     out=oT_sb[:, i2 * 4:(i2 + 1) * 4, :].rearrange("p c n -> p (c n)"), in_=po[:],
                                     func=AF.Copy)),
                                 reads=[Tpo], writes=[ToT])
                        for half in range(2):
                            pa, Tpa = pA[half]
                            for c in range(8):
                                P.op("pe", lambda e, pa=pa, c=c, half=half: e.matmul(
                                    pa[:], lhsT=oT_sb[:, c, :], rhs=wom[:, c, half * 512:(half + 1) * 512],
                                    start=(c == 0), stop=(c == 7)), reads=[ToT, Twom], writes=[Tpa])
                            P.op("dve", lambda e, pa=pa, h=h, half=half: e.tensor_tensor(
                                out=h[:, half * 512:(half + 1) * 512], in0=pa[:], in1=h[:, half * 512:(half + 1) * 512],
                                op=ALU.add), reads=[Tpa, Th], writes=[Th])
                        r0 = (g * 4 + tt) * 128
                        P.dma(h_d[r0:r0 + 128, :], h[:], reads=[Th], writes=[Thd], qeng="pool")
            P.barrier()

        if "d_h2" in dbg_out:
            with contextlib.ExitStack() as ph:
                tmp = alloc(ph, "dbgtmp2", [128, 16, 1024], F32)
                Tt = T("dbgtmp2")
                P.dma(tmp[:], h_d.rearrange("(t p) n -> p t n", p=128), reads=[Thd], writes=[Tt])
                P.dma(dbg_out["d_h2"].rearrange("(t p) n -> p t n", p=128), tmp[:], reads=[Tt], writes=[T("x")])
                P.barrier()

        with contextlib.ExitStack() as sE:
            n3T = alloc(sE, "n3T", [128, 8, 2048], BF16)
            Tn3T = T("n3T")
            IDX0 = alloc(sE, "IDX0", [128, 16, 128], F32)
            IDX1 = alloc(sE, "IDX1", [128, 16, 128], F32)
            GATE = alloc(sE, "GATE", [128, 16, 128], F32)
            TIDX = [T("IDX%d" % i) for i in range(16)]
            iota128 = alloc(sE, "iota128", [128, 128], F32)
            c16 = alloc(sE, "c16", [128, 16], F32)
            i16 = alloc(sE, "i16", [128, 16], F32)
            Tci = T("peer_consts")
            with contextlib.ExitStack() as ph:
                pT = [(palloc(ph, "pT%d" % i, [128, 512]), T("pT%d" % i)) for i in range(2)]
                pA = [(palloc(ph, "pA%d" % i, [128, 512]), T("pA%d" % i)) for i in range(2)]
                pscr = palloc(ph, "pscr", [128, 2048])
                Tpscr = T("pscr")
                R = make_norm_res(ph, pT)
                wqp = alloc(ph, "wqp", [128, 8, 2048], BF16)
                skb = alloc(ph, "skb", [128, 16, 128], BF16)
                Twqp, Tskb = T("wqp"), T("skb")
                ht = [alloc(ph, "ht%d" % i, [128, 1024], F32) for i in range(8)]
                Tht = [T("ht%d" % i) for i in range(8)]
                qTp = alloc(ph, "qTp", [128, 16, 512], BF16)
                TqTp = T("qTp")
                sc_sb = alloc(ph, "sc_sb", [128, 2048], F32)
                Tsc = T("sc_sb")
                scw = alloc(ph, "scw", [128, 256], F32)
                Tscw = T("scw")
                top_s = alloc(ph, "top_s", [128, 16, 16], F32)
                top_i = alloc(ph, "top_i", [128, 16, 16], U32)
                top_if = alloc(ph, "top_if", [128, 16, 16], F32)
                Ttop = T("top")
                cand = alloc(ph, "cand", [128, 8, 256], F32)
                Tcand = T("cand")
                best_s = alloc(ph, "best_s", [128, 8, 16], F32)
                best_j = alloc(ph, "best_j", [128, 8, 16], U32)
                jf = alloc(ph, "jf", [128, 8, 16], F32)
                Tbest = T("best")
                big = [alloc(ph, "big%d" % i, [128, 8, 16, 16], F32) for i in range(3)]
                Tbig = [T("big%d" % i) for i in range(3)]
                sm = [alloc(ph, "sm%d" % i, [128, 8, 16], F32) for i in range(3)]
                Tsm = [T("sm%d" % i) for i in range(3)]
                s8 = alloc(ph, "s8", [128, 8], F32)
                Ts8 = T("s8")
                if stage >= 6:
                    P.dma(wqp[:], w_query[:, :].rearrange("(c p) n -> p c n", p=128), writes=[Twqp], qeng="pool")
                    P.dma(skb[:], skT[:, :, :], writes=[Tskb], qeng="pool")
                    P.op("pool", lambda e: e.iota(iota128[:], pattern=[[1, 128]], base=0, channel_multiplier=0,
                                                  allow_small_or_imprecise_dtypes=True), writes=[Tci])
                    P.op("pool", lambda e: e.iota(c16[:], pattern=[[16, 16]], base=0, channel_multiplier=0,
                                                  allow_small_or_imprecise_dtypes=True), writes=[Tci])
                    P.op("pool", lambda e: e.iota(i16[:], pattern=[[1, 16]], base=0, channel_multiplier=0,
                                                  allow_small_or_imprecise_dtypes=True), writes=[Tci])

                    def load_hgroup(g):
                        for tt in range(4):
                            bb = (g * 4 + tt) % 8
                            r0 = (g * 4 + tt) * 128
                            P.dma(ht[bb][:], h_d[r0:r0 + 128, :], reads=[Thd], writes=[Tht[bb]])
                    load_hgroup(0)
                    B4 = [128, 8, 16, 16]
                    for g in range(4):
                        if g + 1 < 4:
                            load_hgroup(g + 1)
                        srcs = [(ht[(g * 4 + tt) % 8][:], Tht[(g * 4 + tt) % 8]) for tt in range(4)]
                        nTg = n3T[:, :, g * 512:(g + 1) * 512]
                        norm_group(R, srcs, gp[:, G_FFN:G_FFN + 8], nTg, Tn3T)
                        for c in range(16):
                            pa, Tpa = pA[c % 2]
                            for dc in range(8):
                                P.op("pe", lambda e, pa=pa, dc=dc, c=c, g=g: e.matmul(
                                    pa[:], lhsT=wqp[:, dc, c * 128:(c + 1) * 128], rhs=n3T[:, dc, g * 512:(g + 1) * 512],
                                    start=(dc == 0), stop=(dc == 7)), reads=[Twqp, Tn3T], writes=[Tpa])
                            P.op("act", lambda e, pa=pa, c=c: e.activation(out=qTp[:, c, :], in_=pa[:], func=AF.Copy),
                                 reads=[Tpa], writes=[TqTp])
                        for tt in range(4):
                            t = g * 4 + tt
                            for hc in range(16):
                                P.op("pe", lambda e, hc=hc, tt=tt: e.matmul(
                                    pscr[:, hc * 128:(hc + 1) * 128], lhsT=qTp[:, hc, tt * 128:(tt + 1) * 128],
                                    rhs=skb[:, hc, :], start=True, stop=True), reads=[TqTp, Tskb], writes=[Tpscr])
                            P.op("act", lambda e: e.activation(out=sc_sb[:], in_=pscr[:], func=AF.Copy),
                                 reads=[Tpscr], writes=[Tsc])
                            for hc in range(16):
                                src = sc_sb[:, hc * 128:(hc + 1) * 128]
                                P.op("dve", lambda e, hc=hc, src=src: e.max(out=top_s[:, hc, 0:8], in_=src),
                                     reads=[Tsc], writes=[Ttop])
                                P.op("dve", lambda e, hc=hc, src=src: e.max_index(out=top_i[:, hc, 0:8],
                                                                                in_max=top_s[:, hc, 0:8], in_values=src),
                                     reads=[Tsc, Ttop], writes=[Ttop])
                                P.op("dve", lambda e, hc=hc, src=src: e.match_replace(
                                    out=scw[:, 0:128], in_to_replace=top_s[:, hc, 0:8], in_values=src, imm_value=-1e30),
                                    reads=[Tsc, Ttop], writes=[Tscw])
                                P.op("dve", lambda e, hc=hc: e.max(out=top_s[:, hc, 8:16], in_=scw[:, 0:128]),
                                     reads=[Tscw], writes=[Ttop])
                                P.op("dve", lambda e, hc=hc: e.max_index(out=top_i[:, hc, 8:16],
                                                                        in_max=top_s[:, hc, 8:16], in_values=scw[:, 0:128]),
                                     reads=[Tscw, Ttop], writes=[Ttop])
                            P.op("dve", lambda e: e.tensor_copy(out=top_if[:], in_=top_i[:]), reads=[Ttop], writes=[Ttop])
                            ts4 = top_s[:, :, :].rearrange("p (h c) k -> p h c k", c=2)
                            ti4 = top_if[:, :, :].rearrange("p (h c) k -> p h c k", c=2)
                            P.op("dve", lambda e, ts4=ts4: e.tensor_tensor(
                                out=cand[:, :, :].rearrange("p h (a b) -> p h a b", b=16),
                                in0=ts4[:, :, 0, :].unsqueeze(3).broadcast_to(B4),
                                in1=ts4[:, :, 1, :].unsqueeze(2).broadcast_to(B4), op=ALU.add),
                                reads=[Ttop], writes=[Tcand])
                            for h8 in range(8):
                                src = cand[:, h8, :]
                                P.op("dve", lambda e, h8=h8, src=src: e.max(out=best_s[:, h8, 0:8], in_=src),
                                     reads=[Tcand], writes=[Tbest])
                                P.op("dve", lambda e, h8=h8, src=src: e.max_index(out=best_j[:, h8, 0:8],
                                                                                in_max=best_s[:, h8, 0:8], in_values=src),
                                     reads=[Tcand, Tbest], writes=[Tbest])
                                P.op("dve", lambda e, h8=h8, src=src: e.match_replace(
                                    out=scw[:, 0:256], in_to_replace=best_s[:, h8, 0:8], in_values=src, imm_value=-1e30),
                                    reads=[Tcand, Tbest], writes=[Tscw])
                                P.op("dve", lambda e, h8=h8: e.max(out=best_s[:, h8, 8:16], in_=scw[:, 0:256]),
                                     reads=[Tscw], writes=[Tbest])
                                P.op("dve", lambda e, h8=h8: e.max_index(out=best_j[:, h8, 8:16],
                                                                        in_max=best_s[:, h8, 8:16], in_values=scw[:, 0:256]),
                                     reads=[Tscw, Tbest], writes=[Tbest])
                            P.op("dve", lambda e: e.tensor_copy(out=jf[:], in_=best_j[:]), reads=[Tbest], writes=[Tbest])
                            c16b = c16[:, :].unsqueeze(1).unsqueeze(1).broadcast_to(B4)
                            i16b = i16[:, :].unsqueeze(1).unsqueeze(1).broadcast_to(B4)
                            P.op("dve", lambda e, c16b=c16b: e.tensor_tensor(
                                out=big[0][:], in0=jf[:, :, :].unsqueeze(3).broadcast_to(B4), in1=c16b, op=ALU.subtract),
                                reads=[Tbest, Tci], writes=[Tbig[0]])
                            P.op("dve", lambda e: e.tensor_scalar(out=big[1][:], in0=big[0][:], scalar1=0.0, scalar2=None,
                                                                  op0=ALU.is_ge), reads=[Tbig[0]], writes=[Tbig[1]])
                            P.op("dve", lambda e: e.scalar_tensor_tensor(out=big[2][:], in0=big[0][:], scalar=16.0,
                                                                         in1=big[1][:], op0=ALU.is_lt, op1=ALU.mult),
                                 reads=[Tbig[0], Tbig[1]], writes=[Tbig[2]])
                            P.op("dve", lambda e, ti4=ti4: e.tensor_tensor(
                                out=big[0][:], in0=big[2][:], in1=ti4[:, :, 0, :].unsqueeze(2).broadcast_to(B4), op=ALU.mult),
                                reads=[Tbig[2], Ttop], writes=[Tbig[0]])
                            P.op("dve", lambda e, t=t: e.tensor_reduce(
                                out=IDX0[:, t, :].rearrange("p (h k) -> p h k", k=16), in_=big[0][:], axis=AX.X, op=ALU.add),
                                reads=[Tbig[0]], writes=[TIDX[t]])
                            P.op("dve", lambda e, i16b=i16b: e.tensor_tensor(out=big[1][:], in0=big[2][:], in1=i16b, op=ALU.mult),
                                 reads=[Tbig[2], Tci], writes=[Tbig[1]])
                            P.op("dve", lambda e: e.tensor_reduce(out=sm[0][:], in_=big[1][:], axis=AX.X, op=ALU.add),
                                 reads=[Tbig[1]], writes=[Tsm[0]])
                            P.op("dve", lambda e: e.scalar_tensor_tensor(out=sm[1][:], in0=sm[0][:], scalar=-16.0, in1=jf[:],
                                                                         op0=ALU.mult, op1=ALU.add),
                                 reads=[Tsm[0], Tbest], writes=[Tsm[1]])
                            P.op("dve", lambda e, i16b=i16b: e.tensor_tensor(
                                out=big[0][:], in0=sm[1][:, :, :].unsqueeze(3).broadcast_to(B4), in1=i16b, op=ALU.is_equal),
                                reads=[Tsm[1], Tci], writes=[Tbig[0]])
                            P.op("dve", lambda e, ti4=ti4: e.tensor_tensor(
                                out=big[1][:], in0=big[0][:], in1=ti4[:, :, 1, :].unsqueeze(2).broadcast_to(B4), op=ALU.mult),
                                reads=[Tbig[0], Ttop], writes=[Tbig[1]])
                            P.op("dve", lambda e, t=t: e.tensor_reduce(
                                out=IDX1[:, t, :].rearrange("p (h k) -> p h k", k=16), in_=big[1][:], axis=AX.X, op=ALU.add),
                                reads=[Tbig[1]], writes=[TIDX[t]])
                            P.op("dve", lambda e: e.tensor_tensor(
                                out=sm[2][:], in0=best_s[:], in1=best_s[:, :, 0:1].broadcast_to([128, 8, 16]), op=ALU.subtract),
                                reads=[Tbest], writes=[Tsm[2]])
                            P.op("act", lambda e: e.activation(out=sm[2][:], in_=sm[2][:], func=AF.Exp),
                                 reads=[Tsm[2]], writes=[Tsm[2]])
                            P.op("dve", lambda e: e.tensor_reduce(out=s8[:], in_=sm[2][:], axis=AX.X, op=ALU.add),
                                 reads=[Tsm[2]], writes=[Ts8])
                            P.op("dve", lambda e: e.reciprocal(out=s8[:], in_=s8[:]), reads=[Ts8], writes=[Ts8])
                            P.op("dve", lambda e, t=t: e.tensor_tensor(
                                out=GATE[:, t, :].rearrange("p (h k) -> p h k", k=16), in0=sm[2][:],
                                in1=s8[:, :].unsqueeze(2).broadcast_to([128, 8, 16]), op=ALU.mult),
                                reads=[Tsm[2], Ts8], writes=[TIDX[t]])
                P.barrier()

            if "d_idx" in dbg_out:
                P.dma(dbg_out["d_idx"][0].rearrange("(t p) n -> p t n", p=128), IDX0[:], reads=TIDX, writes=[T("x")])
                P.dma(dbg_out["d_idx"][1].rearrange("(t p) n -> p t n", p=128), IDX1[:], reads=TIDX, writes=[T("x")])
                P.dma(dbg_out["d_idx"][2].rearrange("(t p) n -> p t n", p=128), GATE[:], reads=TIDX, writes=[T("x")])
                P.barrier()

            with contextlib.ExitStack() as ph:
                pout = [(palloc(ph, "pout%d" % i, [128, 512]), T("pout%d" % i)) for i in range(4)]
                pact = [(palloc(ph, "pact%d" % i, [128, 512]), T("pact%d" % i)) for i in range(2)]
                pG = palloc(ph, "pG", [128, 512])
                TpG = T("pG")
                ptr = palloc(ph, "ptr", [128, 512])
                Tptr = T("ptr")
                GT = alloc(ph, "GT", [128, 256, 128], BF16)
                TGT = T("GT")
                trT = alloc(ph, "trT", [128, 3, 128], F32)
                TtrT = T("trT")
                NOH = 8
                Aoh = [alloc(ph, "Aoh%d" % i, [128, 128], BF16) for i in range(NOH)]
                Boh = [alloc(ph, "Boh%d" % i, [128, 128], BF16) for i in range(NOH)]
                TAoh = [T("Aoh%d" % i) for i in range(NOH)]
                TBoh = [T("Boh%d" % i) for i in range(NOH)]
                ub = [alloc(ph, "ub%d" % i, [128, 8, 512], BF16) for i in range(2)]
                vb = [alloc(ph, "vb%d" % i, [128, 4, 1024], BF16) for i in range(2)]
                Tub = [T("ub0"), T("ub1")]
                Tvb = [T("vb0"), T("vb1")]
                ga = [alloc(ph, "ga%d" % i, [128, 256], BF16) for i in range(2)]
                coef = [alloc(ph, "coef%d" % i, [128, 256], BF16) for i in range(2)]
                Tga = [T("ga0"), T("ga1")]
                Tcoef = [T("coef0"), T("coef1")]
                hf = [alloc(ph, "hf%d" % i, [128, 1024], F32) for i in range(2)]
                Thf = [T("hf0"), T("hf1")]
                gf = alloc(ph, "gf", [128, 1024], F32)
                Tgf = T("gf")
                junk2 = alloc(ph, "junk2", [128, 1024], BF16)
                Tjunk2 = T("junk2")
                fs = alloc(ph, "fs", [128, 2], F32)
                Tfs = [T("fs0"), T("fs1")]
                To = T("out")
                NPASS = 8 if stage >= 7 else 0
                if NPASS:
                    P.dma(gf[:], gfin[:, :], writes=[Tgf])
                noh = 0
                for ps_ in range(NPASS):
                    for tl in range(2):
                        t = ps_ * 2 + tl
                        for i3, srcI in enumerate((IDX0, IDX1, GATE)):
                            P.op("pe", lambda e, i3=i3, srcI=srcI, t=t: e.transpose(
                                out=ptr[:, i3 * 128:(i3 + 1) * 128], in_=srcI[:, t, :], identity=ident_f[:]),
                                reads=[TIDX[t], Tc], writes=[Tptr])
                        P.op("act", lambda e: e.activation(out=trT[:, :, :].rearrange("p a n -> p (a n)"), in_=ptr[:, 0:384],
                                                           func=AF.Copy), reads=[Tptr], writes=[TtrT])
                        for tk in range(128):
                            k = noh % NOH
                            noh += 1
                            P.op("dve", lambda e, k=k, tk=tk: e.tensor_scalar(
                                out=Boh[k][:], in0=iota128[:], scalar1=trT[:, 1, tk:tk + 1], scalar2=trT[:, 2, tk:tk + 1],
                                op0=ALU.is_equal, op1=ALU.mult), reads=[TtrT, Tci], writes=[TBoh[k]])
                            P.op("dve", lambda e, k=k, tk=tk: e.tensor_scalar(
                                out=Aoh[k][:], in0=iota128[:], scalar1=trT[:, 0, tk:tk + 1], scalar2=None,
                                op0=ALU.is_equal), reads=[TtrT, Tci], writes=[TAoh[k]])
                            P.op("pe", lambda e, k=k, tk=tk: e.matmul(
                                pG[:, (tk % 4) * 128:(tk % 4 + 1) * 128], lhsT=Boh[k][:], rhs=Aoh[k][:],
                                start=True, stop=True), reads=[TBoh[k], TAoh[k]], writes=[TpG])
                            if tk % 4 == 3:
                                tok0 = tl * 128 + tk - 3
                                P.op("act", lambda e, tok0=tok0: e.activation(
                                    out=GT[:, tok0:tok0 + 4, :].rearrange("p t n -> p (t n)"), in_=pG[:], func=AF.Copy),
                                    reads=[TpG], writes=[TGT])
                    def load_blk(bk):
                        bi_ = bk % 2
                        c0 = bk * 4
                        P.dma(ub[bi_][:], euT[:, c0 * 128:c0 * 128 + 512].rearrange("(dc p) e -> p dc e", p=128),
                              writes=[Tub[bi_]], qeng="pool")
                        P.dma(vb[bi_][:], ev[c0 * 128:c0 * 128 + 512, :].rearrange("(k p) d -> p k d", p=128),
                              writes=[Tvb[bi_]], qeng="pool")
                    load_blk(0)
                    for c in range(128):
                        bi = (c // 4) % 2
                        if c % 4 == 0 and c // 4 + 1 < 32:
                            load_blk(c // 4 + 1)
                        pa, Tpa = pact[c % 2]
                        k2 = c % 2
                        for dc in range(8):
                            P.op("pe", lambda e, pa=pa, dc=dc, c=c, bi=bi, ps_=ps_: e.matmul(
                                pa[:, 0:256], lhsT=ub[bi][:, dc, (c % 4) * 128:(c % 4 + 1) * 128],
                                rhs=n3T[:, dc, ps_ * 256:(ps_ + 1) * 256], start=(dc == 0), stop=(dc == 7)),
                                reads=[Tub[bi], Tn3T], writes=[Tpa])
                        P.op("act", lambda e, pa=pa, k2=k2: e.activation(out=ga[k2][:], in_=pa[:, 0:256], func=AF.Gelu),
                             reads=[Tpa], writes=[Tga[k2]])
                        P.op("dve", lambda e, k2=k2, c=c: e.tensor_tensor(out=coef[k2][:], in0=ga[k2][:], in1=GT[:, :, c],
                                                                         op=ALU.mult),
                             reads=[Tga[k2], TGT], writes=[Tcoef[k2]])
                        for tl in range(2):
                            for half in range(2):
                                po, Tpo = pout[tl * 2 + half]
                                P.op("pe", lambda e, po=po, tl=tl, half=half, k2=k2, bi=bi, c=c: e.matmul(
                                    po[:], lhsT=coef[k2][:, tl * 128:(tl + 1) * 128],
                                    rhs=vb[bi][:, c % 4, half * 512:(half + 1) * 512], start=(c == 0), stop=(c == 127)),
                                    reads=[Tcoef[k2], Tvb[bi]], writes=[Tpo])
                    for tl in range(2):
                        t = ps_ * 2 + tl
                        h, Th = hf[tl], Thf[tl]
                        P.dma(h[:], h_d[t * 128:(t + 1) * 128, :], reads=[Thd], writes=[Th])
                        for half in range(2):
                            po, Tpo = pout[tl * 2 + half]
                            P.op("dve", lambda e, po=po, h=h, half=half: e.tensor_tensor(
                                out=h[:, half * 512:(half + 1) * 512], in0=po[:], in1=h[:, half * 512:(half + 1) * 512],
                                op=ALU.add), reads=[Tpo, Th], writes=[Th])
                        P.op("act", lambda e, h=h, tl=tl: e.activation(out=junk2[:], in_=h[:], func=AF.Square,
                                                                      accum_out=fs[:, tl:tl + 1]),
                             reads=[Th], writes=[Tjunk2, Tfs[tl]])
                        P.op("dve", lambda e, tl=tl: e.tensor_scalar(out=fs[:, tl:tl + 1], in0=fs[:, tl:tl + 1],
                                                                    scalar1=1.0 / 1024, scalar2=EPS, op0=ALU.mult, op1=ALU.add),
                             reads=[Tfs[tl]], writes=[Tfs[tl]])
                        P.op("act", lambda e, tl=tl: e.activation(out=fs[:, tl:tl + 1], in_=fs[:, tl:tl + 1], func=AF.Sqrt),
                             reads=[Tfs[tl]], writes=[Tfs[tl]])
                        P.op("dve", lambda e, tl=tl: e.reciprocal(out=fs[:, tl:tl + 1], in_=fs[:, tl:tl + 1]),
                             reads=[Tfs[tl]], writes=[Tfs[tl]])
                        P.op("dve", lambda e, h=h, tl=tl: e.scalar_tensor_tensor(
                            out=h[:], in0=h[:], scalar=fs[:, tl:tl + 1], in1=gf[:], op0=ALU.mult, op1=ALU.mult),
                            reads=[Th, Tfs[tl], Tgf], writes=[Th])
                        P.dma(out[t * 128:(t + 1) * 128, :], h[:], reads=[Th], writes=[To])
                P.barrier()
        P.barrier()
        P.emit()
    return nc, P


def prep_inputs(inputs):
    f32 = np.float32
    x = np.asarray(inputs["x"], f32)
    mem = np.asarray(inputs["mem"], f32)

    def cols(v):
        v = np.asarray(v, f32).reshape(-1, 128)
        return np.ascontiguousarray(v.T)

    gpack = np.zeros((128, NGP), f32)
    gpack[:, G_MIX:G_MIX + 8] = cols(inputs["g_mix"][0])
    gpack[:, G_XATTN:G_XATTN + 8] = cols(inputs["g_xattn"][0])
    gpack[:, G_MEM:G_MEM + 8] = cols(inputs["g_mem"][0])
    gpack[:, G_FFN:G_FFN + 8] = cols(inputs["g_ffn"][0])
    gpack[:, G_SB:G_SB + 4] = cols(inputs["g_sb_out"][0])
    gpack[:, G_CONV:G_CONV + 4] = cols(inputs["g_conv_out"][0])
    cw = np.asarray(inputs["conv_w"][0], f32)
    for cc in range(4):
        for k in range(3):
            gpack[:, G_CW + cc * 3 + k] = cw[k, cc * 128:(cc + 1) * 128]
    gfin = np.ascontiguousarray(np.broadcast_to(np.asarray(inputs["g_final"], f32)[None, :], (128, 1024)))
    sk = np.asarray(inputs["sub_keys"][0], f32)
    skT = np.ascontiguousarray(sk.reshape(16, 128, 128).transpose(2, 0, 1))
    euT = np.ascontiguousarray(np.asarray(inputs["expert_u"][0], f32).T)
    ev = np.ascontiguousarray(np.asarray(inputs["expert_v"][0], f32))
    shared = dict(
        gpack=gpack, gfin=gfin,
        w_in=np.ascontiguousarray(inputs["w_in"][0], dtype=f32),
        w_out=np.ascontiguousarray(inputs["w_out"][0], dtype=f32),
        w_q_mem=np.ascontiguousarray(inputs["w_q_mem"][0], dtype=f32),
        w_kv_mem=np.ascontiguousarray(inputs["w_kv_mem"][0], dtype=f32),
        w_o_mem=np.ascontiguousarray(inputs["w_o_mem"][0], dtype=f32),
        w_query=np.ascontiguousarray(inputs["w_query"][0], dtype=f32),
        skT=skT, euT=euT, ev=ev)
    in_maps = []
    kpos = np.arange(2048)
    for c in range(8):
        b, ci = c // 4, c % 4
        xqs, xhs = [], []
        for s in range(4):
            t0 = (4 * s + ci) * 512
            xqs.append(x[b, t0:t0 + 512])
            if t0 == 0:
                xhs.append(np.zeros((2, 1024), f32))
            else:
                xhs.append(x[b, t0 - 2:t0])
        qpos = ci * 512 + np.arange(512)
        m = (kpos[:, None] < qpos[None, :]).astype(f32)
        m = m.reshape(16, 128, 512).transpose(1, 0, 2)
        d = dict(shared)
        d.update(xb=np.ascontiguousarray(x[b]), xq=np.ascontiguousarray(np.concatenate(xqs, 0)),
                 xh=np.ascontiguousarray(np.concatenate(xhs, 0)), memb=np.ascontiguousarray(mem[b]),
                 mask=np.ascontiguousarray(m).astype(ml_dtypes.bfloat16))
        in_maps.append(d)
    return in_maps


def assemble(results, key="out"):
    out = np.zeros((2, 8192, 1024), np.float32)
    for c in range(8):
        b, ci = c // 4, c % 4
        o = np.asarray(results[c][key])
        for s in range(4):
            t0 = (4 * s + ci) * 512
            out[b, t0:t0 + 512] = o[s * 512:(s + 1) * 512]
    return out


def kernel(**inputs):
    in_maps = prep_inputs(inputs)
    nc, _ = build()
    res = run_bass_kernel_spmd(nc, in_maps, core_ids=list(range(8)))
    return assemble(res.results)
```

```python
import contextlib
import numpy as np
import ml_dtypes
import concourse.bass as bass
import concourse.mybir as mybir
from concourse.alu_op_type import AluOpType as ALU
from concourse.bass_utils import run_bass_kernel_spmd

AF = mybir.ActivationFunctionType
F32 = mybir.dt.float32
BF16 = mybir.dt.bfloat16
U32 = mybir.dt.uint32
AX = mybir.AxisListType

COMPUTE = ("pe", "act", "dve", "pool")
ALLENG = ("pe", "act", "dve", "pool", "sp")
NDSEM = 40
EPS = 1e-6


class T:
    __slots__ = ("name", "w", "r", "dsem")

    def __init__(self, name, dsem=None):
        self.name = name
        self.w = None
        self.r = []
        self.dsem = dsem


class Prog:
    def __init__(self, nc, stack, same_engine_sync=True):
        self.nc = nc
        self.q = {e: [] for e in ALLENG}
        self.cnt = {}
        self.sem = {}
        for e in COMPUTE:
            self.sem[e] = stack.enter_context(nc.semaphore("c_" + e))
            self.cnt[e] = 0
        for i in range(NDSEM):
            k = "d%d" % i
            self.sem[k] = stack.enter_context(nc.semaphore(k))
            self.cnt[k] = 0
        self.waited = {e: {} for e in ALLENG}
        self.same = same_engine_sync
        self._rr = 0
        self.nins = 0

    def _deps(self, reads, writes):
        deps = {}

        def add(d):
            if d is None:
                return
            k, v = d
            if deps.get(k, 0) < v:
                deps[k] = v
        for t in reads:
            add(t.w)
        for t in writes:
            add(t.w)
            for d in t.r:
                add(d)
        return deps

    def _emit_waits(self, eng, deps):
        for k, v in deps.items():
            if k == eng and (eng == "pe" or not self.same):
                continue
            if k[0] == "d" and k[1:].isdigit():
                v = self.cnt[k]
            if self.waited[eng].get(k, 0) >= v:
                continue
            self.waited[eng][k] = v
            sem = self.sem[k]
            self.q[eng].append(lambda e, sem=sem, v=v: e.wait_ge(sem, v))
            self.nins += 1

    def _mark(self, key, val, reads, writes):
        for t in reads:
            t.r.append((key, val))
            if len(t.r) > 64:
                d = {}
                for k, v in t.r:
                    if d.get(k, 0) < v:
                        d[k] = v
                t.r = list(d.items())
        for t in writes:
            t.w = (key, val)
            t.r = []

    def op(self, eng, fn, reads=(), writes=()):
        deps = self._deps(reads, writes)
        self._emit_waits(eng, deps)
        self.cnt[eng] += 1
        val = self.cnt[eng]
        sem = self.sem[eng]
        self.q[eng].append(lambda e, fn=fn, sem=sem: fn(e).then_inc(sem, 1))
        self.nins += 1
        self._mark(eng, val, reads, writes)

    def dma(self, out, in_, reads=(), writes=(), qeng="sp", dsem=None, **kw):
        deps = self._deps(reads, writes)
        self._emit_waits(qeng, deps)
        if dsem is None:
            for t in writes:
                if t.dsem is not None:
                    dsem = t.dsem
                    break
        if dsem is None:
            dsem = self._rr
            self._rr = (self._rr + 1) % NDSEM
            for t in writes:
                t.dsem = dsem
        k = "d%d" % dsem
        self.cnt[k] += 16
        val = self.cnt[k]
        sem = self.sem[k]
        self.q[qeng].append(
            lambda e, out=out, in_=in_, sem=sem, kw=kw: e.dma_start(out=out, in_=in_, **kw).then_inc(sem, 16))
        self.nins += 1
        self._mark(k, val, reads, writes)

    def barrier(self):
        for eng in ALLENG:
            for k, v in self.cnt.items():
                if v == 0 or k == eng:
                    continue
                if self.waited[eng].get(k, 0) >= v:
                    continue
                self.waited[eng][k] = v
                sem = self.sem[k]
                self.q[eng].append(lambda e, sem=sem, v=v: e.wait_ge(sem, v))

    def emit(self):
        nc = self.nc
        with nc.Block() as block:
            @block.tensor
            def _(e):
                for f in self.q["pe"]:
                    f(e)

            @block.scalar
            def _(e):
                for f in self.q["act"]:
                    f(e)

            @block.vector
            def _(e):
                for f in self.q["dve"]:
                    f(e)

            @block.gpsimd
            def _(e):
                for f in self.q["pool"]:
                    f(e)

            @block.sync
            def _(e):
                for f in self.q["sp"]:
                    f(e)


G_MIX, G_XATTN, G_MEM, G_FFN, G_SB, G_CONV, G_CW = 0, 8, 16, 24, 32, 36, 40
NGP = 52


def build(stage=99, dbg=()):
    nc = bass.Bass("TRN2", target_bir_lowering=False)

    def di(n, s, d=F32):
        return nc.dram_tensor(n, list(s), d, kind="ExternalInput").ap()

    xb = di("xb", [8192, 1024])
    xq = di("xq", [2048, 1024])
    xh = di("xh", [8, 1024])
    memb = di("memb", [256, 1024])
    maskd = di("mask", [128, 16, 512], BF16)
    gpack = di("gpack", [128, NGP])
    gfin = di("gfin", [128, 1024])
    w_in = di("w_in", [1024, 3072])
    w_out = di("w_out", [1024, 1024])
    w_q_mem = di("w_q_mem", [1024, 1024])
    w_kv_mem = di("w_kv_mem", [1024, 2048])
    w_o_mem = di("w_o_mem", [1024, 1024])
    w_query = di("w_query", [1024, 2048])
    skT = di("skT", [128, 16, 128])
    euT = di("euT", [1024, 16384])
    ev = di("ev", [16384, 1024])
    out = nc.dram_tensor("out", [2048, 1024], F32, kind="ExternalOutput").ap()
    dbg_out = {}
    for name, shape, dt in dbg:
        dbg_out[name] = nc.dram_tensor(name, list(shape), dt, kind="ExternalOutput").ap()
    kT_d = nc.dram_tensor("kT_d", [4, 128, 8192], BF16, kind="Internal").ap()
    v_d = nc.dram_tensor("v_d", [4, 128, 64, 128], BF16, kind="Internal").ap()
    h_d = nc.dram_tensor("h_d", [2048, 1024], F32, kind="Internal").ap()

    with contextlib.ExitStack() as st:
        P = Prog(nc, st)

        uid = [0]

        def alloc(stk, n, s, d):
            uid[0] += 1
            return stk.enter_context(nc.sbuf_tensor("%s_%d" % (n, uid[0]), list(s), d))

        def palloc(stk, n, s, d=F32):
            uid[0] += 1
            return stk.enter_context(nc.psum_tensor("%s_%d" % (n, uid[0]), list(s), d))

        ident_f = alloc(st, "ident_f", [128, 128], F32)
        ident_b = alloc(st, "ident_b", [128, 128], BF16)
        Uneg = alloc(st, "Uneg", [128, 128], BF16)
        Unegb = alloc(st, "Unegb", [128, 128], BF16)
        ones_f = alloc(st, "ones_f", [128, 1], F32)
        gp = alloc(st, "gp", [128, NGP], F32)
        Tc = T("consts")
        P.dma(gp[:], gpack[:, :], writes=[Tc])
        P.op("pool", lambda e: e.memset(ident_f[:], 1.0), writes=[Tc])
        P.op("pool", lambda e: e.affine_select(out=ident_f[:], in_=ident_f[:], pattern=[[1, 128]],
                                               compare_op=ALU.is_equal, fill=0.0, base=0, channel_multiplier=-1),
             reads=[Tc], writes=[Tc])
        P.op("pool", lambda e: e.tensor_copy(out=ident_b[:], in_=ident_f[:]), reads=[Tc], writes=[Tc])
        P.op("pool", lambda e: e.memset(Uneg[:], -1.0), writes=[Tc])
        P.op("pool", lambda e: e.affine_select(out=Uneg[:], in_=Uneg[:], pattern=[[-1, 128]],
                                               compare_op=ALU.is_gt, fill=0.0, base=0, channel_multiplier=1),
             reads=[Tc], writes=[Tc])
        P.op("pool", lambda e: e.memset(Unegb[:], -1.0), writes=[Tc])
        P.op("pool", lambda e: e.affine_select(out=Unegb[:], in_=Unegb[:], pattern=[[1, 128]],
                                               compare_op=ALU.is_ge, fill=0.0, base=0, channel_multiplier=-1),
             reads=[Tc], writes=[Tc])
        P.op("pool", lambda e: e.memset(ones_f[:], 1.0), writes=[Tc])

        class NormRes:
            pass

        def make_norm_res(stk, pT):
            R = NormRes()
            R.junk = alloc(stk, "n_junk", [128, 1024], BF16)
            R.ssq = alloc(stk, "n_ssq", [128, 4], F32)
            R.rstd = alloc(stk, "n_rstd", [128, 4], F32)
            R.xs = [alloc(stk, "n_xs%d" % i, [128, 1024], F32) for i in range(2)]
            R.Tjunk = T("n_junk")
            R.Tssq = [T("n_ssq%d" % i) for i in range(4)]
            R.Trstd = T("n_rstd")
            R.Txs = [T("n_xs0"), T("n_xs1")]
            R.pT = pT
            R.k = 0
            return R

        def norm_group(R, srcs, gcol, nT, TnT):
            n = len(srcs)
            for i, (xa, Tx) in enumerate(srcs):
                P.op("act", lambda e, xa=xa, i=i: e.activation(out=R.junk[:], in_=xa, func=AF.Square,
                                                                accum_out=R.ssq[:, i:i + 1]),
                     reads=[Tx], writes=[R.Tjunk, R.Tssq[i]])
            P.op("dve", lambda e: e.tensor_scalar(out=R.rstd[:, 0:n], in0=R.ssq[:, 0:n], scalar1=1.0 / 1024,
                                                  scalar2=EPS, op0=ALU.mult, op1=ALU.add),
                 reads=R.Tssq[0:n], writes=[R.Trstd])
            P.op("act", lambda e: e.activation(out=R.rstd[:, 0:n], in_=R.rstd[:, 0:n], func=AF.Sqrt),
                 reads=[R.Trstd], writes=[R.Trstd])
            P.op("dve", lambda e: e.reciprocal(out=R.rstd[:, 0:n], in_=R.rstd[:, 0:n]),
                 reads=[R.Trstd], writes=[R.Trstd])
            for i, (xa, Tx) in enumerate(srcs):
                xs = R.xs[i % 2]
                Txs = R.Txs[i % 2]
                P.op("act", lambda e, xa=xa, xs=xs, i=i: e.activation(out=xs[:], in_=xa, func=AF.Copy,
                                                                      scale=R.rstd[:, i:i + 1]),
                     reads=[Tx, R.Trstd], writes=[Txs])
                for half in range(2):
                    pt, Tp = R.pT[R.k % len(R.pT)]
                    R.k += 1
                    for c in range(4):
                        cc = half * 4 + c
                        P.op("pe", lambda e, pt=pt, c=c, cc=cc, xs=xs: e.transpose(
                            out=pt[:, c * 128:(c + 1) * 128], in_=xs[:, cc * 128:(cc + 1) * 128],
                            identity=ident_f[:]), reads=[Txs, Tc], writes=[Tp])
                    P.op("dve", lambda e, pt=pt, half=half, i=i: e.tensor_tensor(
                        out=nT[:, half * 4:(half + 1) * 4, i * 128:(i + 1) * 128],
                        in0=pt[:, :].rearrange("p (c n) -> p c n", c=4),
                        in1=gcol[:, half * 4:(half + 1) * 4].unsqueeze(2).broadcast_to([128, 4, 128]),
                        op=ALU.mult), reads=[Tp, Tc], writes=[TnT])

        with contextlib.ExitStack() as sAC:
            QT_sb = alloc(sAC, "QT_sb", [128, 8, 2048], BF16)
            convT_sb = alloc(sAC, "convT_sb", [128, 4, 2048], BF16)
            sbT_sb = alloc(sAC, "sbT_sb", [128, 4, 2048], BF16)
            ssq_c = alloc(sAC, "ssq_c", [128, 16], F32)
            ssq_s = alloc(sAC, "ssq_s", [128, 16], F32)
            TQT, TconvT, TsbT, Tssqc, Tssqs = T("QT"), T("convT"), T("sbT"), T("ssqc"), T("ssqs")
            TkTd, Tvd = T("kT_d"), T("v_d")

            with contextlib.ExitStack() as ph:
                pT = [(palloc(ph, "pT%d" % i, [128, 512]), T("pT%d" % i)) for i in range(4)]
                pK = [(palloc(ph, "pK%d" % i, [128, 512]), T("pK%d" % i)) for i in range(2)]
                pV = [(palloc(ph, "pV%d" % i, [128, 512]), T("pV%d" % i)) for i in range(2)]
                R = make_norm_res(ph, pT)
                wkv = alloc(ph, "wkv", [128, 8, 1024], BF16)
                Twkv = T("wkv")
                P.dma(wkv[:], w_in[:, 512:1536].rearrange("(c p) n -> p c n", p=128), writes=[Twkv], qeng="pool")
                NXB = 8
                xt = [alloc(ph, "xt%d" % i, [128, 1024], F32) for i in range(NXB)]
                Txt = [T("xt%d" % i) for i in range(NXB)]
                nTb = [alloc(ph, "nT%d" % i, [128, 8, 512], BF16) for i in range(2)]
                TnTb = [T("nT0"), T("nT1")]
                kst = [alloc(ph, "kst%d" % i, [128, 4, 512], BF16) for i in range(2)]
                vst = [alloc(ph, "vst%d" % i, [128, 4, 512], BF16) for i in range(2)]
                Tkst = [T("kst0"), T("kst1")]
                Tvst = [T("vst0"), T("vst1")]
                NG = 16 if stage >= 1 else 0

                def load_group(g):
                    for tt in range(4):
                        b = (g * 4 + tt) % NXB
                        r0 = (g * 4 + tt) * 128
                        P.dma(xt[b][:], xb[r0:r0 + 128, :], writes=[Txt[b]])
                if NG:
                    load_group(0)
                for g in range(NG):
                    if g + 1 < NG:
                        load_group(g + 1)
                    nT = nTb[g % 2]
                    TnT = TnTb[g % 2]
                    srcs = [(xt[(g * 4 + tt) % NXB][:], Txt[(g * 4 + tt) % NXB]) for tt in range(4)]
                    norm_group(R, srcs, gp[:, G_MIX:G_MIX + 8], nT, TnT)
                    ks, Tks = kst[g % 2], Tkst[g % 2]
                    vs, Tvs = vst[g % 2], Tvst[g % 2]
                    for hp in range(4):
                        pk, Tpk = pK[hp % 2]
                        for dc in range(8):
                            P.op("pe", lambda e, pk=pk, dc=dc, hp=hp, nT=nT: e.matmul(
                                pk[:], lhsT=wkv[:, dc, hp * 128:(hp + 1) * 128], rhs=nT[:, dc, :],
                                start=(dc == 0), stop=(dc == 7)), reads=[Twkv, TnT], writes=[Tpk])
                        P.op("act", lambda e, pk=pk, ks=ks, hp=hp: e.activation(out=ks[:, hp, :], in_=pk[:],
                                                                              func=AF.Copy),
                             reads=[Tpk], writes=[Tks])
                    P.dma(kT_d[:, :, g * 512:(g + 1) * 512].rearrange("h p n -> p h n"), ks[:],
                          reads=[Tks], writes=[TkTd], qeng="pool")
                    for tt in range(4):
                        pv, Tpv = pV[tt % 2]
                        for dc in range(8):
                            P.op("pe", lambda e, pv=pv, dc=dc, tt=tt, nT=nT: e.matmul(
                                pv[:], lhsT=nT[:, dc, tt * 128:(tt + 1) * 128], rhs=wkv[:, dc, 512:1024],
                                start=(dc == 0), stop=(dc == 7)), reads=[Twkv, TnT], writes=[Tpv])
                        P.op("dve", lambda e, pv=pv, vs=vs, tt=tt: e.tensor_copy(out=vs[:, tt, :], in_=pv[:]),
                             reads=[Tpv], writes=[Tvs])
                    for hp in range(4):
                        P.dma(v_d[hp, :, 4 * g:4 * g + 4, :], vs[:, :, hp * 128:(hp + 1) * 128],
                              reads=[Tvs], writes=[Tvd], qeng="pool")
                P.barrier()

            with contextlib.ExitStack() as ph:
                pT = [(palloc(ph, "pT%d" % i, [128, 512]), T("pT%d" % i)) for i in range(2)]
                pQ = [(palloc(ph, "pQ%d" % i, [128, 512]), T("pQ%d" % i)) for i in range(2)]
                pG3 = [(palloc(ph, "pG%d" % i, [128, 512]), T("pG%d" % i)) for i in range(3)]
                pss = palloc(ph, "pss", [128, 512])
                Tpss = T("pss")
                R = make_norm_res(ph, pT)
                wq = alloc(ph, "wq", [128, 8, 512], BF16)
                wg = alloc(ph, "wg", [128, 8, 1536], BF16)
                Twq, Twg = T("wq"), T("wg")
                xt = [alloc(ph, "xt%d" % i, [128, 1024], F32) for i in range(8)]
                Txt = [T("xt%d" % i) for i in range(8)]
                xht = alloc(ph, "xht", [128, 1024], F32)
                Txht = T("xht")
                nTb = [alloc(ph, "nT%d" % i, [128, 8, 512], BF16) for i in range(2)]
                TnTb = [T("nT0"), T("nT1")]
                nhT = alloc(ph, "nhT", [128, 8, 128], BF16)
                TnhT = T("nhT")
                uh = alloc(ph, "uh", [128, 4, 8], F32)
                Tuh = T("uh")
                gch = alloc(ph, "gch", [128, 8], F32)
                Tgch = T("gch")
                gc_sb = alloc(ph, "gc_sb", [128, 512], F32)
                u_sb = alloc(ph, "u_sb", [128, 514], F32)
                acc = alloc(ph, "acc", [128, 512], F32)
                conv = alloc(ph, "conv", [128, 512], F32)
                sqc = alloc(ph, "sqc", [128, 512], F32)
                Tgc, Tu, Tacc, Tconv, Tsqc = T("gc"), T("u"), T("acc"), T("conv"), T("sqc")
                if stage >= 2:
                    P.dma(wq[:], w_in[:, 0:512].rearrange("(c p) n -> p c n", p=128), writes=[Twq], qeng="pool")
                    P.dma(wg[:], w_in[:, 1536:3072].rearrange("(c p) n -> p c n", p=128), writes=[Twg], qeng="pool")
                    P.op("pool", lambda e: e.memset(QT_sb[:], 0.0), writes=[TQT])
                    P.op("pool", lambda e: e.memset(xht[:], 0.0), writes=[Txht])
                    P.dma(xht[0:8, :], xh[:, :], writes=[Txht])

                    def load_slot(s):
                        for tt in range(4):
                            b = (s * 4 + tt) % 8
                            r0 = (s * 4 + tt) * 128
                            P.dma(xt[b][:], xq[r0:r0 + 128, :], writes=[Txt[b]])
                    load_slot(0)
                    norm_group(R, [(xht[:], Txht)], gp[:, G_MIX:G_MIX + 8], nhT, TnhT)
                    for cc in range(4):
                        pa, Tpa = pG3[0]
                        pb, Tpb = pG3[1]
                        for dc in range(8):
                            P.op("pe", lambda e, pa=pa, dc=dc, cc=cc: e.matmul(
                                pa[:, 0:8], lhsT=wg[:, dc, 512 + cc * 128:512 + (cc + 1) * 128], rhs=nhT[:, dc, 0:8],
                                start=(dc == 0), stop=(dc == 7)), reads=[Twg, TnhT], writes=[Tpa])
                        for dc in range(8):
                            P.op("pe", lambda e, pb=pb, dc=dc, cc=cc: e.matmul(
                                pb[:, 0:8], lhsT=wg[:, dc, 1024 + cc * 128:1024 + (cc + 1) * 128], rhs=nhT[:, dc, 0:8],
                                start=(dc == 0), stop=(dc == 7)), reads=[Twg, TnhT], writes=[Tpb])
                        P.op("act", lambda e, pa=pa: e.activation(out=gch[:], in_=pa[:, 0:8], func=AF.Copy),
                             reads=[Tpa], writes=[Tgch])
                        P.op("dve", lambda e, pb=pb, cc=cc: e.tensor_tensor(out=uh[:, cc, :], in0=gch[:], in1=pb[:, 0:8],
                                                                           op=ALU.mult),
                             reads=[Tgch, Tpb], writes=[Tuh])
                    gi = 0
                    for s in range(4):
                        if s + 1 < 4:
                            load_slot(s + 1)
                        nT, TnT = nTb[s % 2], TnTb[s % 2]
                        srcs = [(xt[(s * 4 + tt) % 8][:], Txt[(s * 4 + tt) % 8]) for tt in range(4)]
                        norm_group(R, srcs, gp[:, G_MIX:G_MIX + 8], nT, TnT)
                        for hp in range(4):
                            pq, Tpq = pQ[hp % 2]
                            for dc in range(8):
                                P.op("pe", lambda e, pq=pq, dc=dc, hp=hp, nT=nT: e.matmul(
                                    pq[:], lhsT=wq[:, dc, hp * 128:(hp + 1) * 128], rhs=nT[:, dc, :],
                                    start=(dc == 0), stop=(dc == 7)), reads=[Twq, TnT], writes=[Tpq])
                            for hd in range(2):
                                r0, r1 = hd * 64, hd * 64 + 64
                                P.op("act", lambda e, pq=pq, hp=hp, s=s, hd=hd, r0=r0, r1=r1: e.activation(
                                    out=QT_sb[r0:r1, 2 * hp + hd, s * 512:(s + 1) * 512], in_=pq[r0:r1, :],
                                    func=AF.Copy, scale=0.125), reads=[Tpq], writes=[TQT])
                        for cc in range(4):
                            banks = []
                            for col0 in (512 + cc * 128, 1024 + cc * 128, cc * 128):
                                pg, Tpg = pG3[gi % 3]
                                gi += 1
                                for dc in range(8):
                                    P.op("pe", lambda e, pg=pg, dc=dc, col0=col0, nT=nT: e.matmul(
                                        pg[:], lhsT=wg[:, dc, col0:col0 + 128], rhs=nT[:, dc, :],
                                        start=(dc == 0), stop=(dc == 7)), reads=[Twg, TnT], writes=[Tpg])
                                banks.append((pg, Tpg))
                            (pgc, Tpgc), (pxi, Tpxi), (pgb, Tpgb) = banks
                            P.op("act", lambda e, pgc=pgc: e.activation(out=gc_sb[:], in_=pgc[:], func=AF.Copy),
                                 reads=[Tpgc], writes=[Tgc])
                            P.op("dve", lambda e, cc=cc, s=s: e.tensor_copy(out=u_sb[:, 0:2], in_=uh[:, cc, 2 * s:2 * s + 2]),
                                 reads=[Tuh], writes=[Tu])
                            P.op("dve", lambda e, pxi=pxi: e.tensor_tensor(out=u_sb[:, 2:514], in0=gc_sb[:], in1=pxi[:],
                                                                          op=ALU.mult),
                                 reads=[Tgc, Tpxi], writes=[Tu])
                            P.op("dve", lambda e, cc=cc: e.tensor_scalar(
                                out=acc[:], in0=u_sb[:, 2:514], scalar1=gp[:, G_CW + cc * 3 + 2:G_CW + cc * 3 + 3],
                                scalar2=None, op0=ALU.mult), reads=[Tu, Tc], writes=[Tacc])
                            P.op("dve", lambda e, cc=cc: e.scalar_tensor_tensor(
                                out=acc[:], in0=u_sb[:, 1:513], scalar=gp[:, G_CW + cc * 3 + 1:G_CW + cc * 3 + 2],
                                in1=acc[:], op0=ALU.mult, op1=ALU.add), reads=[Tu, Tc, Tacc], writes=[Tacc])
                            P.op("dve", lambda e, cc=cc: e.scalar_tensor_tensor(
                                out=acc[:], in0=u_sb[:, 0:512], scalar=gp[:, G_CW + cc * 3:G_CW + cc * 3 + 1],
                                in1=acc[:], op0=ALU.mult, op1=ALU.add), reads=[Tu, Tc, Tacc], writes=[Tacc])
                            P.op("dve", lambda e, pgb=pgb: e.tensor_tensor(out=conv[:], in0=acc[:], in1=pgb[:],
                                                                          op=ALU.mult),
                                 reads=[Tacc, Tpgb], writes=[Tconv])
                            P.op("act", lambda e: e.activation(out=sqc[:], in_=conv[:], func=AF.Square),
                                 reads=[Tconv], writes=[Tsqc])
                            P.op("act", lambda e, cc=cc, s=s: e.activation(
                                out=convT_sb[:, cc, s * 512:(s + 1) * 512], in_=conv[:], func=AF.Copy,
                                scale=gp[:, G_CONV + cc:G_CONV + cc + 1]), reads=[Tconv, Tc], writes=[TconvT])
                            for tt in range(4):
                                col = (s * 4 + tt) * 4 + cc
                                P.op("pe", lambda e, tt=tt, col=col: e.matmul(
                                    pss[:, col:col + 1], lhsT=sqc[:, tt * 128:(tt + 1) * 128], rhs=ones_f[:, 0:1],
                                    start=True, stop=True), reads=[Tsqc, Tc], writes=[Tpss])
                    P.op("dve", lambda e: e.tensor_reduce(out=ssq_c[:], in_=pss[:, 0:64].rearrange("p (t c) -> p t c", c=4),
                                                          axis=AX.X, op=ALU.add), reads=[Tpss], writes=[Tssqc])
                P.barrier()

            with contextlib.ExitStack() as ph:
                pz = [(palloc(ph, "pz%d" % i, [128, 512]), T("pz%d" % i)) for i in range(3)]
                pGc = [(palloc(ph, "pGc%d" % i, [128, 512]), T("pGc%d" % i)) for i in range(2)]
                pO = [(palloc(ph, "pO%d" % i, [128, 512]), T("pO%d" % i)) for i in range(2)]
                pss = palloc(ph, "pssb", [128, 512])
                Tpss = T("pssb")
                msk = alloc(ph, "msk", [128, 16, 512], BF16)
                Tmsk = T("msk")
                KT_sb = [alloc(ph, "KT%d" % i, [128, 8192], BF16) for i in range(2)]
                V_sb = [alloc(ph, "V%d" % i, [128, 64, 128], BF16) for i in range(2)]
                TKT = [T("KT0"), T("KT1")]
                TV = [T("V0"), T("V1")]
                NB = 4
                e1 = [alloc(ph, "e1_%d" % i, [128, 512], F32) for i in range(NB)]
                sp = [alloc(ph, "sp_%d" % i, [128, 512], F32) for i in range(NB)]
                Lb = [alloc(ph, "Lb_%d" % i, [128, 512], BF16) for i in range(NB)]
                t2 = [alloc(ph, "t2_%d" % i, [128, 512], F32) for i in range(NB)]
                Ab = [alloc(ph, "Ab_%d" % i, [128, 512], BF16) for i in range(NB)]
                tmpf = [alloc(ph, "tmpf_%d" % i, [128, 512], F32) for i in range(2)]
                Te1 = [T("e1") for _ in range(NB)]
                Tsp = [T("sp") for _ in range(NB)]
                TLb = [T("Lb") for _ in range(NB)]
                Tt2 = [T("t2") for _ in range(NB)]
                TAb = [T("Ab") for _ in range(NB)]
                Ttmpf = [T("tmpf0"), T("tmpf1")]
                sqs = alloc(ph, "sqs", [128, 512], F32)
                Tsqs = T("sqs")
                if stage >= 3:
                    P.dma(msk[:], maskd[:, :, :], writes=[Tmsk])
                    pairs = [(s, hp) for s in range(4) for hp in range(4)]

                    def load_kv(i):
                        s, hp = pairs[i]
                        b = i % 2
                        nk = (4 * s + 4) * 512
                        nblk = nk // 128
                        P.dma(KT_sb[b][:, 0:nk], kT_d[hp, :, 0:nk], reads=[TkTd], writes=[TKT[b]])
                        P.dma(V_sb[b][:, 0:nblk, :], v_d[hp, :, 0:nblk, :], reads=[Tvd], writes=[TV[b]])

                    steps = []
                    for i, (s, hp) in enumerate(pairs):
                        nblk = (4 * s + 4) * 4
                        for blk in range(nblk - 1, -1, -1):
                            for hd in range(2):
                                steps.append(dict(i=i, s=s, hp=hp, blk=blk, hd=hd, first=(blk == nblk - 1),
                                                  last=(blk == 0), pfirst=(blk == nblk - 1 and hd == 0),
                                                  plast=(blk == 0 and hd == 1)))
                    NS = len(steps)

                    def info(n):
                        d = steps[n]
                        blk, s = d["blk"], d["s"]
                        j, kb = blk // 4, blk % 4
                        return d, (j >= 4 * s), (j - 4 * s) * 4 + kb

                    def pe_z(n):
                        d = steps[n]
                        b = d["i"] % 2
                        z, Tz = pz[n % 3]
                        KT = KT_sb[b]
                        blk, hp, hd, s = d["blk"], d["hp"], d["hd"], d["s"]
                        P.op("pe", lambda e: e.matmul(
                            z[:], lhsT=KT[:, blk * 128:(blk + 1) * 128],
                            rhs=QT_sb[:, 2 * hp + hd, s * 512:(s + 1) * 512], start=True, stop=True),
                            reads=[TKT[b], TQT], writes=[Tz])

                    def act_s1(n):
                        z, Tz = pz[n % 3]
                        k = n % NB
                        P.op("act", lambda e: e.activation(out=e1[k][:], in_=z[:], func=AF.Exp, scale=-1.0),
                             reads=[Tz], writes=[Te1[k]])
                        P.op("act", lambda e: e.activation(out=sp[k][:], in_=e1[k][:], func=AF.Ln, bias=1.0),
                             reads=[Te1[k]], writes=[Tsp[k]])

                    def dve_L(n):
                        d, masked, mi = info(n)
                        z, Tz = pz[n % 3]
                        k = n % NB
                        if not masked:
                            P.op("dve", lambda e: e.tensor_tensor(out=Lb[k][:], in0=z[:], in1=sp[k][:], op=ALU.add),
                                 reads=[Tz, Tsp[k]], writes=[TLb[k]])
                        else:
                            tf, Ttf = tmpf[n % 2], Ttmpf[n % 2]
                            P.op("dve", lambda e: e.tensor_tensor(out=tf[:], in0=z[:], in1=sp[k][:], op=ALU.add),
                                 reads=[Tz, Tsp[k]], writes=[Ttf])
                            P.op("pool", lambda e: e.tensor_tensor(out=Lb[k][:], in0=tf[:], in1=msk[:, mi, :], op=ALU.mult),
                                 reads=[Ttf, Tmsk], writes=[TLb[k]])

                    def pe_mm1(n):
                        d = steps[n]
                        k = n % NB
                        G, TG = pGc[d["hd"]]
                        first = d["first"]
                        P.op("pe", lambda e: e.matmul(G[:], lhsT=Uneg[:], rhs=Lb[k][:], start=first, stop=True),
                             reads=[TLb[k], Tc], writes=[TG])

                    def dve_t2(n):
                        d = steps[n]
                        k = n % NB
                        G, TG = pGc[d["hd"]]
                        P.op("dve", lambda e: e.tensor_tensor(out=t2[k][:], in0=G[:], in1=sp[k][:], op=ALU.subtract),
                             reads=[TG, Tsp[k]], writes=[Tt2[k]])

                    def pe_mm2(n):
                        d = steps[n]
                        if d["last"]:
                            return
                        k = n % NB
                        G, TG = pGc[d["hd"]]
                        P.op("pe", lambda e: e.matmul(G[:], lhsT=Unegb[:], rhs=Lb[k][:], start=False, stop=True),
                             reads=[TLb[k], Tc], writes=[TG])

                    def act_A(n):
                        d, masked, mi = info(n)
                        k = n % NB
                        P.op("act", lambda e: e.activation(out=Ab[k][:], in_=t2[k][:], func=AF.Exp),
                             reads=[Tt2[k]], writes=[TAb[k]])
                        if masked:
                            P.op("dve", lambda e: e.tensor_tensor(out=Ab[k][:], in0=Ab[k][:], in1=msk[:, mi, :], op=ALU.mult),
                                 reads=[TAb[k], Tmsk], writes=[TAb[k]])

                    def pe_O(n):
                        d = steps[n]
                        b = d["i"] % 2
                        k = n % NB
                        blk, hd, s, hp = d["blk"], d["hd"], d["s"], d["hp"]
                        O, TO = pO[hd]
                        V = V_sb[b]
                        first, last = d["first"], d["last"]
                        P.op("pe", lambda e: e.matmul(O[:], lhsT=V[:, blk, :], rhs=Ab[k][:], start=first, stop=last),
                             reads=[TV[b], TAb[k]], writes=[TO])
                        if d["plast"]:
                            for hd2 in range(2):
                                r0, r1 = hd2 * 64, hd2 * 64 + 64
                                O2, TO2 = pO[hd2]
                                P.op("act", lambda e, O2=O2, r0=r0, r1=r1: e.activation(
                                    out=sqs[r0:r1, :], in_=O2[r0:r1, :], func=AF.Square), reads=[], writes=[Tsqs, TO2])
                                P.op("dve", lambda e, O2=O2, r0=r0, r1=r1: e.tensor_scalar(
                                    out=sbT_sb[r0:r1, hp, s * 512:(s + 1) * 512], in0=O2[r0:r1, :],
                                    scalar1=gp[r0:r1, G_SB + hp:G_SB + hp + 1], scalar2=None, op0=ALU.mult),
                                    reads=[Tc], writes=[TsbT, TO2])
                            for tt in range(4):
                                col = (s * 4 + tt) * 4 + hp
                                P.op("pe", lambda e, tt=tt, col=col: e.matmul(
                                    pss[:, col:col + 1], lhsT=sqs[:, tt * 128:(tt + 1) * 128], rhs=ones_f[:, 0:1],
                                    start=True, stop=True), reads=[Tsqs, Tc], writes=[Tpss])

                    load_kv(0)
                    ok = lambda m: 0 <= m < NS
                    for n in range(NS + 3):
                        if ok(n - 3):
                            act_A(n - 3)
                        if ok(n):
                            pe_z(n)
                            act_s1(n)
                        if ok(n - 1):
                            dve_L(n - 1)
                        if ok(n - 3):
                            pe_O(n - 3)
                            if steps[n - 3]["pfirst"]:
                                ni = steps[n - 3]["i"] + 1
                                if ni < len(pairs):
                                    load_kv(ni)
                        if ok(n - 2):
                            pe_mm2(n - 2)
                        if ok(n - 1):
                            pe_mm1(n - 1)
                            dve_t2(n - 1)
                    P.op("dve", lambda e: e.tensor_reduce(out=ssq_s[:], in_=pss[:, 0:64].rearrange("p (t c) -> p t c", c=4),
                                                          axis=AX.X, op=ALU.add), reads=[Tpss], writes=[Tssqs])
                P.barrier()

            if "d_sbT" in dbg_out:
                P.dma(dbg_out["d_sbT"].rearrange("c p n -> p c n"), sbT_sb[:], reads=[TsbT], writes=[T("x")])
                P.dma(dbg_out["d_convT"].rearrange("c p n -> p c n"), convT_sb[:], reads=[TconvT], writes=[T("x")])
                P.dma(dbg_out["d_ssq"][:, 0:16], ssq_s[:], reads=[Tssqs], writes=[T("x")])
                P.dma(dbg_out["d_ssq"][:, 16:32], ssq_c[:], reads=[Tssqc], writes=[T("x")])
                P.barrier()

            with contextlib.ExitStack() as ph:
                pP = [(palloc(ph, "pP%d" % i, [128, 512]), T("pP%d" % i)) for i in range(8)]
                wo = alloc(ph, "wo", [128, 8, 1024], BF16)
                Two = T("wo")
                rs = alloc(ph, "rs", [128, 32], F32)
                Trs = T("rs")
                xt = [alloc(ph, "xt%d" % i, [128, 1024], F32) for i in range(4)]
                Txt = [T("xt%d" % i) for i in range(4)]
                ht = [alloc(ph, "ht%d" % i, [128, 1024], F32) for i in range(2)]
                Tht = [T("ht0"), T("ht1")]
                Thd = T("h_d")
                if stage >= 4:
                    P.dma(wo[:], w_out[:, :].rearrange("(c p) n -> p c n", p=128), writes=[Two], qeng="pool")
                    P.op("dve", lambda e: e.tensor_scalar(out=rs[:, 0:16], in0=ssq_s[:], scalar1=1.0 / 512, scalar2=EPS,
                                                          op0=ALU.mult, op1=ALU.add), reads=[Tssqs], writes=[Trs])
                    P.op("dve", lambda e: e.tensor_scalar(out=rs[:, 16:32], in0=ssq_c[:], scalar1=1.0 / 512, scalar2=EPS,
                                                          op0=ALU.mult, op1=ALU.add), reads=[Tssqc, Trs], writes=[Trs])
                    P.op("act", lambda e: e.activation(out=rs[:], in_=rs[:], func=AF.Sqrt), reads=[Trs], writes=[Trs])
                    P.op("dve", lambda e: e.reciprocal(out=rs[:], in_=rs[:]), reads=[Trs], writes=[Trs])
                    for t in range(16):
                        xa, Txa = xt[t % 4], Txt[t % 4]
                        P.dma(xa[:], xq[t * 128:(t + 1) * 128, :], writes=[Txa])
                        h, Th = ht[t % 2], Tht[t % 2]
                        banks = [pP[(t % 2) * 4 + i] for i in range(4)]
                        for src, (Tsrc) in ((0, TsbT), (1, TconvT)):
                            srcT = sbT_sb if src == 0 else convT_sb
                            for half in range(2):
                                pb, Tpb = banks[src * 2 + half]
                                for c in range(4):
                                    P.op("pe", lambda e, pb=pb, srcT=srcT, c=c, t=t, src=src, half=half: e.matmul(
                                        pb[:], lhsT=srcT[:, c, t * 128:(t + 1) * 128],
                                        rhs=wo[:, src * 4 + c, half * 512:(half + 1) * 512],
                                        start=(c == 0), stop=(c == 3)), reads=[Tsrc, Two], writes=[Tpb])
                        for half in range(2):
                            pb, Tpb = banks[half]
                            P.op("dve", lambda e, pb=pb, h=h, xa=xa, half=half, t=t: e.scalar_tensor_tensor(
                                out=h[:, half * 512:(half + 1) * 512], in0=pb[:], scalar=rs[:, t:t + 1],
                                in1=xa[:, half * 512:(half + 1) * 512], op0=ALU.mult, op1=ALU.add),
                                reads=[Tpb, Trs, Txa], writes=[Th])
                        for half in range(2):
                            pb, Tpb = banks[2 + half]
                            P.op("dve", lambda e, pb=pb, h=h, half=half, t=t: e.scalar_tensor_tensor(
                                out=h[:, half * 512:(half + 1) * 512], in0=pb[:], scalar=rs[:, 16 + t:17 + t],
                                in1=h[:, half * 512:(half + 1) * 512], op0=ALU.mult, op1=ALU.add),
                                reads=[Tpb, Trs, Th], writes=[Th])
                        P.dma(h_d[t * 128:(t + 1) * 128, :], h[:], reads=[Th], writes=[Thd], qeng="pool")
                P.barrier()

        if "d_h1" in dbg_out:
            with contextlib.ExitStack() as ph:
                tmp = alloc(ph, "dbgtmp", [128, 16, 1024], F32)
                Tt = T("dbgtmp")
                P.dma(tmp[:], h_d.rearrange("(t p) n -> p t n", p=128), writes=[Tt])
                P.dma(dbg_out["d_h1"].rearrange("(t p) n -> p t n", p=128), tmp[:], reads=[Tt], writes=[T("x")])
                P.barrier()

        Thd = T("h_d")
        with contextlib.ExitStack() as ph:
            pT = [(palloc(ph, "pT%d" % i, [128, 512]), T("pT%d" % i)) for i in range(2)]
            pA = [(palloc(ph, "pA%d" % i, [128, 512]), T("pA%d" % i)) for i in range(2)]
            psc = palloc(ph, "psc", [128, 512])
            Tpsc = T("psc")
            pTp = palloc(ph, "pTp", [128, 1024], BF16)
            TpTp = T("pTp")
            poT = [(palloc(ph, "poT%d" % i, [128, 512]), T("poT%d" % i)) for i in range(2)]
            R = make_norm_res(ph, pT)
            wqm = alloc(ph, "wqm", [128, 8, 1024], BF16)
            wkvm = alloc(ph, "wkvm", [128, 8, 2048], BF16)
            wom = alloc(ph, "wom", [128, 8, 1024], BF16)
            Twqm, Twkvm, Twom = T("wqm"), T("wkvm"), T("wom")
            memt = [alloc(ph, "memt%d" % i, [128, 1024], F32) for i in range(2)]
            Tmemt = [T("memt0"), T("memt1")]
            memT = alloc(ph, "memT", [128, 8, 256], BF16)
            TmemT = T("memT")
            kTm = alloc(ph, "kTm", [128, 8, 256], BF16)
            vm = alloc(ph, "vm", [128, 2, 1024], BF16)
            TkTm, Tvm = T("kTm"), T("vm")
            ht = [alloc(ph, "ht%d" % i, [128, 1024], F32) for i in range(8)]
            Tht = [T("ht%d" % i) for i in range(8)]
            n2T = [alloc(ph, "n2T%d" % i, [128, 8, 512], BF16) for i in range(2)]
            Tn2T = [T("n2T0"), T("n2T1")]
            qTm = alloc(ph, "qTm", [128, 8, 512], BF16)
            TqTm = T("qTm")
            nmx = alloc(ph, "nmx", [128, 4], F32)
            rsum = alloc(ph, "rsum", [128, 4], F32)
            Tnmx = [T("nmx%d" % i) for i in range(4)]
            Trsum = [T("rsum%d" % i) for i in range(4)]
            pexp = [alloc(ph, "pexp%d" % i, [128, 256], F32) for i in range(2)]
            pn = [alloc(ph, "pn%d" % i, [128, 256], BF16) for i in range(2)]
            pTs = [alloc(ph, "pTs%d" % i, [128, 256], BF16) for i in range(2)]
            Tpexp = [T("pexp0"), T("pexp1")]
            Tpn = [T("pn0"), T("pn1")]
            TpTs = [T("pTs0"), T("pTs1")]
            oT_sb = alloc(ph, "oT_sb", [128, 8, 128], BF16)
            ToT = T("oT_sb")
            if stage >= 5:
                P.dma(wqm[:], w_q_mem[:, :].rearrange("(c p) n -> p c n", p=128), writes=[Twqm], qeng="pool")
                P.dma(wkvm[:], w_kv_mem[:, :].rearrange("(c p) n -> p c n", p=128), writes=[Twkvm], qeng="pool")
                P.dma(wom[:], w_o_mem[:, :].rearrange("(c p) n -> p c n", p=128), writes=[Twom], qeng="pool")
                for i in range(2):
                    P.dma(memt[i][:], memb[i * 128:(i + 1) * 128, :], writes=[Tmemt[i]])

                def load_hgroup(g):
                    for tt in range(4):
                        bb = (g * 4 + tt) % 8
                        r0 = (g * 4 + tt) * 128
                        P.dma(ht[bb][:], h_d[r0:r0 + 128, :], reads=[Thd], writes=[Tht[bb]])
                load_hgroup(0)
                norm_group(R, [(memt[0][:], Tmemt[0]), (memt[1][:], Tmemt[1])], gp[:, G_MEM:G_MEM + 8], memT, TmemT)
                for c in range(8):
                    pa, Tpa = pA[c % 2]
                    for dc in range(8):
                        P.op("pe", lambda e, pa=pa, dc=dc, c=c: e.matmul(
                            pa[:, 0:256], lhsT=wkvm[:, dc, c * 128:(c + 1) * 128], rhs=memT[:, dc, :],
                            start=(dc == 0), stop=(dc == 7)), reads=[Twkvm, TmemT], writes=[Tpa])
                    P.op("act", lambda e, pa=pa, c=c: e.activation(out=kTm[:, c, :], in_=pa[:, 0:256], func=AF.Copy),
                         reads=[Tpa], writes=[TkTm])
                for mc in range(2):
                    for half in range(2):
                        pa, Tpa = pA[half]
                        for dc in range(8):
                            P.op("pe", lambda e, pa=pa, dc=dc, mc=mc, half=half: e.matmul(
                                pa[:], lhsT=memT[:, dc, mc * 128:(mc + 1) * 128],
                                rhs=wkvm[:, dc, 1024 + half * 512:1024 + (half + 1) * 512],
                                start=(dc == 0), stop=(dc == 7)), reads=[Twkvm, TmemT], writes=[Tpa])
                        P.op("act", lambda e, pa=pa, mc=mc, half=half: e.activation(
                            out=vm[:, mc, half * 512:(half + 1) * 512], in_=pa[:], func=AF.Copy),
                            reads=[Tpa], writes=[Tvm])
                hk = 0
                for g in range(4):
                    if g + 1 < 4:
                        load_hgroup(g + 1)
                    nT, TnT = n2T[g % 2], Tn2T[g % 2]
                    srcs = [(ht[(g * 4 + tt) % 8][:], Tht[(g * 4 + tt) % 8]) for tt in range(4)]
                    norm_group(R, srcs, gp[:, G_XATTN:G_XATTN + 8], nT, TnT)
                    for c in range(8):
                        pa, Tpa = pA[c % 2]
                        for dc in range(8):
                            P.op("pe", lambda e, pa=pa, dc=dc, c=c, nT=nT: e.matmul(
                                pa[:], lhsT=wqm[:, dc, c * 128:(c + 1) * 128], rhs=nT[:, dc, :],
                                start=(dc == 0), stop=(dc == 7)), reads=[Twqm, TnT], writes=[Tpa])
                        P.op("act", lambda e, pa=pa, c=c: e.activation(out=qTm[:, c, :], in_=pa[:], func=AF.Copy,
                                                                      scale=1.0 / 16), reads=[Tpa], writes=[TqTm])
                    for tt in range(4):
                        bb = (g * 4 + tt) % 8
                        h, Th = ht[bb], Tht[bb]
                        for hd in range(4):
                            k2 = hk % 2
                            hk += 1
                            for c in range(2):
                                P.op("pe", lambda e, c=c, hd=hd, tt=tt: e.matmul(
                                    psc[:, 0:256], lhsT=qTm[:, 2 * hd + c, tt * 128:(tt + 1) * 128],
                                    rhs=kTm[:, 2 * hd + c, :], start=(c == 0), stop=(c == 1)),
                                    reads=[TqTm, TkTm], writes=[Tpsc])
                            P.op("dve", lambda e, hd=hd: e.tensor_reduce(out=nmx[:, hd:hd + 1], in_=psc[:, 0:256],
                                                                        axis=AX.X, op=ALU.max, negate=True),
                                 reads=[Tpsc], writes=[Tnmx[hd]])
                            P.op("act", lambda e, hd=hd, k2=k2: e.activation(
                                out=pexp[k2][:], in_=psc[:, 0:256], func=AF.Exp, bias=nmx[:, hd:hd + 1],
                                accum_out=rsum[:, hd:hd + 1]), reads=[Tnmx[hd]], writes=[Tpexp[k2], Trsum[hd], Tpsc])
                            P.op("dve", lambda e, hd=hd: e.reciprocal(out=rsum[:, hd:hd + 1], in_=rsum[:, hd:hd + 1]),
                                 reads=[Trsum[hd]], writes=[Trsum[hd]])
                            P.op("dve", lambda e, hd=hd, k2=k2: e.tensor_scalar(
                                out=pn[k2][:], in0=pexp[k2][:], scalar1=rsum[:, hd:hd + 1], scalar2=None, op0=ALU.mult),
                                reads=[Tpexp[k2], Trsum[hd]], writes=[Tpn[k2]])
                            for mc in range(2):
                                P.op("pe", lambda e, mc=mc, k2=k2: e.transpose(
                                    out=pTp[:, mc * 128:(mc + 1) * 128], in_=pn[k2][:, mc * 128:(mc + 1) * 128],
                                    identity=ident_b[:]), reads=[Tpn[k2], Tc], writes=[TpTp])
                            P.op("act", lambda e, k2=k2: e.activation(out=pTs[k2][:], in_=pTp[:, 0:256], func=AF.Copy),
                                 reads=[TpTp], writes=[TpTs[k2]])
                            for dch in range(2):
                                ch = 2 * hd + dch
                                po, Tpo = poT[ch // 4]
                                for mc in range(2):
                                    P.op("pe", lambda e, po=po, ch=ch, mc=mc, hd=hd, dch=dch, k2=k2: e.matmul(
                                        po[:, (ch % 4) * 128:(ch % 4 + 1) * 128],
                                        lhsT=vm[:, mc, hd * 256 + dch * 128:hd * 256 + (dch + 1) * 128],
                                        rhs=pTs[k2][:, mc * 128:(mc + 1) * 128], start=(mc == 0), stop=(mc == 1)),
                                        reads=[Tvm, TpTs[k2]], writes=[Tpo])
                        for i2 in range(2):
                            po, Tpo = poT[i2]
                            P.op("dve" if i2 == 0 else "act",
                                 (lambda e, po=po, i2=i2: e.tensor_copy(
                                     out=oT_sb[:, i2 * 4:(i2 + 1) * 4, :].rearrange("p c n -> p (c n)"), in_=po[:]))
                                 if i2 == 0 else
                                 (lambda e, po=po, i2=i2: e.activation(
                                     out=oT_sb[:, i2 * 4:(i2 + 1) * 4, :].rearrange("p c n -> p (c n)"), in_=po[:],
                                     func=AF.Copy)),
                                 reads=[Tpo], writes=[ToT])
                        for half in range(2):
                            pa, Tpa = pA[half]
                            for c in range(8):
                                P.op("pe", lambda e, pa=pa, c=c, half=half: e.matmul(
                                    pa[:], lhsT=oT_sb[:, c, :], rhs=wom[:, c, half * 512:(half + 1) * 512],
                                    start=(c == 0), stop=(c == 7)), reads=[ToT, Twom], writes=[Tpa])
                            P.op("dve", lambda e, pa=pa, h=h, half=half: e.tensor_tensor(
                                out=h[:, half * 512:(half + 1) * 512], in0=pa[:], in1=h[:, half * 512:(half + 1) * 512],
                                op=ALU.add), reads=[Tpa, Th], writes=[Th])
                        r0 = (g * 4 + tt) * 128
                        P.dma(h_d[r0:r0 + 128, :], h[:], reads=[Th], writes=[Thd], qeng="pool")
            P.barrier()

        if "d_h2" in dbg_out:
            with contextlib.ExitStack() as ph:
                tmp = alloc(ph, "dbgtmp2", [128, 16, 1024], F32)
                Tt = T("dbgtmp2")
                P.dma(tmp[:], h_d.rearrange("(t p) n -> p t n", p=128), reads=[Thd], writes=[Tt])
                P.dma(dbg_out["d_h2"].rearrange("(t p) n -> p t n", p=128), tmp[:], reads=[Tt], writes=[T("x")])
                P.barrier()

        with contextlib.ExitStack() as sE:
            n3T = alloc(sE, "n3T", [128, 8, 2048], BF16)
            Tn3T = T("n3T")
            IDX0 = alloc(sE, "IDX0", [128, 16, 128], F32)
            IDX1 = alloc(sE, "IDX1", [128, 16, 128], F32)
            GATE = alloc(sE, "GATE", [128, 16, 128], F32)
            TIDX = [T("IDX%d" % i) for i in range(16)]
            iota128 = alloc(sE, "iota128", [128, 128], F32)
            c16 = alloc(sE, "c16", [128, 16], F32)
            i16 = alloc(sE, "i16", [128, 16], F32)
            Tci = T("peer_consts")
            with contextlib.ExitStack() as ph:
                pT = [(palloc(ph, "pT%d" % i, [128, 512]), T("pT%d" % i)) for i in range(2)]
                pA = [(palloc(ph, "pA%d" % i, [128, 512]), T("pA%d" % i)) for i in range(2)]
                pscr = palloc(ph, "pscr", [128, 2048])
                Tpscr = T("pscr")
                R = make_norm_res(ph, pT)
                wqp = alloc(ph, "wqp", [128, 8, 2048], BF16)
                skb = alloc(ph, "skb", [128, 16, 128], BF16)
                Twqp, Tskb = T("wqp"), T("skb")
                ht = [alloc(ph, "ht%d" % i, [128, 1024], F32) for i in range(8)]
                Tht = [T("ht%d" % i) for i in range(8)]
                qTp = alloc(ph, "qTp", [128, 16, 512], BF16)
                TqTp = T("qTp")
                sc_sb = alloc(ph, "sc_sb", [128, 2048], F32)
                Tsc = T("sc_sb")
                scw = alloc(ph, "scw", [128, 256], F32)
                Tscw = T("scw")
                top_s = alloc(ph, "top_s", [128, 16, 16], F32)
                top_i = alloc(ph, "top_i", [128, 16, 16], U32)
                top_if = alloc(ph, "top_if", [128, 16, 16], F32)
                Ttop = T("top")
                cand = alloc(ph, "cand", [128, 8, 256], F32)
                Tcand = T("cand")
                best_s = alloc(ph, "best_s", [128, 8, 16], F32)
                best_j = alloc(ph, "best_j", [128, 8, 16], U32)
                jf = alloc(ph, "jf", [128, 8, 16], F32)
                Tbest = T("best")
                big = [alloc(ph, "big%d" % i, [128, 8, 16, 16], F32) for i in range(3)]
                Tbig = [T("big%d" % i) for i in range(3)]
                sm = [alloc(ph, "sm%d" % i, [128, 8, 16], F32) for i in range(3)]
                Tsm = [T("sm%d" % i) for i in range(3)]
                s8 = alloc(ph, "s8", [128, 8], F32)
                Ts8 = T("s8")
                if stage >= 6:
                    P.dma(wqp[:], w_query[:, :].rearrange("(c p) n -> p c n", p=128), writes=[Twqp], qeng="pool")
                    P.dma(skb[:], skT[:, :, :], writes=[Tskb], qeng="pool")
                    P.op("pool", lambda e: e.iota(iota128[:], pattern=[[1, 128]], base=0, channel_multiplier=0,
                                                  allow_small_or_imprecise_dtypes=True), writes=[Tci])
                    P.op("pool", lambda e: e.iota(c16[:], pattern=[[16, 16]], base=0, channel_multiplier=0,
                                                  allow_small_or_imprecise_dtypes=True), writes=[Tci])
                    P.op("pool", lambda e: e.iota(i16[:], pattern=[[1, 16]], base=0, channel_multiplier=0,
                                                  allow_small_or_imprecise_dtypes=True), writes=[Tci])

                    def load_hgroup(g):
                        for tt in range(4):
                            bb = (g * 4 + tt) % 8
                            r0 = (g * 4 + tt) * 128
                            P.dma(ht[bb][:], h_d[r0:r0 + 128, :], reads=[Thd], writes=[Tht[bb]])
                    load_hgroup(0)
                    B4 = [128, 8, 16, 16]
                    for g in range(4):
                        if g + 1 < 4:
                            load_hgroup(g + 1)
                        srcs = [(ht[(g * 4 + tt) % 8][:], Tht[(g * 4 + tt) % 8]) for tt in range(4)]
                        nTg = n3T[:, :, g * 512:(g + 1) * 512]
                        norm_group(R, srcs, gp[:, G_FFN:G_FFN + 8], nTg, Tn3T)
                        for c in range(16):
                            pa, Tpa = pA[c % 2]
                            for dc in range(8):
                                P.op("pe", lambda e, pa=pa, dc=dc, c=c, g=g: e.matmul(
                                    pa[:], lhsT=wqp[:, dc, c * 128:(c + 1) * 128], rhs=n3T[:, dc, g * 512:(g + 1) * 512],
                                    start=(dc == 0), stop=(dc == 7)), reads=[Twqp, Tn3T], writes=[Tpa])
                            P.op("act", lambda e, pa=pa, c=c: e.activation(out=qTp[:, c, :], in_=pa[:], func=AF.Copy),
                                 reads=[Tpa], writes=[TqTp])
                        for tt in range(4):
                            t = g * 4 + tt
                            for hc in range(16):
                                P.op("pe", lambda e, hc=hc, tt=tt: e.matmul(
                                    pscr[:, hc * 128:(hc + 1) * 128], lhsT=qTp[:, hc, tt * 128:(tt + 1) * 128],
                                    rhs=skb[:, hc, :], start=True, stop=True), reads=[TqTp, Tskb], writes=[Tpscr])
                            P.op("act", lambda e: e.activation(out=sc_sb[:], in_=pscr[:], func=AF.Copy),
                                 reads=[Tpscr], writes=[Tsc])
                            for hc in range(16):
                                src = sc_sb[:, hc * 128:(hc + 1) * 128]
                                P.op("dve", lambda e, hc=hc, src=src: e.max(out=top_s[:, hc, 0:8], in_=src),
                                     reads=[Tsc], writes=[Ttop])
                                P.op("dve", lambda e, hc=hc, src=src: e.max_index(out=top_i[:, hc, 0:8],
                                                                                in_max=top_s[:, hc, 0:8], in_values=src),
                                     reads=[Tsc, Ttop], writes=[Ttop])
                                P.op("dve", lambda e, hc=hc, src=src: e.match_replace(
                                    out=scw[:, 0:128], in_to_replace=top_s[:, hc, 0:8], in_values=src, imm_value=-1e30),
                                    reads=[Tsc, Ttop], writes=[Tscw])
                                P.op("dve", lambda e, hc=hc: e.max(out=top_s[:, hc, 8:16], in_=scw[:, 0:128]),
                                     reads=[Tscw], writes=[Ttop])
                                P.op("dve", lambda e, hc=hc: e.max_index(out=top_i[:, hc, 8:16],
                                                                        in_max=top_s[:, hc, 8:16], in_values=scw[:, 0:128]),
                                     reads=[Tscw, Ttop], writes=[Ttop])
                            P.op("dve", lambda e: e.tensor_copy(out=top_if[:], in_=top_i[:]), reads=[Ttop], writes=[Ttop])
                            ts4 = top_s[:, :, :].rearrange("p (h c) k -> p h c k", c=2)
                            ti4 = top_if[:, :, :].rearrange("p (h c) k -> p h c k", c=2)
                            P.op("dve", lambda e, ts4=ts4: e.tensor_tensor(
                                out=cand[:, :, :].rearrange("p h (a b) -> p h a b", b=16),
                                in0=ts4[:, :, 0, :].unsqueeze(3).broadcast_to(B4),
                                in1=ts4[:, :, 1, :].unsqueeze(2).broadcast_to(B4), op=ALU.add),
                                reads=[Ttop], writes=[Tcand])
                            for h8 in range(8):
                                src = cand[:, h8, :]
                                P.op("dve", lambda e, h8=h8, src=src: e.max(out=best_s[:, h8, 0:8], in_=src),
                                     reads=[Tcand], writes=[Tbest])
                                P.op("dve", lambda e, h8=h8, src=src: e.max_index(out=best_j[:, h8, 0:8],
                                                                                in_max=best_s[:, h8, 0:8], in_values=src),
                                     reads=[Tcand, Tbest], writes=[Tbest])
                                P.op("dve", lambda e, h8=h8, src=src: e.match_replace(
                                    out=scw[:, 0:256], in_to_replace=best_s[:, h8, 0:8], in_values=src, imm_value=-1e30),
                                    reads=[Tcand, Tbest], writes=[Tscw])
                                P.op("dve", lambda e, h8=h8: e.max(out=best_s[:, h8, 8:16], in_=scw[:, 0:256]),
                                     reads=[Tscw], writes=[Tbest])
                                P.op("dve", lambda e, h8=h8: e.max_index(out=best_j[:, h8, 8:16],
                                                                        in_max=best_s[:, h8, 8:16], in_values=scw[:, 0:256]),
                                     reads=[Tscw, Tbest], writes=[Tbest])
                            P.op("dve", lambda e: e.tensor_copy(out=jf[:], in_=best_j[:]), reads=[Tbest], writes=[Tbest])
                            c16b = c16[:, :].unsqueeze(1).unsqueeze(1).broadcast_to(B4)
                            i16b = i16[:, :].unsqueeze(1).unsqueeze(1).broadcast_to(B4)
                            P.op("dve", lambda e, c16b=c16b: e.tensor_tensor(
                                out=big[0][:], in0=jf[:, :, :].unsqueeze(3).broadcast_to(B4), in1=c16b, op=ALU.subtract),
                                reads=[Tbest, Tci], writes=[Tbig[0]])
                            P.op("dve", lambda e: e.tensor_scalar(out=big[1][:], in0=big[0][:], scalar1=0.0, scalar2=None,
                                                                  op0=ALU.is_ge), reads=[Tbig[0]], writes=[Tbig[1]])
                            P.op("dve", lambda e: e.scalar_tensor_tensor(out=big[2][:], in0=big[0][:], scalar=16.0,
                                                                         in1=big[1][:], op0=ALU.is_lt, op1=ALU.mult),
                                 reads=[Tbig[0], Tbig[1]], writes=[Tbig[2]])
                            P.op("dve", lambda e, ti4=ti4: e.tensor_tensor(
                                out=big[0][:], in0=big[2][:], in1=ti4[:, :, 0, :].unsqueeze(2).broadcast_to(B4), op=ALU.mult),
                                reads=[Tbig[2], Ttop], writes=[Tbig[0]])
                            P.op("dve", lambda e, t=t: e.tensor_reduce(
                                out=IDX0[:, t, :].rearrange("p (h k) -> p h k", k=16), in_=big[0][:], axis=AX.X, op=ALU.add),
                                reads=[Tbig[0]], writes=[TIDX[t]])
                            P.op("dve", lambda e, i16b=i16b: e.tensor_tensor(out=big[1][:], in0=big[2][:], in1=i16b, op=ALU.mult),
                                 reads=[Tbig[2], Tci], writes=[Tbig[1]])
                            P.op("dve", lambda e: e.tensor_reduce(out=sm[0][:], in_=big[1][:], axis=AX.X, op=ALU.add),
                                 reads=[Tbig[1]], writes=[Tsm[0]])
                            P.op("dve", lambda e: e.scalar_tensor_tensor(out=sm[1][:], in0=sm[0][:], scalar=-16.0, in1=jf[:],
                                                                         op0=ALU.mult, op1=ALU.add),
                                 reads=[Tsm[0], Tbest], writes=[Tsm[1]])
                            P.op("dve", lambda e, i16b=i16b: e.tensor_tensor(
                                out=big[0][:], in0=sm[1][:, :, :].unsqueeze(3).broadcast_to(B4), in1=i16b, op=ALU.is_equal),
                                reads=[Tsm[1], Tci], writes=[Tbig[0]])
                            P.op("dve", lambda e, ti4=ti4: e.tensor_tensor(
                                out=big[1][:], in0=big[0][:], in1=ti4[:, :, 1, :].unsqueeze(2).broadcast_to(B4), op=ALU.mult),
                                reads=[Tbig[0], Ttop], writes=[Tbig[1]])
                            P.op("dve", lambda e, t=t: e.tensor_reduce(
                                out=IDX1[:, t, :].rearrange("p (h k) -> p h k", k=16), in_=big[1][:], axis=AX.X, op=ALU.add),
                                reads=[Tbig[1]], writes=[TIDX[t]])
                            P.op("dve", lambda e: e.tensor_tensor(
                                out=sm[2][:], in0=best_s[:], in1=best_s[:, :, 0:1].broadcast_to([128, 8, 16]), op=ALU.subtract),
                                reads=[Tbest], writes=[Tsm[2]])
                            P.op("act", lambda e: e.activation(out=sm[2][:], in_=sm[2][:], func=AF.Exp),
                                 reads=[Tsm[2]], writes=[Tsm[2]])
                            P.op("dve", lambda e: e.tensor_reduce(out=s8[:], in_=sm[2][:], axis=AX.X, op=ALU.add),
                                 reads=[Tsm[2]], writes=[Ts8])
                            P.op("dve", lambda e: e.reciprocal(out=s8[:], in_=s8[:]), reads=[Ts8], writes=[Ts8])
                            P.op("dve", lambda e, t=t: e.tensor_tensor(
                                out=GATE[:, t, :].rearrange("p (h k) -> p h k", k=16), in0=sm[2][:],
                                in1=s8[:, :].unsqueeze(2).broadcast_to([128, 8, 16]), op=ALU.mult),
                                reads=[Tsm[2], Ts8], writes=[TIDX[t]])
                P.barrier()

            if "d_idx" in dbg_out:
                P.dma(dbg_out["d_idx"][0].rearrange("(t p) n -> p t n", p=128), IDX0[:], reads=TIDX, writes=[T("x")])
                P.dma(dbg_out["d_idx"][1].rearrange("(t p) n -> p t n", p=128), IDX1[:], reads=TIDX, writes=[T("x")])
                P.dma(dbg_out["d_idx"][2].rearrange("(t p) n -> p t n", p=128), GATE[:], reads=TIDX, writes=[T("x")])
                P.barrier()

            with contextlib.ExitStack() as ph:
                pout = [(palloc(ph, "pout%d" % i, [128, 512]), T("pout%d" % i)) for i in range(4)]
                pact = [(palloc(ph, "pact%d" % i, [128, 512]), T("pact%d" % i)) for i in range(2)]
                pG = palloc(ph, "pG", [128, 512])
                TpG = T("pG")
                ptr = palloc(ph, "ptr", [128, 512])
                Tptr = T("ptr")
                GT = alloc(ph, "GT", [128, 256, 128], BF16)
                TGT = T("GT")
                trT = alloc(ph, "trT", [128, 3, 128], F32)
                TtrT = T("trT")
                NOH = 8
                Aoh = [alloc(ph, "Aoh%d" % i, [128, 128], BF16) for i in range(NOH)]
                Boh = [alloc(ph, "Boh%d" % i, [128, 128], BF16) for i in range(NOH)]
                TAoh = [T("Aoh%d" % i) for i in range(NOH)]
                TBoh = [T("Boh%d" % i) for i in range(NOH)]
                ub = [alloc(ph, "ub%d" % i, [128, 8, 512], BF16) for i in range(3)]
                vb = [alloc(ph, "vb%d" % i, [128, 4, 1024], BF16) for i in range(3)]
                Tub = [T("ub%d" % i) for i in range(3)]
                Tvb = [T("vb%d" % i) for i in range(3)]
                ga = [alloc(ph, "ga%d" % i, [128, 256], BF16) for i in range(3)]
                coef = [alloc(ph, "coef%d" % i, [128, 256], BF16) for i in range(3)]
                Tga = [T("ga%d" % i) for i in range(3)]
                Tcoef = [T("coef%d" % i) for i in range(3)]
                hf = [alloc(ph, "hf%d" % i, [128, 1024], F32) for i in range(2)]
                Thf = [T("hf0"), T("hf1")]
                gf = alloc(ph, "gf", [128, 1024], F32)
                Tgf = T("gf")
                junk2 = alloc(ph, "junk2", [128, 1024], BF16)
                Tjunk2 = T("junk2")
                fs = alloc(ph, "fs", [128, 2], F32)
                Tfs = [T("fs0"), T("fs1")]
                To = T("out")
                NPASS = 8 if stage >= 7 else 0
                if NPASS:
                    P.dma(gf[:], gfin[:, :], writes=[Tgf])
                noh = 0
                for ps_ in range(NPASS):
                    for tl in range(2):
                        t = ps_ * 2 + tl
                        for i3, srcI in enumerate((IDX0, IDX1, GATE)):
                            P.op("pe", lambda e, i3=i3, srcI=srcI, t=t: e.transpose(
                                out=ptr[:, i3 * 128:(i3 + 1) * 128], in_=srcI[:, t, :], identity=ident_f[:]),
                                reads=[TIDX[t], Tc], writes=[Tptr])
                        P.op("act", lambda e: e.activation(out=trT[:, :, :].rearrange("p a n -> p (a n)"), in_=ptr[:, 0:384],
                                                           func=AF.Copy), reads=[Tptr], writes=[TtrT])
                        for tk in range(128):
                            k = noh % NOH
                            noh += 1
                            P.op("dve", lambda e, k=k, tk=tk: e.tensor_scalar(
                                out=Boh[k][:], in0=iota128[:], scalar1=trT[:, 1, tk:tk + 1], scalar2=trT[:, 2, tk:tk + 1],
                                op0=ALU.is_equal, op1=ALU.mult), reads=[TtrT, Tci], writes=[TBoh[k]])
                            P.op("dve", lambda e, k=k, tk=tk: e.tensor_scalar(
                                out=Aoh[k][:], in0=iota128[:], scalar1=trT[:, 0, tk:tk + 1], scalar2=None,
                                op0=ALU.is_equal), reads=[TtrT, Tci], writes=[TAoh[k]])
                            P.op("pe", lambda e, k=k, tk=tk: e.matmul(
                                pG[:, (tk % 4) * 128:(tk % 4 + 1) * 128], lhsT=Boh[k][:], rhs=Aoh[k][:],
                                start=True, stop=True), reads=[TBoh[k], TAoh[k]], writes=[TpG])
                            if tk % 4 == 3:
                                tok0 = tl * 128 + tk - 3
                                P.op("act", lambda e, tok0=tok0: e.activation(
                                    out=GT[:, tok0:tok0 + 4, :].rearrange("p t n -> p (t n)"), in_=pG[:], func=AF.Copy),
                                    reads=[TpG], writes=[TGT])
                    def load_blk(bk):
                        bi_ = bk % 3
                        c0 = bk * 4
                        P.dma(ub[bi_][:], euT[:, c0 * 128:c0 * 128 + 512].rearrange("(dc p) e -> p dc e", p=128),
                              writes=[Tub[bi_]], qeng="pool")
                        P.dma(vb[bi_][:], ev[c0 * 128:c0 * 128 + 512, :].rearrange("(k p) d -> p k d", p=128),
                              writes=[Tvb[bi_]], qeng="pool")
                    load_blk(0)
                    load_blk(1)

                    def U(c, ps_=ps_):
                        bi = (c // 4) % 3
                        if c % 4 == 2 and c // 4 + 2 < 32:
                            load_blk(c // 4 + 2)
                        pa, Tpa = pact[c % 2]
                        k3 = c % 3
                        for dc in range(8):
                            P.op("pe", lambda e, dc=dc: e.matmul(
                                pa[:, 0:256], lhsT=ub[bi][:, dc, (c % 4) * 128:(c % 4 + 1) * 128],
                                rhs=n3T[:, dc, ps_ * 256:(ps_ + 1) * 256], start=(dc == 0), stop=(dc == 7)),
                                reads=[Tub[bi], Tn3T], writes=[Tpa])
                        P.op("act", lambda e: e.activation(out=ga[k3][:], in_=pa[:, 0:256], func=AF.Gelu),
                             reads=[Tpa], writes=[Tga[k3]])
                        P.op("dve", lambda e: e.tensor_tensor(out=coef[k3][:], in0=ga[k3][:], in1=GT[:, :, c], op=ALU.mult),
                             reads=[Tga[k3], TGT], writes=[Tcoef[k3]])

                    def Vv(c):
                        bi = (c // 4) % 3
                        k3 = c % 3
                        for tl in range(2):
                            for half in range(2):
                                po, Tpo = pout[tl * 2 + half]
                                P.op("pe", lambda e, po=po, tl=tl, half=half: e.matmul(
                                    po[:], lhsT=coef[k3][:, tl * 128:(tl + 1) * 128],
                                    rhs=vb[bi][:, c % 4, half * 512:(half + 1) * 512], start=(c == 0), stop=(c == 127)),
                                    reads=[Tcoef[k3], Tvb[bi]], writes=[Tpo])
                    for c in range(128 + 2):
                        if c < 128:
                            U(c)
                        if c - 2 >= 0:
                            Vv(c - 2)
                    for tl in range(2):
                        t = ps_ * 2 + tl
                        h, Th = hf[tl], Thf[tl]
                        P.dma(h[:], h_d[t * 128:(t + 1) * 128, :], reads=[Thd], writes=[Th])
                        for half in range(2):
                            po, Tpo = pout[tl * 2 + half]
                            P.op("dve", lambda e, po=po, h=h, half=half: e.tensor_tensor(
                                out=h[:, half * 512:(half + 1) * 512], in0=po[:], in1=h[:, half * 512:(half + 1) * 512],
                                op=ALU.add), reads=[Tpo, Th], writes=[Th])
                        P.op("act", lambda e, h=h, tl=tl: e.activation(out=junk2[:], in_=h[:], func=AF.Square,
                                                                      accum_out=fs[:, tl:tl + 1]),
                             reads=[Th], writes=[Tjunk2, Tfs[tl]])
                        P.op("dve", lambda e, tl=tl: e.tensor_scalar(out=fs[:, tl:tl + 1], in0=fs[:, tl:tl + 1],
                                                                    scalar1=1.0 / 1024, scalar2=EPS, op0=ALU.mult, op1=ALU.add),
                             reads=[Tfs[tl]], writes=[Tfs[tl]])
                        P.op("act", lambda e, tl=tl: e.activation(out=fs[:, tl:tl + 1], in_=fs[:, tl:tl + 1], func=AF.Sqrt),
                             reads=[Tfs[tl]], writes=[Tfs[tl]])
                        P.op("dve", lambda e, tl=tl: e.reciprocal(out=fs[:, tl:tl + 1], in_=fs[:, tl:tl + 1]),
                             reads=[Tfs[tl]], writes=[Tfs[tl]])
                        P.op("dve", lambda e, h=h, tl=tl: e.scalar_tensor_tensor(
                            out=h[:], in0=h[:], scalar=fs[:, tl:tl + 1], in1=gf[:], op0=ALU.mult, op1=ALU.mult),
                            reads=[Th, Tfs[tl], Tgf], writes=[Th])
                        P.dma(out[t * 128:(t + 1) * 128, :], h[:], reads=[Th], writes=[To])
                P.barrier()
        P.barrier()
        P.emit()
    return nc, P


def prep_inputs(inputs):
    f32 = np.float32
    x = np.asarray(inputs["x"], f32)
    mem = np.asarray(inputs["mem"], f32)

    def cols(v):
        v = np.asarray(v, f32).reshape(-1, 128)
        return np.ascontiguousarray(v.T)

    gpack = np.zeros((128, NGP), f32)
    gpack[:, G_MIX:G_MIX + 8] = cols(inputs["g_mix"][0])
    gpack[:, G_XATTN:G_XATTN + 8] = cols(inputs["g_xattn"][0])
    gpack[:, G_MEM:G_MEM + 8] = cols(inputs["g_mem"][0])
    gpack[:, G_FFN:G_FFN + 8] = cols(inputs["g_ffn"][0])
    gpack[:, G_SB:G_SB + 4] = cols(inputs["g_sb_out"][0])
    gpack[:, G_CONV:G_CONV + 4] = cols(inputs["g_conv_out"][0])
    cw = np.asarray(inputs["conv_w"][0], f32)
    for cc in range(4):
        for k in range(3):
            gpack[:, G_CW + cc * 3 + k] = cw[k, cc * 128:(cc + 1) * 128]
    gfin = np.ascontiguousarray(np.broadcast_to(np.asarray(inputs["g_final"], f32)[None, :], (128, 1024)))
    sk = np.asarray(inputs["sub_keys"][0], f32)
    skT = np.ascontiguousarray(sk.reshape(16, 128, 128).transpose(2, 0, 1))
    euT = np.ascontiguousarray(np.asarray(inputs["expert_u"][0], f32).T)
    ev = np.ascontiguousarray(np.asarray(inputs["expert_v"][0], f32))
    shared = dict(
        gpack=gpack, gfin=gfin,
        w_in=np.ascontiguousarray(inputs["w_in"][0], dtype=f32),
        w_out=np.ascontiguousarray(inputs["w_out"][0], dtype=f32),
        w_q_mem=np.ascontiguousarray(inputs["w_q_mem"][0], dtype=f32),
        w_kv_mem=np.ascontiguousarray(inputs["w_kv_mem"][0], dtype=f32),
        w_o_mem=np.ascontiguousarray(inputs["w_o_mem"][0], dtype=f32),
        w_query=np.ascontiguousarray(inputs["w_query"][0], dtype=f32),
        skT=skT, euT=euT, ev=ev)
    in_maps = []
    kpos = np.arange(2048)
    for c in range(8):
        b, ci = c // 4, c % 4
        xqs, xhs = [], []
        for s in range(4):
            t0 = (4 * s + ci) * 512
            xqs.append(x[b, t0:t0 + 512])
            if t0 == 0:
                xhs.append(np.zeros((2, 1024), f32))
            else:
                xhs.append(x[b, t0 - 2:t0])
        qpos = ci * 512 + np.arange(512)
        m = (kpos[:, None] < qpos[None, :]).astype(f32)
        m = m.reshape(16, 128, 512).transpose(1, 0, 2)
        d = dict(shared)
        d.update(xb=np.ascontiguousarray(x[b]), xq=np.ascontiguousarray(np.concatenate(xqs, 0)),
                 xh=np.ascontiguousarray(np.concatenate(xhs, 0)), memb=np.ascontiguousarray(mem[b]),
                 mask=np.ascontiguousarray(m).astype(ml_dtypes.bfloat16))
        in_maps.append(d)
    return in_maps


def assemble(results, key="out"):
    out = np.zeros((2, 8192, 1024), np.float32)
    for c in range(8):
        b, ci = c // 4, c % 4
        o = np.asarray(results[c][key])
        for s in range(4):
            t0 = (4 * s + ci) * 512
            out[b, t0:t0 + 512] = o[s * 512:(s + 1) * 512]
    return out


def kernel(**inputs):
    in_maps = prep_inputs(inputs)
    nc, _ = build()
    res = run_bass_kernel_spmd(nc, in_maps, core_ids=list(range(8)))
    return assemble(res.results)
```

```python
import contextlib
import numpy as np
import ml_dtypes
import concourse.bass as bass
import concourse.mybir as mybir
from concourse.alu_op_type import AluOpType as ALU
from concourse.bass_utils import run_bass_kernel_spmd

AF = mybir.ActivationFunctionType
F32 = mybir.dt.float32
BF16 = mybir.dt.bfloat16
U32 = mybir.dt.uint32
AX = mybir.AxisListType

COMPUTE = ("pe", "act", "dve", "pool")
ALLENG = ("pe", "act", "dve", "pool", "sp")
NDSEM = 40
EPS = 1e-6


class T:
    __slots__ = ("name", "w", "r", "dsem")

    def __init__(self, name, dsem=None):
        self.name = name
        self.w = None
        self.r = []
        self.dsem = dsem


class Prog:
    def __init__(self, nc, stack, same_engine_sync=True):
        self.nc = nc
        self.q = {e: [] for e in ALLENG}
        self.cnt = {}
        self.sem = {}
        for e in COMPUTE:
            self.sem[e] = stack.enter_context(nc.semaphore("c_" + e))
            self.cnt[e] = 0
        for i in range(NDSEM):
            k = "d%d" % i
            self.sem[k] = stack.enter_context(nc.semaphore(k))
            self.cnt[k] = 0
        self.waited = {e: {} for e in ALLENG}
        self.same = same_engine_sync
        self._rr = 0
        self.nins = 0

    def _deps(self, reads, writes):
        deps = {}

        def add(d):
            if d is None:
                return
            k, v = d
            if deps.get(k, 0) < v:
                deps[k] = v
        for t in reads:
            add(t.w)
        for t in writes:
            add(t.w)
            for d in t.r:
                add(d)
        return deps

    def _emit_waits(self, eng, deps):
        for k, v in deps.items():
            if k == eng and (eng == "pe" or not self.same):
                continue
            if k[0] == "d" and k[1:].isdigit():
                v = self.cnt[k]
            if self.waited[eng].get(k, 0) >= v:
                continue
            self.waited[eng][k] = v
            sem = self.sem[k]
            self.q[eng].append(lambda e, sem=sem, v=v: e.wait_ge(sem, v))
            self.nins += 1

    def _mark(self, key, val, reads, writes):
        for t in reads:
            t.r.append((key, val))
            if len(t.r) > 64:
                d = {}
                for k, v in t.r:
                    if d.get(k, 0) < v:
                        d[k] = v
                t.r = list(d.items())
        for t in writes:
            t.w = (key, val)
            t.r = []

    def op(self, eng, fn, reads=(), writes=()):
        deps = self._deps(reads, writes)
        self._emit_waits(eng, deps)
        self.cnt[eng] += 1
        val = self.cnt[eng]
        sem = self.sem[eng]
        self.q[eng].append(lambda e, fn=fn, sem=sem: fn(e).then_inc(sem, 1))
        self.nins += 1
        self._mark(eng, val, reads, writes)

    def dma(self, out, in_, reads=(), writes=(), qeng="sp", dsem=None, **kw):
        deps = self._deps(reads, writes)
        self._emit_waits(qeng, deps)
        if dsem is None:
            for t in writes:
                if t.dsem is not None:
                    dsem = t.dsem
                    break
        if dsem is None:
            dsem = self._rr
            self._rr = (self._rr + 1) % NDSEM
            for t in writes:
                t.dsem = dsem
        k = "d%d" % dsem
        self.cnt[k] += 16
        val = self.cnt[k]
        sem = self.sem[k]
        self.q[qeng].append(
            lambda e, out=out, in_=in_, sem=sem, kw=kw: e.dma_start(out=out, in_=in_, **kw).then_inc(sem, 16))
        self.nins += 1
        self._mark(k, val, reads, writes)

    def barrier(self):
        for eng in ALLENG:
            for k, v in self.cnt.items():
                if v == 0 or k == eng:
                    continue
                if self.waited[eng].get(k, 0) >= v:
                    continue
                self.waited[eng][k] = v
                sem = self.sem[k]
                self.q[eng].append(lambda e, sem=sem, v=v: e.wait_ge(sem, v))

    def emit(self):
        nc = self.nc
        with nc.Block() as block:
            @block.tensor
            def _(e):
                for f in self.q["pe"]:
                    f(e)

            @block.scalar
            def _(e):
                for f in self.q["act"]:
                    f(e)

            @block.vector
            def _(e):
                for f in self.q["dve"]:
                    f(e)

            @block.gpsimd
            def _(e):
                for f in self.q["pool"]:
                    f(e)

            @block.sync
            def _(e):
                for f in self.q["sp"]:
                    f(e)


G_MIX, G_XATTN, G_MEM, G_FFN, G_SB, G_CONV, G_CW = 0, 8, 16, 24, 32, 36, 40
NGP = 52


def build(stage=99, dbg=()):
    nc = bass.Bass("TRN2", target_bir_lowering=False)

    def di(n, s, d=F32):
        return nc.dram_tensor(n, list(s), d, kind="ExternalInput").ap()

    xb = di("xb", [8192, 1024])
    xq = di("xq", [2048, 1024])
    xh = di("xh", [8, 1024])
    memb = di("memb", [256, 1024])
    maskd = di("mask", [128, 16, 512], BF16)
    gpack = di("gpack", [128, NGP])
    gfin = di("gfin", [128, 1024])
    w_in = di("w_in", [1024, 3072])
    w_out = di("w_out", [1024, 1024])
    w_q_mem = di("w_q_mem", [1024, 1024])
    w_kv_mem = di("w_kv_mem", [1024, 2048])
    w_o_mem = di("w_o_mem", [1024, 1024])
    w_query = di("w_query", [1024, 2048])
    skT = di("skT", [128, 16, 128])
    euT = di("euT", [1024, 16384])
    ev = di("ev", [16384, 1024])
    out = nc.dram_tensor("out", [2048, 1024], F32, kind="ExternalOutput").ap()
    dbg_out = {}
    for name, shape, dt in dbg:
        dbg_out[name] = nc.dram_tensor(name, list(shape), dt, kind="ExternalOutput").ap()
    kT_d = nc.dram_tensor("kT_d", [4, 128, 8192], BF16, kind="Internal").ap()
    v_d = nc.dram_tensor("v_d", [4, 128, 64, 128], BF16, kind="Internal").ap()
    h_d = nc.dram_tensor("h_d", [2048, 1024], F32, kind="Internal").ap()
    eu_b = nc.dram_tensor("eu_b", [32, 128, 8, 512], BF16, kind="Internal").ap()
    ev_b = nc.dram_tensor("ev_b", [16384, 1024], BF16, kind="Internal").ap()

    with contextlib.ExitStack() as st:
        P = Prog(nc, st)

        uid = [0]

        def alloc(stk, n, s, d):
            uid[0] += 1
            return stk.enter_context(nc.sbuf_tensor("%s_%d" % (n, uid[0]), list(s), d))

        def palloc(stk, n, s, d=F32):
            uid[0] += 1
            return stk.enter_context(nc.psum_tensor("%s_%d" % (n, uid[0]), list(s), d))

        Teub, Tevb = T("eu_b"), T("ev_b")
        if stage >= 7:
            for dc in range(8):
                P.dma(eu_b[:, :, dc, :].rearrange("b p e -> p b e"),
                      euT[dc * 128:(dc + 1) * 128, :].rearrange("p (b e) -> p b e", e=512),
                      writes=[Teub], qeng="pool")
            for i in range(16):
                P.dma(ev_b[i * 1024:(i + 1) * 1024, :], ev[i * 1024:(i + 1) * 1024, :], writes=[Tevb], qeng="pool")

        ident_f = alloc(st, "ident_f", [128, 128], F32)
        ident_b = alloc(st, "ident_b", [128, 128], BF16)
        Uneg = alloc(st, "Uneg", [128, 128], BF16)
        Unegb = alloc(st, "Unegb", [128, 128], BF16)
        ones_f = alloc(st, "ones_f", [128, 1], F32)
        gp = alloc(st, "gp", [128, NGP], F32)
        Tc = T("consts")
        P.dma(gp[:], gpack[:, :], writes=[Tc])
        P.op("pool", lambda e: e.memset(ident_f[:], 1.0), writes=[Tc])
        P.op("pool", lambda e: e.affine_select(out=ident_f[:], in_=ident_f[:], pattern=[[1, 128]],
                                               compare_op=ALU.is_equal, fill=0.0, base=0, channel_multiplier=-1),
             reads=[Tc], writes=[Tc])
        P.op("pool", lambda e: e.tensor_copy(out=ident_b[:], in_=ident_f[:]), reads=[Tc], writes=[Tc])
        P.op("pool", lambda e: e.memset(Uneg[:], -1.0), writes=[Tc])
        P.op("pool", lambda e: e.affine_select(out=Uneg[:], in_=Uneg[:], pattern=[[-1, 128]],
                                               compare_op=ALU.is_gt, fill=0.0, base=0, channel_multiplier=1),
             reads=[Tc], writes=[Tc])
        P.op("pool", lambda e: e.memset(Unegb[:], -1.0), writes=[Tc])
        P.op("pool", lambda e: e.affine_select(out=Unegb[:], in_=Unegb[:], pattern=[[1, 128]],
                                               compare_op=ALU.is_ge, fill=0.0, base=0, channel_multiplier=-1),
             reads=[Tc], writes=[Tc])
        P.op("pool", lambda e: e.memset(ones_f[:], 1.0), writes=[Tc])

        class NormRes:
            pass

        def make_norm_res(stk, pT):
            R = NormRes()
            R.junk = alloc(stk, "n_junk", [128, 1024], BF16)
            R.ssq = alloc(stk, "n_ssq", [128, 4], F32)
            R.rstd = alloc(stk, "n_rstd", [128, 4], F32)
            R.xs = [alloc(stk, "n_xs%d" % i, [128, 1024], F32) for i in range(2)]
            R.Tjunk = T("n_junk")
            R.Tssq = [T("n_ssq%d" % i) for i in range(4)]
            R.Trstd = T("n_rstd")
            R.Txs = [T("n_xs0"), T("n_xs1")]
            R.pT = pT
            R.k = 0
            return R

        def norm_group(R, srcs, gcol, nT, TnT):
            n = len(srcs)
            for i, (xa, Tx) in enumerate(srcs):
                P.op("act", lambda e, xa=xa, i=i: e.activation(out=R.junk[:], in_=xa, func=AF.Square,
                                                                accum_out=R.ssq[:, i:i + 1]),
                     reads=[Tx], writes=[R.Tjunk, R.Tssq[i]])
            P.op("dve", lambda e: e.tensor_scalar(out=R.rstd[:, 0:n], in0=R.ssq[:, 0:n], scalar1=1.0 / 1024,
                                                  scalar2=EPS, op0=ALU.mult, op1=ALU.add),
                 reads=R.Tssq[0:n], writes=[R.Trstd])
            P.op("act", lambda e: e.activation(out=R.rstd[:, 0:n], in_=R.rstd[:, 0:n], func=AF.Sqrt),
                 reads=[R.Trstd], writes=[R.Trstd])
            P.op("dve", lambda e: e.reciprocal(out=R.rstd[:, 0:n], in_=R.rstd[:, 0:n]),
                 reads=[R.Trstd], writes=[R.Trstd])
            for i, (xa, Tx) in enumerate(srcs):
                xs = R.xs[i % 2]
                Txs = R.Txs[i % 2]
                P.op("act", lambda e, xa=xa, xs=xs, i=i: e.activation(out=xs[:], in_=xa, func=AF.Copy,
                                                                      scale=R.rstd[:, i:i + 1]),
                     reads=[Tx, R.Trstd], writes=[Txs])
                for half in range(2):
                    pt, Tp = R.pT[R.k % len(R.pT)]
                    R.k += 1
                    for c in range(4):
                        cc = half * 4 + c
                        P.op("pe", lambda e, pt=pt, c=c, cc=cc, xs=xs: e.transpose(
                            out=pt[:, c * 128:(c + 1) * 128], in_=xs[:, cc * 128:(cc + 1) * 128],
                            identity=ident_f[:]), reads=[Txs, Tc], writes=[Tp])
                    P.op("dve", lambda e, pt=pt, half=half, i=i: e.tensor_tensor(
                        out=nT[:, half * 4:(half + 1) * 4, i * 128:(i + 1) * 128],
                        in0=pt[:, :].rearrange("p (c n) -> p c n", c=4),
                        in1=gcol[:, half * 4:(half + 1) * 4].unsqueeze(2).broadcast_to([128, 4, 128]),
                        op=ALU.mult), reads=[Tp, Tc], writes=[TnT])

        with contextlib.ExitStack() as sAC:
            QT_sb = alloc(sAC, "QT_sb", [128, 8, 2048], BF16)
            convT_sb = alloc(sAC, "convT_sb", [128, 4, 2048], BF16)
            sbT_sb = alloc(sAC, "sbT_sb", [128, 4, 2048], BF16)
            ssq_c = alloc(sAC, "ssq_c", [128, 16], F32)
            ssq_s = alloc(sAC, "ssq_s", [128, 16], F32)
            TQT, TconvT, TsbT, Tssqc, Tssqs = T("QT"), T("convT"), T("sbT"), T("ssqc"), T("ssqs")
            TkTd, Tvd = T("kT_d"), T("v_d")

            with contextlib.ExitStack() as ph:
                pT = [(palloc(ph, "pT%d" % i, [128, 512]), T("pT%d" % i)) for i in range(4)]
                pK = [(palloc(ph, "pK%d" % i, [128, 512]), T("pK%d" % i)) for i in range(2)]
                pV = [(palloc(ph, "pV%d" % i, [128, 512]), T("pV%d" % i)) for i in range(2)]
                R = make_norm_res(ph, pT)
                wkv = alloc(ph, "wkv", [128, 8, 1024], BF16)
                Twkv = T("wkv")
                P.dma(wkv[:], w_in[:, 512:1536].rearrange("(c p) n -> p c n", p=128), writes=[Twkv], qeng="pool")
                NXB = 8
                xt = [alloc(ph, "xt%d" % i, [128, 1024], F32) for i in range(NXB)]
                Txt = [T("xt%d" % i) for i in range(NXB)]
                nTb = [alloc(ph, "nT%d" % i, [128, 8, 512], BF16) for i in range(2)]
                TnTb = [T("nT0"), T("nT1")]
                kst = [alloc(ph, "kst%d" % i, [128, 4, 512], BF16) for i in range(2)]
                vst = [alloc(ph, "vst%d" % i, [128, 4, 512], BF16) for i in range(2)]
                Tkst = [T("kst0"), T("kst1")]
                Tvst = [T("vst0"), T("vst1")]
                NG = 16 if stage >= 1 else 0

                def load_group(g):
                    for tt in range(4):
                        b = (g * 4 + tt) % NXB
                        r0 = (g * 4 + tt) * 128
                        P.dma(xt[b][:], xb[r0:r0 + 128, :], writes=[Txt[b]])
                if NG:
                    load_group(0)
                for g in range(NG):
                    if g + 1 < NG:
                        load_group(g + 1)
                    nT = nTb[g % 2]
                    TnT = TnTb[g % 2]
                    srcs = [(xt[(g * 4 + tt) % NXB][:], Txt[(g * 4 + tt) % NXB]) for tt in range(4)]
                    norm_group(R, srcs, gp[:, G_MIX:G_MIX + 8], nT, TnT)
                    ks, Tks = kst[g % 2], Tkst[g % 2]
                    vs, Tvs = vst[g % 2], Tvst[g % 2]
                    for hp in range(4):
                        pk, Tpk = pK[hp % 2]
                        for dc in range(8):
                            P.op("pe", lambda e, pk=pk, dc=dc, hp=hp, nT=nT: e.matmul(
                                pk[:], lhsT=wkv[:, dc, hp * 128:(hp + 1) * 128], rhs=nT[:, dc, :],
                                start=(dc == 0), stop=(dc == 7)), reads=[Twkv, TnT], writes=[Tpk])
                        P.op("act", lambda e, pk=pk, ks=ks, hp=hp: e.activation(out=ks[:, hp, :], in_=pk[:],
                                                                              func=AF.Copy),
                             reads=[Tpk], writes=[Tks])
                    P.dma(kT_d[:, :, g * 512:(g + 1) * 512].rearrange("h p n -> p h n"), ks[:],
                          reads=[Tks], writes=[TkTd], qeng="pool")
                    for tt in range(4):
                        pv, Tpv = pV[tt % 2]
                        for dc in range(8):
                            P.op("pe", lambda e, pv=pv, dc=dc, tt=tt, nT=nT: e.matmul(
                                pv[:], lhsT=nT[:, dc, tt * 128:(tt + 1) * 128], rhs=wkv[:, dc, 512:1024],
                                start=(dc == 0), stop=(dc == 7)), reads=[Twkv, TnT], writes=[Tpv])
                        P.op("dve", lambda e, pv=pv, vs=vs, tt=tt: e.tensor_copy(out=vs[:, tt, :], in_=pv[:]),
                             reads=[Tpv], writes=[Tvs])
                    for hp in range(4):
                        P.dma(v_d[hp, :, 4 * g:4 * g + 4, :], vs[:, :, hp * 128:(hp + 1) * 128],
                              reads=[Tvs], writes=[Tvd], qeng="pool")
                P.barrier()

            with contextlib.ExitStack() as ph:
                pT = [(palloc(ph, "pT%d" % i, [128, 512]), T("pT%d" % i)) for i in range(2)]
                pQ = [(palloc(ph, "pQ%d" % i, [128, 512]), T("pQ%d" % i)) for i in range(2)]
                pG3 = [(palloc(ph, "pG%d" % i, [128, 512]), T("pG%d" % i)) for i in range(3)]
                pss = palloc(ph, "pss", [128, 512])
                Tpss = T("pss")
                R = make_norm_res(ph, pT)
                wq = alloc(ph, "wq", [128, 8, 512], BF16)
                wg = alloc(ph, "wg", [128, 8, 1536], BF16)
                Twq, Twg = T("wq"), T("wg")
                xt = [alloc(ph, "xt%d" % i, [128, 1024], F32) for i in range(8)]
                Txt = [T("xt%d" % i) for i in range(8)]
                xht = alloc(ph, "xht", [128, 1024], F32)
                Txht = T("xht")
                nTb = [alloc(ph, "nT%d" % i, [128, 8, 512], BF16) for i in range(2)]
                TnTb = [T("nT0"), T("nT1")]
                nhT = alloc(ph, "nhT", [128, 8, 128], BF16)
                TnhT = T("nhT")
                uh = alloc(ph, "uh", [128, 4, 8], F32)
                Tuh = T("uh")
                gch = alloc(ph, "gch", [128, 8], F32)
                Tgch = T("gch")
                gc_sb = alloc(ph, "gc_sb", [128, 512], F32)
                u_sb = alloc(ph, "u_sb", [128, 514], F32)
                acc = alloc(ph, "acc", [128, 512], F32)
                conv = alloc(ph, "conv", [128, 512], F32)
                sqc = alloc(ph, "sqc", [128, 512], F32)
                Tgc, Tu, Tacc, Tconv, Tsqc = T("gc"), T("u"), T("acc"), T("conv"), T("sqc")
                if stage >= 2:
                    P.dma(wq[:], w_in[:, 0:512].rearrange("(c p) n -> p c n", p=128), writes=[Twq], qeng="pool")
                    P.dma(wg[:], w_in[:, 1536:3072].rearrange("(c p) n -> p c n", p=128), writes=[Twg], qeng="pool")
                    P.op("pool", lambda e: e.memset(QT_sb[:], 0.0), writes=[TQT])
                    P.op("pool", lambda e: e.memset(xht[:], 0.0), writes=[Txht])
                    P.dma(xht[0:8, :], xh[:, :], writes=[Txht])

                    def load_slot(s):
                        for tt in range(4):
                            b = (s * 4 + tt) % 8
                            r0 = (s * 4 + tt) * 128
                            P.dma(xt[b][:], xq[r0:r0 + 128, :], writes=[Txt[b]])
                    load_slot(0)
                    norm_group(R, [(xht[:], Txht)], gp[:, G_MIX:G_MIX + 8], nhT, TnhT)
                    for cc in range(4):
                        pa, Tpa = pG3[0]
                        pb, Tpb = pG3[1]
                        for dc in range(8):
                            P.op("pe", lambda e, pa=pa, dc=dc, cc=cc: e.matmul(
                                pa[:, 0:8], lhsT=wg[:, dc, 512 + cc * 128:512 + (cc + 1) * 128], rhs=nhT[:, dc, 0:8],
                                start=(dc == 0), stop=(dc == 7)), reads=[Twg, TnhT], writes=[Tpa])
                        for dc in range(8):
                            P.op("pe", lambda e, pb=pb, dc=dc, cc=cc: e.matmul(
                                pb[:, 0:8], lhsT=wg[:, dc, 1024 + cc * 128:1024 + (cc + 1) * 128], rhs=nhT[:, dc, 0:8],
                                start=(dc == 0), stop=(dc == 7)), reads=[Twg, TnhT], writes=[Tpb])
                        P.op("act", lambda e, pa=pa: e.activation(out=gch[:], in_=pa[:, 0:8], func=AF.Copy),
                             reads=[Tpa], writes=[Tgch])
                        P.op("dve", lambda e, pb=pb, cc=cc: e.tensor_tensor(out=uh[:, cc, :], in0=gch[:], in1=pb[:, 0:8],
                                                                           op=ALU.mult),
                             reads=[Tgch, Tpb], writes=[Tuh])
                    gi = 0
                    for s in range(4):
                        if s + 1 < 4:
                            load_slot(s + 1)
                        nT, TnT = nTb[s % 2], TnTb[s % 2]
                        srcs = [(xt[(s * 4 + tt) % 8][:], Txt[(s * 4 + tt) % 8]) for tt in range(4)]
                        norm_group(R, srcs, gp[:, G_MIX:G_MIX + 8], nT, TnT)
                        for hp in range(4):
                            pq, Tpq = pQ[hp % 2]
                            for dc in range(8):
                                P.op("pe", lambda e, pq=pq, dc=dc, hp=hp, nT=nT: e.matmul(
                                    pq[:], lhsT=wq[:, dc, hp * 128:(hp + 1) * 128], rhs=nT[:, dc, :],
                                    start=(dc == 0), stop=(dc == 7)), reads=[Twq, TnT], writes=[Tpq])
                            for hd in range(2):
                                r0, r1 = hd * 64, hd * 64 + 64
                                P.op("act", lambda e, pq=pq, hp=hp, s=s, hd=hd, r0=r0, r1=r1: e.activation(
                                    out=QT_sb[r0:r1, 2 * hp + hd, s * 512:(s + 1) * 512], in_=pq[r0:r1, :],
                                    func=AF.Copy, scale=0.125), reads=[Tpq], writes=[TQT])
                        for cc in range(4):
                            banks = []
                            for col0 in (512 + cc * 128, 1024 + cc * 128, cc * 128):
                                pg, Tpg = pG3[gi % 3]
                                gi += 1
                                for dc in range(8):
                                    P.op("pe", lambda e, pg=pg, dc=dc, col0=col0, nT=nT: e.matmul(
                                        pg[:], lhsT=wg[:, dc, col0:col0 + 128], rhs=nT[:, dc, :],
                                        start=(dc == 0), stop=(dc == 7)), reads=[Twg, TnT], writes=[Tpg])
                                banks.append((pg, Tpg))
                            (pgc, Tpgc), (pxi, Tpxi), (pgb, Tpgb) = banks
                            P.op("act", lambda e, pgc=pgc: e.activation(out=gc_sb[:], in_=pgc[:], func=AF.Copy),
                                 reads=[Tpgc], writes=[Tgc])
                            P.op("dve", lambda e, cc=cc, s=s: e.tensor_copy(out=u_sb[:, 0:2], in_=uh[:, cc, 2 * s:2 * s + 2]),
                                 reads=[Tuh], writes=[Tu])
                            P.op("dve", lambda e, pxi=pxi: e.tensor_tensor(out=u_sb[:, 2:514], in0=gc_sb[:], in1=pxi[:],
                                                                          op=ALU.mult),
                                 reads=[Tgc, Tpxi], writes=[Tu])
                            P.op("dve", lambda e, cc=cc: e.tensor_scalar(
                                out=acc[:], in0=u_sb[:, 2:514], scalar1=gp[:, G_CW + cc * 3 + 2:G_CW + cc * 3 + 3],
                                scalar2=None, op0=ALU.mult), reads=[Tu, Tc], writes=[Tacc])
                            P.op("dve", lambda e, cc=cc: e.scalar_tensor_tensor(
                                out=acc[:], in0=u_sb[:, 1:513], scalar=gp[:, G_CW + cc * 3 + 1:G_CW + cc * 3 + 2],
                                in1=acc[:], op0=ALU.mult, op1=ALU.add), reads=[Tu, Tc, Tacc], writes=[Tacc])
                            P.op("dve", lambda e, cc=cc: e.scalar_tensor_tensor(
                                out=acc[:], in0=u_sb[:, 0:512], scalar=gp[:, G_CW + cc * 3:G_CW + cc * 3 + 1],
                                in1=acc[:], op0=ALU.mult, op1=ALU.add), reads=[Tu, Tc, Tacc], writes=[Tacc])
                            P.op("dve", lambda e, pgb=pgb: e.tensor_tensor(out=conv[:], in0=acc[:], in1=pgb[:],
                                                                          op=ALU.mult),
                                 reads=[Tacc, Tpgb], writes=[Tconv])
                            P.op("act", lambda e: e.activation(out=sqc[:], in_=conv[:], func=AF.Square),
                                 reads=[Tconv], writes=[Tsqc])
                            P.op("act", lambda e, cc=cc, s=s: e.activation(
                                out=convT_sb[:, cc, s * 512:(s + 1) * 512], in_=conv[:], func=AF.Copy,
                                scale=gp[:, G_CONV + cc:G_CONV + cc + 1]), reads=[Tconv, Tc], writes=[TconvT])
                            for tt in range(4):
                                col = (s * 4 + tt) * 4 + cc
                                P.op("pe", lambda e, tt=tt, col=col: e.matmul(
                                    pss[:, col:col + 1], lhsT=sqc[:, tt * 128:(tt + 1) * 128], rhs=ones_f[:, 0:1],
                                    start=True, stop=True), reads=[Tsqc, Tc], writes=[Tpss])
                    P.op("dve", lambda e: e.tensor_reduce(out=ssq_c[:], in_=pss[:, 0:64].rearrange("p (t c) -> p t c", c=4),
                                                          axis=AX.X, op=ALU.add), reads=[Tpss], writes=[Tssqc])
                P.barrier()

            with contextlib.ExitStack() as ph:
                pz = [(palloc(ph, "pz%d" % i, [128, 512]), T("pz%d" % i)) for i in range(3)]
                pGc = [(palloc(ph, "pGc%d" % i, [128, 512]), T("pGc%d" % i)) for i in range(2)]
                pO = [(palloc(ph, "pO%d" % i, [128, 512]), T("pO%d" % i)) for i in range(2)]
                pss = palloc(ph, "pssb", [128, 512])
                Tpss = T("pssb")
                msk = alloc(ph, "msk", [128, 16, 512], BF16)
                Tmsk = T("msk")
                KT_sb = [alloc(ph, "KT%d" % i, [128, 8192], BF16) for i in range(2)]
                V_sb = [alloc(ph, "V%d" % i, [128, 64, 128], BF16) for i in range(2)]
                TKT = [T("KT0"), T("KT1")]
                TV = [T("V0"), T("V1")]
                NB = 4
                e1 = [alloc(ph, "e1_%d" % i, [128, 512], F32) for i in range(NB)]
                sp = [alloc(ph, "sp_%d" % i, [128, 512], F32) for i in range(NB)]
                Lb = [alloc(ph, "Lb_%d" % i, [128, 512], BF16) for i in range(NB)]
                t2 = [alloc(ph, "t2_%d" % i, [128, 512], F32) for i in range(NB)]
                Ab = [alloc(ph, "Ab_%d" % i, [128, 512], BF16) for i in range(NB)]
                tmpf = [alloc(ph, "tmpf_%d" % i, [128, 512], F32) for i in range(2)]
                Te1 = [T("e1") for _ in range(NB)]
                Tsp = [T("sp") for _ in range(NB)]
                TLb = [T("Lb") for _ in range(NB)]
                Tt2 = [T("t2") for _ in range(NB)]
                TAb = [T("Ab") for _ in range(NB)]
                Ttmpf = [T("tmpf0"), T("tmpf1")]
                sqs = alloc(ph, "sqs", [128, 512], F32)
                Tsqs = T("sqs")
                if stage >= 3:
                    P.dma(msk[:], maskd[:, :, :], writes=[Tmsk])
                    pairs = [(s, hp) for s in range(4) for hp in range(4)]

                    def load_kv(i):
                        s, hp = pairs[i]
                        b = i % 2
                        nk = (4 * s + 4) * 512
                        nblk = nk // 128
                        P.dma(KT_sb[b][:, 0:nk], kT_d[hp, :, 0:nk], reads=[TkTd], writes=[TKT[b]])
                        P.dma(V_sb[b][:, 0:nblk, :], v_d[hp, :, 0:nblk, :], reads=[Tvd], writes=[TV[b]])

                    steps = []
                    for i, (s, hp) in enumerate(pairs):
                        nblk = (4 * s + 4) * 4
                        for blk in range(nblk - 1, -1, -1):
                            for hd in range(2):
                                steps.append(dict(i=i, s=s, hp=hp, blk=blk, hd=hd, first=(blk == nblk - 1),
                                                  last=(blk == 0), pfirst=(blk == nblk - 1 and hd == 0),
                                                  plast=(blk == 0 and hd == 1)))
                    NS = len(steps)

                    def info(n):
                        d = steps[n]
                        blk, s = d["blk"], d["s"]
                        j, kb = blk // 4, blk % 4
                        return d, (j >= 4 * s), (j - 4 * s) * 4 + kb

                    def pe_z(n):
                        d = steps[n]
                        b = d["i"] % 2
                        z, Tz = pz[n % 3]
                        KT = KT_sb[b]
                        blk, hp, hd, s = d["blk"], d["hp"], d["hd"], d["s"]
                        P.op("pe", lambda e: e.matmul(
                            z[:], lhsT=KT[:, blk * 128:(blk + 1) * 128],
                            rhs=QT_sb[:, 2 * hp + hd, s * 512:(s + 1) * 512], start=True, stop=True),
                            reads=[TKT[b], TQT], writes=[Tz])

                    def act_s1(n):
                        z, Tz = pz[n % 3]
                        k = n % NB
                        P.op("act", lambda e: e.activation(out=e1[k][:], in_=z[:], func=AF.Exp, scale=-1.0),
                             reads=[Tz], writes=[Te1[k]])
                        P.op("act", lambda e: e.activation(out=sp[k][:], in_=e1[k][:], func=AF.Ln, bias=1.0),
                             reads=[Te1[k]], writes=[Tsp[k]])

                    def dve_L(n):
                        d, masked, mi = info(n)
                        z, Tz = pz[n % 3]
                        k = n % NB
                        if not masked:
                            P.op("dve", lambda e: e.tensor_tensor(out=Lb[k][:], in0=z[:], in1=sp[k][:], op=ALU.add),
                                 reads=[Tz, Tsp[k]], writes=[TLb[k]])
                        else:
                            tf, Ttf = tmpf[n % 2], Ttmpf[n % 2]
                            P.op("dve", lambda e: e.tensor_tensor(out=tf[:], in0=z[:], in1=sp[k][:], op=ALU.add),
                                 reads=[Tz, Tsp[k]], writes=[Ttf])
                            P.op("pool", lambda e: e.tensor_tensor(out=Lb[k][:], in0=tf[:], in1=msk[:, mi, :], op=ALU.mult),
                                 reads=[Ttf, Tmsk], writes=[TLb[k]])

                    def pe_mm1(n):
                        d = steps[n]
                        k = n % NB
                        G, TG = pGc[d["hd"]]
                        first = d["first"]
                        P.op("pe", lambda e: e.matmul(G[:], lhsT=Uneg[:], rhs=Lb[k][:], start=first, stop=True),
                             reads=[TLb[k], Tc], writes=[TG])

                    def dve_t2(n):
                        d = steps[n]
                        k = n % NB
                        G, TG = pGc[d["hd"]]
                        P.op("dve", lambda e: e.tensor_tensor(out=t2[k][:], in0=G[:], in1=sp[k][:], op=ALU.subtract),
                             reads=[TG, Tsp[k]], writes=[Tt2[k]])

                    def pe_mm2(n):
                        d = steps[n]
                        if d["last"]:
                            return
                        k = n % NB
                        G, TG = pGc[d["hd"]]
                        P.op("pe", lambda e: e.matmul(G[:], lhsT=Unegb[:], rhs=Lb[k][:], start=False, stop=True),
                             reads=[TLb[k], Tc], writes=[TG])

                    def act_A(n):
                        d, masked, mi = info(n)
                        k = n % NB
                        P.op("act", lambda e: e.activation(out=Ab[k][:], in_=t2[k][:], func=AF.Exp),
                             reads=[Tt2[k]], writes=[TAb[k]])

                    def mask_A(n):
                        d, masked, mi = info(n)
                        k = n % NB
                        if masked:
                            P.op("dve", lambda e: e.tensor_tensor(out=Ab[k][:], in0=Ab[k][:], in1=msk[:, mi, :], op=ALU.mult),
                                 reads=[TAb[k], Tmsk], writes=[TAb[k]])

                    def pe_O(n):
                        d = steps[n]
                        b = d["i"] % 2
                        k = n % NB
                        blk, hd, s, hp = d["blk"], d["hd"], d["s"], d["hp"]
                        O, TO = pO[hd]
                        V = V_sb[b]
                        first, last = d["first"], d["last"]
                        P.op("pe", lambda e: e.matmul(O[:], lhsT=V[:, blk, :], rhs=Ab[k][:], start=first, stop=last),
                             reads=[TV[b], TAb[k]], writes=[TO])
                        if d["plast"]:
                            for hd2 in range(2):
                                r0, r1 = hd2 * 64, hd2 * 64 + 64
                                O2, TO2 = pO[hd2]
                                P.op("act", lambda e, O2=O2, r0=r0, r1=r1: e.activation(
                                    out=sqs[r0:r1, :], in_=O2[r0:r1, :], func=AF.Square), reads=[], writes=[Tsqs, TO2])
                                P.op("dve", lambda e, O2=O2, r0=r0, r1=r1: e.tensor_scalar(
                                    out=sbT_sb[r0:r1, hp, s * 512:(s + 1) * 512], in0=O2[r0:r1, :],
                                    scalar1=gp[r0:r1, G_SB + hp:G_SB + hp + 1], scalar2=None, op0=ALU.mult),
                                    reads=[Tc], writes=[TsbT, TO2])
                            for tt in range(4):
                                col = (s * 4 + tt) * 4 + hp
                                P.op("pe", lambda e, tt=tt, col=col: e.matmul(
                                    pss[:, col:col + 1], lhsT=sqs[:, tt * 128:(tt + 1) * 128], rhs=ones_f[:, 0:1],
                                    start=True, stop=True), reads=[Tsqs, Tc], writes=[Tpss])

                    load_kv(0)
                    ok = lambda m: 0 <= m < NS
                    for n in range(NS + 5):
                        if ok(n):
                            pe_z(n)
                            act_s1(n)
                        if ok(n - 1):
                            dve_L(n - 1)
                        if ok(n - 2):
                            pe_mm1(n - 2)
                            dve_t2(n - 2)
                        if ok(n - 3):
                            pe_mm2(n - 3)
                        if ok(n - 4):
                            act_A(n - 4)
                        if ok(n - 5):
                            mask_A(n - 5)
                            pe_O(n - 5)
                            if steps[n - 5]["pfirst"]:
                                ni = steps[n - 5]["i"] + 1
                                if ni < len(pairs):
                                    load_kv(ni)
                    P.op("dve", lambda e: e.tensor_reduce(out=ssq_s[:], in_=pss[:, 0:64].rearrange("p (t c) -> p t c", c=4),
                                                          axis=AX.X, op=ALU.add), reads=[Tpss], writes=[Tssqs])
                P.barrier()

            if "d_sbT" in dbg_out:
                P.dma(dbg_out["d_sbT"].rearrange("c p n -> p c n"), sbT_sb[:], reads=[TsbT], writes=[T("x")])
                P.dma(dbg_out["d_convT"].rearrange("c p n -> p c n"), convT_sb[:], reads=[TconvT], writes=[T("x")])
                P.dma(dbg_out["d_ssq"][:, 0:16], ssq_s[:], reads=[Tssqs], writes=[T("x")])
                P.dma(dbg_out["d_ssq"][:, 16:32], ssq_c[:], reads=[Tssqc], writes=[T("x")])
                P.barrier()

            with contextlib.ExitStack() as ph:
                pP = [(palloc(ph, "pP%d" % i, [128, 512]), T("pP%d" % i)) for i in range(8)]
                wo = alloc(ph, "wo", [128, 8, 1024], BF16)
                Two = T("wo")
                rs = alloc(ph, "rs", [128, 32], F32)
                Trs = T("rs")
                xt = [alloc(ph, "xt%d" % i, [128, 1024], F32) for i in range(4)]
                Txt = [T("xt%d" % i) for i in range(4)]
                ht = [alloc(ph, "ht%d" % i, [128, 1024], F32) for i in range(2)]
                Tht = [T("ht0"), T("ht1")]
                Thd = T("h_d")
                if stage >= 4:
                    P.dma(wo[:], w_out[:, :].rearrange("(c p) n -> p c n", p=128), writes=[Two], qeng="pool")
                    P.op("dve", lambda e: e.tensor_scalar(out=rs[:, 0:16], in0=ssq_s[:], scalar1=1.0 / 512, scalar2=EPS,
                                                          op0=ALU.mult, op1=ALU.add), reads=[Tssqs], writes=[Trs])
                    P.op("dve", lambda e: e.tensor_scalar(out=rs[:, 16:32], in0=ssq_c[:], scalar1=1.0 / 512, scalar2=EPS,
                                                          op0=ALU.mult, op1=ALU.add), reads=[Tssqc, Trs], writes=[Trs])
                    P.op("act", lambda e: e.activation(out=rs[:], in_=rs[:], func=AF.Sqrt), reads=[Trs], writes=[Trs])
                    P.op("dve", lambda e: e.reciprocal(out=rs[:], in_=rs[:]), reads=[Trs], writes=[Trs])
                    for t in range(16):
                        xa, Txa = xt[t % 4], Txt[t % 4]
                        P.dma(xa[:], xq[t * 128:(t + 1) * 128, :], writes=[Txa])
                        h, Th = ht[t % 2], Tht[t % 2]
                        banks = [pP[(t % 2) * 4 + i] for i in range(4)]
                        for src, (Tsrc) in ((0, TsbT), (1, TconvT)):
                            srcT = sbT_sb if src == 0 else convT_sb
                            for half in range(2):
                                pb, Tpb = banks[src * 2 + half]
                                for c in range(4):
                                    P.op("pe", lambda e, pb=pb, srcT=srcT, c=c, t=t, src=src, half=half: e.matmul(
                                        pb[:], lhsT=srcT[:, c, t * 128:(t + 1) * 128],
                                        rhs=wo[:, src * 4 + c, half * 512:(half + 1) * 512],
                                        start=(c == 0), stop=(c == 3)), reads=[Tsrc, Two], writes=[Tpb])
                        for half in range(2):
                            pb, Tpb = banks[half]
                            P.op("dve", lambda e, pb=pb, h=h, xa=xa, half=half, t=t: e.scalar_tensor_tensor(
                                out=h[:, half * 512:(half + 1) * 512], in0=pb[:], scalar=rs[:, t:t + 1],
                                in1=xa[:, half * 512:(half + 1) * 512], op0=ALU.mult, op1=ALU.add),
                                reads=[Tpb, Trs, Txa], writes=[Th])
                        for half in range(2):
                            pb, Tpb = banks[2 + half]
                            P.op("dve", lambda e, pb=pb, h=h, half=half, t=t: e.scalar_tensor_tensor(
                                out=h[:, half * 512:(half + 1) * 512], in0=pb[:], scalar=rs[:, 16 + t:17 + t],
                                in1=h[:, half * 512:(half + 1) * 512], op0=ALU.mult, op1=ALU.add),
                                reads=[Tpb, Trs, Th], writes=[Th])
                        P.dma(h_d[t * 128:(t + 1) * 128, :], h[:], reads=[Th], writes=[Thd], qeng="pool")
                P.barrier()

        if "d_h1" in dbg_out:
            with contextlib.ExitStack() as ph:
                tmp = alloc(ph, "dbgtmp", [128, 16, 1024], F32)
                Tt = T("dbgtmp")
                P.dma(tmp[:], h_d.rearrange("(t p) n -> p t n", p=128), writes=[Tt])
                P.dma(dbg_out["d_h1"].rearrange("(t p) n -> p t n", p=128), tmp[:], reads=[Tt], writes=[T("x")])
                P.barrier()

        Thd = T("h_d")
        with contextlib.ExitStack() as ph:
            pT = [(palloc(ph, "pT%d" % i, [128, 512]), T("pT%d" % i)) for i in range(2)]
            pA = [(palloc(ph, "pA%d" % i, [128, 512]), T("pA%d" % i)) for i in range(2)]
            psc = palloc(ph, "psc", [128, 512])
            Tpsc = T("psc")
            pTp = palloc(ph, "pTp", [128, 1024], BF16)
            TpTp = T("pTp")
            poT = [(palloc(ph, "poT%d" % i, [128, 512]), T("poT%d" % i)) for i in range(2)]
            R = make_norm_res(ph, pT)
            wqm = alloc(ph, "wqm", [128, 8, 1024], BF16)
            wkvm = alloc(ph, "wkvm", [128, 8, 2048], BF16)
            wom = alloc(ph, "wom", [128, 8, 1024], BF16)
            Twqm, Twkvm, Twom = T("wqm"), T("wkvm"), T("wom")
            memt = [alloc(ph, "memt%d" % i, [128, 1024], F32) for i in range(2)]
            Tmemt = [T("memt0"), T("memt1")]
            memT = alloc(ph, "memT", [128, 8, 256], BF16)
            TmemT = T("memT")
            kTm = alloc(ph, "kTm", [128, 8, 256], BF16)
            vm = alloc(ph, "vm", [128, 2, 1024], BF16)
            TkTm, Tvm = T("kTm"), T("vm")
            ht = [alloc(ph, "ht%d" % i, [128, 1024], F32) for i in range(8)]
            Tht = [T("ht%d" % i) for i in range(8)]
            n2T = [alloc(ph, "n2T%d" % i, [128, 8, 512], BF16) for i in range(2)]
            Tn2T = [T("n2T0"), T("n2T1")]
            qTm = alloc(ph, "qTm", [128, 8, 512], BF16)
            TqTm = T("qTm")
            nmx = alloc(ph, "nmx", [128, 4], F32)
            rsum = alloc(ph, "rsum", [128, 4], F32)
            Tnmx = [T("nmx%d" % i) for i in range(4)]
            Trsum = [T("rsum%d" % i) for i in range(4)]
            pexp = [alloc(ph, "pexp%d" % i, [128, 256], F32) for i in range(2)]
            pn = [alloc(ph, "pn%d" % i, [128, 256], BF16) for i in range(2)]
            pTs = [alloc(ph, "pTs%d" % i, [128, 256], BF16) for i in range(2)]
            Tpexp = [T("pexp0"), T("pexp1")]
            Tpn = [T("pn0"), T("pn1")]
            TpTs = [T("pTs0"), T("pTs1")]
            oT_sb = alloc(ph, "oT_sb", [128, 8, 128], BF16)
            ToT = T("oT_sb")
            if stage >= 5:
                P.dma(wqm[:], w_q_mem[:, :].rearrange("(c p) n -> p c n", p=128), writes=[Twqm], qeng="pool")
                P.dma(wkvm[:], w_kv_mem[:, :].rearrange("(c p) n -> p c n", p=128), writes=[Twkvm], qeng="pool")
                P.dma(wom[:], w_o_mem[:, :].rearrange("(c p) n -> p c n", p=128), writes=[Twom], qeng="pool")
                for i in range(2):
                    P.dma(memt[i][:], memb[i * 128:(i + 1) * 128, :], writes=[Tmemt[i]])

                def load_hgroup(g):
                    for tt in range(4):
                        bb = (g * 4 + tt) % 8
                        r0 = (g * 4 + tt) * 128
                        P.dma(ht[bb][:], h_d[r0:r0 + 128, :], reads=[Thd], writes=[Tht[bb]])
                load_hgroup(0)
                norm_group(R, [(memt[0][:], Tmemt[0]), (memt[1][:], Tmemt[1])], gp[:, G_MEM:G_MEM + 8], memT, TmemT)
                for c in range(8):
                    pa, Tpa = pA[c % 2]
                    for dc in range(8):
                        P.op("pe", lambda e, pa=pa, dc=dc, c=c: e.matmul(
                            pa[:, 0:256], lhsT=wkvm[:, dc, c * 128:(c + 1) * 128], rhs=memT[:, dc, :],
                            start=(dc == 0), stop=(dc == 7)), reads=[Twkvm, TmemT], writes=[Tpa])
                    P.op("act", lambda e, pa=pa, c=c: e.activation(out=kTm[:, c, :], in_=pa[:, 0:256], func=AF.Copy),
                         reads=[Tpa], writes=[TkTm])
                for mc in range(2):
                    for half in range(2):
                        pa, Tpa = pA[half]
                        for dc in range(8):
                            P.op("pe", lambda e, pa=pa, dc=dc, mc=mc, half=half: e.matmul(
                                pa[:], lhsT=memT[:, dc, mc * 128:(mc + 1) * 128],
                                rhs=wkvm[:, dc, 1024 + half * 512:1024 + (half + 1) * 512],
                                start=(dc == 0), stop=(dc == 7)), reads=[Twkvm, TmemT], writes=[Tpa])
                        P.op("act", lambda e, pa=pa, mc=mc, half=half: e.activation(
                            out=vm[:, mc, half * 512:(half + 1) * 512], in_=pa[:], func=AF.Copy),
                            reads=[Tpa], writes=[Tvm])
                hk = 0
                for g in range(4):
                    if g + 1 < 4:
                        load_hgroup(g + 1)
                    nT, TnT = n2T[g % 2], Tn2T[g % 2]
                    srcs = [(ht[(g * 4 + tt) % 8][:], Tht[(g * 4 + tt) % 8]) for tt in range(4)]
                    norm_group(R, srcs, gp[:, G_XATTN:G_XATTN + 8], nT, TnT)
                    for c in range(8):
                        pa, Tpa = pA[c % 2]
                        for dc in range(8):
                            P.op("pe", lambda e, pa=pa, dc=dc, c=c, nT=nT: e.matmul(
                                pa[:], lhsT=wqm[:, dc, c * 128:(c + 1) * 128], rhs=nT[:, dc, :],
                                start=(dc == 0), stop=(dc == 7)), reads=[Twqm, TnT], writes=[Tpa])
                        P.op("act", lambda e, pa=pa, c=c: e.activation(out=qTm[:, c, :], in_=pa[:], func=AF.Copy,
                                                                      scale=1.0 / 16), reads=[Tpa], writes=[TqTm])
                    for tt in range(4):
                        bb = (g * 4 + tt) % 8
                        h, Th = ht[bb], Tht[bb]
                        for hd in range(4):
                            k2 = hk % 2
                            hk += 1
                            for c in range(2):
                                P.op("pe", lambda e, c=c, hd=hd, tt=tt: e.matmul(
                                    psc[:, 0:256], lhsT=qTm[:, 2 * hd + c, tt * 128:(tt + 1) * 128],
                                    rhs=kTm[:, 2 * hd + c, :], start=(c == 0), stop=(c == 1)),
                                    reads=[TqTm, TkTm], writes=[Tpsc])
                            P.op("dve", lambda e, hd=hd: e.tensor_reduce(out=nmx[:, hd:hd + 1], in_=psc[:, 0:256],
                                                                        axis=AX.X, op=ALU.max, negate=True),
                                 reads=[Tpsc], writes=[Tnmx[hd]])
                            P.op("act", lambda e, hd=hd, k2=k2: e.activation(
                                out=pexp[k2][:], in_=psc[:, 0:256], func=AF.Exp, bias=nmx[:, hd:hd + 1],
                                accum_out=rsum[:, hd:hd + 1]), reads=[Tnmx[hd]], writes=[Tpexp[k2], Trsum[hd], Tpsc])
                            P.op("dve", lambda e, hd=hd: e.reciprocal(out=rsum[:, hd:hd + 1], in_=rsum[:, hd:hd + 1]),
                                 reads=[Trsum[hd]], writes=[Trsum[hd]])
                            P.op("dve", lambda e, hd=hd, k2=k2: e.tensor_scalar(
                                out=pn[k2][:], in0=pexp[k2][:], scalar1=rsum[:, hd:hd + 1], scalar2=None, op0=ALU.mult),
                                reads=[Tpexp[k2], Trsum[hd]], writes=[Tpn[k2]])
                            for mc in range(2):
                                P.op("pe", lambda e, mc=mc, k2=k2: e.transpose(
                                    out=pTp[:, mc * 128:(mc + 1) * 128], in_=pn[k2][:, mc * 128:(mc + 1) * 128],
                                    identity=ident_b[:]), reads=[Tpn[k2], Tc], writes=[TpTp])
                            P.op("act", lambda e, k2=k2: e.activation(out=pTs[k2][:], in_=pTp[:, 0:256], func=AF.Copy),
                                 reads=[TpTp], writes=[TpTs[k2]])
                            for dch in range(2):
                                ch = 2 * hd + dch
                                po, Tpo = poT[ch // 4]
                                for mc in range(2):
                                    P.op("pe", lambda e, po=po, ch=ch, mc=mc, hd=hd, dch=dch, k2=k2: e.matmul(
                                        po[:, (ch % 4) * 128:(ch % 4 + 1) * 128],
                                        lhsT=vm[:, mc, hd * 256 + dch * 128:hd * 256 + (dch + 1) * 128],
                                        rhs=pTs[k2][:, mc * 128:(mc + 1) * 128], start=(mc == 0), stop=(mc == 1)),
                                        reads=[Tvm, TpTs[k2]], writes=[Tpo])
                        for i2 in range(2):
                            po, Tpo = poT[i2]
                            P.op("dve" if i2 == 0 else "act",
                                 (lambda e, po=po, i2=i2: e.tensor_copy(
                                     out=oT_sb[:, i2 * 4:(i2 + 1) * 4, :].rearrange("p c n -> p (c n)"), in_=po[:]))
                                 if i2 == 0 else
                                 (lambda e, po=po, i2=i2: e.activation(
                                     out=oT_sb[:, i2 * 4:(i2 + 1) * 4, :].rearrange("p c n -> p (c n)"), in_=po[:],
                                     func=AF.Copy)),
                                 reads=[Tpo], writes=[ToT])
                        for half in range(2):
                            pa, Tpa = pA[half]
                            for c in range(8):
                                P.op("pe", lambda e, pa=pa, c=c, half=half: e.matmul(
                                    pa[:], lhsT=oT_sb[:, c, :], rhs=wom[:, c, half * 512:(half + 1) * 512],
                                    start=(c == 0), stop=(c == 7)), reads=[ToT, Twom], writes=[Tpa])
                            P.op("dve", lambda e, pa=pa, h=h, half=half: e.tensor_tensor(
                                out=h[:, half * 512:(half + 1) * 512], in0=pa[:], in1=h[:, half * 512:(half + 1) * 512],
                                op=ALU.add), reads=[Tpa, Th], writes=[Th])
                        r0 = (g * 4 + tt) * 128
                        P.dma(h_d[r0:r0 + 128, :], h[:], reads=[Th], writes=[Thd], qeng="pool")
            P.barrier()

        if "d_h2" in dbg_out:
            with contextlib.ExitStack() as ph:
                tmp = alloc(ph, "dbgtmp2", [128, 16, 1024], F32)
                Tt = T("dbgtmp2")
                P.dma(tmp[:], h_d.rearrange("(t p) n -> p t n", p=128), reads=[Thd], writes=[Tt])
                P.dma(dbg_out["d_h2"].rearrange("(t p) n -> p t n", p=128), tmp[:], reads=[Tt], writes=[T("x")])
                P.barrier()

        with contextlib.ExitStack() as sE:
            n3T = alloc(sE, "n3T", [128, 8, 2048], BF16)
            Tn3T = T("n3T")
            IDX0 = alloc(sE, "IDX0", [128, 16, 128], F32)
            IDX1 = alloc(sE, "IDX1", [128, 16, 128], F32)
            GATE = alloc(sE, "GATE", [128, 16, 128], F32)
            TIDX = [T("IDX%d" % i) for i in range(16)]
            iota128 = alloc(sE, "iota128", [128, 128], F32)
            c16 = alloc(sE, "c16", [128, 16], F32)
            i16 = alloc(sE, "i16", [128, 16], F32)
            Tci = T("peer_consts")
            with contextlib.ExitStack() as ph:
                pT = [(palloc(ph, "pT%d" % i, [128, 512]), T("pT%d" % i)) for i in range(2)]
                pA = [(palloc(ph, "pA%d" % i, [128, 512]), T("pA%d" % i)) for i in range(2)]
                pscr = palloc(ph, "pscr", [128, 2048])
                Tpscr = T("pscr")
                R = make_norm_res(ph, pT)
                wqp = alloc(ph, "wqp", [128, 8, 2048], BF16)
                skb = alloc(ph, "skb", [128, 16, 128], BF16)
                Twqp, Tskb = T("wqp"), T("skb")
                ht = [alloc(ph, "ht%d" % i, [128, 1024], F32) for i in range(8)]
                Tht = [T("ht%d" % i) for i in range(8)]
                qTp = alloc(ph, "qTp", [128, 16, 512], BF16)
                TqTp = T("qTp")
                sc_sb = alloc(ph, "sc_sb", [128, 2048], F32)
                Tsc = T("sc_sb")
                scw = alloc(ph, "scw", [128, 256], F32)
                Tscw = T("scw")
                top_s = alloc(ph, "top_s", [128, 16, 16], F32)
                top_i = alloc(ph, "top_i", [128, 16, 16], U32)
                top_if = alloc(ph, "top_if", [128, 16, 16], F32)
                Ttop = T("top")
                cand = alloc(ph, "cand", [128, 8, 256], F32)
                Tcand = T("cand")
                best_s = alloc(ph, "best_s", [128, 8, 16], F32)
                best_j = alloc(ph, "best_j", [128, 8, 16], U32)
                jf = alloc(ph, "jf", [128, 8, 16], F32)
                Tbest = T("best")
                big = [alloc(ph, "big%d" % i, [128, 8, 16, 16], F32) for i in range(3)]
                Tbig = [T("big%d" % i) for i in range(3)]
                sm = [alloc(ph, "sm%d" % i, [128, 8, 16], F32) for i in range(3)]
                Tsm = [T("sm%d" % i) for i in range(3)]
                s8 = alloc(ph, "s8", [128, 8], F32)
                Ts8 = T("s8")
                if stage >= 6:
                    P.dma(wqp[:], w_query[:, :].rearrange("(c p) n -> p c n", p=128), writes=[Twqp], qeng="pool")
                    P.dma(skb[:], skT[:, :, :], writes=[Tskb], qeng="pool")
                    P.op("pool", lambda e: e.iota(iota128[:], pattern=[[1, 128]], base=0, channel_multiplier=0,
                                                  allow_small_or_imprecise_dtypes=True), writes=[Tci])
                    P.op("pool", lambda e: e.iota(c16[:], pattern=[[16, 16]], base=0, channel_multiplier=0,
                                                  allow_small_or_imprecise_dtypes=True), writes=[Tci])
                    P.op("pool", lambda e: e.iota(i16[:], pattern=[[1, 16]], base=0, channel_multiplier=0,
                                                  allow_small_or_imprecise_dtypes=True), writes=[Tci])

                    def load_hgroup(g):
                        for tt in range(4):
                            bb = (g * 4 + tt) % 8
                            r0 = (g * 4 + tt) * 128
                            P.dma(ht[bb][:], h_d[r0:r0 + 128, :], reads=[Thd], writes=[Tht[bb]])
                    load_hgroup(0)
                    B4 = [128, 8, 16, 16]
                    for g in range(4):
                        if g + 1 < 4:
                            load_hgroup(g + 1)
                        srcs = [(ht[(g * 4 + tt) % 8][:], Tht[(g * 4 + tt) % 8]) for tt in range(4)]
                        nTg = n3T[:, :, g * 512:(g + 1) * 512]
                        norm_group(R, srcs, gp[:, G_FFN:G_FFN + 8], nTg, Tn3T)
                        for c in range(16):
                            pa, Tpa = pA[c % 2]
                            for dc in range(8):
                                P.op("pe", lambda e, pa=pa, dc=dc, c=c, g=g: e.matmul(
                                    pa[:], lhsT=wqp[:, dc, c * 128:(c + 1) * 128], rhs=n3T[:, dc, g * 512:(g + 1) * 512],
                                    start=(dc == 0), stop=(dc == 7)), reads=[Twqp, Tn3T], writes=[Tpa])
                            P.op("act", lambda e, pa=pa, c=c: e.activation(out=qTp[:, c, :], in_=pa[:], func=AF.Copy),
                                 reads=[Tpa], writes=[TqTp])
                        for tt in range(4):
                            t = g * 4 + tt
                            for hc in range(16):
                                P.op("pe", lambda e, hc=hc, tt=tt: e.matmul(
                                    pscr[:, hc * 128:(hc + 1) * 128], lhsT=qTp[:, hc, tt * 128:(tt + 1) * 128],
                                    rhs=skb[:, hc, :], start=True, stop=True), reads=[TqTp, Tskb], writes=[Tpscr])
                            P.op("act", lambda e: e.activation(out=sc_sb[:], in_=pscr[:], func=AF.Copy),
                                 reads=[Tpscr], writes=[Tsc])
                            for hc in range(16):
                                src = sc_sb[:, hc * 128:(hc + 1) * 128]
                                P.op("dve", lambda e, hc=hc, src=src: e.max(out=top_s[:, hc, 0:8], in_=src),
                                     reads=[Tsc], writes=[Ttop])
                                P.op("dve", lambda e, hc=hc, src=src: e.max_index(out=top_i[:, hc, 0:8],
                                                                                in_max=top_s[:, hc, 0:8], in_values=src),
                                     reads=[Tsc, Ttop], writes=[Ttop])
                                P.op("dve", lambda e, hc=hc, src=src: e.match_replace(
                                    out=scw[:, 0:128], in_to_replace=top_s[:, hc, 0:8], in_values=src, imm_value=-1e30),
                                    reads=[Tsc, Ttop], writes=[Tscw])
                                P.op("dve", lambda e, hc=hc: e.max(out=top_s[:, hc, 8:16], in_=scw[:, 0:128]),
                                     reads=[Tscw], writes=[Ttop])
                                P.op("dve", lambda e, hc=hc: e.max_index(out=top_i[:, hc, 8:16],
                                                                        in_max=top_s[:, hc, 8:16], in_values=scw[:, 0:128]),
                                     reads=[Tscw, Ttop], writes=[Ttop])
                            P.op("dve", lambda e: e.tensor_copy(out=top_if[:], in_=top_i[:]), reads=[Ttop], writes=[Ttop])
                            ts4 = top_s[:, :, :].rearrange("p (h c) k -> p h c k", c=2)
                            ti4 = top_if[:, :, :].rearrange("p (h c) k -> p h c k", c=2)
                            P.op("dve", lambda e, ts4=ts4: e.tensor_tensor(
                                out=cand[:, :, :].rearrange("p h (a b) -> p h a b", b=16),
                                in0=ts4[:, :, 0, :].unsqueeze(3).broadcast_to(B4),
                                in1=ts4[:, :, 1, :].unsqueeze(2).broadcast_to(B4), op=ALU.add),
                                reads=[Ttop], writes=[Tcand])
                            for h8 in range(8):
                                src = cand[:, h8, :]
                                P.op("dve", lambda e, h8=h8, src=src: e.max(out=best_s[:, h8, 0:8], in_=src),
                                     reads=[Tcand], writes=[Tbest])
                                P.op("dve", lambda e, h8=h8, src=src: e.max_index(out=best_j[:, h8, 0:8],
                                                                                in_max=best_s[:, h8, 0:8], in_values=src),
                                     reads=[Tcand, Tbest], writes=[Tbest])
                                P.op("dve", lambda e, h8=h8, src=src: e.match_replace(
                                    out=scw[:, 0:256], in_to_replace=best_s[:, h8, 0:8], in_values=src, imm_value=-1e30),
                                    reads=[Tcand, Tbest], writes=[Tscw])
                                P.op("dve", lambda e, h8=h8: e.max(out=best_s[:, h8, 8:16], in_=scw[:, 0:256]),
                                     reads=[Tscw], writes=[Tbest])
                                P.op("dve", lambda e, h8=h8: e.max_index(out=best_j[:, h8, 8:16],
                                                                        in_max=best_s[:, h8, 8:16], in_values=scw[:, 0:256]),
                                     reads=[Tscw, Tbest], writes=[Tbest])
                            P.op("dve", lambda e: e.tensor_copy(out=jf[:], in_=best_j[:]), reads=[Tbest], writes=[Tbest])
                            c16b = c16[:, :].unsqueeze(1).unsqueeze(1).broadcast_to(B4)
                            i16b = i16[:, :].unsqueeze(1).unsqueeze(1).broadcast_to(B4)
                            P.op("dve", lambda e, c16b=c16b: e.tensor_tensor(
                                out=big[0][:], in0=jf[:, :, :].unsqueeze(3).broadcast_to(B4), in1=c16b, op=ALU.subtract),
                                reads=[Tbest, Tci], writes=[Tbig[0]])
                            P.op("dve", lambda e: e.tensor_scalar(out=big[1][:], in0=big[0][:], scalar1=0.0, scalar2=None,
                                                                  op0=ALU.is_ge), reads=[Tbig[0]], writes=[Tbig[1]])
                            P.op("dve", lambda e: e.scalar_tensor_tensor(out=big[2][:], in0=big[0][:], scalar=16.0,
                                                                         in1=big[1][:], op0=ALU.is_lt, op1=ALU.mult),
                                 reads=[Tbig[0], Tbig[1]], writes=[Tbig[2]])
                            P.op("dve", lambda e, ti4=ti4: e.tensor_tensor(
                                out=big[0][:], in0=big[2][:], in1=ti4[:, :, 0, :].unsqueeze(2).broadcast_to(B4), op=ALU.mult),
                                reads=[Tbig[2], Ttop], writes=[Tbig[0]])
                            P.op("dve", lambda e, t=t: e.tensor_reduce(
                                out=IDX0[:, t, :].rearrange("p (h k) -> p h k", k=16), in_=big[0][:], axis=AX.X, op=ALU.add),
                                reads=[Tbig[0]], writes=[TIDX[t]])
                            P.op("dve", lambda e, i16b=i16b: e.tensor_tensor(out=big[1][:], in0=big[2][:], in1=i16b, op=ALU.mult),
                                 reads=[Tbig[2], Tci], writes=[Tbig[1]])
                            P.op("dve", lambda e: e.tensor_reduce(out=sm[0][:], in_=big[1][:], axis=AX.X, op=ALU.add),
                                 reads=[Tbig[1]], writes=[Tsm[0]])
                            P.op("dve", lambda e: e.scalar_tensor_tensor(out=sm[1][:], in0=sm[0][:], scalar=-16.0, in1=jf[:],
                                                                         op0=ALU.mult, op1=ALU.add),
                                 reads=[Tsm[0], Tbest], writes=[Tsm[1]])
                            P.op("dve", lambda e, i16b=i16b: e.tensor_tensor(
                                out=big[0][:], in0=sm[1][:, :, :].unsqueeze(3).broadcast_to(B4), in1=i16b, op=ALU.is_equal),
                                reads=[Tsm[1], Tci], writes=[Tbig[0]])
                            P.op("dve", lambda e, ti4=ti4: e.tensor_tensor(
                                out=big[1][:], in0=big[0][:], in1=ti4[:, :, 1, :].unsqueeze(2).broadcast_to(B4), op=ALU.mult),
                                reads=[Tbig[0], Ttop], writes=[Tbig[1]])
                            P.op("dve", lambda e, t=t: e.tensor_reduce(
                                out=IDX1[:, t, :].rearrange("p (h k) -> p h k", k=16), in_=big[1][:], axis=AX.X, op=ALU.add),
                                reads=[Tbig[1]], writes=[TIDX[t]])
                            P.op("dve", lambda e: e.tensor_tensor(
                                out=sm[2][:], in0=best_s[:], in1=best_s[:, :, 0:1].broadcast_to([128, 8, 16]), op=ALU.subtract),
                                reads=[Tbest], writes=[Tsm[2]])
                            P.op("act", lambda e: e.activation(out=sm[2][:], in_=sm[2][:], func=AF.Exp),
                                 reads=[Tsm[2]], writes=[Tsm[2]])
                            P.op("dve", lambda e: e.tensor_reduce(out=s8[:], in_=sm[2][:], axis=AX.X, op=ALU.add),
                                 reads=[Tsm[2]], writes=[Ts8])
                            P.op("dve", lambda e: e.reciprocal(out=s8[:], in_=s8[:]), reads=[Ts8], writes=[Ts8])
                            P.op("dve", lambda e, t=t: e.tensor_tensor(
                                out=GATE[:, t, :].rearrange("p (h k) -> p h k", k=16), in0=sm[2][:],
                                in1=s8[:, :].unsqueeze(2).broadcast_to([128, 8, 16]), op=ALU.mult),
                                reads=[Tsm[2], Ts8], writes=[TIDX[t]])
                P.barrier()

            if "d_idx" in dbg_out:
                P.dma(dbg_out["d_idx"][0].rearrange("(t p) n -> p t n", p=128), IDX0[:], reads=TIDX, writes=[T("x")])
                P.dma(dbg_out["d_idx"][1].rearrange("(t p) n -> p t n", p=128), IDX1[:], reads=TIDX, writes=[T("x")])
                P.dma(dbg_out["d_idx"][2].rearrange("(t p) n -> p t n", p=128), GATE[:], reads=TIDX, writes=[T("x")])
                P.barrier()

            with contextlib.ExitStack() as ph:
                pout = [(palloc(ph, "pout%d" % i, [128, 512]), T("pout%d" % i)) for i in range(4)]
                pact = [(palloc(ph, "pact%d" % i, [128, 512]), T("pact%d" % i)) for i in range(2)]
                pG = palloc(ph, "pG", [128, 512])
                TpG = T("pG")
                ptr = palloc(ph, "ptr", [128, 512])
                Tptr = T("ptr")
                GT = alloc(ph, "GT", [128, 256, 128], BF16)
                TGT = T("GT")
                trT = alloc(ph, "trT", [128, 3, 128], F32)
                TtrT = T("trT")
                NOH = 8
                Aoh = [alloc(ph, "Aoh%d" % i, [128, 128], BF16) for i in range(NOH)]
                Boh = [alloc(ph, "Boh%d" % i, [128, 128], BF16) for i in range(NOH)]
                TAoh = [T("Aoh%d" % i) for i in range(NOH)]
                TBoh = [T("Boh%d" % i) for i in range(NOH)]
                ub = [alloc(ph, "ub%d" % i, [128, 8, 512], BF16) for i in range(3)]
                vb = [alloc(ph, "vb%d" % i, [128, 4, 1024], BF16) for i in range(3)]
                Tub = [T("ub%d" % i) for i in range(3)]
                Tvb = [T("vb%d" % i) for i in range(3)]
                ga = [alloc(ph, "ga%d" % i, [128, 256], BF16) for i in range(3)]
                coef = [alloc(ph, "coef%d" % i, [128, 256], BF16) for i in range(3)]
                Tga = [T("ga%d" % i) for i in range(3)]
                Tcoef = [T("coef%d" % i) for i in range(3)]
                hf = [alloc(ph, "hf%d" % i, [128, 1024], F32) for i in range(2)]
                Thf = [T("hf0"), T("hf1")]
                gf = alloc(ph, "gf", [128, 1024], F32)
                Tgf = T("gf")
                junk2 = alloc(ph, "junk2", [128, 1024], BF16)
                Tjunk2 = T("junk2")
                fs = alloc(ph, "fs", [128, 2], F32)
                Tfs = [T("fs0"), T("fs1")]
                To = T("out")
                NPASS = 8 if stage >= 7 else 0
                if NPASS:
                    P.dma(gf[:], gfin[:, :], writes=[Tgf])
                noh = 0
                for ps_ in range(NPASS):
                    for tl in range(2):
                        t = ps_ * 2 + tl
                        for i3, srcI in enumerate((IDX0, IDX1, GATE)):
                            P.op("pe", lambda e, i3=i3, srcI=srcI, t=t: e.transpose(
                                out=ptr[:, i3 * 128:(i3 + 1) * 128], in_=srcI[:, t, :], identity=ident_f[:]),
                                reads=[TIDX[t], Tc], writes=[Tptr])
                        P.op("act", lambda e: e.activation(out=trT[:, :, :].rearrange("p a n -> p (a n)"), in_=ptr[:, 0:384],
                                                           func=AF.Copy), reads=[Tptr], writes=[TtrT])
                        for tk in range(128):
                            k = noh % NOH
                            noh += 1
                            P.op("dve", lambda e, k=k, tk=tk: e.tensor_scalar(
                                out=Boh[k][:], in0=iota128[:], scalar1=trT[:, 1, tk:tk + 1], scalar2=trT[:, 2, tk:tk + 1],
                                op0=ALU.is_equal, op1=ALU.mult), reads=[TtrT, Tci], writes=[TBoh[k]])
                            P.op("dve", lambda e, k=k, tk=tk: e.tensor_scalar(
                                out=Aoh[k][:], in0=iota128[:], scalar1=trT[:, 0, tk:tk + 1], scalar2=None,
                                op0=ALU.is_equal), reads=[TtrT, Tci], writes=[TAoh[k]])
                            P.op("pe", lambda e, k=k, tk=tk: e.matmul(
                                pG[:, (tk % 4) * 128:(tk % 4 + 1) * 128], lhsT=Boh[k][:], rhs=Aoh[k][:],
                                start=True, stop=True), reads=[TBoh[k], TAoh[k]], writes=[TpG])
                            if tk % 4 == 3:
                                tok0 = tl * 128 + tk - 3
                                P.op("act", lambda e, tok0=tok0: e.activation(
                                    out=GT[:, tok0:tok0 + 4, :].rearrange("p t n -> p (t n)"), in_=pG[:], func=AF.Copy),
                                    reads=[TpG], writes=[TGT])
                    def load_blk(bk):
                        bi_ = bk % 3
                        c0 = bk * 4
                        P.dma(ub[bi_][:], eu_b[bk, :, :, :], reads=[Teub], writes=[Tub[bi_]])
                        P.dma(vb[bi_][:], ev_b[c0 * 128:c0 * 128 + 512, :].rearrange("(k p) d -> p k d", p=128),
                              reads=[Tevb], writes=[Tvb[bi_]])
                    load_blk(0)
                    load_blk(1)

                    def U(c, ps_=ps_):
                        bi = (c // 4) % 3
                        if c % 4 == 2 and c // 4 + 2 < 32:
                            load_blk(c // 4 + 2)
                        pa, Tpa = pact[c % 2]
                        k3 = c % 3
                        for dc in range(8):
                            P.op("pe", lambda e, dc=dc: e.matmul(
                                pa[:, 0:256], lhsT=ub[bi][:, dc, (c % 4) * 128:(c % 4 + 1) * 128],
                                rhs=n3T[:, dc, ps_ * 256:(ps_ + 1) * 256], start=(dc == 0), stop=(dc == 7)),
                                reads=[Tub[bi], Tn3T], writes=[Tpa])
                        P.op("act", lambda e: e.activation(out=ga[k3][:], in_=pa[:, 0:256], func=AF.Gelu),
                             reads=[Tpa], writes=[Tga[k3]])
                        P.op("dve", lambda e: e.tensor_tensor(out=coef[k3][:], in0=ga[k3][:], in1=GT[:, :, c], op=ALU.mult),
                             reads=[Tga[k3], TGT], writes=[Tcoef[k3]])

                    def Vv(c):
                        bi = (c // 4) % 3
                        k3 = c % 3
                        for tl in range(2):
                            for half in range(2):
                                po, Tpo = pout[tl * 2 + half]
                                P.op("pe", lambda e, po=po, tl=tl, half=half: e.matmul(
                                    po[:], lhsT=coef[k3][:, tl * 128:(tl + 1) * 128],
                                    rhs=vb[bi][:, c % 4, half * 512:(half + 1) * 512], start=(c == 0), stop=(c == 127)),
                                    reads=[Tcoef[k3], Tvb[bi]], writes=[Tpo])
                    for c in range(128 + 2):
                        if c < 128:
                            U(c)
                        if c - 2 >= 0:
                            Vv(c - 2)
                    for tl in range(2):
                        t = ps_ * 2 + tl
                        h, Th = hf[tl], Thf[tl]
                        P.dma(h[:], h_d[t * 128:(t + 1) * 128, :], reads=[Thd], writes=[Th])
                        for half in range(2):
                            po, Tpo = pout[tl * 2 + half]
                            P.op("dve", lambda e, po=po, h=h, half=half: e.tensor_tensor(
                                out=h[:, half * 512:(half + 1) * 512], in0=po[:], in1=h[:, half * 512:(half + 1) * 512],
                                op=ALU.add), reads=[Tpo, Th], writes=[Th])
                        P.op("act", lambda e, h=h, tl=tl: e.activation(out=junk2[:], in_=h[:], func=AF.Square,
                                                                      accum_out=fs[:, tl:tl + 1]),
                             reads=[Th], writes=[Tjunk2, Tfs[tl]])
                        P.op("dve", lambda e, tl=tl: e.tensor_scalar(out=fs[:, tl:tl + 1], in0=fs[:, tl:tl + 1],
                                                                    scalar1=1.0 / 1024, scalar2=EPS, op0=ALU.mult, op1=ALU.add),
                             reads=[Tfs[tl]], writes=[Tfs[tl]])
                        P.op("act", lambda e, tl=tl: e.activation(out=fs[:, tl:tl + 1], in_=fs[:, tl:tl + 1], func=AF.Sqrt),
                             reads=[Tfs[tl]], writes=[Tfs[tl]])
                        P.op("dve", lambda e, tl=tl: e.reciprocal(out=fs[:, tl:tl + 1], in_=fs[:, tl:tl + 1]),
                             reads=[Tfs[tl]], writes=[Tfs[tl]])
                        P.op("dve", lambda e, h=h, tl=tl: e.scalar_tensor_tensor(
                            out=h[:], in0=h[:], scalar=fs[:, tl:tl + 1], in1=gf[:], op0=ALU.mult, op1=ALU.mult),
                            reads=[Th, Tfs[tl], Tgf], writes=[Th])
                        P.dma(out[t * 128:(t + 1) * 128, :], h[:], reads=[Th], writes=[To])
                P.barrier()
        P.barrier()
        P.emit()
    return nc, P


def prep_inputs(inputs):
    f32 = np.float32
    x = np.asarray(inputs["x"], f32)
    mem = np.asarray(inputs["mem"], f32)

    def cols(v):
        v = np.asarray(v, f32).reshape(-1, 128)
        return np.ascontiguousarray(v.T)

    gpack = np.zeros((128, NGP), f32)
    gpack[:, G_MIX:G_MIX + 8] = cols(inputs["g_mix"][0])
    gpack[:, G_XATTN:G_XATTN + 8] = cols(inputs["g_xattn"][0])
    gpack[:, G_MEM:G_MEM + 8] = cols(inputs["g_mem"][0])
    gpack[:, G_FFN:G_FFN + 8] = cols(inputs["g_ffn"][0])
    gpack[:, G_SB:G_SB + 4] = cols(inputs["g_sb_out"][0])
    gpack[:, G_CONV:G_CONV + 4] = cols(inputs["g_conv_out"][0])
    cw = np.asarray(inputs["conv_w"][0], f32)
    for cc in range(4):
        for k in range(3):
            gpack[:, G_CW + cc * 3 + k] = cw[k, cc * 128:(cc + 1) * 128]
    gfin = np.ascontiguousarray(np.broadcast_to(np.asarray(inputs["g_final"], f32)[None, :], (128, 1024)))
    sk = np.asarray(inputs["sub_keys"][0], f32)
    skT = np.ascontiguousarray(sk.reshape(16, 128, 128).transpose(2, 0, 1))
    euT = np.ascontiguousarray(np.asarray(inputs["expert_u"][0], f32).T)
    ev = np.ascontiguousarray(np.asarray(inputs["expert_v"][0], f32))
    shared = dict(
        gpack=gpack, gfin=gfin,
        w_in=np.ascontiguousarray(inputs["w_in"][0], dtype=f32),
        w_out=np.ascontiguousarray(inputs["w_out"][0], dtype=f32),
        w_q_mem=np.ascontiguousarray(inputs["w_q_mem"][0], dtype=f32),
        w_kv_mem=np.ascontiguousarray(inputs["w_kv_mem"][0], dtype=f32),
        w_o_mem=np.ascontiguousarray(inputs["w_o_mem"][0], dtype=f32),
        w_query=np.ascontiguousarray(inputs["w_query"][0], dtype=f32),
        skT=skT, euT=euT, ev=ev)
    in_maps = []
    kpos = np.arange(2048)
    for c in range(8):
        b, ci = c // 4, c % 4
        xqs, xhs = [], []
        for s in range(4):
            t0 = (4 * s + ci) * 512
            xqs.append(x[b, t0:t0 + 512])
            if t0 == 0:
                xhs.append(np.zeros((2, 1024), f32))
            else:
                xhs.append(x[b, t0 - 2:t0])
        qpos = ci * 512 + np.arange(512)
        m = (kpos[:, None] < qpos[None, :]).astype(f32)
        m = m.reshape(16, 128, 512).transpose(1, 0, 2)
        d = dict(shared)
        d.update(xb=np.ascontiguousarray(x[b]), xq=np.ascontiguousarray(np.concatenate(xqs, 0)),
                 xh=np.ascontiguousarray(np.concatenate(xhs, 0)), memb=np.ascontiguousarray(mem[b]),
                 mask=np.ascontiguousarray(m).astype(ml_dtypes.bfloat16))
        in_maps.append(d)
    return in_maps


def assemble(results, key="out"):
    out = np.zeros((2, 8192, 1024), np.float32)
    for c in range(8):
        b, ci = c // 4, c % 4
        o = np.asarray(results[c][key])
        for s in range(4):
            t0 = (4 * s + ci) * 512
            out[b, t0:t0 + 512] = o[s * 512:(s + 1) * 512]
    return out


def kernel(**inputs):
    in_maps = prep_inputs(inputs)
    nc, _ = build()
    res = run_bass_kernel_spmd(nc, in_maps, core_ids=list(range(8)))
    return assemble(res.results)
```

```python
import contextlib
import numpy as np
import ml_dtypes
import concourse.bass as bass
import concourse.mybir as mybir
from concourse.alu_op_type import AluOpType as ALU
from concourse.bass_utils import run_bass_kernel_spmd

AF = mybir.ActivationFunctionType
F32 = mybir.dt.float32
BF16 = mybir.dt.bfloat16
U32 = mybir.dt.uint32
AX = mybir.AxisListType

COMPUTE = ("pe", "act", "dve", "pool")
ALLENG = ("pe", "act", "dve", "pool", "sp")
NDSEM = 40
AOH_ENG = "dve"
EPS = 1e-6


class T:
    __slots__ = ("name", "w", "r", "dsem")

    def __init__(self, name, dsem=None):
        self.name = name
        self.w = None
        self.r = []
        self.dsem = dsem


class Prog:
    def __init__(self, nc, stack, same_engine_sync=True):
        self.nc = nc
        self.q = {e: [] for e in ALLENG}
        self.cnt = {}
        self.sem = {}
        for e in COMPUTE:
            self.sem[e] = stack.enter_context(nc.semaphore("c_" + e))
            self.cnt[e] = 0
        for i in range(NDSEM):
            k = "d%d" % i
            self.sem[k] = stack.enter_context(nc.semaphore(k))
            self.cnt[k] = 0
        self.waited = {e: {} for e in ALLENG}
        self.same = same_engine_sync
        self._rr = 0
        self.nins = 0

    def _deps(self, reads, writes):
        deps = {}

        def add(d):
            if d is None:
                return
            k, v = d
            if deps.get(k, 0) < v:
                deps[k] = v
        for t in reads:
            add(t.w)
        for t in writes:
            add(t.w)
            for d in t.r:
                add(d)
        return deps

    def _emit_waits(self, eng, deps):
        for k, v in deps.items():
            if k == eng and (eng == "pe" or not self.same):
                continue
            if k[0] == "d" and k[1:].isdigit():
                v = self.cnt[k]
            if self.waited[eng].get(k, 0) >= v:
                continue
            self.waited[eng][k] = v
            sem = self.sem[k]
            self.q[eng].append(lambda e, sem=sem, v=v: e.wait_ge(sem, v))
            self.nins += 1

    def _mark(self, key, val, reads, writes):
        for t in reads:
            t.r.append((key, val))
            if len(t.r) > 64:
                d = {}
                for k, v in t.r:
                    if d.get(k, 0) < v:
                        d[k] = v
                t.r = list(d.items())
        for t in writes:
            t.w = (key, val)
            t.r = []

    def op(self, eng, fn, reads=(), writes=()):
        deps = self._deps(reads, writes)
        self._emit_waits(eng, deps)
        self.cnt[eng] += 1
        val = self.cnt[eng]
        sem = self.sem[eng]
        self.q[eng].append(lambda e, fn=fn, sem=sem: fn(e).then_inc(sem, 1))
        self.nins += 1
        self._mark(eng, val, reads, writes)

    def dma(self, out, in_, reads=(), writes=(), qeng="sp", dsem=None, **kw):
        deps = self._deps(reads, writes)
        self._emit_waits(qeng, deps)
        if dsem is None:
            for t in writes:
                if t.dsem is not None:
                    dsem = t.dsem
                    break
        if dsem is None:
            dsem = self._rr
            self._rr = (self._rr + 1) % NDSEM
            for t in writes:
                t.dsem = dsem
        k = "d%d" % dsem
        self.cnt[k] += 16
        val = self.cnt[k]
        sem = self.sem[k]
        self.q[qeng].append(
            lambda e, out=out, in_=in_, sem=sem, kw=kw: e.dma_start(out=out, in_=in_, **kw).then_inc(sem, 16))
        self.nins += 1
        self._mark(k, val, reads, writes)

    def barrier(self):
        for eng in ALLENG:
            for k, v in self.cnt.items():
                if v == 0 or k == eng:
                    continue
                if self.waited[eng].get(k, 0) >= v:
                    continue
                self.waited[eng][k] = v
                sem = self.sem[k]
                self.q[eng].append(lambda e, sem=sem, v=v: e.wait_ge(sem, v))

    def emit(self):
        nc = self.nc
        with nc.Block() as block:
            @block.tensor
            def _(e):
                for f in self.q["pe"]:
                    f(e)

            @block.scalar
            def _(e):
                for f in self.q["act"]:
                    f(e)

            @block.vector
            def _(e):
                for f in self.q["dve"]:
                    f(e)

            @block.gpsimd
            def _(e):
                for f in self.q["pool"]:
                    f(e)

            @block.sync
            def _(e):
                for f in self.q["sp"]:
                    f(e)


G_MIX, G_XATTN, G_MEM, G_FFN, G_SB, G_CONV, G_CW = 0, 8, 16, 24, 32, 36, 40
NGP = 52


def build(stage=99, dbg=()):
    nc = bass.Bass("TRN2", target_bir_lowering=False)

    def di(n, s, d=F32):
        return nc.dram_tensor(n, list(s), d, kind="ExternalInput").ap()

    xb = di("xb", [8192, 1024])
    xq = di("xq", [2048, 1024])
    xh = di("xh", [8, 1024])
    memb = di("memb", [256, 1024])
    maskd = di("mask", [128, 16, 512], BF16)
    gpack = di("gpack", [128, NGP])
    gfin = di("gfin", [128, 1024])
    w_in = di("w_in", [1024, 3072])
    w_out = di("w_out", [1024, 1024])
    w_q_mem = di("w_q_mem", [1024, 1024])
    w_kv_mem = di("w_kv_mem", [1024, 2048])
    w_o_mem = di("w_o_mem", [1024, 1024])
    w_query = di("w_query", [1024, 2048])
    skT = di("skT", [128, 16, 128])
    euT = di("euT", [1024, 16384])
    ev = di("ev", [16384, 1024])
    out = nc.dram_tensor("out", [2048, 1024], F32, kind="ExternalOutput").ap()
    dbg_out = {}
    for name, shape, dt in dbg:
        dbg_out[name] = nc.dram_tensor(name, list(shape), dt, kind="ExternalOutput").ap()
    kT_d = nc.dram_tensor("kT_d", [4, 128, 8192], BF16, kind="Internal").ap()
    v_d = nc.dram_tensor("v_d", [4, 128, 64, 128], BF16, kind="Internal").ap()
    h_d = nc.dram_tensor("h_d", [2048, 1024], F32, kind="Internal").ap()
    eu_b = nc.dram_tensor("eu_b", [64, 128, 8, 256], BF16, kind="Internal").ap()
    n3_d = nc.dram_tensor("n3_d", [128, 8, 2048], BF16, kind="Internal").ap()
    idx_d = nc.dram_tensor("idx_d", [3, 128, 16, 128], F32, kind="Internal").ap()
    ev_b = nc.dram_tensor("ev_b", [16384, 1024], BF16, kind="Internal").ap()

    with contextlib.ExitStack() as st:
        P = Prog(nc, st)

        uid = [0]

        def alloc(stk, n, s, d):
            uid[0] += 1
            return stk.enter_context(nc.sbuf_tensor("%s_%d" % (n, uid[0]), list(s), d))

        def palloc(stk, n, s, d=F32):
            uid[0] += 1
            return stk.enter_context(nc.psum_tensor("%s_%d" % (n, uid[0]), list(s), d))

        Teub, Tevb = T("eu_b"), T("ev_b")
        precast_jobs = []
        if stage >= 7:
            for dc in range(8):
                for hb in range(2):
                    precast_jobs.append((eu_b[hb * 32:(hb + 1) * 32, :, dc, :].rearrange("b p e -> p b e"),
                                         euT[dc * 128:(dc + 1) * 128, hb * 8192:(hb + 1) * 8192].rearrange(
                                             "p (b e) -> p b e", e=256), Teub))
            for i in range(32):
                precast_jobs.append((ev_b[i * 512:(i + 1) * 512, :], ev[i * 512:(i + 1) * 512, :], Tevb))
            precast_jobs = [j for pair in zip(precast_jobs[:16] + precast_jobs[16:32], precast_jobs[32:] + [None] * 16)
                            for j in pair if j is not None]

        def precast_some(n):
            for _ in range(n):
                if precast_jobs:
                    o, i_, Tt = precast_jobs.pop(0)
                    P.dma(o, i_, writes=[Tt], qeng="pool")

        ident_f = alloc(st, "ident_f", [128, 128], F32)
        ident_b = alloc(st, "ident_b", [128, 128], BF16)
        Uneg = alloc(st, "Uneg", [128, 128], BF16)
        Unegb = alloc(st, "Unegb", [128, 128], BF16)
        ones_f = alloc(st, "ones_f", [128, 1], F32)
        gp = alloc(st, "gp", [128, NGP], F32)
        Tc = T("consts")
        P.dma(gp[:], gpack[:, :], writes=[Tc])
        P.op("pool", lambda e: e.memset(ident_f[:], 1.0), writes=[Tc])
        P.op("pool", lambda e: e.affine_select(out=ident_f[:], in_=ident_f[:], pattern=[[1, 128]],
                                               compare_op=ALU.is_equal, fill=0.0, base=0, channel_multiplier=-1),
             reads=[Tc], writes=[Tc])
        P.op("pool", lambda e: e.tensor_copy(out=ident_b[:], in_=ident_f[:]), reads=[Tc], writes=[Tc])
        P.op("pool", lambda e: e.memset(Uneg[:], -1.0), writes=[Tc])
        P.op("pool", lambda e: e.affine_select(out=Uneg[:], in_=Uneg[:], pattern=[[-1, 128]],
                                               compare_op=ALU.is_gt, fill=0.0, base=0, channel_multiplier=1),
             reads=[Tc], writes=[Tc])
        P.op("pool", lambda e: e.memset(Unegb[:], -1.0), writes=[Tc])
        P.op("pool", lambda e: e.affine_select(out=Unegb[:], in_=Unegb[:], pattern=[[1, 128]],
                                               compare_op=ALU.is_ge, fill=0.0, base=0, channel_multiplier=-1),
             reads=[Tc], writes=[Tc])
        P.op("pool", lambda e: e.memset(ones_f[:], 1.0), writes=[Tc])

        class NormRes:
            pass

        def make_norm_res(stk, pT):
            R = NormRes()
            R.junk = alloc(stk, "n_junk", [128, 1024], BF16)
            R.ssq = alloc(stk, "n_ssq", [128, 4], F32)
            R.rstd = alloc(stk, "n_rstd", [128, 4], F32)
            R.xs = [alloc(stk, "n_xs%d" % i, [128, 1024], F32) for i in range(2)]
            R.Tjunk = T("n_junk")
            R.Tssq = [T("n_ssq%d" % i) for i in range(4)]
            R.Trstd = T("n_rstd")
            R.Txs = [T("n_xs0"), T("n_xs1")]
            R.pT = pT
            R.k = 0
            return R

        def norm_group(R, srcs, gcol, nT, TnT):
            n = len(srcs)
            for i, (xa, Tx) in enumerate(srcs):
                P.op("act", lambda e, xa=xa, i=i: e.activation(out=R.junk[:], in_=xa, func=AF.Square,
                                                                accum_out=R.ssq[:, i:i + 1]),
                     reads=[Tx], writes=[R.Tjunk, R.Tssq[i]])
            P.op("dve", lambda e: e.tensor_scalar(out=R.rstd[:, 0:n], in0=R.ssq[:, 0:n], scalar1=1.0 / 1024,
                                                  scalar2=EPS, op0=ALU.mult, op1=ALU.add),
                 reads=R.Tssq[0:n], writes=[R.Trstd])
            P.op("act", lambda e: e.activation(out=R.rstd[:, 0:n], in_=R.rstd[:, 0:n], func=AF.Sqrt),
                 reads=[R.Trstd], writes=[R.Trstd])
            P.op("dve", lambda e: e.reciprocal(out=R.rstd[:, 0:n], in_=R.rstd[:, 0:n]),
                 reads=[R.Trstd], writes=[R.Trstd])
            for i, (xa, Tx) in enumerate(srcs):
                xs = R.xs[i % 2]
                Txs = R.Txs[i % 2]
                P.op("act", lambda e, xa=xa, xs=xs, i=i: e.activation(out=xs[:], in_=xa, func=AF.Copy,
                                                                      scale=R.rstd[:, i:i + 1]),
                     reads=[Tx, R.Trstd], writes=[Txs])
                for half in range(2):
                    pt, Tp = R.pT[R.k % len(R.pT)]
                    R.k += 1
                    for c in range(4):
                        cc = half * 4 + c
                        P.op("pe", lambda e, pt=pt, c=c, cc=cc, xs=xs: e.transpose(
                            out=pt[:, c * 128:(c + 1) * 128], in_=xs[:, cc * 128:(cc + 1) * 128],
                            identity=ident_f[:]), reads=[Txs, Tc], writes=[Tp])
                    P.op("dve", lambda e, pt=pt, half=half, i=i: e.tensor_tensor(
                        out=nT[:, half * 4:(half + 1) * 4, i * 128:(i + 1) * 128],
                        in0=pt[:, :].rearrange("p (c n) -> p c n", c=4),
                        in1=gcol[:, half * 4:(half + 1) * 4].unsqueeze(2).broadcast_to([128, 4, 128]),
                        op=ALU.mult), reads=[Tp, Tc], writes=[TnT])

        with contextlib.ExitStack() as sAC:
            QT_sb = alloc(sAC, "QT_sb", [128, 8, 2048], BF16)
            convT_sb = alloc(sAC, "convT_sb", [128, 4, 2048], BF16)
            sbT_sb = alloc(sAC, "sbT_sb", [128, 4, 2048], BF16)
            ssq_c = alloc(sAC, "ssq_c", [128, 16], F32)
            ssq_s = alloc(sAC, "ssq_s", [128, 16], F32)
            TQT, TconvT, TsbT, Tssqc, Tssqs = T("QT"), T("convT"), T("sbT"), T("ssqc"), T("ssqs")
            TkTd, Tvd = T("kT_d"), T("v_d")

            with contextlib.ExitStack() as ph:
                pT = [(palloc(ph, "pT%d" % i, [128, 512]), T("pT%d" % i)) for i in range(4)]
                pK = [(palloc(ph, "pK%d" % i, [128, 512]), T("pK%d" % i)) for i in range(2)]
                pV = [(palloc(ph, "pV%d" % i, [128, 512]), T("pV%d" % i)) for i in range(2)]
                R = make_norm_res(ph, pT)
                wkv = alloc(ph, "wkv", [128, 8, 1024], BF16)
                Twkv = T("wkv")
                P.dma(wkv[:], w_in[:, 512:1536].rearrange("(c p) n -> p c n", p=128), writes=[Twkv], qeng="pool")
                NXB = 8
                xt = [alloc(ph, "xt%d" % i, [128, 1024], F32) for i in range(NXB)]
                Txt = [T("xt%d" % i) for i in range(NXB)]
                nTb = [alloc(ph, "nT%d" % i, [128, 8, 512], BF16) for i in range(2)]
                TnTb = [T("nT0"), T("nT1")]
                kst = [alloc(ph, "kst%d" % i, [128, 4, 512], BF16) for i in range(2)]
                vst = [alloc(ph, "vst%d" % i, [128, 4, 512], BF16) for i in range(2)]
                Tkst = [T("kst0"), T("kst1")]
                Tvst = [T("vst0"), T("vst1")]
                NG = 16 if stage >= 1 else 0

                def load_group(g):
                    for tt in range(4):
                        b = (g * 4 + tt) % NXB
                        r0 = (g * 4 + tt) * 128
                        P.dma(xt[b][:], xb[r0:r0 + 128, :], writes=[Txt[b]])
                if NG:
                    load_group(0)
                for g in range(NG):
                    if g + 1 < NG:
                        load_group(g + 1)
                    nT = nTb[g % 2]
                    TnT = TnTb[g % 2]
                    srcs = [(xt[(g * 4 + tt) % NXB][:], Txt[(g * 4 + tt) % NXB]) for tt in range(4)]
                    norm_group(R, srcs, gp[:, G_MIX:G_MIX + 8], nT, TnT)
                    ks, Tks = kst[g % 2], Tkst[g % 2]
                    vs, Tvs = vst[g % 2], Tvst[g % 2]
                    for hp in range(4):
                        pk, Tpk = pK[hp % 2]
                        for dc in range(8):
                            P.op("pe", lambda e, pk=pk, dc=dc, hp=hp, nT=nT: e.matmul(
                                pk[:], lhsT=wkv[:, dc, hp * 128:(hp + 1) * 128], rhs=nT[:, dc, :],
                                start=(dc == 0), stop=(dc == 7)), reads=[Twkv, TnT], writes=[Tpk])
                        P.op("act", lambda e, pk=pk, ks=ks, hp=hp: e.activation(out=ks[:, hp, :], in_=pk[:],
                                                                              func=AF.Copy),
                             reads=[Tpk], writes=[Tks])
                    P.dma(kT_d[:, :, g * 512:(g + 1) * 512].rearrange("h p n -> p h n"), ks[:],
                          reads=[Tks], writes=[TkTd], qeng="pool")
                    precast_some(1)
                    for tt in range(4):
                        pv, Tpv = pV[tt % 2]
                        for dc in range(8):
                            P.op("pe", lambda e, pv=pv, dc=dc, tt=tt, nT=nT: e.matmul(
                                pv[:], lhsT=nT[:, dc, tt * 128:(tt + 1) * 128], rhs=wkv[:, dc, 512:1024],
                                start=(dc == 0), stop=(dc == 7)), reads=[Twkv, TnT], writes=[Tpv])
                        P.op("dve", lambda e, pv=pv, vs=vs, tt=tt: e.tensor_copy(out=vs[:, tt, :], in_=pv[:]),
                             reads=[Tpv], writes=[Tvs])
                    for hp in range(4):
                        P.dma(v_d[hp, :, 4 * g:4 * g + 4, :], vs[:, :, hp * 128:(hp + 1) * 128],
                              reads=[Tvs], writes=[Tvd], qeng="pool")
                P.barrier()

            with contextlib.ExitStack() as ph:
                pT = [(palloc(ph, "pT%d" % i, [128, 512]), T("pT%d" % i)) for i in range(2)]
                pQ = [(palloc(ph, "pQ%d" % i, [128, 512]), T("pQ%d" % i)) for i in range(2)]
                pG3 = [(palloc(ph, "pG%d" % i, [128, 512]), T("pG%d" % i)) for i in range(3)]
                pss = palloc(ph, "pss", [128, 512])
                Tpss = T("pss")
                R = make_norm_res(ph, pT)
                wq = alloc(ph, "wq", [128, 8, 512], BF16)
                wg = alloc(ph, "wg", [128, 8, 1536], BF16)
                Twq, Twg = T("wq"), T("wg")
                xt = [alloc(ph, "xt%d" % i, [128, 1024], F32) for i in range(8)]
                Txt = [T("xt%d" % i) for i in range(8)]
                xht = alloc(ph, "xht", [128, 1024], F32)
                Txht = T("xht")
                nTb = [alloc(ph, "nT%d" % i, [128, 8, 512], BF16) for i in range(2)]
                TnTb = [T("nT0"), T("nT1")]
                nhT = alloc(ph, "nhT", [128, 8, 128], BF16)
                TnhT = T("nhT")
                uh = alloc(ph, "uh", [128, 4, 8], F32)
                Tuh = T("uh")
                gch = alloc(ph, "gch", [128, 8], F32)
                Tgch = T("gch")
                gc_sb = alloc(ph, "gc_sb", [128, 512], F32)
                u_sb = alloc(ph, "u_sb", [128, 514], F32)
                acc = alloc(ph, "acc", [128, 512], F32)
                conv = alloc(ph, "conv", [128, 512], F32)
                sqc = alloc(ph, "sqc", [128, 512], F32)
                Tgc, Tu, Tacc, Tconv, Tsqc = T("gc"), T("u"), T("acc"), T("conv"), T("sqc")
                if stage >= 2:
                    P.dma(wq[:], w_in[:, 0:512].rearrange("(c p) n -> p c n", p=128), writes=[Twq], qeng="pool")
                    P.dma(wg[:], w_in[:, 1536:3072].rearrange("(c p) n -> p c n", p=128), writes=[Twg], qeng="pool")
                    P.op("pool", lambda e: e.memset(QT_sb[:], 0.0), writes=[TQT])
                    P.op("pool", lambda e: e.memset(xht[:], 0.0), writes=[Txht])
                    P.dma(xht[0:8, :], xh[:, :], writes=[Txht])

                    def load_slot(s):
                        for tt in range(4):
                            b = (s * 4 + tt) % 8
                            r0 = (s * 4 + tt) * 128
                            P.dma(xt[b][:], xq[r0:r0 + 128, :], writes=[Txt[b]])
                    load_slot(0)
                    norm_group(R, [(xht[:], Txht)], gp[:, G_MIX:G_MIX + 8], nhT, TnhT)
                    for cc in range(4):
                        pa, Tpa = pG3[0]
                        pb, Tpb = pG3[1]
                        for dc in range(8):
                            P.op("pe", lambda e, pa=pa, dc=dc, cc=cc: e.matmul(
                                pa[:, 0:8], lhsT=wg[:, dc, 512 + cc * 128:512 + (cc + 1) * 128], rhs=nhT[:, dc, 0:8],
                                start=(dc == 0), stop=(dc == 7)), reads=[Twg, TnhT], writes=[Tpa])
                        for dc in range(8):
                            P.op("pe", lambda e, pb=pb, dc=dc, cc=cc: e.matmul(
                                pb[:, 0:8], lhsT=wg[:, dc, 1024 + cc * 128:1024 + (cc + 1) * 128], rhs=nhT[:, dc, 0:8],
                                start=(dc == 0), stop=(dc == 7)), reads=[Twg, TnhT], writes=[Tpb])
                        P.op("act", lambda e, pa=pa: e.activation(out=gch[:], in_=pa[:, 0:8], func=AF.Copy),
                             reads=[Tpa], writes=[Tgch])
                        P.op("dve", lambda e, pb=pb, cc=cc: e.tensor_tensor(out=uh[:, cc, :], in0=gch[:], in1=pb[:, 0:8],
                                                                           op=ALU.mult),
                             reads=[Tgch, Tpb], writes=[Tuh])
                    gi = 0
                    for s in range(4):
                        if s + 1 < 4:
                            load_slot(s + 1)
                        nT, TnT = nTb[s % 2], TnTb[s % 2]
                        srcs = [(xt[(s * 4 + tt) % 8][:], Txt[(s * 4 + tt) % 8]) for tt in range(4)]
                        norm_group(R, srcs, gp[:, G_MIX:G_MIX + 8], nT, TnT)
                        for hp in range(4):
                            pq, Tpq = pQ[hp % 2]
                            for dc in range(8):
                                P.op("pe", lambda e, pq=pq, dc=dc, hp=hp, nT=nT: e.matmul(
                                    pq[:], lhsT=wq[:, dc, hp * 128:(hp + 1) * 128], rhs=nT[:, dc, :],
                                    start=(dc == 0), stop=(dc == 7)), reads=[Twq, TnT], writes=[Tpq])
                            for hd in range(2):
                                r0, r1 = hd * 64, hd * 64 + 64
                                P.op("act", lambda e, pq=pq, hp=hp, s=s, hd=hd, r0=r0, r1=r1: e.activation(
                                    out=QT_sb[r0:r1, 2 * hp + hd, s * 512:(s + 1) * 512], in_=pq[r0:r1, :],
                                    func=AF.Copy, scale=0.125), reads=[Tpq], writes=[TQT])
                        for cc in range(4):
                            banks = []
                            for col0 in (512 + cc * 128, 1024 + cc * 128, cc * 128):
                                pg, Tpg = pG3[gi % 3]
                                gi += 1
                                for dc in range(8):
                                    P.op("pe", lambda e, pg=pg, dc=dc, col0=col0, nT=nT: e.matmul(
                                        pg[:], lhsT=wg[:, dc, col0:col0 + 128], rhs=nT[:, dc, :],
                                        start=(dc == 0), stop=(dc == 7)), reads=[Twg, TnT], writes=[Tpg])
                                banks.append((pg, Tpg))
                            (pgc, Tpgc), (pxi, Tpxi), (pgb, Tpgb) = banks
                            P.op("act", lambda e, pgc=pgc: e.activation(out=gc_sb[:], in_=pgc[:], func=AF.Copy),
                                 reads=[Tpgc], writes=[Tgc])
                            P.op("dve", lambda e, cc=cc, s=s: e.tensor_copy(out=u_sb[:, 0:2], in_=uh[:, cc, 2 * s:2 * s + 2]),
                                 reads=[Tuh], writes=[Tu])
                            P.op("dve", lambda e, pxi=pxi: e.tensor_tensor(out=u_sb[:, 2:514], in0=gc_sb[:], in1=pxi[:],
                                                                          op=ALU.mult),
                                 reads=[Tgc, Tpxi], writes=[Tu])
                            P.op("dve", lambda e, cc=cc: e.tensor_scalar(
                                out=acc[:], in0=u_sb[:, 2:514], scalar1=gp[:, G_CW + cc * 3 + 2:G_CW + cc * 3 + 3],
                                scalar2=None, op0=ALU.mult), reads=[Tu, Tc], writes=[Tacc])
                            P.op("dve", lambda e, cc=cc: e.scalar_tensor_tensor(
                                out=acc[:], in0=u_sb[:, 1:513], scalar=gp[:, G_CW + cc * 3 + 1:G_CW + cc * 3 + 2],
                                in1=acc[:], op0=ALU.mult, op1=ALU.add), reads=[Tu, Tc, Tacc], writes=[Tacc])
                            P.op("dve", lambda e, cc=cc: e.scalar_tensor_tensor(
                                out=acc[:], in0=u_sb[:, 0:512], scalar=gp[:, G_CW + cc * 3:G_CW + cc * 3 + 1],
                                in1=acc[:], op0=ALU.mult, op1=ALU.add), reads=[Tu, Tc, Tacc], writes=[Tacc])
                            P.op("dve", lambda e, pgb=pgb: e.tensor_tensor(out=conv[:], in0=acc[:], in1=pgb[:],
                                                                          op=ALU.mult),
                                 reads=[Tacc, Tpgb], writes=[Tconv])
                            P.op("act", lambda e: e.activation(out=sqc[:], in_=conv[:], func=AF.Square),
                                 reads=[Tconv], writes=[Tsqc])
                            P.op("act", lambda e, cc=cc, s=s: e.activation(
                                out=convT_sb[:, cc, s * 512:(s + 1) * 512], in_=conv[:], func=AF.Copy,
                                scale=gp[:, G_CONV + cc:G_CONV + cc + 1]), reads=[Tconv, Tc], writes=[TconvT])
                            for tt in range(4):
                                col = (s * 4 + tt) * 4 + cc
                                P.op("pe", lambda e, tt=tt, col=col: e.matmul(
                                    pss[:, col:col + 1], lhsT=sqc[:, tt * 128:(tt + 1) * 128], rhs=ones_f[:, 0:1],
                                    start=True, stop=True), reads=[Tsqc, Tc], writes=[Tpss])
                    P.op("dve", lambda e: e.tensor_reduce(out=ssq_c[:], in_=pss[:, 0:64].rearrange("p (t c) -> p t c", c=4),
                                                          axis=AX.X, op=ALU.add), reads=[Tpss], writes=[Tssqc])
                P.barrier()

            with contextlib.ExitStack() as ph:
                pz = [(palloc(ph, "pz%d" % i, [128, 512]), T("pz%d" % i)) for i in range(3)]
                pGc = [(palloc(ph, "pGc%d" % i, [128, 512]), T("pGc%d" % i)) for i in range(2)]
                pO = [(palloc(ph, "pO%d" % i, [128, 512]), T("pO%d" % i)) for i in range(2)]
                pss = palloc(ph, "pssb", [128, 512])
                Tpss = T("pssb")
                msk = alloc(ph, "msk", [128, 16, 512], BF16)
                Tmsk = T("msk")
                KT_sb = [alloc(ph, "KT%d" % i, [128, 8192], BF16) for i in range(2)]
                V_sb = [alloc(ph, "V%d" % i, [128, 64, 128], BF16) for i in range(2)]
                TKT = [T("KT0"), T("KT1")]
                TV = [T("V0"), T("V1")]
                NB = 4
                e1 = [alloc(ph, "e1_%d" % i, [128, 512], F32) for i in range(NB)]
                sp = [alloc(ph, "sp_%d" % i, [128, 512], F32) for i in range(NB)]
                Lb = [alloc(ph, "Lb_%d" % i, [128, 512], BF16) for i in range(NB)]
                t2 = [alloc(ph, "t2_%d" % i, [128, 512], F32) for i in range(NB)]
                Ab = [alloc(ph, "Ab_%d" % i, [128, 512], BF16) for i in range(NB)]
                tmpf = [alloc(ph, "tmpf_%d" % i, [128, 512], F32) for i in range(2)]
                Te1 = [T("e1") for _ in range(NB)]
                Tsp = [T("sp") for _ in range(NB)]
                TLb = [T("Lb") for _ in range(NB)]
                Tt2 = [T("t2") for _ in range(NB)]
                TAb = [T("Ab") for _ in range(NB)]
                Ttmpf = [T("tmpf0"), T("tmpf1")]
                sqs = alloc(ph, "sqs", [128, 512], F32)
                Tsqs = T("sqs")
                if stage >= 3:
                    P.dma(msk[:], maskd[:, :, :], writes=[Tmsk])
                    pairs = [(s, hp) for s in range(4) for hp in range(4)]

                    def load_kv(i):
                        s, hp = pairs[i]
                        b = i % 2
                        nk = (4 * s + 4) * 512
                        nblk = nk // 128
                        P.dma(KT_sb[b][:, 0:nk], kT_d[hp, :, 0:nk], reads=[TkTd], writes=[TKT[b]])
                        P.dma(V_sb[b][:, 0:nblk, :], v_d[hp, :, 0:nblk, :], reads=[Tvd], writes=[TV[b]])
                        precast_some(2)

                    steps = []
                    for i, (s, hp) in enumerate(pairs):
                        nblk = (4 * s + 4) * 4
                        for blk in range(nblk - 1, -1, -1):
                            for hd in range(2):
                                steps.append(dict(i=i, s=s, hp=hp, blk=blk, hd=hd, first=(blk == nblk - 1),
                                                  last=(blk == 0), pfirst=(blk == nblk - 1 and hd == 0),
                                                  plast=(blk == 0 and hd == 1)))
                    NS = len(steps)

                    def info(n):
                        d = steps[n]
                        blk, s = d["blk"], d["s"]
                        j, kb = blk // 4, blk % 4
                        return d, (j >= 4 * s), (j - 4 * s) * 4 + kb

                    def pe_z(n):
                        d = steps[n]
                        b = d["i"] % 2
                        z, Tz = pz[n % 3]
                        KT = KT_sb[b]
                        blk, hp, hd, s = d["blk"], d["hp"], d["hd"], d["s"]
                        P.op("pe", lambda e: e.matmul(
                            z[:], lhsT=KT[:, blk * 128:(blk + 1) * 128],
                            rhs=QT_sb[:, 2 * hp + hd, s * 512:(s + 1) * 512], start=True, stop=True),
                            reads=[TKT[b], TQT], writes=[Tz])

                    def act_s1(n):
                        z, Tz = pz[n % 3]
                        k = n % NB
                        P.op("act", lambda e: e.activation(out=e1[k][:], in_=z[:], func=AF.Exp, scale=-1.0),
                             reads=[Tz], writes=[Te1[k]])
                        P.op("act", lambda e: e.activation(out=sp[k][:], in_=e1[k][:], func=AF.Ln, bias=1.0),
                             reads=[Te1[k]], writes=[Tsp[k]])

                    def dve_L(n):
                        d, masked, mi = info(n)
                        z, Tz = pz[n % 3]
                        k = n % NB
                        if not masked:
                            P.op("dve", lambda e: e.tensor_tensor(out=Lb[k][:], in0=z[:], in1=sp[k][:], op=ALU.add),
                                 reads=[Tz, Tsp[k]], writes=[TLb[k]])
                        else:
                            tf, Ttf = tmpf[n % 2], Ttmpf[n % 2]
                            P.op("dve", lambda e: e.tensor_tensor(out=tf[:], in0=z[:], in1=sp[k][:], op=ALU.add),
                                 reads=[Tz, Tsp[k]], writes=[Ttf])
                            P.op("pool", lambda e: e.tensor_tensor(out=Lb[k][:], in0=tf[:], in1=msk[:, mi, :], op=ALU.mult),
                                 reads=[Ttf, Tmsk], writes=[TLb[k]])

                    def pe_mm1(n):
                        d = steps[n]
                        k = n % NB
                        G, TG = pGc[d["hd"]]
                        first = d["first"]
                        P.op("pe", lambda e: e.matmul(G[:], lhsT=Uneg[:], rhs=Lb[k][:], start=first, stop=True, skip_group_check=True),
                             reads=[TLb[k], Tc], writes=[TG])

                    def dve_t2(n):
                        d = steps[n]
                        k = n % NB
                        G, TG = pGc[d["hd"]]
                        P.op("dve", lambda e: e.tensor_tensor(out=t2[k][:], in0=G[:], in1=sp[k][:], op=ALU.subtract),
                             reads=[TG, Tsp[k]], writes=[Tt2[k]])

                    def pe_mm2(n):
                        d = steps[n]
                        if d["last"]:
                            return
                        k = n % NB
                        G, TG = pGc[d["hd"]]
                        P.op("pe", lambda e: e.matmul(G[:], lhsT=Unegb[:], rhs=Lb[k][:], start=False, stop=True, skip_group_check=True),
                             reads=[TLb[k], Tc], writes=[TG])

                    def act_A(n):
                        d, masked, mi = info(n)
                        k = n % NB
                        P.op("act", lambda e: e.activation(out=Ab[k][:], in_=t2[k][:], func=AF.Exp),
                             reads=[Tt2[k]], writes=[TAb[k]])

                    def mask_A(n):
                        d, masked, mi = info(n)
                        k = n % NB
                        if masked:
                            P.op("dve", lambda e: e.tensor_tensor(out=Ab[k][:], in0=Ab[k][:], in1=msk[:, mi, :], op=ALU.mult),
                                 reads=[TAb[k], Tmsk], writes=[TAb[k]])

                    def pe_O(n):
                        d = steps[n]
                        b = d["i"] % 2
                        k = n % NB
                        blk, hd, s, hp = d["blk"], d["hd"], d["s"], d["hp"]
                        O, TO = pO[hd]
                        V = V_sb[b]
                        first, last = d["first"], d["last"]
                        P.op("pe", lambda e: e.matmul(O[:], lhsT=V[:, blk, :], rhs=Ab[k][:], start=first, stop=last),
                             reads=[TV[b], TAb[k]], writes=[TO])
                        if d["plast"]:
                            for hd2 in range(2):
                                r0, r1 = hd2 * 64, hd2 * 64 + 64
                                O2, TO2 = pO[hd2]
                                P.op("act", lambda e, O2=O2, r0=r0, r1=r1: e.activation(
                                    out=sqs[r0:r1, :], in_=O2[r0:r1, :], func=AF.Square), reads=[], writes=[Tsqs, TO2])
                                P.op("dve", lambda e, O2=O2, r0=r0, r1=r1: e.tensor_scalar(
                                    out=sbT_sb[r0:r1, hp, s * 512:(s + 1) * 512], in0=O2[r0:r1, :],
                                    scalar1=gp[r0:r1, G_SB + hp:G_SB + hp + 1], scalar2=None, op0=ALU.mult),
                                    reads=[Tc], writes=[TsbT, TO2])
                            for tt in range(4):
                                col = (s * 4 + tt) * 4 + hp
                                P.op("pe", lambda e, tt=tt, col=col: e.matmul(
                                    pss[:, col:col + 1], lhsT=sqs[:, tt * 128:(tt + 1) * 128], rhs=ones_f[:, 0:1],
                                    start=True, stop=True), reads=[Tsqs, Tc], writes=[Tpss])

                    load_kv(0)
                    ok = lambda m: 0 <= m < NS
                    for n in range(NS + 5):
                        if ok(n):
                            pe_z(n)
                            act_s1(n)
                        if ok(n - 1):
                            dve_L(n - 1)
                        if ok(n - 2):
                            pe_mm1(n - 2)
                            dve_t2(n - 2)
                        if ok(n - 3):
                            pe_mm2(n - 3)
                        if ok(n - 4):
                            act_A(n - 4)
                        if ok(n - 5):
                            mask_A(n - 5)
                            pe_O(n - 5)
                            if steps[n - 5]["pfirst"]:
                                ni = steps[n - 5]["i"] + 1
                                if ni < len(pairs):
                                    load_kv(ni)
                    P.op("dve", lambda e: e.tensor_reduce(out=ssq_s[:], in_=pss[:, 0:64].rearrange("p (t c) -> p t c", c=4),
                                                          axis=AX.X, op=ALU.add), reads=[Tpss], writes=[Tssqs])
                P.barrier()

            if "d_sbT" in dbg_out:
                P.dma(dbg_out["d_sbT"].rearrange("c p n -> p c n"), sbT_sb[:], reads=[TsbT], writes=[T("x")])
                P.dma(dbg_out["d_convT"].rearrange("c p n -> p c n"), convT_sb[:], reads=[TconvT], writes=[T("x")])
                P.dma(dbg_out["d_ssq"][:, 0:16], ssq_s[:], reads=[Tssqs], writes=[T("x")])
                P.dma(dbg_out["d_ssq"][:, 16:32], ssq_c[:], reads=[Tssqc], writes=[T("x")])
                P.barrier()

            with contextlib.ExitStack() as ph:
                pP = [(palloc(ph, "pP%d" % i, [128, 512]), T("pP%d" % i)) for i in range(8)]
                wo = alloc(ph, "wo", [128, 8, 1024], BF16)
                Two = T("wo")
                rs = alloc(ph, "rs", [128, 32], F32)
                Trs = T("rs")
                xt = [alloc(ph, "xt%d" % i, [128, 1024], F32) for i in range(4)]
                Txt = [T("xt%d" % i) for i in range(4)]
                ht = [alloc(ph, "ht%d" % i, [128, 1024], F32) for i in range(2)]
                Tht = [T("ht0"), T("ht1")]
                Thd = T("h_d")
                if stage >= 4:
                    P.dma(wo[:], w_out[:, :].rearrange("(c p) n -> p c n", p=128), writes=[Two], qeng="pool")
                    P.op("dve", lambda e: e.tensor_scalar(out=rs[:, 0:16], in0=ssq_s[:], scalar1=1.0 / 512, scalar2=EPS,
                                                          op0=ALU.mult, op1=ALU.add), reads=[Tssqs], writes=[Trs])
                    P.op("dve", lambda e: e.tensor_scalar(out=rs[:, 16:32], in0=ssq_c[:], scalar1=1.0 / 512, scalar2=EPS,
                                                          op0=ALU.mult, op1=ALU.add), reads=[Tssqc, Trs], writes=[Trs])
                    P.op("act", lambda e: e.activation(out=rs[:], in_=rs[:], func=AF.Sqrt), reads=[Trs], writes=[Trs])
                    P.op("dve", lambda e: e.reciprocal(out=rs[:], in_=rs[:]), reads=[Trs], writes=[Trs])
                    for t in range(16):
                        xa, Txa = xt[t % 4], Txt[t % 4]
                        P.dma(xa[:], xq[t * 128:(t + 1) * 128, :], writes=[Txa])
                        h, Th = ht[t % 2], Tht[t % 2]
                        banks = [pP[(t % 2) * 4 + i] for i in range(4)]
                        for src, (Tsrc) in ((0, TsbT), (1, TconvT)):
                            srcT = sbT_sb if src == 0 else convT_sb
                            for half in range(2):
                                pb, Tpb = banks[src * 2 + half]
                                for c in range(4):
                                    P.op("pe", lambda e, pb=pb, srcT=srcT, c=c, t=t, src=src, half=half: e.matmul(
                                        pb[:], lhsT=srcT[:, c, t * 128:(t + 1) * 128],
                                        rhs=wo[:, src * 4 + c, half * 512:(half + 1) * 512],
                                        start=(c == 0), stop=(c == 3)), reads=[Tsrc, Two], writes=[Tpb])
                        for half in range(2):
                            pb, Tpb = banks[half]
                            P.op("dve", lambda e, pb=pb, h=h, xa=xa, half=half, t=t: e.scalar_tensor_tensor(
                                out=h[:, half * 512:(half + 1) * 512], in0=pb[:], scalar=rs[:, t:t + 1],
                                in1=xa[:, half * 512:(half + 1) * 512], op0=ALU.mult, op1=ALU.add),
                                reads=[Tpb, Trs, Txa], writes=[Th])
                        for half in range(2):
                            pb, Tpb = banks[2 + half]
                            P.op("dve", lambda e, pb=pb, h=h, half=half, t=t: e.scalar_tensor_tensor(
                                out=h[:, half * 512:(half + 1) * 512], in0=pb[:], scalar=rs[:, 16 + t:17 + t],
                                in1=h[:, half * 512:(half + 1) * 512], op0=ALU.mult, op1=ALU.add),
                                reads=[Tpb, Trs, Th], writes=[Th])
                        P.dma(h_d[t * 128:(t + 1) * 128, :], h[:], reads=[Th], writes=[Thd], qeng="pool")
                P.barrier()

        if "d_h1" in dbg_out:
            with contextlib.ExitStack() as ph:
                tmp = alloc(ph, "dbgtmp", [128, 16, 1024], F32)
                Tt = T("dbgtmp")
                P.dma(tmp[:], h_d.rearrange("(t p) n -> p t n", p=128), writes=[Tt])
                P.dma(dbg_out["d_h1"].rearrange("(t p) n -> p t n", p=128), tmp[:], reads=[Tt], writes=[T("x")])
                P.barrier()

        Thd = T("h_d")
        with contextlib.ExitStack() as ph:
            pT = [(palloc(ph, "pT%d" % i, [128, 512]), T("pT%d" % i)) for i in range(2)]
            pA = [(palloc(ph, "pA%d" % i, [128, 512]), T("pA%d" % i)) for i in range(2)]
            psc = palloc(ph, "psc", [128, 512])
            Tpsc = T("psc")
            pTp = palloc(ph, "pTp", [128, 1024], BF16)
            TpTp = T("pTp")
            poT = [(palloc(ph, "poT%d" % i, [128, 512]), T("poT%d" % i)) for i in range(2)]
            R = make_norm_res(ph, pT)
            wqm = alloc(ph, "wqm", [128, 8, 1024], BF16)
            wkvm = alloc(ph, "wkvm", [128, 8, 2048], BF16)
            wom = alloc(ph, "wom", [128, 8, 1024], BF16)
            Twqm, Twkvm, Twom = T("wqm"), T("wkvm"), T("wom")
            memt = [alloc(ph, "memt%d" % i, [128, 1024], F32) for i in range(2)]
            Tmemt = [T("memt0"), T("memt1")]
            memT = alloc(ph, "memT", [128, 8, 256], BF16)
            TmemT = T("memT")
            kTm = alloc(ph, "kTm", [128, 8, 256], BF16)
            vm = alloc(ph, "vm", [128, 2, 1024], BF16)
            TkTm, Tvm = T("kTm"), T("vm")
            ht = [alloc(ph, "ht%d" % i, [128, 1024], F32) for i in range(8)]
            Tht = [T("ht%d" % i) for i in range(8)]
            n2T = [alloc(ph, "n2T%d" % i, [128, 8, 512], BF16) for i in range(2)]
            Tn2T = [T("n2T0"), T("n2T1")]
            qTm = alloc(ph, "qTm", [128, 8, 512], BF16)
            TqTm = T("qTm")
            nmx = alloc(ph, "nmx", [128, 4], F32)
            rsum = alloc(ph, "rsum", [128, 4], F32)
            Tnmx = [T("nmx%d" % i) for i in range(4)]
            Trsum = [T("rsum%d" % i) for i in range(4)]
            pexp = [alloc(ph, "pexp%d" % i, [128, 256], F32) for i in range(2)]
            pn = [alloc(ph, "pn%d" % i, [128, 256], BF16) for i in range(2)]
            pTs = [alloc(ph, "pTs%d" % i, [128, 256], BF16) for i in range(2)]
            Tpexp = [T("pexp0"), T("pexp1")]
            Tpn = [T("pn0"), T("pn1")]
            TpTs = [T("pTs0"), T("pTs1")]
            oT_sb = alloc(ph, "oT_sb", [128, 8, 128], BF16)
            ToT = T("oT_sb")
            if stage >= 5:
                P.dma(wqm[:], w_q_mem[:, :].rearrange("(c p) n -> p c n", p=128), writes=[Twqm], qeng="pool")
                P.dma(wkvm[:], w_kv_mem[:, :].rearrange("(c p) n -> p c n", p=128), writes=[Twkvm], qeng="pool")
                P.dma(wom[:], w_o_mem[:, :].rearrange("(c p) n -> p c n", p=128), writes=[Twom], qeng="pool")
                for i in range(2):
                    P.dma(memt[i][:], memb[i * 128:(i + 1) * 128, :], writes=[Tmemt[i]])

                def load_hgroup(g):
                    for tt in range(4):
                        bb = (g * 4 + tt) % 8
                        r0 = (g * 4 + tt) * 128
                        P.dma(ht[bb][:], h_d[r0:r0 + 128, :], reads=[Thd], writes=[Tht[bb]])
                load_hgroup(0)
                norm_group(R, [(memt[0][:], Tmemt[0]), (memt[1][:], Tmemt[1])], gp[:, G_MEM:G_MEM + 8], memT, TmemT)
                for c in range(8):
                    pa, Tpa = pA[c % 2]
                    for dc in range(8):
                        P.op("pe", lambda e, pa=pa, dc=dc, c=c: e.matmul(
                            pa[:, 0:256], lhsT=wkvm[:, dc, c * 128:(c + 1) * 128], rhs=memT[:, dc, :],
                            start=(dc == 0), stop=(dc == 7)), reads=[Twkvm, TmemT], writes=[Tpa])
                    P.op("act", lambda e, pa=pa, c=c: e.activation(out=kTm[:, c, :], in_=pa[:, 0:256], func=AF.Copy),
                         reads=[Tpa], writes=[TkTm])
                for mc in range(2):
                    for half in range(2):
                        pa, Tpa = pA[half]
                        for dc in range(8):
                            P.op("pe", lambda e, pa=pa, dc=dc, mc=mc, half=half: e.matmul(
                                pa[:], lhsT=memT[:, dc, mc * 128:(mc + 1) * 128],
                                rhs=wkvm[:, dc, 1024 + half * 512:1024 + (half + 1) * 512],
                                start=(dc == 0), stop=(dc == 7)), reads=[Twkvm, TmemT], writes=[Tpa])
                        P.op("act", lambda e, pa=pa, mc=mc, half=half: e.activation(
                            out=vm[:, mc, half * 512:(half + 1) * 512], in_=pa[:], func=AF.Copy),
                            reads=[Tpa], writes=[Tvm])
                hk = 0
                for g in range(4):
                    if g + 1 < 4:
                        load_hgroup(g + 1)
                    nT, TnT = n2T[g % 2], Tn2T[g % 2]
                    srcs = [(ht[(g * 4 + tt) % 8][:], Tht[(g * 4 + tt) % 8]) for tt in range(4)]
                    norm_group(R, srcs, gp[:, G_XATTN:G_XATTN + 8], nT, TnT)
                    for c in range(8):
                        pa, Tpa = pA[c % 2]
                        for dc in range(8):
                            P.op("pe", lambda e, pa=pa, dc=dc, c=c, nT=nT: e.matmul(
                                pa[:], lhsT=wqm[:, dc, c * 128:(c + 1) * 128], rhs=nT[:, dc, :],
                                start=(dc == 0), stop=(dc == 7)), reads=[Twqm, TnT], writes=[Tpa])
                        P.op("act", lambda e, pa=pa, c=c: e.activation(out=qTm[:, c, :], in_=pa[:], func=AF.Copy,
                                                                      scale=1.0 / 16), reads=[Tpa], writes=[TqTm])
                    for tt in range(4):
                        bb = (g * 4 + tt) % 8
                        h, Th = ht[bb], Tht[bb]
                        for hd in range(4):
                            k2 = hk % 2
                            hk += 1
                            for c in range(2):
                                P.op("pe", lambda e, c=c, hd=hd, tt=tt: e.matmul(
                                    psc[:, 0:256], lhsT=qTm[:, 2 * hd + c, tt * 128:(tt + 1) * 128],
                                    rhs=kTm[:, 2 * hd + c, :], start=(c == 0), stop=(c == 1)),
                                    reads=[TqTm, TkTm], writes=[Tpsc])
                            P.op("dve", lambda e, hd=hd: e.tensor_reduce(out=nmx[:, hd:hd + 1], in_=psc[:, 0:256],
                                                                        axis=AX.X, op=ALU.max, negate=True),
                                 reads=[Tpsc], writes=[Tnmx[hd]])
                            P.op("act", lambda e, hd=hd, k2=k2: e.activation(
                                out=pexp[k2][:], in_=psc[:, 0:256], func=AF.Exp, bias=nmx[:, hd:hd + 1],
                                accum_out=rsum[:, hd:hd + 1]), reads=[Tnmx[hd]], writes=[Tpexp[k2], Trsum[hd], Tpsc])
                            P.op("dve", lambda e, hd=hd: e.reciprocal(out=rsum[:, hd:hd + 1], in_=rsum[:, hd:hd + 1]),
                                 reads=[Trsum[hd]], writes=[Trsum[hd]])
                            P.op("dve", lambda e, hd=hd, k2=k2: e.tensor_scalar(
                                out=pn[k2][:], in0=pexp[k2][:], scalar1=rsum[:, hd:hd + 1], scalar2=None, op0=ALU.mult),
                                reads=[Tpexp[k2], Trsum[hd]], writes=[Tpn[k2]])
                            for mc in range(2):
                                P.op("pe", lambda e, mc=mc, k2=k2: e.transpose(
                                    out=pTp[:, mc * 128:(mc + 1) * 128], in_=pn[k2][:, mc * 128:(mc + 1) * 128],
                                    identity=ident_b[:]), reads=[Tpn[k2], Tc], writes=[TpTp])
                            P.op("act", lambda e, k2=k2: e.activation(out=pTs[k2][:], in_=pTp[:, 0:256], func=AF.Copy),
                                 reads=[TpTp], writes=[TpTs[k2]])
                            for dch in range(2):
                                ch = 2 * hd + dch
                                po, Tpo = poT[ch // 4]
                                for mc in range(2):
                                    P.op("pe", lambda e, po=po, ch=ch, mc=mc, hd=hd, dch=dch, k2=k2: e.matmul(
                                        po[:, (ch % 4) * 128:(ch % 4 + 1) * 128],
                                        lhsT=vm[:, mc, hd * 256 + dch * 128:hd * 256 + (dch + 1) * 128],
                                        rhs=pTs[k2][:, mc * 128:(mc + 1) * 128], start=(mc == 0), stop=(mc == 1)),
                                        reads=[Tvm, TpTs[k2]], writes=[Tpo])
                        for i2 in range(2):
                            po, Tpo = poT[i2]
                            P.op("dve" if i2 == 0 else "act",
                                 (lambda e, po=po, i2=i2: e.tensor_copy(
                                     out=oT_sb[:, i2 * 4:(i2 + 1) * 4, :].rearrange("p c n -> p (c n)"), in_=po[:]))
                                 if i2 == 0 else
                                 (lambda e, po=po, i2=i2: e.activation(
                                     out=oT_sb[:, i2 * 4:(i2 + 1) * 4, :].rearrange("p c n -> p (c n)"), in_=po[:],
                                     func=AF.Copy)),
                                 reads=[Tpo], writes=[ToT])
                        for half in range(2):
                            pa, Tpa = pA[half]
                            for c in range(8):
                                P.op("pe", lambda e, pa=pa, c=c, half=half: e.matmul(
                                    pa[:], lhsT=oT_sb[:, c, :], rhs=wom[:, c, half * 512:(half + 1) * 512],
                                    start=(c == 0), stop=(c == 7)), reads=[ToT, Twom], writes=[Tpa])
                            P.op("dve", lambda e, pa=pa, h=h, half=half: e.tensor_tensor(
                                out=h[:, half * 512:(half + 1) * 512], in0=pa[:], in1=h[:, half * 512:(half + 1) * 512],
                                op=ALU.add), reads=[Tpa, Th], writes=[Th])
                        r0 = (g * 4 + tt) * 128
                        P.dma(h_d[r0:r0 + 128, :], h[:], reads=[Th], writes=[Thd], qeng="pool")
            P.barrier()

        if "d_h2" in dbg_out:
            with contextlib.ExitStack() as ph:
                tmp = alloc(ph, "dbgtmp2", [128, 16, 1024], F32)
                Tt = T("dbgtmp2")
                P.dma(tmp[:], h_d.rearrange("(t p) n -> p t n", p=128), reads=[Thd], writes=[Tt])
                P.dma(dbg_out["d_h2"].rearrange("(t p) n -> p t n", p=128), tmp[:], reads=[Tt], writes=[T("x")])
                P.barrier()

        with contextlib.ExitStack() as sE:
            iota128 = alloc(sE, "iota128", [128, 128], F32)
            c16 = alloc(sE, "c16", [128, 16], F32)
            i16 = alloc(sE, "i16", [128, 16], F32)
            sE1 = contextlib.ExitStack()
            n3T = alloc(sE1, "n3T", [128, 8, 2048], BF16)
            Tn3T = T("n3T")
            IDX0 = alloc(sE1, "IDX0", [128, 16, 128], F32)
            IDX1 = alloc(sE1, "IDX1", [128, 16, 128], F32)
            GATE = alloc(sE1, "GATE", [128, 16, 128], F32)
            TIDX = [T("IDX%d" % i) for i in range(16)]
            Tci = T("peer_consts")
            with contextlib.ExitStack() as ph:
                pT = [(palloc(ph, "pT%d" % i, [128, 512]), T("pT%d" % i)) for i in range(2)]
                pA = [(palloc(ph, "pA%d" % i, [128, 512]), T("pA%d" % i)) for i in range(2)]
                pscr = palloc(ph, "pscr", [128, 2048])
                Tpscr = T("pscr")
                R = make_norm_res(ph, pT)
                wqp = alloc(ph, "wqp", [128, 8, 2048], BF16)
                skb = alloc(ph, "skb", [128, 16, 128], BF16)
                Twqp, Tskb = T("wqp"), T("skb")
                ht = [alloc(ph, "ht%d" % i, [128, 1024], F32) for i in range(8)]
                Tht = [T("ht%d" % i) for i in range(8)]
                qTp = alloc(ph, "qTp", [128, 16, 512], BF16)
                TqTp = T("qTp")
                sc_sb = alloc(ph, "sc_sb", [128, 2048], F32)
                Tsc = T("sc_sb")
                scw = alloc(ph, "scw", [128, 256], F32)
                Tscw = T("scw")
                top_s = alloc(ph, "top_s", [128, 16, 16], F32)
                top_i = alloc(ph, "top_i", [128, 16, 16], U32)
                top_if = alloc(ph, "top_if", [128, 16, 16], F32)
                Ttop = T("top")
                Ttops = [T("tops%d" % i) for i in range(16)]
                Ttops2 = [T("tops2_%d" % i) for i in range(16)]
                Ttopi = [T("topi%d" % i) for i in range(16)]
                Ttopi2 = [T("topi2_%d" % i) for i in range(16)]
                scw4 = [alloc(ph, "scw4_%d" % i, [128, 256], F32) for i in range(4)]
                Tscw4 = [T("scw4_%d" % i) for i in range(4)]
                Tbs = [T("bs%d" % i) for i in range(8)]
                Tbs2 = [T("bs2_%d" % i) for i in range(8)]
                Tbj = [T("bj%d" % i) for i in range(8)]
                Tbj2 = [T("bj2_%d" % i) for i in range(8)]
                cand = alloc(ph, "cand", [128, 8, 256], F32)
                Tcand = T("cand")
                best_s = alloc(ph, "best_s", [128, 8, 16], F32)
                best_j = alloc(ph, "best_j", [128, 8, 16], U32)
                jf = alloc(ph, "jf", [128, 8, 16], F32)
                Tbest = T("best")
                big = [alloc(ph, "big%d" % i, [128, 8, 16, 16], F32) for i in range(3)]
                Tbig = [T("big%d" % i) for i in range(3)]
                sm = [alloc(ph, "sm%d" % i, [128, 8, 16], F32) for i in range(3)]
                Tsm = [T("sm%d" % i) for i in range(3)]
                s8 = alloc(ph, "s8", [128, 8], F32)
                Ts8 = T("s8")
                if stage >= 6:
                    P.dma(wqp[:], w_query[:, :].rearrange("(c p) n -> p c n", p=128), writes=[Twqp], qeng="pool")
                    P.dma(skb[:], skT[:, :, :], writes=[Tskb], qeng="pool")
                    P.op("pool", lambda e: e.iota(iota128[:], pattern=[[1, 128]], base=0, channel_multiplier=0,
                                                  allow_small_or_imprecise_dtypes=True), writes=[Tci])
                    P.op("pool", lambda e: e.iota(c16[:], pattern=[[16, 16]], base=0, channel_multiplier=0,
                                                  allow_small_or_imprecise_dtypes=True), writes=[Tci])
                    P.op("pool", lambda e: e.iota(i16[:], pattern=[[1, 16]], base=0, channel_multiplier=0,
                                                  allow_small_or_imprecise_dtypes=True), writes=[Tci])

                    def load_hgroup(g):
                        for tt in range(4):
                            bb = (g * 4 + tt) % 8
                            r0 = (g * 4 + tt) * 128
                            P.dma(ht[bb][:], h_d[r0:r0 + 128, :], reads=[Thd], writes=[Tht[bb]])
                    load_hgroup(0)
                    B4 = [128, 8, 16, 16]
                    for g in range(4):
                        if g + 1 < 4:
                            load_hgroup(g + 1)
                        srcs = [(ht[(g * 4 + tt) % 8][:], Tht[(g * 4 + tt) % 8]) for tt in range(4)]
                        nTg = n3T[:, :, g * 512:(g + 1) * 512]
                        norm_group(R, srcs, gp[:, G_FFN:G_FFN + 8], nTg, Tn3T)
                        for c in range(16):
                            pa, Tpa = pA[c % 2]
                            for dc in range(8):
                                P.op("pe", lambda e, pa=pa, dc=dc, c=c, g=g: e.matmul(
                                    pa[:], lhsT=wqp[:, dc, c * 128:(c + 1) * 128], rhs=n3T[:, dc, g * 512:(g + 1) * 512],
                                    start=(dc == 0), stop=(dc == 7)), reads=[Twqp, Tn3T], writes=[Tpa])
                            P.op("act", lambda e, pa=pa, c=c: e.activation(out=qTp[:, c, :], in_=pa[:], func=AF.Copy),
                                 reads=[Tpa], writes=[TqTp])
                        for tt in range(4):
                            t = g * 4 + tt
                            for hc in range(16):
                                P.op("pe", lambda e, hc=hc, tt=tt: e.matmul(
                                    pscr[:, hc * 128:(hc + 1) * 128], lhsT=qTp[:, hc, tt * 128:(tt + 1) * 128],
                                    rhs=skb[:, hc, :], start=True, stop=True), reads=[TqTp, Tskb], writes=[Tpscr])
                            P.op("act", lambda e: e.activation(out=sc_sb[:], in_=pscr[:], func=AF.Copy),
                                 reads=[Tpscr], writes=[Tsc])
                            for hc0 in range(0, 16, 4):
                                grp = list(range(hc0, hc0 + 4))
                                srcs_ = {hc: sc_sb[:, hc * 128:(hc + 1) * 128] for hc in grp}
                                for hc in grp:
                                    P.op("dve", lambda e, hc=hc: e.max(out=top_s[:, hc, 0:8], in_=srcs_[hc]) if False else
                                         e.max(out=top_s[:, hc, 0:8], in_=sc_sb[:, hc * 128:(hc + 1) * 128]),
                                         reads=[Tsc], writes=[Ttops[hc]])
                                for hc in grp:
                                    P.op("dve", lambda e, hc=hc: e.max_index(
                                        out=top_i[:, hc, 0:8], in_max=top_s[:, hc, 0:8],
                                        in_values=sc_sb[:, hc * 128:(hc + 1) * 128]),
                                        reads=[Tsc, Ttops[hc]], writes=[Ttopi[hc]])
                                for hc in grp:
                                    P.op("dve", lambda e, hc=hc: e.match_replace(
                                        out=scw4[hc % 4][:, 0:128], in_to_replace=top_s[:, hc, 0:8],
                                        in_values=sc_sb[:, hc * 128:(hc + 1) * 128], imm_value=-1e30),
                                        reads=[Tsc, Ttops[hc]], writes=[Tscw4[hc % 4]])
                                for hc in grp:
                                    P.op("dve", lambda e, hc=hc: e.max(out=top_s[:, hc, 8:16], in_=scw4[hc % 4][:, 0:128]),
                                         reads=[Tscw4[hc % 4]], writes=[Ttops2[hc]])
                                for hc in grp:
                                    P.op("dve", lambda e, hc=hc: e.max_index(
                                        out=top_i[:, hc, 8:16], in_max=top_s[:, hc, 8:16], in_values=scw4[hc % 4][:, 0:128]),
                                        reads=[Tscw4[hc % 4], Ttops2[hc]], writes=[Ttopi2[hc]])
                            P.op("dve", lambda e: e.tensor_copy(out=top_if[:], in_=top_i[:]),
                                 reads=Ttops + Ttops2 + Ttopi + Ttopi2, writes=[Ttop])
                            ts4 = top_s[:, :, :].rearrange("p (h c) k -> p h c k", c=2)
                            ti4 = top_if[:, :, :].rearrange("p (h c) k -> p h c k", c=2)
                            P.op("dve", lambda e, ts4=ts4: e.tensor_tensor(
                                out=cand[:, :, :].rearrange("p h (a b) -> p h a b", b=16),
                                in0=ts4[:, :, 0, :].unsqueeze(3).broadcast_to(B4),
                                in1=ts4[:, :, 1, :].unsqueeze(2).broadcast_to(B4), op=ALU.add),
                                reads=[Ttop] + Ttops + Ttops2, writes=[Tcand])
                            for h0 in range(0, 8, 4):
                                grp = list(range(h0, h0 + 4))
                                for h8 in grp:
                                    P.op("dve", lambda e, h8=h8: e.max(out=best_s[:, h8, 0:8], in_=cand[:, h8, :]),
                                         reads=[Tcand], writes=[Tbs[h8]])
                                for h8 in grp:
                                    P.op("dve", lambda e, h8=h8: e.max_index(out=best_j[:, h8, 0:8],
                                                                            in_max=best_s[:, h8, 0:8], in_values=cand[:, h8, :]),
                                         reads=[Tcand, Tbs[h8]], writes=[Tbj[h8]])
                                for h8 in grp:
                                    P.op("dve", lambda e, h8=h8: e.match_replace(
                                        out=scw4[h8 % 4][:, 0:256], in_to_replace=best_s[:, h8, 0:8], in_values=cand[:, h8, :],
                                        imm_value=-1e30), reads=[Tcand, Tbs[h8]], writes=[Tscw4[h8 % 4]])
                                for h8 in grp:
                                    P.op("dve", lambda e, h8=h8: e.max(out=best_s[:, h8, 8:16], in_=scw4[h8 % 4][:, 0:256]),
                                         reads=[Tscw4[h8 % 4]], writes=[Tbs2[h8]])
                                for h8 in grp:
                                    P.op("dve", lambda e, h8=h8: e.max_index(out=best_j[:, h8, 8:16],
                                                                            in_max=best_s[:, h8, 8:16],
                                                                            in_values=scw4[h8 % 4][:, 0:256]),
                                         reads=[Tscw4[h8 % 4], Tbs2[h8]], writes=[Tbj2[h8]])
                            P.op("dve", lambda e: e.tensor_copy(out=jf[:], in_=best_j[:]),
                                 reads=Tbs + Tbs2 + Tbj + Tbj2, writes=[Tbest])
                            c16b = c16[:, :].unsqueeze(1).unsqueeze(1).broadcast_to(B4)
                            i16b = i16[:, :].unsqueeze(1).unsqueeze(1).broadcast_to(B4)
                            P.op("dve", lambda e, c16b=c16b: e.tensor_tensor(
                                out=big[0][:], in0=jf[:, :, :].unsqueeze(3).broadcast_to(B4), in1=c16b, op=ALU.subtract),
                                reads=[Tbest, Tci], writes=[Tbig[0]])
                            P.op("dve", lambda e: e.tensor_scalar(out=big[1][:], in0=big[0][:], scalar1=0.0, scalar2=None,
                                                                  op0=ALU.is_ge), reads=[Tbig[0]], writes=[Tbig[1]])
                            P.op("dve", lambda e: e.scalar_tensor_tensor(out=big[2][:], in0=big[0][:], scalar=16.0,
                                                                         in1=big[1][:], op0=ALU.is_lt, op1=ALU.mult),
                                 reads=[Tbig[0], Tbig[1]], writes=[Tbig[2]])
                            P.op("dve", lambda e, ti4=ti4: e.tensor_tensor(
                                out=big[0][:], in0=big[2][:], in1=ti4[:, :, 0, :].unsqueeze(2).broadcast_to(B4), op=ALU.mult),
                                reads=[Tbig[2], Ttop], writes=[Tbig[0]])
                            P.op("dve", lambda e, t=t: e.tensor_reduce(
                                out=IDX0[:, t, :].rearrange("p (h k) -> p h k", k=16), in_=big[0][:], axis=AX.X, op=ALU.add),
                                reads=[Tbig[0]], writes=[TIDX[t]])
                            P.op("dve", lambda e, i16b=i16b: e.tensor_tensor(out=big[1][:], in0=big[2][:], in1=i16b, op=ALU.mult),
                                 reads=[Tbig[2], Tci], writes=[Tbig[1]])
                            P.op("dve", lambda e: e.tensor_reduce(out=sm[0][:], in_=big[1][:], axis=AX.X, op=ALU.add),
                                 reads=[Tbig[1]], writes=[Tsm[0]])
                            P.op("dve", lambda e: e.scalar_tensor_tensor(out=sm[1][:], in0=sm[0][:], scalar=-16.0, in1=jf[:],
                                                                         op0=ALU.mult, op1=ALU.add),
                                 reads=[Tsm[0], Tbest], writes=[Tsm[1]])
                            P.op("dve", lambda e, i16b=i16b: e.tensor_tensor(
                                out=big[0][:], in0=sm[1][:, :, :].unsqueeze(3).broadcast_to(B4), in1=i16b, op=ALU.is_equal),
                                reads=[Tsm[1], Tci], writes=[Tbig[0]])
                            P.op("dve", lambda e, ti4=ti4: e.tensor_tensor(
                                out=big[1][:], in0=big[0][:], in1=ti4[:, :, 1, :].unsqueeze(2).broadcast_to(B4), op=ALU.mult),
                                reads=[Tbig[0], Ttop], writes=[Tbig[1]])
                            P.op("dve", lambda e, t=t: e.tensor_reduce(
                                out=IDX1[:, t, :].rearrange("p (h k) -> p h k", k=16), in_=big[1][:], axis=AX.X, op=ALU.add),
                                reads=[Tbig[1]], writes=[TIDX[t]])
                            P.op("dve", lambda e: e.tensor_tensor(
                                out=sm[2][:], in0=best_s[:], in1=best_s[:, :, 0:1].broadcast_to([128, 8, 16]), op=ALU.subtract),
                                reads=[Tbest], writes=[Tsm[2]])
                            P.op("act", lambda e: e.activation(out=sm[2][:], in_=sm[2][:], func=AF.Exp),
                                 reads=[Tsm[2]], writes=[Tsm[2]])
                            P.op("dve", lambda e: e.tensor_reduce(out=s8[:], in_=sm[2][:], axis=AX.X, op=ALU.add),
                                 reads=[Tsm[2]], writes=[Ts8])
                            P.op("dve", lambda e: e.reciprocal(out=s8[:], in_=s8[:]), reads=[Ts8], writes=[Ts8])
                            P.op("dve", lambda e, t=t: e.tensor_tensor(
                                out=GATE[:, t, :].rearrange("p (h k) -> p h k", k=16), in0=sm[2][:],
                                in1=s8[:, :].unsqueeze(2).broadcast_to([128, 8, 16]), op=ALU.mult),
                                reads=[Tsm[2], Ts8], writes=[TIDX[t]])
                P.barrier()

            if "d_idx" in dbg_out:
                P.dma(dbg_out["d_idx"][0].rearrange("(t p) n -> p t n", p=128), IDX0[:], reads=TIDX, writes=[T("x")])
                P.dma(dbg_out["d_idx"][1].rearrange("(t p) n -> p t n", p=128), IDX1[:], reads=TIDX, writes=[T("x")])
                P.dma(dbg_out["d_idx"][2].rearrange("(t p) n -> p t n", p=128), GATE[:], reads=TIDX, writes=[T("x")])
                P.barrier()

            precast_some(len(precast_jobs))
            Tn3d, Tidxd = T("n3_d"), T("idx_d")
            if stage >= 7:
                P.dma(n3_d[:, :, :], n3T[:], reads=[Tn3T], writes=[Tn3d], qeng="pool")
                for i3, srcI in enumerate((IDX0, IDX1, GATE)):
                    P.dma(idx_d[i3], srcI[:], reads=TIDX, writes=[Tidxd], qeng="pool")
            P.barrier()
            sE1.close()
            with contextlib.ExitStack() as ph:
                pout = [(palloc(ph, "pout%d" % i, [128, 512]), T("pout%d" % i)) for i in range(4)]
                pact = [(palloc(ph, "pact%d" % i, [128, 512]), T("pact%d" % i)) for i in range(2)]
                pG = palloc(ph, "pG", [128, 512])
                TpG = T("pG")
                ptr = palloc(ph, "ptr", [128, 512])
                Tptr = T("ptr")
                GT = [alloc(ph, "GT%d" % i, [128, 256, 128], BF16) for i in range(2)]
                TGT = [T("GT0"), T("GT1")]
                n3p = [alloc(ph, "n3p%d" % i, [128, 8, 256], BF16) for i in range(2)]
                Tn3p = [T("n3p0"), T("n3p1")]
                ip = [alloc(ph, "ip%d" % i, [128, 3, 2, 128], F32) for i in range(2)]
                Tip = [T("ip0"), T("ip1")]
                trT = [alloc(ph, "trT%d" % i, [128, 3, 128], F32) for i in range(2)]
                TtrT = [T("trT0"), T("trT1")]
                NOH = 8
                Aoh = [alloc(ph, "Aoh%d" % i, [128, 128], BF16) for i in range(NOH)]
                Boh = [alloc(ph, "Boh%d" % i, [128, 128], BF16) for i in range(NOH)]
                TAoh = [T("Aoh%d" % i) for i in range(NOH)]
                TBoh = [T("Boh%d" % i) for i in range(NOH)]
                NUB = 4
                ub = [alloc(ph, "ub%d" % i, [128, 8, 256], BF16) for i in range(NUB)]
                vb = [alloc(ph, "vb%d" % i, [128, 2, 1024], BF16) for i in range(NUB)]
                Tub = [T("ub%d" % i) for i in range(NUB)]
                Tvb = [T("vb%d" % i) for i in range(NUB)]
                ga = [alloc(ph, "ga%d" % i, [128, 256], BF16) for i in range(3)]
                coef = [alloc(ph, "coef%d" % i, [128, 256], BF16) for i in range(3)]
                Tga = [T("ga%d" % i) for i in range(3)]
                Tcoef = [T("coef%d" % i) for i in range(3)]
                hf = [alloc(ph, "hf%d" % i, [128, 1024], F32) for i in range(2)]
                Thf = [T("hf0"), T("hf1")]
                gf = alloc(ph, "gf", [128, 1024], F32)
                Tgf = T("gf")
                junk2 = alloc(ph, "junk2", [128, 1024], BF16)
                Tjunk2 = T("junk2")
                fs = alloc(ph, "fs", [128, 2], F32)
                Tfs = [T("fs0"), T("fs1")]
                To = T("out")
                NPASS = 8 if stage >= 7 else 0
                if NPASS:
                    P.dma(gf[:], gfin[:, :], writes=[Tgf])

                def load_pass(p):
                    b = p % 2
                    P.dma(n3p[b][:], n3_d[:, :, p * 256:(p + 1) * 256], reads=[Tn3d], writes=[Tn3p[b]])
                    for i3 in range(3):
                        P.dma(ip[b][:, i3, :, :], idx_d[i3, :, 2 * p:2 * p + 2, :], reads=[Tidxd], writes=[Tip[b]])

                def load_blk(bk):
                    bi_ = bk % NUB
                    c0 = bk * 2
                    P.dma(ub[bi_][:], eu_b[bk, :, :, :], reads=[Teub], writes=[Tub[bi_]])
                    P.dma(vb[bi_][:], ev_b[c0 * 128:c0 * 128 + 256, :].rearrange("(k p) d -> p k d", p=128),
                          reads=[Tevb], writes=[Tvb[bi_]])

                noh = [0]

                def gb_tr(p, tl):
                    b = p % 2
                    for i3 in range(3):
                        P.op("pe", lambda e, i3=i3: e.transpose(
                            out=ptr[:, i3 * 128:(i3 + 1) * 128], in_=ip[b][:, i3, tl, :], identity=ident_f[:]),
                            reads=[Tip[b], Tc], writes=[Tptr])
                    P.op("act", lambda e: e.activation(out=trT[tl][:, :, :].rearrange("p a n -> p (a n)"),
                                                       in_=ptr[:, 0:384], func=AF.Copy), reads=[Tptr], writes=[TtrT[tl]])

                def gb_oh(p, tok):
                    tl, tk = tok // 128, tok % 128
                    k = noh[0] % NOH
                    noh[0] += 1
                    P.op("dve", lambda e: e.tensor_scalar(
                        out=Boh[k][:], in0=iota128[:], scalar1=trT[tl][:, 1, tk:tk + 1], scalar2=trT[tl][:, 2, tk:tk + 1],
                        op0=ALU.is_equal, op1=ALU.mult), reads=[TtrT[tl], Tci], writes=[TBoh[k]])
                    P.op("dve", lambda e: e.tensor_scalar(
                        out=Aoh[k][:], in0=iota128[:], scalar1=trT[tl][:, 0, tk:tk + 1], scalar2=None,
                        op0=ALU.is_equal), reads=[TtrT[tl], Tci], writes=[TAoh[k]])
                    return k

                def gb_mm(tok, k):
                    P.op("pe", lambda e: e.matmul(pG[:, (tok % 4) * 128:(tok % 4 + 1) * 128], lhsT=Boh[k][:], rhs=Aoh[k][:],
                                                  start=True, stop=True), reads=[TBoh[k], TAoh[k]], writes=[TpG])

                def gb_evac(p, tok0):
                    b = p % 2
                    P.op("act", lambda e: e.activation(
                        out=GT[b][:, tok0:tok0 + 4, :].rearrange("p t n -> p (t n)"), in_=pG[:], func=AF.Copy),
                        reads=[TpG], writes=[TGT[b]])

                def U(c, p):
                    b = p % 2
                    bi = (c // 2) % NUB
                    pa, Tpa = pact[c % 2]
                    k3 = c % 3
                    for dc in range(8):
                        P.op("pe", lambda e, dc=dc: e.matmul(
                            pa[:, 0:256], lhsT=ub[bi][:, dc, (c % 2) * 128:(c % 2 + 1) * 128],
                            rhs=n3p[b][:, dc, :], start=(dc == 0), stop=(dc == 7)),
                            reads=[Tub[bi], Tn3p[b]], writes=[Tpa])
                    P.op("act", lambda e: e.activation(out=ga[k3][:], in_=pa[:, 0:256], func=AF.Gelu),
                         reads=[Tpa], writes=[Tga[k3]])
                    P.op("dve", lambda e: e.tensor_tensor(out=coef[k3][:], in0=ga[k3][:], in1=GT[b][:, :, c], op=ALU.mult),
                         reads=[Tga[k3], TGT[b]], writes=[Tcoef[k3]])

                def Vv(c):
                    bi = (c // 2) % NUB
                    k3 = c % 3
                    for tl in range(2):
                        for half in range(2):
                            po, Tpo = pout[tl * 2 + half]
                            P.op("pe", lambda e, po=po, tl=tl, half=half: e.matmul(
                                po[:], lhsT=coef[k3][:, tl * 128:(tl + 1) * 128],
                                rhs=vb[bi][:, c % 2, half * 512:(half + 1) * 512], start=(c == 0), stop=(c == 127)),
                                reads=[Tcoef[k3], Tvb[bi]], writes=[Tpo])

                def finish_pass(p):
                    for tl in range(2):
                        t = p * 2 + tl
                        h, Th = hf[tl], Thf[tl]
                        P.dma(h[:], h_d[t * 128:(t + 1) * 128, :], reads=[Thd], writes=[Th])
                        for half in range(2):
                            po, Tpo = pout[tl * 2 + half]
                            P.op("dve", lambda e, po=po, h=h, half=half: e.tensor_tensor(
                                out=h[:, half * 512:(half + 1) * 512], in0=po[:], in1=h[:, half * 512:(half + 1) * 512],
                                op=ALU.add), reads=[Tpo, Th], writes=[Th])
                        P.op("act", lambda e, h=h, tl=tl: e.activation(out=junk2[:], in_=h[:], func=AF.Square,
                                                                      accum_out=fs[:, tl:tl + 1]),
                             reads=[Th], writes=[Tjunk2, Tfs[tl]])
                        P.op("dve", lambda e, tl=tl: e.tensor_scalar(out=fs[:, tl:tl + 1], in0=fs[:, tl:tl + 1],
                                                                    scalar1=1.0 / 1024, scalar2=EPS, op0=ALU.mult, op1=ALU.add),
                             reads=[Tfs[tl]], writes=[Tfs[tl]])
                        P.op("act", lambda e, tl=tl: e.activation(out=fs[:, tl:tl + 1], in_=fs[:, tl:tl + 1], func=AF.Sqrt),
                             reads=[Tfs[tl]], writes=[Tfs[tl]])
                        P.op("dve", lambda e, tl=tl: e.reciprocal(out=fs[:, tl:tl + 1], in_=fs[:, tl:tl + 1]),
                             reads=[Tfs[tl]], writes=[Tfs[tl]])
                        P.op("dve", lambda e, h=h, tl=tl: e.scalar_tensor_tensor(
                            out=h[:], in0=h[:], scalar=fs[:, tl:tl + 1], in1=gf[:], op0=ALU.mult, op1=ALU.mult),
                            reads=[Th, Tfs[tl], Tgf], writes=[Th])
                        P.dma(out[t * 128:(t + 1) * 128, :], h[:], reads=[Th], writes=[To])

                if NPASS:
                    load_pass(0)
                    load_pass(1)
                    for tl in range(2):
                        gb_tr(0, tl)
                        pend = []
                        for tk in range(128):
                            tok = tl * 128 + tk
                            k = gb_oh(0, tok)
                            gb_mm(tok, k)
                            if tok % 4 == 3:
                                gb_evac(0, tok - 3)
                for p in range(NPASS):
                    nxt = p + 1 if p + 1 < NPASS else None
                    for bk in range(3):
                        load_blk(bk)
                    if nxt is not None:
                        gb_tr(nxt, 0)
                    pend = []
                    for c in range(128 + 2):
                        if nxt is not None:
                            if c >= 3 and c % 2 == 1:
                                gb_evac(nxt, (c - 3) * 2)
                            if c == 63:
                                gb_tr(nxt, 1)
                        if c < 128:
                            U(c, p)
                        if nxt is not None:
                            for tok, k in pend:
                                gb_mm(tok, k)
                            pend = []
                            if c < 128:
                                for tok in (2 * c, 2 * c + 1):
                                    pend.append((tok, gb_oh(nxt, tok)))
                        if c - 2 >= 0:
                            Vv(c - 2)
                            if (c - 2) % 2 == 1:
                                nb = (c - 2) // 2 + 3
                                if nb < 64:
                                    load_blk(nb)
                    finish_pass(p)
                    if p + 2 < NPASS:
                        load_pass(p + 2)
                P.barrier()
        P.barrier()
        P.emit()
    return nc, P


def prep_inputs(inputs):
    f32 = np.float32
    x = np.asarray(inputs["x"], f32)
    mem = np.asarray(inputs["mem"], f32)

    def cols(v):
        v = np.asarray(v, f32).reshape(-1, 128)
        return np.ascontiguousarray(v.T)

    gpack = np.zeros((128, NGP), f32)
    gpack[:, G_MIX:G_MIX + 8] = cols(inputs["g_mix"][0])
    gpack[:, G_XATTN:G_XATTN + 8] = cols(inputs["g_xattn"][0])
    gpack[:, G_MEM:G_MEM + 8] = cols(inputs["g_mem"][0])
    gpack[:, G_FFN:G_FFN + 8] = cols(inputs["g_ffn"][0])
    gpack[:, G_SB:G_SB + 4] = cols(inputs["g_sb_out"][0])
    gpack[:, G_CONV:G_CONV + 4] = cols(inputs["g_conv_out"][0])
    cw = np.asarray(inputs["conv_w"][0], f32)
    for cc in range(4):
        for k in range(3):
            gpack[:, G_CW + cc * 3 + k] = cw[k, cc * 128:(cc + 1) * 128]
    gfin = np.ascontiguousarray(np.broadcast_to(np.asarray(inputs["g_final"], f32)[None, :], (128, 1024)))
    sk = np.asarray(inputs["sub_keys"][0], f32)
    skT = np.ascontiguousarray(sk.reshape(16, 128, 128).transpose(2, 0, 1))
    euT = np.ascontiguousarray(np.asarray(inputs["expert_u"][0], f32).T)
    ev = np.ascontiguousarray(np.asarray(inputs["expert_v"][0], f32))
    shared = dict(
        gpack=gpack, gfin=gfin,
        w_in=np.ascontiguousarray(inputs["w_in"][0], dtype=f32),
        w_out=np.ascontiguousarray(inputs["w_out"][0], dtype=f32),
        w_q_mem=np.ascontiguousarray(inputs["w_q_mem"][0], dtype=f32),
        w_kv_mem=np.ascontiguousarray(inputs["w_kv_mem"][0], dtype=f32),
        w_o_mem=np.ascontiguousarray(inputs["w_o_mem"][0], dtype=f32),
        w_query=np.ascontiguousarray(inputs["w_query"][0], dtype=f32),
        skT=skT, euT=euT, ev=ev)
    in_maps = []
    kpos = np.arange(2048)
    for c in range(8):
        b, ci = c // 4, c % 4
        xqs, xhs = [], []
        for s in range(4):
            t0 = (4 * s + ci) * 512
            xqs.append(x[b, t0:t0 + 512])
            if t0 == 0:
                xhs.append(np.zeros((2, 1024), f32))
            else:
                xhs.append(x[b, t0 - 2:t0])
        qpos = ci * 512 + np.arange(512)
        m = (kpos[:, None] < qpos[None, :]).astype(f32)
        m = m.reshape(16, 128, 512).transpose(1, 0, 2)
        d = dict(shared)
        d.update(xb=np.ascontiguousarray(x[b]), xq=np.ascontiguousarray(np.concatenate(xqs, 0)),
                 xh=np.ascontiguousarray(np.concatenate(xhs, 0)), memb=np.ascontiguousarray(mem[b]),
                 mask=np.ascontiguousarray(m).astype(ml_dtypes.bfloat16))
        in_maps.append(d)
    return in_maps


def assemble(results, key="out"):
    out = np.zeros((2, 8192, 1024), np.float32)
    for c in range(8):
        b, ci = c // 4, c % 4
        o = np.asarray(results[c][key])
        for s in range(4):
            t0 = (4 * s + ci) * 512
            out[b, t0:t0 + 512] = o[s * 512:(s + 1) * 512]
    return out


def kernel(**inputs):
    in_maps = prep_inputs(inputs)
    nc, _ = build()
    res = run_bass_kernel_spmd(nc, in_maps, core_ids=list(range(8)))
    return assemble(res.results)
```

```python
import contextlib
import numpy as np
import ml_dtypes
import concourse.bass as bass
import concourse.mybir as mybir
from concourse.alu_op_type import AluOpType as ALU
from concourse.bass_utils import run_bass_kernel_spmd

AF = mybir.ActivationFunctionType
F32 = mybir.dt.float32
BF16 = mybir.dt.bfloat16
U32 = mybir.dt.uint32
AX = mybir.AxisListType

COMPUTE = ("pe", "act", "dve", "pool")
ALLENG = ("pe", "act", "dve", "pool", "sp")
NDSEM = 40
AOH_ENG = "dve"
EPS = 1e-6


class T:
    __slots__ = ("name", "w", "r", "dsem")

    def __init__(self, name, dsem=None):
        self.name = name
        self.w = None
        self.r = []
        self.dsem = dsem


class Prog:
    def __init__(self, nc, stack, same_engine_sync=True):
        self.nc = nc
        self.q = {e: [] for e in ALLENG}
        self.cnt = {}
        self.sem = {}
        for e in COMPUTE:
            self.sem[e] = stack.enter_context(nc.semaphore("c_" + e))
            self.cnt[e] = 0
        for i in range(NDSEM):
            k = "d%d" % i
            self.sem[k] = stack.enter_context(nc.semaphore(k))
            self.cnt[k] = 0
        self.waited = {e: {} for e in ALLENG}
        self.same = same_engine_sync
        self._rr = 0
        self.nins = 0

    def _deps(self, reads, writes):
        deps = {}

        def add(d):
            if d is None:
                return
            k, v = d
            if deps.get(k, 0) < v:
                deps[k] = v
        for t in reads:
            add(t.w)
        for t in writes:
            add(t.w)
            for d in t.r:
                add(d)
        return deps

    def _emit_waits(self, eng, deps):
        for k, v in deps.items():
            if k == eng and (eng == "pe" or not self.same):
                continue
            if k[0] == "d" and k[1:].isdigit():
                v = self.cnt[k]
            if self.waited[eng].get(k, 0) >= v:
                continue
            self.waited[eng][k] = v
            sem = self.sem[k]
            self.q[eng].append(lambda e, sem=sem, v=v: e.wait_ge(sem, v))
            self.nins += 1

    def _mark(self, key, val, reads, writes):
        for t in reads:
            t.r.append((key, val))
            if len(t.r) > 64:
                d = {}
                for k, v in t.r:
                    if d.get(k, 0) < v:
                        d[k] = v
                t.r = list(d.items())
        for t in writes:
            t.w = (key, val)
            t.r = []

    def op(self, eng, fn, reads=(), writes=()):
        deps = self._deps(reads, writes)
        self._emit_waits(eng, deps)
        self.cnt[eng] += 1
        val = self.cnt[eng]
        sem = self.sem[eng]
        self.q[eng].append(lambda e, fn=fn, sem=sem: fn(e).then_inc(sem, 1))
        self.nins += 1
        self._mark(eng, val, reads, writes)

    def dma(self, out, in_, reads=(), writes=(), qeng="sp", dsem=None, **kw):
        deps = self._deps(reads, writes)
        self._emit_waits(qeng, deps)
        if dsem is None:
            for t in writes:
                if t.dsem is not None:
                    dsem = t.dsem
                    break
        if dsem is None:
            dsem = self._rr
            self._rr = (self._rr + 1) % NDSEM
            for t in writes:
                t.dsem = dsem
        k = "d%d" % dsem
        self.cnt[k] += 16
        val = self.cnt[k]
        sem = self.sem[k]
        self.q[qeng].append(
            lambda e, out=out, in_=in_, sem=sem, kw=kw: e.dma_start(out=out, in_=in_, **kw).then_inc(sem, 16))
        self.nins += 1
        self._mark(k, val, reads, writes)

    def barrier(self):
        for eng in ALLENG:
            for k, v in self.cnt.items():
                if v == 0 or k == eng:
                    continue
                if self.waited[eng].get(k, 0) >= v:
                    continue
                self.waited[eng][k] = v
                sem = self.sem[k]
                self.q[eng].append(lambda e, sem=sem, v=v: e.wait_ge(sem, v))

    def emit(self):
        nc = self.nc
        with nc.Block() as block:
            @block.tensor
            def _(e):
                for f in self.q["pe"]:
                    f(e)

            @block.scalar
            def _(e):
                for f in self.q["act"]:
                    f(e)

            @block.vector
            def _(e):
                for f in self.q["dve"]:
                    f(e)

            @block.gpsimd
            def _(e):
                for f in self.q["pool"]:
                    f(e)

            @block.sync
            def _(e):
                for f in self.q["sp"]:
                    f(e)


G_MIX, G_XATTN, G_MEM, G_FFN, G_SB, G_CONV, G_CW = 0, 8, 16, 24, 32, 36, 40
NGP = 52


def build(stage=99, dbg=()):
    nc = bass.Bass("TRN2", target_bir_lowering=False)

    def di(n, s, d=F32):
        return nc.dram_tensor(n, list(s), d, kind="ExternalInput").ap()

    xb = di("xb", [8192, 1024])
    xq = di("xq", [2048, 1024])
    xh = di("xh", [8, 1024])
    memb = di("memb", [256, 1024])
    maskd = di("mask", [128, 16, 512], BF16)
    gpack = di("gpack", [128, NGP])
    gfin = di("gfin", [128, 1024])
    w_in = di("w_in", [1024, 3072])
    w_out = di("w_out", [1024, 1024])
    w_q_mem = di("w_q_mem", [1024, 1024])
    w_kv_mem = di("w_kv_mem", [1024, 2048])
    w_o_mem = di("w_o_mem", [1024, 1024])
    w_query = di("w_query", [1024, 2048])
    skT = di("skT", [128, 16, 128])
    euT = di("euT", [1024, 16384])
    ev = di("ev", [16384, 1024])
    out = nc.dram_tensor("out", [2048, 1024], F32, kind="ExternalOutput").ap()
    dbg_out = {}
    for name, shape, dt in dbg:
        dbg_out[name] = nc.dram_tensor(name, list(shape), dt, kind="ExternalOutput").ap()
    kT_d = nc.dram_tensor("kT_d", [4, 128, 8192], BF16, kind="Internal").ap()
    v_d = nc.dram_tensor("v_d", [4, 128, 64, 128], BF16, kind="Internal").ap()
    h_d = nc.dram_tensor("h_d", [2048, 1024], F32, kind="Internal").ap()
    eu_b = nc.dram_tensor("eu_b", [64, 128, 8, 256], BF16, kind="Internal").ap()
    n3_d = nc.dram_tensor("n3_d", [128, 8, 2048], BF16, kind="Internal").ap()
    idx_d = nc.dram_tensor("idx_d", [3, 128, 16, 128], F32, kind="Internal").ap()
    ev_b = nc.dram_tensor("ev_b", [16384, 1024], BF16, kind="Internal").ap()

    with contextlib.ExitStack() as st:
        P = Prog(nc, st)

        uid = [0]

        def alloc(stk, n, s, d):
            uid[0] += 1
            return stk.enter_context(nc.sbuf_tensor("%s_%d" % (n, uid[0]), list(s), d))

        def palloc(stk, n, s, d=F32):
            uid[0] += 1
            return stk.enter_context(nc.psum_tensor("%s_%d" % (n, uid[0]), list(s), d))

        Teub, Tevb = T("eu_b"), T("ev_b")
        precast_jobs = []
        if stage >= 7:
            for dc in range(8):
                for hb in range(2):
                    precast_jobs.append((eu_b[hb * 32:(hb + 1) * 32, :, dc, :].rearrange("b p e -> p b e"),
                                         euT[dc * 128:(dc + 1) * 128, hb * 8192:(hb + 1) * 8192].rearrange(
                                             "p (b e) -> p b e", e=256), Teub))
            for i in range(32):
                precast_jobs.append((ev_b[i * 512:(i + 1) * 512, :], ev[i * 512:(i + 1) * 512, :], Tevb))
            precast_jobs = [j for pair in zip(precast_jobs[:16] + precast_jobs[16:32], precast_jobs[32:] + [None] * 16)
                            for j in pair if j is not None]

        def precast_some(n):
            for _ in range(n):
                if precast_jobs:
                    o, i_, Tt = precast_jobs.pop(0)
                    P.dma(o, i_, writes=[Tt], qeng="pool")

        ident_f = alloc(st, "ident_f", [128, 128], F32)
        ident_b = alloc(st, "ident_b", [128, 128], BF16)
        Uneg = alloc(st, "Uneg", [128, 128], BF16)
        Unegb = alloc(st, "Unegb", [128, 128], BF16)
        ones_f = alloc(st, "ones_f", [128, 1], F32)
        gp = alloc(st, "gp", [128, NGP], F32)
        Tc = T("consts")
        P.dma(gp[:], gpack[:, :], writes=[Tc])
        P.op("pool", lambda e: e.memset(ident_f[:], 1.0), writes=[Tc])
        P.op("pool", lambda e: e.affine_select(out=ident_f[:], in_=ident_f[:], pattern=[[1, 128]],
                                               compare_op=ALU.is_equal, fill=0.0, base=0, channel_multiplier=-1),
             reads=[Tc], writes=[Tc])
        P.op("pool", lambda e: e.tensor_copy(out=ident_b[:], in_=ident_f[:]), reads=[Tc], writes=[Tc])
        P.op("pool", lambda e: e.memset(Uneg[:], -1.0), writes=[Tc])
        P.op("pool", lambda e: e.affine_select(out=Uneg[:], in_=Uneg[:], pattern=[[-1, 128]],
                                               compare_op=ALU.is_gt, fill=0.0, base=0, channel_multiplier=1),
             reads=[Tc], writes=[Tc])
        P.op("pool", lambda e: e.memset(Unegb[:], -1.0), writes=[Tc])
        P.op("pool", lambda e: e.affine_select(out=Unegb[:], in_=Unegb[:], pattern=[[1, 128]],
                                               compare_op=ALU.is_ge, fill=0.0, base=0, channel_multiplier=-1),
             reads=[Tc], writes=[Tc])
        P.op("pool", lambda e: e.memset(ones_f[:], 1.0), writes=[Tc])

        class NormRes:
            pass

        def make_norm_res(stk, pT):
            R = NormRes()
            R.junk = alloc(stk, "n_junk", [128, 1024], BF16)
            R.ssq = alloc(stk, "n_ssq", [128, 4], F32)
            R.rstd = alloc(stk, "n_rstd", [128, 4], F32)
            R.xs = [alloc(stk, "n_xs%d" % i, [128, 1024], F32) for i in range(2)]
            R.Tjunk = T("n_junk")
            R.Tssq = [T("n_ssq%d" % i) for i in range(4)]
            R.Trstd = T("n_rstd")
            R.Txs = [T("n_xs0"), T("n_xs1")]
            R.pT = pT
            R.k = 0
            return R

        def norm_group(R, srcs, gcol, nT, TnT):
            n = len(srcs)
            for i, (xa, Tx) in enumerate(srcs):
                P.op("act", lambda e, xa=xa, i=i: e.activation(out=R.junk[:], in_=xa, func=AF.Square,
                                                                accum_out=R.ssq[:, i:i + 1]),
                     reads=[Tx], writes=[R.Tjunk, R.Tssq[i]])
            P.op("dve", lambda e: e.tensor_scalar(out=R.rstd[:, 0:n], in0=R.ssq[:, 0:n], scalar1=1.0 / 1024,
                                                  scalar2=EPS, op0=ALU.mult, op1=ALU.add),
                 reads=R.Tssq[0:n], writes=[R.Trstd])
            P.op("act", lambda e: e.activation(out=R.rstd[:, 0:n], in_=R.rstd[:, 0:n], func=AF.Sqrt),
                 reads=[R.Trstd], writes=[R.Trstd])
            P.op("dve", lambda e: e.reciprocal(out=R.rstd[:, 0:n], in_=R.rstd[:, 0:n]),
                 reads=[R.Trstd], writes=[R.Trstd])
            for i, (xa, Tx) in enumerate(srcs):
                xs = R.xs[i % 2]
                Txs = R.Txs[i % 2]
                P.op("act", lambda e, xa=xa, xs=xs, i=i: e.activation(out=xs[:], in_=xa, func=AF.Copy,
                                                                      scale=R.rstd[:, i:i + 1]),
                     reads=[Tx, R.Trstd], writes=[Txs])
                for half in range(2):
                    pt, Tp = R.pT[R.k % len(R.pT)]
                    R.k += 1
                    for c in range(4):
                        cc = half * 4 + c
                        P.op("pe", lambda e, pt=pt, c=c, cc=cc, xs=xs: e.transpose(
                            out=pt[:, c * 128:(c + 1) * 128], in_=xs[:, cc * 128:(cc + 1) * 128],
                            identity=ident_f[:]), reads=[Txs, Tc], writes=[Tp])
                    P.op("dve", lambda e, pt=pt, half=half, i=i: e.tensor_tensor(
                        out=nT[:, half * 4:(half + 1) * 4, i * 128:(i + 1) * 128],
                        in0=pt[:, :].rearrange("p (c n) -> p c n", c=4),
                        in1=gcol[:, half * 4:(half + 1) * 4].unsqueeze(2).broadcast_to([128, 4, 128]),
                        op=ALU.mult), reads=[Tp, Tc], writes=[TnT])

        with contextlib.ExitStack() as sAC:
            QT_sb = alloc(sAC, "QT_sb", [128, 8, 2048], BF16)
            convT_sb = alloc(sAC, "convT_sb", [128, 4, 2048], BF16)
            sbT_sb = alloc(sAC, "sbT_sb", [128, 4, 2048], BF16)
            ssq_c = alloc(sAC, "ssq_c", [128, 16], F32)
            ssq_s = alloc(sAC, "ssq_s", [128, 16], F32)
            TQT, TconvT, TsbT, Tssqc, Tssqs = T("QT"), T("convT"), T("sbT"), T("ssqc"), T("ssqs")
            TkTd, Tvd = T("kT_d"), T("v_d")

            with contextlib.ExitStack() as ph:
                pT = [(palloc(ph, "pT%d" % i, [128, 512]), T("pT%d" % i)) for i in range(4)]
                pK = [(palloc(ph, "pK%d" % i, [128, 512]), T("pK%d" % i)) for i in range(2)]
                pV = [(palloc(ph, "pV%d" % i, [128, 512]), T("pV%d" % i)) for i in range(2)]
                R = make_norm_res(ph, pT)
                wkv = alloc(ph, "wkv", [128, 8, 1024], BF16)
                Twkv = T("wkv")
                P.dma(wkv[:], w_in[:, 512:1536].rearrange("(c p) n -> p c n", p=128), writes=[Twkv], qeng="pool")
                NXB = 8
                xt = [alloc(ph, "xt%d" % i, [128, 1024], F32) for i in range(NXB)]
                Txt = [T("xt%d" % i) for i in range(NXB)]
                nTb = [alloc(ph, "nT%d" % i, [128, 8, 512], BF16) for i in range(2)]
                TnTb = [T("nT0"), T("nT1")]
                kst = [alloc(ph, "kst%d" % i, [128, 4, 512], BF16) for i in range(2)]
                vst = [alloc(ph, "vst%d" % i, [128, 4, 512], BF16) for i in range(2)]
                Tkst = [T("kst0"), T("kst1")]
                Tvst = [T("vst0"), T("vst1")]
                NG = 16 if stage >= 1 else 0

                def load_group(g):
                    for tt in range(4):
                        b = (g * 4 + tt) % NXB
                        r0 = (g * 4 + tt) * 128
                        P.dma(xt[b][:], xb[r0:r0 + 128, :], writes=[Txt[b]])
                if NG:
                    load_group(0)
                for g in range(NG):
                    if g + 1 < NG:
                        load_group(g + 1)
                    nT = nTb[g % 2]
                    TnT = TnTb[g % 2]
                    srcs = [(xt[(g * 4 + tt) % NXB][:], Txt[(g * 4 + tt) % NXB]) for tt in range(4)]
                    norm_group(R, srcs, gp[:, G_MIX:G_MIX + 8], nT, TnT)
                    ks, Tks = kst[g % 2], Tkst[g % 2]
                    vs, Tvs = vst[g % 2], Tvst[g % 2]
                    for hp in range(4):
                        pk, Tpk = pK[hp % 2]
                        for dc in range(8):
                            P.op("pe", lambda e, pk=pk, dc=dc, hp=hp, nT=nT: e.matmul(
                                pk[:], lhsT=wkv[:, dc, hp * 128:(hp + 1) * 128], rhs=nT[:, dc, :],
                                start=(dc == 0), stop=(dc == 7)), reads=[Twkv, TnT], writes=[Tpk])
                        P.op("act", lambda e, pk=pk, ks=ks, hp=hp: e.activation(out=ks[:, hp, :], in_=pk[:],
                                                                              func=AF.Copy),
                             reads=[Tpk], writes=[Tks])
                    P.dma(kT_d[:, :, g * 512:(g + 1) * 512].rearrange("h p n -> p h n"), ks[:],
                          reads=[Tks], writes=[TkTd], qeng="pool")
                    precast_some(1)
                    for tt in range(4):
                        pv, Tpv = pV[tt % 2]
                        for dc in range(8):
                            P.op("pe", lambda e, pv=pv, dc=dc, tt=tt, nT=nT: e.matmul(
                                pv[:], lhsT=nT[:, dc, tt * 128:(tt + 1) * 128], rhs=wkv[:, dc, 512:1024],
                                start=(dc == 0), stop=(dc == 7)), reads=[Twkv, TnT], writes=[Tpv])
                        P.op("dve", lambda e, pv=pv, vs=vs, tt=tt: e.tensor_copy(out=vs[:, tt, :], in_=pv[:]),
                             reads=[Tpv], writes=[Tvs])
                    for hp in range(4):
                        P.dma(v_d[hp, :, 4 * g:4 * g + 4, :], vs[:, :, hp * 128:(hp + 1) * 128],
                              reads=[Tvs], writes=[Tvd], qeng="pool")
                P.barrier()

            with contextlib.ExitStack() as ph:
                pT = [(palloc(ph, "pT%d" % i, [128, 512]), T("pT%d" % i)) for i in range(2)]
                pQ = [(palloc(ph, "pQ%d" % i, [128, 512]), T("pQ%d" % i)) for i in range(2)]
                pG3 = [(palloc(ph, "pG%d" % i, [128, 512]), T("pG%d" % i)) for i in range(3)]
                pss = palloc(ph, "pss", [128, 512])
                Tpss = T("pss")
                R = make_norm_res(ph, pT)
                wq = alloc(ph, "wq", [128, 8, 512], BF16)
                wg = alloc(ph, "wg", [128, 8, 1536], BF16)
                Twq, Twg = T("wq"), T("wg")
                xt = [alloc(ph, "xt%d" % i, [128, 1024], F32) for i in range(8)]
                Txt = [T("xt%d" % i) for i in range(8)]
                xht = alloc(ph, "xht", [128, 1024], F32)
                Txht = T("xht")
                nTb = [alloc(ph, "nT%d" % i, [128, 8, 512], BF16) for i in range(2)]
                TnTb = [T("nT0"), T("nT1")]
                nhT = alloc(ph, "nhT", [128, 8, 128], BF16)
                TnhT = T("nhT")
                uh = alloc(ph, "uh", [128, 4, 8], F32)
                Tuh = T("uh")
                gch = alloc(ph, "gch", [128, 8], F32)
                Tgch = T("gch")
                gc_sb = alloc(ph, "gc_sb", [128, 512], F32)
                u_sb = alloc(ph, "u_sb", [128, 514], F32)
                acc = alloc(ph, "acc", [128, 512], F32)
                conv = alloc(ph, "conv", [128, 512], F32)
                sqc = alloc(ph, "sqc", [128, 512], F32)
                Tgc, Tu, Tacc, Tconv, Tsqc = T("gc"), T("u"), T("acc"), T("conv"), T("sqc")
                if stage >= 2:
                    P.dma(wq[:], w_in[:, 0:512].rearrange("(c p) n -> p c n", p=128), writes=[Twq], qeng="pool")
                    P.dma(wg[:], w_in[:, 1536:3072].rearrange("(c p) n -> p c n", p=128), writes=[Twg], qeng="pool")
                    P.op("pool", lambda e: e.memset(QT_sb[:], 0.0), writes=[TQT])
                    P.op("pool", lambda e: e.memset(xht[:], 0.0), writes=[Txht])
                    P.dma(xht[0:8, :], xh[:, :], writes=[Txht])

                    def load_slot(s):
                        for tt in range(4):
                            b = (s * 4 + tt) % 8
                            r0 = (s * 4 + tt) * 128
                            P.dma(xt[b][:], xq[r0:r0 + 128, :], writes=[Txt[b]])
                    load_slot(0)
                    norm_group(R, [(xht[:], Txht)], gp[:, G_MIX:G_MIX + 8], nhT, TnhT)
                    for cc in range(4):
                        pa, Tpa = pG3[0]
                        pb, Tpb = pG3[1]
                        for dc in range(8):
                            P.op("pe", lambda e, pa=pa, dc=dc, cc=cc: e.matmul(
                                pa[:, 0:8], lhsT=wg[:, dc, 512 + cc * 128:512 + (cc + 1) * 128], rhs=nhT[:, dc, 0:8],
                                start=(dc == 0), stop=(dc == 7)), reads=[Twg, TnhT], writes=[Tpa])
                        for dc in range(8):
                            P.op("pe", lambda e, pb=pb, dc=dc, cc=cc: e.matmul(
                                pb[:, 0:8], lhsT=wg[:, dc, 1024 + cc * 128:1024 + (cc + 1) * 128], rhs=nhT[:, dc, 0:8],
                                start=(dc == 0), stop=(dc == 7)), reads=[Twg, TnhT], writes=[Tpb])
                        P.op("act", lambda e, pa=pa: e.activation(out=gch[:], in_=pa[:, 0:8], func=AF.Copy),
                             reads=[Tpa], writes=[Tgch])
                        P.op("dve", lambda e, pb=pb, cc=cc: e.tensor_tensor(out=uh[:, cc, :], in0=gch[:], in1=pb[:, 0:8],
                                                                           op=ALU.mult),
                             reads=[Tgch, Tpb], writes=[Tuh])
                    gi = 0
                    for s in range(4):
                        if s + 1 < 4:
                            load_slot(s + 1)
                        nT, TnT = nTb[s % 2], TnTb[s % 2]
                        srcs = [(xt[(s * 4 + tt) % 8][:], Txt[(s * 4 + tt) % 8]) for tt in range(4)]
                        norm_group(R, srcs, gp[:, G_MIX:G_MIX + 8], nT, TnT)
                        for hp in range(4):
                            pq, Tpq = pQ[hp % 2]
                            for dc in range(8):
                                P.op("pe", lambda e, pq=pq, dc=dc, hp=hp, nT=nT: e.matmul(
                                    pq[:], lhsT=wq[:, dc, hp * 128:(hp + 1) * 128], rhs=nT[:, dc, :],
                                    start=(dc == 0), stop=(dc == 7)), reads=[Twq, TnT], writes=[Tpq])
                            for hd in range(2):
                                r0, r1 = hd * 64, hd * 64 + 64
                                P.op("act", lambda e, pq=pq, hp=hp, s=s, hd=hd, r0=r0, r1=r1: e.activation(
                                    out=QT_sb[r0:r1, 2 * hp + hd, s * 512:(s + 1) * 512], in_=pq[r0:r1, :],
                                    func=AF.Copy, scale=0.125), reads=[Tpq], writes=[TQT])
                        for cc in range(4):
                            banks = []
                            for col0 in (512 + cc * 128, 1024 + cc * 128, cc * 128):
                                pg, Tpg = pG3[gi % 3]
                                gi += 1
                                for dc in range(8):
                                    P.op("pe", lambda e, pg=pg, dc=dc, col0=col0, nT=nT: e.matmul(
                                        pg[:], lhsT=wg[:, dc, col0:col0 + 128], rhs=nT[:, dc, :],
                                        start=(dc == 0), stop=(dc == 7)), reads=[Twg, TnT], writes=[Tpg])
                                banks.append((pg, Tpg))
                            (pgc, Tpgc), (pxi, Tpxi), (pgb, Tpgb) = banks
                            P.op("act", lambda e, pgc=pgc: e.activation(out=gc_sb[:], in_=pgc[:], func=AF.Copy),
                                 reads=[Tpgc], writes=[Tgc])
                            P.op("dve", lambda e, cc=cc, s=s: e.tensor_copy(out=u_sb[:, 0:2], in_=uh[:, cc, 2 * s:2 * s + 2]),
                                 reads=[Tuh], writes=[Tu])
                            P.op("dve", lambda e, pxi=pxi: e.tensor_tensor(out=u_sb[:, 2:514], in0=gc_sb[:], in1=pxi[:],
                                                                          op=ALU.mult),
                                 reads=[Tgc, Tpxi], writes=[Tu])
                            P.op("dve", lambda e, cc=cc: e.tensor_scalar(
                                out=acc[:], in0=u_sb[:, 2:514], scalar1=gp[:, G_CW + cc * 3 + 2:G_CW + cc * 3 + 3],
                                scalar2=None, op0=ALU.mult), reads=[Tu, Tc], writes=[Tacc])
                            P.op("dve", lambda e, cc=cc: e.scalar_tensor_tensor(
                                out=acc[:], in0=u_sb[:, 1:513], scalar=gp[:, G_CW + cc * 3 + 1:G_CW + cc * 3 + 2],
                                in1=acc[:], op0=ALU.mult, op1=ALU.add), reads=[Tu, Tc, Tacc], writes=[Tacc])
                            P.op("dve", lambda e, cc=cc: e.scalar_tensor_tensor(
                                out=acc[:], in0=u_sb[:, 0:512], scalar=gp[:, G_CW + cc * 3:G_CW + cc * 3 + 1],
                                in1=acc[:], op0=ALU.mult, op1=ALU.add), reads=[Tu, Tc, Tacc], writes=[Tacc])
                            P.op("dve", lambda e, pgb=pgb: e.tensor_tensor(out=conv[:], in0=acc[:], in1=pgb[:],
                                                                          op=ALU.mult),
                                 reads=[Tacc, Tpgb], writes=[Tconv])
                            P.op("act", lambda e: e.activation(out=sqc[:], in_=conv[:], func=AF.Square),
                                 reads=[Tconv], writes=[Tsqc])
                            P.op("act", lambda e, cc=cc, s=s: e.activation(
                                out=convT_sb[:, cc, s * 512:(s + 1) * 512], in_=conv[:], func=AF.Copy,
                                scale=gp[:, G_CONV + cc:G_CONV + cc + 1]), reads=[Tconv, Tc], writes=[TconvT])
                            for tt in range(4):
                                col = (s * 4 + tt) * 4 + cc
                                P.op("pe", lambda e, tt=tt, col=col: e.matmul(
                                    pss[:, col:col + 1], lhsT=sqc[:, tt * 128:(tt + 1) * 128], rhs=ones_f[:, 0:1],
                                    start=True, stop=True), reads=[Tsqc, Tc], writes=[Tpss])
                    P.op("dve", lambda e: e.tensor_reduce(out=ssq_c[:], in_=pss[:, 0:64].rearrange("p (t c) -> p t c", c=4),
                                                          axis=AX.X, op=ALU.add), reads=[Tpss], writes=[Tssqc])
                P.barrier()

            with contextlib.ExitStack() as ph:
                pz = [(palloc(ph, "pz%d" % i, [128, 512]), T("pz%d" % i)) for i in range(3)]
                pGc = [(palloc(ph, "pGc%d" % i, [128, 512]), T("pGc%d" % i)) for i in range(2)]
                pO = [(palloc(ph, "pO%d" % i, [128, 512]), T("pO%d" % i)) for i in range(2)]
                pss = palloc(ph, "pssb", [128, 512])
                Tpss = T("pssb")
                msk = alloc(ph, "msk", [128, 16, 512], BF16)
                Tmsk = T("msk")
                KT_sb = [alloc(ph, "KT%d" % i, [128, 8192], BF16) for i in range(2)]
                V_sb = [alloc(ph, "V%d" % i, [128, 64, 128], BF16) for i in range(2)]
                TKT = [T("KT0"), T("KT1")]
                TV = [T("V0"), T("V1")]
                NB = 4
                e1 = [alloc(ph, "e1_%d" % i, [128, 512], F32) for i in range(NB)]
                sp = [alloc(ph, "sp_%d" % i, [128, 512], F32) for i in range(NB)]
                Lb = [alloc(ph, "Lb_%d" % i, [128, 512], BF16) for i in range(NB)]
                t2 = [alloc(ph, "t2_%d" % i, [128, 512], F32) for i in range(NB)]
                Ab = [alloc(ph, "Ab_%d" % i, [128, 512], BF16) for i in range(NB)]
                tmpf = [alloc(ph, "tmpf_%d" % i, [128, 512], F32) for i in range(2)]
                Te1 = [T("e1") for _ in range(NB)]
                Tsp = [T("sp") for _ in range(NB)]
                TLb = [T("Lb") for _ in range(NB)]
                Tt2 = [T("t2") for _ in range(NB)]
                TAb = [T("Ab") for _ in range(NB)]
                Ttmpf = [T("tmpf0"), T("tmpf1")]
                sqs = alloc(ph, "sqs", [128, 512], F32)
                Tsqs = T("sqs")
                if stage >= 3:
                    P.dma(msk[:], maskd[:, :, :], writes=[Tmsk])
                    pairs = [(s, hp) for s in range(4) for hp in range(4)]

                    def load_kv(i):
                        s, hp = pairs[i]
                        b = i % 2
                        nk = (4 * s + 4) * 512
                        nblk = nk // 128
                        P.dma(KT_sb[b][:, 0:nk], kT_d[hp, :, 0:nk], reads=[TkTd], writes=[TKT[b]])
                        P.dma(V_sb[b][:, 0:nblk, :], v_d[hp, :, 0:nblk, :], reads=[Tvd], writes=[TV[b]])
                        precast_some(2)

                    steps = []
                    for i, (s, hp) in enumerate(pairs):
                        nblk = (4 * s + 4) * 4
                        for blk in range(nblk - 1, -1, -1):
                            for hd in range(2):
                                steps.append(dict(i=i, s=s, hp=hp, blk=blk, hd=hd, first=(blk == nblk - 1),
                                                  last=(blk == 0), pfirst=(blk == nblk - 1 and hd == 0),
                                                  plast=(blk == 0 and hd == 1)))
                    NS = len(steps)

                    def info(n):
                        d = steps[n]
                        blk, s = d["blk"], d["s"]
                        j, kb = blk // 4, blk % 4
                        return d, (j >= 4 * s), (j - 4 * s) * 4 + kb

                    def pe_z(n):
                        d = steps[n]
                        b = d["i"] % 2
                        z, Tz = pz[n % 3]
                        KT = KT_sb[b]
                        blk, hp, hd, s = d["blk"], d["hp"], d["hd"], d["s"]
                        P.op("pe", lambda e: e.matmul(
                            z[:], lhsT=KT[:, blk * 128:(blk + 1) * 128],
                            rhs=QT_sb[:, 2 * hp + hd, s * 512:(s + 1) * 512], start=True, stop=True),
                            reads=[TKT[b], TQT], writes=[Tz])

                    def act_s1(n):
                        z, Tz = pz[n % 3]
                        k = n % NB
                        P.op("act", lambda e: e.activation(out=e1[k][:], in_=z[:], func=AF.Exp, scale=-1.0),
                             reads=[Tz], writes=[Te1[k]])
                        P.op("act", lambda e: e.activation(out=sp[k][:], in_=e1[k][:], func=AF.Ln, bias=1.0),
                             reads=[Te1[k]], writes=[Tsp[k]])

                    def dve_L(n):
                        d, masked, mi = info(n)
                        z, Tz = pz[n % 3]
                        k = n % NB
                        if not masked:
                            P.op("dve", lambda e: e.tensor_tensor(out=Lb[k][:], in0=z[:], in1=sp[k][:], op=ALU.add),
                                 reads=[Tz, Tsp[k]], writes=[TLb[k]])
                        else:
                            tf, Ttf = tmpf[n % 2], Ttmpf[n % 2]
                            P.op("dve", lambda e: e.tensor_tensor(out=tf[:], in0=z[:], in1=sp[k][:], op=ALU.add),
                                 reads=[Tz, Tsp[k]], writes=[Ttf])
                            P.op("pool", lambda e: e.tensor_tensor(out=Lb[k][:], in0=tf[:], in1=msk[:, mi, :], op=ALU.mult),
                                 reads=[Ttf, Tmsk], writes=[TLb[k]])

                    def pe_mm1(n):
                        d = steps[n]
                        k = n % NB
                        G, TG = pGc[d["hd"]]
                        first = d["first"]
                        P.op("pe", lambda e: e.matmul(G[:], lhsT=Uneg[:], rhs=Lb[k][:], start=first, stop=True, skip_group_check=True),
                             reads=[TLb[k], Tc], writes=[TG])

                    def dve_t2(n):
                        d = steps[n]
                        k = n % NB
                        G, TG = pGc[d["hd"]]
                        P.op("dve", lambda e: e.tensor_tensor(out=t2[k][:], in0=G[:], in1=sp[k][:], op=ALU.subtract),
                             reads=[TG, Tsp[k]], writes=[Tt2[k]])

                    def pe_mm2(n):
                        d = steps[n]
                        if d["last"]:
                            return
                        k = n % NB
                        G, TG = pGc[d["hd"]]
                        P.op("pe", lambda e: e.matmul(G[:], lhsT=Unegb[:], rhs=Lb[k][:], start=False, stop=True, skip_group_check=True),
                             reads=[TLb[k], Tc], writes=[TG])

                    def act_A(n):
                        d, masked, mi = info(n)
                        k = n % NB
                        P.op("act", lambda e: e.activation(out=Ab[k][:], in_=t2[k][:], func=AF.Exp),
                             reads=[Tt2[k]], writes=[TAb[k]])

                    def mask_A(n):
                        d, masked, mi = info(n)
                        k = n % NB
                        if masked:
                            P.op("dve", lambda e: e.tensor_tensor(out=Ab[k][:], in0=Ab[k][:], in1=msk[:, mi, :], op=ALU.mult),
                                 reads=[TAb[k], Tmsk], writes=[TAb[k]])

                    def pe_O(n):
                        d = steps[n]
                        b = d["i"] % 2
                        k = n % NB
                        blk, hd, s, hp = d["blk"], d["hd"], d["s"], d["hp"]
                        O, TO = pO[hd]
                        V = V_sb[b]
                        first, last = d["first"], d["last"]
                        P.op("pe", lambda e: e.matmul(O[:], lhsT=V[:, blk, :], rhs=Ab[k][:], start=first, stop=last),
                             reads=[TV[b], TAb[k]], writes=[TO])
                        if d["plast"]:
                            for hd2 in range(2):
                                r0, r1 = hd2 * 64, hd2 * 64 + 64
                                O2, TO2 = pO[hd2]
                                P.op("act", lambda e, O2=O2, r0=r0, r1=r1: e.activation(
                                    out=sqs[r0:r1, :], in_=O2[r0:r1, :], func=AF.Square), reads=[], writes=[Tsqs, TO2])
                                P.op("dve", lambda e, O2=O2, r0=r0, r1=r1: e.tensor_scalar(
                                    out=sbT_sb[r0:r1, hp, s * 512:(s + 1) * 512], in0=O2[r0:r1, :],
                                    scalar1=gp[r0:r1, G_SB + hp:G_SB + hp + 1], scalar2=None, op0=ALU.mult),
                                    reads=[Tc], writes=[TsbT, TO2])
                            for tt in range(4):
                                col = (s * 4 + tt) * 4 + hp
                                P.op("pe", lambda e, tt=tt, col=col: e.matmul(
                                    pss[:, col:col + 1], lhsT=sqs[:, tt * 128:(tt + 1) * 128], rhs=ones_f[:, 0:1],
                                    start=True, stop=True), reads=[Tsqs, Tc], writes=[Tpss])

                    load_kv(0)
                    ok = lambda m: 0 <= m < NS
                    for n in range(NS + 5):
                        if ok(n):
                            pe_z(n)
                            act_s1(n)
                        if ok(n - 5):
                            mask_A(n - 5)
                            pe_O(n - 5)
                            if steps[n - 5]["pfirst"]:
                                ni = steps[n - 5]["i"] + 1
                                if ni < len(pairs):
                                    load_kv(ni)
                        if ok(n - 1):
                            dve_L(n - 1)
                        if ok(n - 2):
                            pe_mm1(n - 2)
                            dve_t2(n - 2)
                        if ok(n - 3):
                            pe_mm2(n - 3)
                        if ok(n - 4):
                            act_A(n - 4)
                    P.op("dve", lambda e: e.tensor_reduce(out=ssq_s[:], in_=pss[:, 0:64].rearrange("p (t c) -> p t c", c=4),
                                                          axis=AX.X, op=ALU.add), reads=[Tpss], writes=[Tssqs])
                P.barrier()

            if "d_sbT" in dbg_out:
                P.dma(dbg_out["d_sbT"].rearrange("c p n -> p c n"), sbT_sb[:], reads=[TsbT], writes=[T("x")])
                P.dma(dbg_out["d_convT"].rearrange("c p n -> p c n"), convT_sb[:], reads=[TconvT], writes=[T("x")])
                P.dma(dbg_out["d_ssq"][:, 0:16], ssq_s[:], reads=[Tssqs], writes=[T("x")])
                P.dma(dbg_out["d_ssq"][:, 16:32], ssq_c[:], reads=[Tssqc], writes=[T("x")])
                P.barrier()

            with contextlib.ExitStack() as ph:
                pP = [(palloc(ph, "pP%d" % i, [128, 512]), T("pP%d" % i)) for i in range(8)]
                wo = alloc(ph, "wo", [128, 8, 1024], BF16)
                Two = T("wo")
                rs = alloc(ph, "rs", [128, 32], F32)
                Trs = T("rs")
                xt = [alloc(ph, "xt%d" % i, [128, 1024], F32) for i in range(4)]
                Txt = [T("xt%d" % i) for i in range(4)]
                ht = [alloc(ph, "ht%d" % i, [128, 1024], F32) for i in range(2)]
                Tht = [T("ht0"), T("ht1")]
                Thd = T("h_d")
                if stage >= 4:
                    P.dma(wo[:], w_out[:, :].rearrange("(c p) n -> p c n", p=128), writes=[Two], qeng="pool")
                    P.op("dve", lambda e: e.tensor_scalar(out=rs[:, 0:16], in0=ssq_s[:], scalar1=1.0 / 512, scalar2=EPS,
                                                          op0=ALU.mult, op1=ALU.add), reads=[Tssqs], writes=[Trs])
                    P.op("dve", lambda e: e.tensor_scalar(out=rs[:, 16:32], in0=ssq_c[:], scalar1=1.0 / 512, scalar2=EPS,
                                                          op0=ALU.mult, op1=ALU.add), reads=[Tssqc, Trs], writes=[Trs])
                    P.op("act", lambda e: e.activation(out=rs[:], in_=rs[:], func=AF.Sqrt), reads=[Trs], writes=[Trs])
                    P.op("dve", lambda e: e.reciprocal(out=rs[:], in_=rs[:]), reads=[Trs], writes=[Trs])
                    for t in range(16):
                        xa, Txa = xt[t % 4], Txt[t % 4]
                        P.dma(xa[:], xq[t * 128:(t + 1) * 128, :], writes=[Txa])
                        h, Th = ht[t % 2], Tht[t % 2]
                        banks = [pP[(t % 2) * 4 + i] for i in range(4)]
                        for src, (Tsrc) in ((0, TsbT), (1, TconvT)):
                            srcT = sbT_sb if src == 0 else convT_sb
                            for half in range(2):
                                pb, Tpb = banks[src * 2 + half]
                                for c in range(4):
                                    P.op("pe", lambda e, pb=pb, srcT=srcT, c=c, t=t, src=src, half=half: e.matmul(
                                        pb[:], lhsT=srcT[:, c, t * 128:(t + 1) * 128],
                                        rhs=wo[:, src * 4 + c, half * 512:(half + 1) * 512],
                                        start=(c == 0), stop=(c == 3)), reads=[Tsrc, Two], writes=[Tpb])
                        for half in range(2):
                            pb, Tpb = banks[half]
                            P.op("dve", lambda e, pb=pb, h=h, xa=xa, half=half, t=t: e.scalar_tensor_tensor(
                                out=h[:, half * 512:(half + 1) * 512], in0=pb[:], scalar=rs[:, t:t + 1],
                                in1=xa[:, half * 512:(half + 1) * 512], op0=ALU.mult, op1=ALU.add),
                                reads=[Tpb, Trs, Txa], writes=[Th])
                        for half in range(2):
                            pb, Tpb = banks[2 + half]
                            P.op("dve", lambda e, pb=pb, h=h, half=half, t=t: e.scalar_tensor_tensor(
                                out=h[:, half * 512:(half + 1) * 512], in0=pb[:], scalar=rs[:, 16 + t:17 + t],
                                in1=h[:, half * 512:(half + 1) * 512], op0=ALU.mult, op1=ALU.add),
                                reads=[Tpb, Trs, Th], writes=[Th])
                        P.dma(h_d[t * 128:(t + 1) * 128, :], h[:], reads=[Th], writes=[Thd], qeng="pool")
                P.barrier()

        if "d_h1" in dbg_out:
            with contextlib.ExitStack() as ph:
                tmp = alloc(ph, "dbgtmp", [128, 16, 1024], F32)
                Tt = T("dbgtmp")
                P.dma(tmp[:], h_d.rearrange("(t p) n -> p t n", p=128), writes=[Tt])
                P.dma(dbg_out["d_h1"].rearrange("(t p) n -> p t n", p=128), tmp[:], reads=[Tt], writes=[T("x")])
                P.barrier()

        Thd = T("h_d")
        with contextlib.ExitStack() as ph:
            pT = [(palloc(ph, "pT%d" % i, [128, 512]), T("pT%d" % i)) for i in range(2)]
            pA = [(palloc(ph, "pA%d" % i, [128, 512]), T("pA%d" % i)) for i in range(2)]
            psc = palloc(ph, "psc", [128, 512])
            Tpsc = T("psc")
            pTp = palloc(ph, "pTp", [128, 1024], BF16)
            TpTp = T("pTp")
            poT = [(palloc(ph, "poT%d" % i, [128, 512]), T("poT%d" % i)) for i in range(2)]
            R = make_norm_res(ph, pT)
            wqm = alloc(ph, "wqm", [128, 8, 1024], BF16)
            wkvm = alloc(ph, "wkvm", [128, 8, 2048], BF16)
            wom = alloc(ph, "wom", [128, 8, 1024], BF16)
            Twqm, Twkvm, Twom = T("wqm"), T("wkvm"), T("wom")
            memt = [alloc(ph, "memt%d" % i, [128, 1024], F32) for i in range(2)]
            Tmemt = [T("memt0"), T("memt1")]
            memT = alloc(ph, "memT", [128, 8, 256], BF16)
            TmemT = T("memT")
            kTm = alloc(ph, "kTm", [128, 8, 256], BF16)
            vm = alloc(ph, "vm", [128, 2, 1024], BF16)
            TkTm, Tvm = T("kTm"), T("vm")
            ht = [alloc(ph, "ht%d" % i, [128, 1024], F32) for i in range(8)]
            Tht = [T("ht%d" % i) for i in range(8)]
            n2T = [alloc(ph, "n2T%d" % i, [128, 8, 512], BF16) for i in range(2)]
            Tn2T = [T("n2T0"), T("n2T1")]
            qTm = alloc(ph, "qTm", [128, 8, 512], BF16)
            TqTm = T("qTm")
            nmx = alloc(ph, "nmx", [128, 4], F32)
            rsum = alloc(ph, "rsum", [128, 4], F32)
            Tnmx = [T("nmx%d" % i) for i in range(4)]
            Trsum = [T("rsum%d" % i) for i in range(4)]
            pexp = [alloc(ph, "pexp%d" % i, [128, 256], F32) for i in range(2)]
            pn = [alloc(ph, "pn%d" % i, [128, 256], BF16) for i in range(2)]
            pTs = [alloc(ph, "pTs%d" % i, [128, 256], BF16) for i in range(2)]
            Tpexp = [T("pexp0"), T("pexp1")]
            Tpn = [T("pn0"), T("pn1")]
            TpTs = [T("pTs0"), T("pTs1")]
            oT_sb = alloc(ph, "oT_sb", [128, 8, 128], BF16)
            ToT = T("oT_sb")
            if stage >= 5:
                P.dma(wqm[:], w_q_mem[:, :].rearrange("(c p) n -> p c n", p=128), writes=[Twqm], qeng="pool")
                P.dma(wkvm[:], w_kv_mem[:, :].rearrange("(c p) n -> p c n", p=128), writes=[Twkvm], qeng="pool")
                P.dma(wom[:], w_o_mem[:, :].rearrange("(c p) n -> p c n", p=128), writes=[Twom], qeng="pool")
                for i in range(2):
                    P.dma(memt[i][:], memb[i * 128:(i + 1) * 128, :], writes=[Tmemt[i]])

                def load_hgroup(g):
                    for tt in range(4):
                        bb = (g * 4 + tt) % 8
                        r0 = (g * 4 + tt) * 128
                        P.dma(ht[bb][:], h_d[r0:r0 + 128, :], reads=[Thd], writes=[Tht[bb]])
                load_hgroup(0)
                norm_group(R, [(memt[0][:], Tmemt[0]), (memt[1][:], Tmemt[1])], gp[:, G_MEM:G_MEM + 8], memT, TmemT)
                for c in range(8):
                    pa, Tpa = pA[c % 2]
                    for dc in range(8):
                        P.op("pe", lambda e, pa=pa, dc=dc, c=c: e.matmul(
                            pa[:, 0:256], lhsT=wkvm[:, dc, c * 128:(c + 1) * 128], rhs=memT[:, dc, :],
                            start=(dc == 0), stop=(dc == 7)), reads=[Twkvm, TmemT], writes=[Tpa])
                    P.op("act", lambda e, pa=pa, c=c: e.activation(out=kTm[:, c, :], in_=pa[:, 0:256], func=AF.Copy),
                         reads=[Tpa], writes=[TkTm])
                for mc in range(2):
                    for half in range(2):
                        pa, Tpa = pA[half]
                        for dc in range(8):
                            P.op("pe", lambda e, pa=pa, dc=dc, mc=mc, half=half: e.matmul(
                                pa[:], lhsT=memT[:, dc, mc * 128:(mc + 1) * 128],
                                rhs=wkvm[:, dc, 1024 + half * 512:1024 + (half + 1) * 512],
                                start=(dc == 0), stop=(dc == 7)), reads=[Twkvm, TmemT], writes=[Tpa])
                        P.op("act", lambda e, pa=pa, mc=mc, half=half: e.activation(
                            out=vm[:, mc, half * 512:(half + 1) * 512], in_=pa[:], func=AF.Copy),
                            reads=[Tpa], writes=[Tvm])
                hk = 0
                for g in range(4):
                    if g + 1 < 4:
                        load_hgroup(g + 1)
                    nT, TnT = n2T[g % 2], Tn2T[g % 2]
                    srcs = [(ht[(g * 4 + tt) % 8][:], Tht[(g * 4 + tt) % 8]) for tt in range(4)]
                    norm_group(R, srcs, gp[:, G_XATTN:G_XATTN + 8], nT, TnT)
                    for c in range(8):
                        pa, Tpa = pA[c % 2]
                        for dc in range(8):
                            P.op("pe", lambda e, pa=pa, dc=dc, c=c, nT=nT: e.matmul(
                                pa[:], lhsT=wqm[:, dc, c * 128:(c + 1) * 128], rhs=nT[:, dc, :],
                                start=(dc == 0), stop=(dc == 7)), reads=[Twqm, TnT], writes=[Tpa])
                        P.op("act", lambda e, pa=pa, c=c: e.activation(out=qTm[:, c, :], in_=pa[:], func=AF.Copy,
                                                                      scale=1.0 / 16), reads=[Tpa], writes=[TqTm])
                    for tt in range(4):
                        bb = (g * 4 + tt) % 8
                        h, Th = ht[bb], Tht[bb]
                        for hd in range(4):
                            k2 = hk % 2
                            hk += 1
                            for c in range(2):
                                P.op("pe", lambda e, c=c, hd=hd, tt=tt: e.matmul(
                                    psc[:, 0:256], lhsT=qTm[:, 2 * hd + c, tt * 128:(tt + 1) * 128],
                                    rhs=kTm[:, 2 * hd + c, :], start=(c == 0), stop=(c == 1)),
                                    reads=[TqTm, TkTm], writes=[Tpsc])
                            P.op("dve", lambda e, hd=hd: e.tensor_reduce(out=nmx[:, hd:hd + 1], in_=psc[:, 0:256],
                                                                        axis=AX.X, op=ALU.max, negate=True),
                                 reads=[Tpsc], writes=[Tnmx[hd]])
                            P.op("act", lambda e, hd=hd, k2=k2: e.activation(
                                out=pexp[k2][:], in_=psc[:, 0:256], func=AF.Exp, bias=nmx[:, hd:hd + 1],
                                accum_out=rsum[:, hd:hd + 1]), reads=[Tnmx[hd]], writes=[Tpexp[k2], Trsum[hd], Tpsc])
                            P.op("dve", lambda e, hd=hd: e.reciprocal(out=rsum[:, hd:hd + 1], in_=rsum[:, hd:hd + 1]),
                                 reads=[Trsum[hd]], writes=[Trsum[hd]])
                            P.op("dve", lambda e, hd=hd, k2=k2: e.tensor_scalar(
                                out=pn[k2][:], in0=pexp[k2][:], scalar1=rsum[:, hd:hd + 1], scalar2=None, op0=ALU.mult),
                                reads=[Tpexp[k2], Trsum[hd]], writes=[Tpn[k2]])
                            for mc in range(2):
                                P.op("pe", lambda e, mc=mc, k2=k2: e.transpose(
                                    out=pTp[:, mc * 128:(mc + 1) * 128], in_=pn[k2][:, mc * 128:(mc + 1) * 128],
                                    identity=ident_b[:]), reads=[Tpn[k2], Tc], writes=[TpTp])
                            P.op("act", lambda e, k2=k2: e.activation(out=pTs[k2][:], in_=pTp[:, 0:256], func=AF.Copy),
                                 reads=[TpTp], writes=[TpTs[k2]])
                            for dch in range(2):
                                ch = 2 * hd + dch
                                po, Tpo = poT[ch // 4]
                                for mc in range(2):
                                    P.op("pe", lambda e, po=po, ch=ch, mc=mc, hd=hd, dch=dch, k2=k2: e.matmul(
                                        po[:, (ch % 4) * 128:(ch % 4 + 1) * 128],
                                        lhsT=vm[:, mc, hd * 256 + dch * 128:hd * 256 + (dch + 1) * 128],
                                        rhs=pTs[k2][:, mc * 128:(mc + 1) * 128], start=(mc == 0), stop=(mc == 1)),
                                        reads=[Tvm, TpTs[k2]], writes=[Tpo])
                        for i2 in range(2):
                            po, Tpo = poT[i2]
                            P.op("dve" if i2 == 0 else "act",
                                 (lambda e, po=po, i2=i2: e.tensor_copy(
                                     out=oT_sb[:, i2 * 4:(i2 + 1) * 4, :].rearrange("p c n -> p (c n)"), in_=po[:]))
                                 if i2 == 0 else
                                 (lambda e, po=po, i2=i2: e.activation(
                                     out=oT_sb[:, i2 * 4:(i2 + 1) * 4, :].rearrange("p c n -> p (c n)"), in_=po[:],
                                     func=AF.Copy)),
                                 reads=[Tpo], writes=[ToT])
                        for half in range(2):
                            pa, Tpa = pA[half]
                            for c in range(8):
                                P.op("pe", lambda e, pa=pa, c=c, half=half: e.matmul(
                                    pa[:], lhsT=oT_sb[:, c, :], rhs=wom[:, c, half * 512:(half + 1) * 512],
                                    start=(c == 0), stop=(c == 7)), reads=[ToT, Twom], writes=[Tpa])
                            P.op("dve", lambda e, pa=pa, h=h, half=half: e.tensor_tensor(
                                out=h[:, half * 512:(half + 1) * 512], in0=pa[:], in1=h[:, half * 512:(half + 1) * 512],
                                op=ALU.add), reads=[Tpa, Th], writes=[Th])
                        r0 = (g * 4 + tt) * 128
                        P.dma(h_d[r0:r0 + 128, :], h[:], reads=[Th], writes=[Thd], qeng="pool")
            P.barrier()

        if "d_h2" in dbg_out:
            with contextlib.ExitStack() as ph:
                tmp = alloc(ph, "dbgtmp2", [128, 16, 1024], F32)
                Tt = T("dbgtmp2")
                P.dma(tmp[:], h_d.rearrange("(t p) n -> p t n", p=128), reads=[Thd], writes=[Tt])
                P.dma(dbg_out["d_h2"].rearrange("(t p) n -> p t n", p=128), tmp[:], reads=[Tt], writes=[T("x")])
                P.barrier()

        with contextlib.ExitStack() as sE:
            iota128 = alloc(sE, "iota128", [128, 128], F32)
            c16 = alloc(sE, "c16", [128, 16], F32)
            i16 = alloc(sE, "i16", [128, 16], F32)
            sE1 = contextlib.ExitStack()
            n3T = alloc(sE1, "n3T", [128, 8, 2048], BF16)
            Tn3T = T("n3T")
            IDX0 = alloc(sE1, "IDX0", [128, 16, 128], F32)
            IDX1 = alloc(sE1, "IDX1", [128, 16, 128], F32)
            GATE = alloc(sE1, "GATE", [128, 16, 128], F32)
            TIDX = [T("IDX%d" % i) for i in range(16)]
            Tci = T("peer_consts")
            with contextlib.ExitStack() as ph:
                pT = [(palloc(ph, "pT%d" % i, [128, 512]), T("pT%d" % i)) for i in range(2)]
                pA = [(palloc(ph, "pA%d" % i, [128, 512]), T("pA%d" % i)) for i in range(2)]
                pscr = palloc(ph, "pscr", [128, 2048])
                Tpscr = T("pscr")
                R = make_norm_res(ph, pT)
                wqp = alloc(ph, "wqp", [128, 8, 2048], BF16)
                skb = alloc(ph, "skb", [128, 16, 128], BF16)
                Twqp, Tskb = T("wqp"), T("skb")
                ht = [alloc(ph, "ht%d" % i, [128, 1024], F32) for i in range(8)]
                Tht = [T("ht%d" % i) for i in range(8)]
                qTp = alloc(ph, "qTp", [128, 16, 512], BF16)
                TqTp = T("qTp")
                sc_sb = alloc(ph, "sc_sb", [128, 2048], F32)
                Tsc = T("sc_sb")
                scw = alloc(ph, "scw", [128, 256], F32)
                Tscw = T("scw")
                top_s = alloc(ph, "top_s", [128, 16, 16], F32)
                top_i = alloc(ph, "top_i", [128, 16, 16], U32)
                top_if = alloc(ph, "top_if", [128, 16, 16], F32)
                Ttop = T("top")
                Ttops = [T("tops%d" % i) for i in range(16)]
                Ttops2 = [T("tops2_%d" % i) for i in range(16)]
                Ttopi = [T("topi%d" % i) for i in range(16)]
                Ttopi2 = [T("topi2_%d" % i) for i in range(16)]
                scw4 = [alloc(ph, "scw4_%d" % i, [128, 256], F32) for i in range(4)]
                Tscw4 = [T("scw4_%d" % i) for i in range(4)]
                Tbs = [T("bs%d" % i) for i in range(8)]
                Tbs2 = [T("bs2_%d" % i) for i in range(8)]
                Tbj = [T("bj%d" % i) for i in range(8)]
                Tbj2 = [T("bj2_%d" % i) for i in range(8)]
                cand = alloc(ph, "cand", [128, 8, 256], F32)
                Tcand = T("cand")
                best_s = alloc(ph, "best_s", [128, 8, 16], F32)
                best_j = alloc(ph, "best_j", [128, 8, 16], U32)
                jf = alloc(ph, "jf", [128, 8, 16], F32)
                Tbest = T("best")
                big = [alloc(ph, "big%d" % i, [128, 8, 16, 16], F32) for i in range(3)]
                Tbig = [T("big%d" % i) for i in range(3)]
                sm = [alloc(ph, "sm%d" % i, [128, 8, 16], F32) for i in range(3)]
                Tsm = [T("sm%d" % i) for i in range(3)]
                s8 = alloc(ph, "s8", [128, 8], F32)
                Ts8 = T("s8")
                if stage >= 6:
                    P.dma(wqp[:], w_query[:, :].rearrange("(c p) n -> p c n", p=128), writes=[Twqp], qeng="pool")
                    P.dma(skb[:], skT[:, :, :], writes=[Tskb], qeng="pool")
                    P.op("pool", lambda e: e.iota(iota128[:], pattern=[[1, 128]], base=0, channel_multiplier=0,
                                                  allow_small_or_imprecise_dtypes=True), writes=[Tci])
                    P.op("pool", lambda e: e.iota(c16[:], pattern=[[16, 16]], base=0, channel_multiplier=0,
                                                  allow_small_or_imprecise_dtypes=True), writes=[Tci])
                    P.op("pool", lambda e: e.iota(i16[:], pattern=[[1, 16]], base=0, channel_multiplier=0,
                                                  allow_small_or_imprecise_dtypes=True), writes=[Tci])

                    def load_hgroup(g):
                        for tt in range(4):
                            bb = (g * 4 + tt) % 8
                            r0 = (g * 4 + tt) * 128
                            P.dma(ht[bb][:], h_d[r0:r0 + 128, :], reads=[Thd], writes=[Tht[bb]])
                    load_hgroup(0)
                    B4 = [128, 8, 16, 16]
                    for g in range(4):
                        if g + 1 < 4:
                            load_hgroup(g + 1)
                        srcs = [(ht[(g * 4 + tt) % 8][:], Tht[(g * 4 + tt) % 8]) for tt in range(4)]
                        nTg = n3T[:, :, g * 512:(g + 1) * 512]
                        norm_group(R, srcs, gp[:, G_FFN:G_FFN + 8], nTg, Tn3T)
                        for c in range(16):
                            pa, Tpa = pA[c % 2]
                            for dc in range(8):
                                P.op("pe", lambda e, pa=pa, dc=dc, c=c, g=g: e.matmul(
                                    pa[:], lhsT=wqp[:, dc, c * 128:(c + 1) * 128], rhs=n3T[:, dc, g * 512:(g + 1) * 512],
                                    start=(dc == 0), stop=(dc == 7)), reads=[Twqp, Tn3T], writes=[Tpa])
                            P.op("act", lambda e, pa=pa, c=c: e.activation(out=qTp[:, c, :], in_=pa[:], func=AF.Copy),
                                 reads=[Tpa], writes=[TqTp])
                        for tt in range(4):
                            t = g * 4 + tt
                            for hc in range(16):
                                P.op("pe", lambda e, hc=hc, tt=tt: e.matmul(
                                    pscr[:, hc * 128:(hc + 1) * 128], lhsT=qTp[:, hc, tt * 128:(tt + 1) * 128],
                                    rhs=skb[:, hc, :], start=True, stop=True), reads=[TqTp, Tskb], writes=[Tpscr])
                            P.op("act", lambda e: e.activation(out=sc_sb[:], in_=pscr[:], func=AF.Copy),
                                 reads=[Tpscr], writes=[Tsc])
                            for hc0 in range(0, 16, 4):
                                grp = list(range(hc0, hc0 + 4))
                                srcs_ = {hc: sc_sb[:, hc * 128:(hc + 1) * 128] for hc in grp}
                                for hc in grp:
                                    P.op("dve", lambda e, hc=hc: e.max(out=top_s[:, hc, 0:8], in_=srcs_[hc]) if False else
                                         e.max(out=top_s[:, hc, 0:8], in_=sc_sb[:, hc * 128:(hc + 1) * 128]),
                                         reads=[Tsc], writes=[Ttops[hc]])
                                for hc in grp:
                                    P.op("dve", lambda e, hc=hc: e.max_index(
                                        out=top_i[:, hc, 0:8], in_max=top_s[:, hc, 0:8],
                                        in_values=sc_sb[:, hc * 128:(hc + 1) * 128]),
                                        reads=[Tsc, Ttops[hc]], writes=[Ttopi[hc]])
                                for hc in grp:
                                    P.op("dve", lambda e, hc=hc: e.match_replace(
                                        out=scw4[hc % 4][:, 0:128], in_to_replace=top_s[:, hc, 0:8],
                                        in_values=sc_sb[:, hc * 128:(hc + 1) * 128], imm_value=-1e30),
                                        reads=[Tsc, Ttops[hc]], writes=[Tscw4[hc % 4]])
                                for hc in grp:
                                    P.op("dve", lambda e, hc=hc: e.max(out=top_s[:, hc, 8:16], in_=scw4[hc % 4][:, 0:128]),
                                         reads=[Tscw4[hc % 4]], writes=[Ttops2[hc]])
                                for hc in grp:
                                    P.op("dve", lambda e, hc=hc: e.max_index(
                                        out=top_i[:, hc, 8:16], in_max=top_s[:, hc, 8:16], in_values=scw4[hc % 4][:, 0:128]),
                                        reads=[Tscw4[hc % 4], Ttops2[hc]], writes=[Ttopi2[hc]])
                            P.op("dve", lambda e: e.tensor_copy(out=top_if[:], in_=top_i[:]),
                                 reads=Ttops + Ttops2 + Ttopi + Ttopi2, writes=[Ttop])
                            ts4 = top_s[:, :, :].rearrange("p (h c) k -> p h c k", c=2)
                            ti4 = top_if[:, :, :].rearrange("p (h c) k -> p h c k", c=2)
                            P.op("dve", lambda e, ts4=ts4: e.tensor_tensor(
                                out=cand[:, :, :].rearrange("p h (a b) -> p h a b", b=16),
                                in0=ts4[:, :, 0, :].unsqueeze(3).broadcast_to(B4),
                                in1=ts4[:, :, 1, :].unsqueeze(2).broadcast_to(B4), op=ALU.add),
                                reads=[Ttop] + Ttops + Ttops2, writes=[Tcand])
                            for h0 in range(0, 8, 4):
                                grp = list(range(h0, h0 + 4))
                                for h8 in grp:
                                    P.op("dve", lambda e, h8=h8: e.max(out=best_s[:, h8, 0:8], in_=cand[:, h8, :]),
                                         reads=[Tcand], writes=[Tbs[h8]])
                                for h8 in grp:
                                    P.op("dve", lambda e, h8=h8: e.max_index(out=best_j[:, h8, 0:8],
                                                                            in_max=best_s[:, h8, 0:8], in_values=cand[:, h8, :]),
                                         reads=[Tcand, Tbs[h8]], writes=[Tbj[h8]])
                                for h8 in grp:
                                    P.op("dve", lambda e, h8=h8: e.match_replace(
                                        out=scw4[h8 % 4][:, 0:256], in_to_replace=best_s[:, h8, 0:8], in_values=cand[:, h8, :],
                                        imm_value=-1e30), reads=[Tcand, Tbs[h8]], writes=[Tscw4[h8 % 4]])
                                for h8 in grp:
                                    P.op("dve", lambda e, h8=h8: e.max(out=best_s[:, h8, 8:16], in_=scw4[h8 % 4][:, 0:256]),
                                         reads=[Tscw4[h8 % 4]], writes=[Tbs2[h8]])
                                for h8 in grp:
                                    P.op("dve", lambda e, h8=h8: e.max_index(out=best_j[:, h8, 8:16],
                                                                            in_max=best_s[:, h8, 8:16],
                                                                            in_values=scw4[h8 % 4][:, 0:256]),
                                         reads=[Tscw4[h8 % 4], Tbs2[h8]], writes=[Tbj2[h8]])
                            P.op("dve", lambda e: e.tensor_copy(out=jf[:], in_=best_j[:]),
                                 reads=Tbs + Tbs2 + Tbj + Tbj2, writes=[Tbest])
                            c16b = c16[:, :].unsqueeze(1).unsqueeze(1).broadcast_to(B4)
                            i16b = i16[:, :].unsqueeze(1).unsqueeze(1).broadcast_to(B4)
                            P.op("dve", lambda e, c16b=c16b: e.tensor_tensor(
                                out=big[0][:], in0=jf[:, :, :].unsqueeze(3).broadcast_to(B4), in1=c16b, op=ALU.subtract),
                                reads=[Tbest, Tci], writes=[Tbig[0]])
                            P.op("dve", lambda e: e.tensor_scalar(out=big[1][:], in0=big[0][:], scalar1=0.0, scalar2=None,
                                                                  op0=ALU.is_ge), reads=[Tbig[0]], writes=[Tbig[1]])
                            P.op("dve", lambda e: e.scalar_tensor_tensor(out=big[2][:], in0=big[0][:], scalar=16.0,
                                                                         in1=big[1][:], op0=ALU.is_lt, op1=ALU.mult),
                                 reads=[Tbig[0], Tbig[1]], writes=[Tbig[2]])
                            P.op("dve", lambda e, ti4=ti4: e.tensor_tensor(
                                out=big[0][:], in0=big[2][:], in1=ti4[:, :, 0, :].unsqueeze(2).broadcast_to(B4), op=ALU.mult),
                                reads=[Tbig[2], Ttop], writes=[Tbig[0]])
                            P.op("dve", lambda e, t=t: e.tensor_reduce(
                                out=IDX0[:, t, :].rearrange("p (h k) -> p h k", k=16), in_=big[0][:], axis=AX.X, op=ALU.add),
                                reads=[Tbig[0]], writes=[TIDX[t]])
                            P.op("dve", lambda e, i16b=i16b: e.tensor_tensor(out=big[1][:], in0=big[2][:], in1=i16b, op=ALU.mult),
                                 reads=[Tbig[2], Tci], writes=[Tbig[1]])
                            P.op("dve", lambda e: e.tensor_reduce(out=sm[0][:], in_=big[1][:], axis=AX.X, op=ALU.add),
                                 reads=[Tbig[1]], writes=[Tsm[0]])
                            P.op("dve", lambda e: e.scalar_tensor_tensor(out=sm[1][:], in0=sm[0][:], scalar=-16.0, in1=jf[:],
                                                                         op0=ALU.mult, op1=ALU.add),
                                 reads=[Tsm[0], Tbest], writes=[Tsm[1]])
                            P.op("dve", lambda e, i16b=i16b: e.tensor_tensor(
                                out=big[0][:], in0=sm[1][:, :, :].unsqueeze(3).broadcast_to(B4), in1=i16b, op=ALU.is_equal),
                                reads=[Tsm[1], Tci], writes=[Tbig[0]])
                            P.op("dve", lambda e, ti4=ti4: e.tensor_tensor(
                                out=big[1][:], in0=big[0][:], in1=ti4[:, :, 1, :].unsqueeze(2).broadcast_to(B4), op=ALU.mult),
                                reads=[Tbig[0], Ttop], writes=[Tbig[1]])
                            P.op("dve", lambda e, t=t: e.tensor_reduce(
                                out=IDX1[:, t, :].rearrange("p (h k) -> p h k", k=16), in_=big[1][:], axis=AX.X, op=ALU.add),
                                reads=[Tbig[1]], writes=[TIDX[t]])
                            P.op("dve", lambda e: e.tensor_tensor(
                                out=sm[2][:], in0=best_s[:], in1=best_s[:, :, 0:1].broadcast_to([128, 8, 16]), op=ALU.subtract),
                                reads=[Tbest], writes=[Tsm[2]])
                            P.op("act", lambda e: e.activation(out=sm[2][:], in_=sm[2][:], func=AF.Exp),
                                 reads=[Tsm[2]], writes=[Tsm[2]])
                            P.op("dve", lambda e: e.tensor_reduce(out=s8[:], in_=sm[2][:], axis=AX.X, op=ALU.add),
                                 reads=[Tsm[2]], writes=[Ts8])
                            P.op("dve", lambda e: e.reciprocal(out=s8[:], in_=s8[:]), reads=[Ts8], writes=[Ts8])
                            P.op("dve", lambda e, t=t: e.tensor_tensor(
                                out=GATE[:, t, :].rearrange("p (h k) -> p h k", k=16), in0=sm[2][:],
                                in1=s8[:, :].unsqueeze(2).broadcast_to([128, 8, 16]), op=ALU.mult),
                                reads=[Tsm[2], Ts8], writes=[TIDX[t]])
                P.barrier()

            if "d_idx" in dbg_out:
                P.dma(dbg_out["d_idx"][0].rearrange("(t p) n -> p t n", p=128), IDX0[:], reads=TIDX, writes=[T("x")])
                P.dma(dbg_out["d_idx"][1].rearrange("(t p) n -> p t n", p=128), IDX1[:], reads=TIDX, writes=[T("x")])
                P.dma(dbg_out["d_idx"][2].rearrange("(t p) n -> p t n", p=128), GATE[:], reads=TIDX, writes=[T("x")])
                P.barrier()

            precast_some(len(precast_jobs))
            Tn3d, Tidxd = T("n3_d"), T("idx_d")
            if stage >= 7:
                P.dma(n3_d[:, :, :], n3T[:], reads=[Tn3T], writes=[Tn3d], qeng="pool")
                for i3, srcI in enumerate((IDX0, IDX1, GATE)):
                    P.dma(idx_d[i3], srcI[:], reads=TIDX, writes=[Tidxd], qeng="pool")
            P.barrier()
            sE1.close()
            with contextlib.ExitStack() as ph:
                pout = [(palloc(ph, "pout%d" % i, [128, 512]), T("pout%d" % i)) for i in range(4)]
                pact = [(palloc(ph, "pact%d" % i, [128, 512]), T("pact%d" % i)) for i in range(2)]
                pG = palloc(ph, "pG", [128, 512])
                TpG = T("pG")
                ptr = palloc(ph, "ptr", [128, 512])
                Tptr = T("ptr")
                GT = [alloc(ph, "GT%d" % i, [128, 256, 128], BF16) for i in range(2)]
                TGT = [T("GT0"), T("GT1")]
                n3p = [alloc(ph, "n3p%d" % i, [128, 8, 256], BF16) for i in range(2)]
                Tn3p = [T("n3p0"), T("n3p1")]
                ip = [alloc(ph, "ip%d" % i, [128, 3, 2, 128], F32) for i in range(2)]
                Tip = [T("ip0"), T("ip1")]
                trT = [alloc(ph, "trT%d" % i, [128, 3, 128], F32) for i in range(2)]
                TtrT = [T("trT0"), T("trT1")]
                NOH = 8
                Aoh = [alloc(ph, "Aoh%d" % i, [128, 128], BF16) for i in range(NOH)]
                Boh = [alloc(ph, "Boh%d" % i, [128, 128], BF16) for i in range(NOH)]
                TAoh = [T("Aoh%d" % i) for i in range(NOH)]
                TBoh = [T("Boh%d" % i) for i in range(NOH)]
                NUB = 4
                ub = [alloc(ph, "ub%d" % i, [128, 8, 256], BF16) for i in range(NUB)]
                vb = [alloc(ph, "vb%d" % i, [128, 2, 1024], BF16) for i in range(NUB)]
                Tub = [T("ub%d" % i) for i in range(NUB)]
                Tvb = [T("vb%d" % i) for i in range(NUB)]
                ga = [alloc(ph, "ga%d" % i, [128, 256], BF16) for i in range(3)]
                coef = [alloc(ph, "coef%d" % i, [128, 256], BF16) for i in range(3)]
                Tga = [T("ga%d" % i) for i in range(3)]
                Tcoef = [T("coef%d" % i) for i in range(3)]
                hf = [alloc(ph, "hf%d" % i, [128, 1024], F32) for i in range(2)]
                Thf = [T("hf0"), T("hf1")]
                gf = alloc(ph, "gf", [128, 1024], F32)
                Tgf = T("gf")
                junk2 = alloc(ph, "junk2", [128, 1024], BF16)
                Tjunk2 = T("junk2")
                fs = alloc(ph, "fs", [128, 2], F32)
                Tfs = [T("fs0"), T("fs1")]
                To = T("out")
                NPASS = 8 if stage >= 7 else 0
                if NPASS:
                    P.dma(gf[:], gfin[:, :], writes=[Tgf])

                def load_pass(p):
                    b = p % 2
                    P.dma(n3p[b][:], n3_d[:, :, p * 256:(p + 1) * 256], reads=[Tn3d], writes=[Tn3p[b]])
                    for i3 in range(3):
                        P.dma(ip[b][:, i3, :, :], idx_d[i3, :, 2 * p:2 * p + 2, :], reads=[Tidxd], writes=[Tip[b]])

                def load_blk(bk):
                    bi_ = bk % NUB
                    c0 = bk * 2
                    P.dma(ub[bi_][:], eu_b[bk, :, :, :], reads=[Teub], writes=[Tub[bi_]])
                    P.dma(vb[bi_][:], ev_b[c0 * 128:c0 * 128 + 256, :].rearrange("(k p) d -> p k d", p=128),
                          reads=[Tevb], writes=[Tvb[bi_]])

                noh = [0]

                def gb_tr(p, tl):
                    b = p % 2
                    for i3 in range(3):
                        P.op("pe", lambda e, i3=i3: e.transpose(
                            out=ptr[:, i3 * 128:(i3 + 1) * 128], in_=ip[b][:, i3, tl, :], identity=ident_f[:]),
                            reads=[Tip[b], Tc], writes=[Tptr])
                    P.op("act", lambda e: e.activation(out=trT[tl][:, :, :].rearrange("p a n -> p (a n)"),
                                                       in_=ptr[:, 0:384], func=AF.Copy), reads=[Tptr], writes=[TtrT[tl]])

                def gb_oh(p, tok):
                    tl, tk = tok // 128, tok % 128
                    k = noh[0] % NOH
                    noh[0] += 1
                    P.op("dve", lambda e: e.tensor_scalar(
                        out=Boh[k][:], in0=iota128[:], scalar1=trT[tl][:, 1, tk:tk + 1], scalar2=trT[tl][:, 2, tk:tk + 1],
                        op0=ALU.is_equal, op1=ALU.mult), reads=[TtrT[tl], Tci], writes=[TBoh[k]])
                    P.op("dve", lambda e: e.tensor_scalar(
                        out=Aoh[k][:], in0=iota128[:], scalar1=trT[tl][:, 0, tk:tk + 1], scalar2=None,
                        op0=ALU.is_equal), reads=[TtrT[tl], Tci], writes=[TAoh[k]])
                    return k

                def gb_mm(tok, k):
                    P.op("pe", lambda e: e.matmul(pG[:, (tok % 4) * 128:(tok % 4 + 1) * 128], lhsT=Boh[k][:], rhs=Aoh[k][:],
                                                  start=True, stop=True), reads=[TBoh[k], TAoh[k]], writes=[TpG])

                def gb_evac(p, tok0):
                    b = p % 2
                    P.op("act", lambda e: e.activation(
                        out=GT[b][:, tok0:tok0 + 4, :].rearrange("p t n -> p (t n)"), in_=pG[:], func=AF.Copy),
                        reads=[TpG], writes=[TGT[b]])

                def U(c, p):
                    b = p % 2
                    bi = (c // 2) % NUB
                    pa, Tpa = pact[c % 2]
                    k3 = c % 3
                    for dc in range(8):
                        P.op("pe", lambda e, dc=dc: e.matmul(
                            pa[:, 0:256], lhsT=ub[bi][:, dc, (c % 2) * 128:(c % 2 + 1) * 128],
                            rhs=n3p[b][:, dc, :], start=(dc == 0), stop=(dc == 7)),
                            reads=[Tub[bi], Tn3p[b]], writes=[Tpa])
                    P.op("act", lambda e: e.activation(out=ga[k3][:], in_=pa[:, 0:256], func=AF.Gelu),
                         reads=[Tpa], writes=[Tga[k3]])
                    P.op("dve", lambda e: e.tensor_tensor(out=coef[k3][:], in0=ga[k3][:], in1=GT[b][:, :, c], op=ALU.mult),
                         reads=[Tga[k3], TGT[b]], writes=[Tcoef[k3]])

                def Vv(c):
                    bi = (c // 2) % NUB
                    k3 = c % 3
                    for tl in range(2):
                        for half in range(2):
                            po, Tpo = pout[tl * 2 + half]
                            P.op("pe", lambda e, po=po, tl=tl, half=half: e.matmul(
                                po[:], lhsT=coef[k3][:, tl * 128:(tl + 1) * 128],
                                rhs=vb[bi][:, c % 2, half * 512:(half + 1) * 512], start=(c == 0), stop=(c == 127)),
                                reads=[Tcoef[k3], Tvb[bi]], writes=[Tpo])

                def finish_pass(p):
                    for tl in range(2):
                        t = p * 2 + tl
                        h, Th = hf[tl], Thf[tl]
                        P.dma(h[:], h_d[t * 128:(t + 1) * 128, :], reads=[Thd], writes=[Th])
                        for half in range(2):
                            po, Tpo = pout[tl * 2 + half]
                            P.op("dve", lambda e, po=po, h=h, half=half: e.tensor_tensor(
                                out=h[:, half * 512:(half + 1) * 512], in0=po[:], in1=h[:, half * 512:(half + 1) * 512],
                                op=ALU.add), reads=[Tpo, Th], writes=[Th])
                        P.op("act", lambda e, h=h, tl=tl: e.activation(out=junk2[:], in_=h[:], func=AF.Square,
                                                                      accum_out=fs[:, tl:tl + 1]),
                             reads=[Th], writes=[Tjunk2, Tfs[tl]])
                        P.op("dve", lambda e, tl=tl: e.tensor_scalar(out=fs[:, tl:tl + 1], in0=fs[:, tl:tl + 1],
                                                                    scalar1=1.0 / 1024, scalar2=EPS, op0=ALU.mult, op1=ALU.add),
                             reads=[Tfs[tl]], writes=[Tfs[tl]])
                        P.op("act", lambda e, tl=tl: e.activation(out=fs[:, tl:tl + 1], in_=fs[:, tl:tl + 1], func=AF.Sqrt),
                             reads=[Tfs[tl]], writes=[Tfs[tl]])
                        P.op("dve", lambda e, tl=tl: e.reciprocal(out=fs[:, tl:tl + 1], in_=fs[:, tl:tl + 1]),
                             reads=[Tfs[tl]], writes=[Tfs[tl]])
                        P.op("dve", lambda e, h=h, tl=tl: e.scalar_tensor_tensor(
                            out=h[:], in0=h[:], scalar=fs[:, tl:tl + 1], in1=gf[:], op0=ALU.mult, op1=ALU.mult),
                            reads=[Th, Tfs[tl], Tgf], writes=[Th])
                        P.dma(out[t * 128:(t + 1) * 128, :], h[:], reads=[Th], writes=[To])

                if NPASS:
                    load_pass(0)
                    load_pass(1)
                    for tl in range(2):
                        gb_tr(0, tl)
                        pend = []
                        for tk in range(128):
                            tok = tl * 128 + tk
                            k = gb_oh(0, tok)
                            gb_mm(tok, k)
                            if tok % 4 == 3:
                                gb_evac(0, tok - 3)
                for p in range(NPASS):
                    nxt = p + 1 if p + 1 < NPASS else None
                    for bk in range(3):
                        load_blk(bk)
                    if nxt is not None:
                        gb_tr(nxt, 0)
                    pend = []
                    for c in range(128 + 2):
                        if nxt is not None:
                            if c >= 3 and c % 2 == 1:
                                gb_evac(nxt, (c - 3) * 2)
                            if c == 63:
                                gb_tr(nxt, 1)
                        if c < 128:
                            U(c, p)
                        if nxt is not None:
                            for tok, k in pend:
                                gb_mm(tok, k)
                            pend = []
                            if c < 128:
                                for tok in (2 * c, 2 * c + 1):
                                    pend.append((tok, gb_oh(nxt, tok)))
                        if c - 2 >= 0:
                            Vv(c - 2)
                            if (c - 2) % 2 == 1:
                                nb = (c - 2) // 2 + 3
                                if nb < 64:
                                    load_blk(nb)
                    finish_pass(p)
                    if p + 2 < NPASS:
                        load_pass(p + 2)
                P.barrier()
        P.barrier()
        P.emit()
    return nc, P


def prep_inputs(inputs):
    f32 = np.float32
    x = np.asarray(inputs["x"], f32)
    mem = np.asarray(inputs["mem"], f32)

    def cols(v):
        v = np.asarray(v, f32).reshape(-1, 128)
        return np.ascontiguousarray(v.T)

    gpack = np.zeros((128, NGP), f32)
    gpack[:, G_MIX:G_MIX + 8] = cols(inputs["g_mix"][0])
    gpack[:, G_XATTN:G_XATTN + 8] = cols(inputs["g_xattn"][0])
    gpack[:, G_MEM:G_MEM + 8] = cols(inputs["g_mem"][0])
    gpack[:, G_FFN:G_FFN + 8] = cols(inputs["g_ffn"][0])
    gpack[:, G_SB:G_SB + 4] = cols(inputs["g_sb_out"][0])
    gpack[:, G_CONV:G_CONV + 4] = cols(inputs["g_conv_out"][0])
    cw = np.asarray(inputs["conv_w"][0], f32)
    for cc in range(4):
        for k in range(3):
            gpack[:, G_CW + cc * 3 + k] = cw[k, cc * 128:(cc + 1) * 128]
    gfin = np.ascontiguousarray(np.broadcast_to(np.asarray(inputs["g_final"], f32)[None, :], (128, 1024)))
    sk = np.asarray(inputs["sub_keys"][0], f32)
    skT = np.ascontiguousarray(sk.reshape(16, 128, 128).transpose(2, 0, 1))
    euT = np.ascontiguousarray(np.asarray(inputs["expert_u"][0], f32).T)
    ev = np.ascontiguousarray(np.asarray(inputs["expert_v"][0], f32))
    shared = dict(
        gpack=gpack, gfin=gfin,
        w_in=np.ascontiguousarray(inputs["w_in"][0], dtype=f32),
        w_out=np.ascontiguousarray(inputs["w_out"][0], dtype=f32),
        w_q_mem=np.ascontiguousarray(inputs["w_q_mem"][0], dtype=f32),
        w_kv_mem=np.ascontiguousarray(inputs["w_kv_mem"][0], dtype=f32),
        w_o_mem=np.ascontiguousarray(inputs["w_o_mem"][0], dtype=f32),
        w_query=np.ascontiguousarray(inputs["w_query"][0], dtype=f32),
        skT=skT, euT=euT, ev=ev)
    in_maps = []
    kpos = np.arange(2048)
    for c in range(8):
        b, ci = c // 4, c % 4
        xqs, xhs = [], []
        for s in range(4):
            t0 = (4 * s + ci) * 512
            xqs.append(x[b, t0:t0 + 512])
            if t0 == 0:
                xhs.append(np.zeros((2, 1024), f32))
            else:
                xhs.append(x[b, t0 - 2:t0])
        qpos = ci * 512 + np.arange(512)
        m = (kpos[:, None] < qpos[None, :]).astype(f32)
        m = m.reshape(16, 128, 512).transpose(1, 0, 2)
        d = dict(shared)
        d.update(xb=np.ascontiguousarray(x[b]), xq=np.ascontiguousarray(np.concatenate(xqs, 0)),
                 xh=np.ascontiguousarray(np.concatenate(xhs, 0)), memb=np.ascontiguousarray(mem[b]),
                 mask=np.ascontiguousarray(m).astype(ml_dtypes.bfloat16))
        in_maps.append(d)
    return in_maps


def assemble(results, key="out"):
    out = np.zeros((2, 8192, 1024), np.float32)
    for c in range(8):
        b, ci = c // 4, c % 4
        o = np.asarray(results[c][key])
        for s in range(4):
            t0 = (4 * s + ci) * 512
            out[b, t0:t0 + 512] = o[s * 512:(s + 1) * 512]
    return out


def kernel(**inputs):
    in_maps = prep_inputs(inputs)
    nc, _ = build()
    res = run_bass_kernel_spmd(nc, in_maps, core_ids=list(range(8)))
    return assemble(res.results)
```

```python
import contextlib
import numpy as np
import ml_dtypes
import concourse.bass as bass
import concourse.mybir as mybir
from concourse.alu_op_type import AluOpType as ALU
from concourse.bass_utils import run_bass_kernel_spmd

AF = mybir.ActivationFunctionType
F32 = mybir.dt.float32
BF16 = mybir.dt.bfloat16
U32 = mybir.dt.uint32
AX = mybir.AxisListType

COMPUTE = ("pe", "act", "dve", "pool")
ALLENG = ("pe", "act", "dve", "pool", "sp")
NDSEM = 40
NSWSEM = 12
AOH_ENG = "dve"
EPS = 1e-6


class T:
    __slots__ = ("name", "w", "r", "dsem")

    def __init__(self, name, dsem=None):
        self.name = name
        self.w = None
        self.r = []
        self.dsem = dsem


class Prog:
    def __init__(self, nc, stack, same_engine_sync=True):
        self.nc = nc
        self.q = {e: [] for e in ALLENG}
        self.cnt = {}
        self.sem = {}
        for e in COMPUTE:
            self.sem[e] = stack.enter_context(nc.semaphore("c_" + e))
            self.cnt[e] = 0
        for i in range(NDSEM):
            k = "d%d" % i
            self.sem[k] = stack.enter_context(nc.semaphore(k))
            self.cnt[k] = 0
        self.waited = {e: {} for e in ALLENG}
        self.same = same_engine_sync
        self._rr = 0
        self._rr_sw = 0
        self.nins = 0

    def _deps(self, reads, writes):
        deps = {}

        def add(d):
            if d is None:
                return
            k, v = d
            if deps.get(k, 0) < v:
                deps[k] = v
        for t in reads:
            add(t.w)
        for t in writes:
            add(t.w)
            for d in t.r:
                add(d)
        return deps

    def _emit_waits(self, eng, deps):
        for k, v in deps.items():
            if k == eng and (eng == "pe" or not self.same):
                continue
            if k[0] == "d" and k[1:].isdigit():
                v = self.cnt[k]
            if self.waited[eng].get(k, 0) >= v:
                continue
            self.waited[eng][k] = v
            sem = self.sem[k]
            self.q[eng].append(lambda e, sem=sem, v=v: e.wait_ge(sem, v))
            self.nins += 1

    def _mark(self, key, val, reads, writes):
        for t in reads:
            t.r.append((key, val))
            if len(t.r) > 64:
                d = {}
                for k, v in t.r:
                    if d.get(k, 0) < v:
                        d[k] = v
                t.r = list(d.items())
        for t in writes:
            t.w = (key, val)
            t.r = []

    def op(self, eng, fn, reads=(), writes=()):
        deps = self._deps(reads, writes)
        self._emit_waits(eng, deps)
        self.cnt[eng] += 1
        val = self.cnt[eng]
        sem = self.sem[eng]
        self.q[eng].append(lambda e, fn=fn, sem=sem: fn(e).then_inc(sem, 1))
        self.nins += 1
        self._mark(eng, val, reads, writes)

    def dma(self, out, in_, reads=(), writes=(), qeng="sp", dsem=None, **kw):
        deps = self._deps(reads, writes)
        self._emit_waits(qeng, deps)
        kind = "sw" if qeng == "pool" else "hw"
        if dsem is None:
            for t in writes:
                if t.dsem is not None and kind in t.dsem:
                    dsem = t.dsem[kind]
                    break
        if dsem is None:
            if kind == "sw":
                dsem = self._rr_sw
                self._rr_sw = (self._rr_sw + 1) % NSWSEM
            else:
                dsem = NSWSEM + self._rr
                self._rr = (self._rr + 1) % (NDSEM - NSWSEM)
            for t in writes:
                if t.dsem is None:
                    t.dsem = {}
                t.dsem.setdefault(kind, dsem)
        k = "d%d" % dsem
        self.cnt[k] += 16
        val = self.cnt[k]
        sem = self.sem[k]
        self.q[qeng].append(
            lambda e, out=out, in_=in_, sem=sem, kw=kw: e.dma_start(out=out, in_=in_, **kw).then_inc(sem, 16))
        self.nins += 1
        self._mark(k, val, reads, writes)

    def barrier(self):
        for eng in ALLENG:
            for k, v in self.cnt.items():
                if v == 0 or k == eng:
                    continue
                if self.waited[eng].get(k, 0) >= v:
                    continue
                self.waited[eng][k] = v
                sem = self.sem[k]
                self.q[eng].append(lambda e, sem=sem, v=v: e.wait_ge(sem, v))

    def emit(self):
        nc = self.nc
        with nc.Block() as block:
            @block.tensor
            def _(e):
                for f in self.q["pe"]:
                    f(e)

            @block.scalar
            def _(e):
                for f in self.q["act"]:
                    f(e)

            @block.vector
            def _(e):
                for f in self.q["dve"]:
                    f(e)

            @block.gpsimd
            def _(e):
                for f in self.q["pool"]:
                    f(e)

            @block.sync
            def _(e):
                for f in self.q["sp"]:
                    f(e)


G_MIX, G_XATTN, G_MEM, G_FFN, G_SB, G_CONV, G_CW = 0, 8, 16, 24, 32, 36, 40
NGP = 52


def build(stage=99, dbg=()):
    nc = bass.Bass("TRN2", target_bir_lowering=False)

    def di(n, s, d=F32):
        return nc.dram_tensor(n, list(s), d, kind="ExternalInput").ap()

    xb = di("xb", [8192, 1024])
    xq = di("xq", [2048, 1024])
    xh = di("xh", [8, 1024])
    memb = di("memb", [256, 1024])
    maskd = di("mask", [128, 16, 512], BF16)
    gpack = di("gpack", [128, NGP])
    gfin = di("gfin", [128, 1024])
    w_in = di("w_in", [1024, 3072])
    w_out = di("w_out", [1024, 1024])
    w_q_mem = di("w_q_mem", [1024, 1024])
    w_kv_mem = di("w_kv_mem", [1024, 2048])
    w_o_mem = di("w_o_mem", [1024, 1024])
    w_query = di("w_query", [1024, 2048])
    skT = di("skT", [128, 16, 128])
    euT = di("euT", [1024, 16384])
    ev = di("ev", [16384, 1024])
    out = nc.dram_tensor("out", [2048, 1024], F32, kind="ExternalOutput").ap()
    dbg_out = {}
    for name, shape, dt in dbg:
        dbg_out[name] = nc.dram_tensor(name, list(shape), dt, kind="ExternalOutput").ap()
    kT_d = nc.dram_tensor("kT_d", [4, 128, 8192], BF16, kind="Internal").ap()
    v_d = nc.dram_tensor("v_d", [4, 128, 64, 128], BF16, kind="Internal").ap()
    h_d = nc.dram_tensor("h_d", [2048, 1024], F32, kind="Internal").ap()
    eu_b = nc.dram_tensor("eu_b", [64, 128, 8, 256], BF16, kind="Internal").ap()
    n3_d = nc.dram_tensor("n3_d", [128, 8, 2048], BF16, kind="Internal").ap()
    idx_d = nc.dram_tensor("idx_d", [3, 128, 16, 128], F32, kind="Internal").ap()
    ev_b = nc.dram_tensor("ev_b", [16384, 1024], BF16, kind="Internal").ap()

    with contextlib.ExitStack() as st:
        P = Prog(nc, st)

        uid = [0]

        def alloc(stk, n, s, d):
            uid[0] += 1
            return stk.enter_context(nc.sbuf_tensor("%s_%d" % (n, uid[0]), list(s), d))

        def palloc(stk, n, s, d=F32):
            uid[0] += 1
            return stk.enter_context(nc.psum_tensor("%s_%d" % (n, uid[0]), list(s), d))

        Teub, Tevb = T("eu_b"), T("ev_b")
        precast_jobs = []
        if stage >= 7:
            for dc in range(8):
                for hb in range(2):
                    precast_jobs.append((eu_b[hb * 32:(hb + 1) * 32, :, dc, :].rearrange("b p e -> p b e"),
                                         euT[dc * 128:(dc + 1) * 128, hb * 8192:(hb + 1) * 8192].rearrange(
                                             "p (b e) -> p b e", e=256), Teub))
            for i in range(32):
                precast_jobs.append((ev_b[i * 512:(i + 1) * 512, :], ev[i * 512:(i + 1) * 512, :], Tevb))
            precast_jobs = [j for pair in zip(precast_jobs[:16] + precast_jobs[16:32], precast_jobs[32:] + [None] * 16)
                            for j in pair if j is not None]

        def precast_some(n):
            for _ in range(n):
                if precast_jobs:
                    o, i_, Tt = precast_jobs.pop(0)
                    P.dma(o, i_, writes=[Tt], qeng="pool")

        ident_f = alloc(st, "ident_f", [128, 128], F32)
        ident_b = alloc(st, "ident_b", [128, 128], BF16)
        Uneg = alloc(st, "Uneg", [128, 128], BF16)
        Unegb = alloc(st, "Unegb", [128, 128], BF16)
        ones_f = alloc(st, "ones_f", [128, 1], F32)
        gp = alloc(st, "gp", [128, NGP], F32)
        Tc = T("consts")
        P.dma(gp[:], gpack[:, :], writes=[Tc])
        P.op("pool", lambda e: e.memset(ident_f[:], 1.0), writes=[Tc])
        P.op("pool", lambda e: e.affine_select(out=ident_f[:], in_=ident_f[:], pattern=[[1, 128]],
                                               compare_op=ALU.is_equal, fill=0.0, base=0, channel_multiplier=-1),
             reads=[Tc], writes=[Tc])
        P.op("pool", lambda e: e.tensor_copy(out=ident_b[:], in_=ident_f[:]), reads=[Tc], writes=[Tc])
        P.op("pool", lambda e: e.memset(Uneg[:], -1.0), writes=[Tc])
        P.op("pool", lambda e: e.affine_select(out=Uneg[:], in_=Uneg[:], pattern=[[-1, 128]],
                                               compare_op=ALU.is_gt, fill=0.0, base=0, channel_multiplier=1),
             reads=[Tc], writes=[Tc])
        P.op("pool", lambda e: e.memset(Unegb[:], -1.0), writes=[Tc])
        P.op("pool", lambda e: e.affine_select(out=Unegb[:], in_=Unegb[:], pattern=[[1, 128]],
                                               compare_op=ALU.is_ge, fill=0.0, base=0, channel_multiplier=-1),
             reads=[Tc], writes=[Tc])
        P.op("pool", lambda e: e.memset(ones_f[:], 1.0), writes=[Tc])

        class NormRes:
            pass

        def make_norm_res(stk, pT):
            R = NormRes()
            R.junk = alloc(stk, "n_junk", [128, 1024], BF16)
            R.ssq = alloc(stk, "n_ssq", [128, 4], F32)
            R.rstd = alloc(stk, "n_rstd", [128, 4], F32)
            R.xs = [alloc(stk, "n_xs%d" % i, [128, 1024], F32) for i in range(2)]
            R.Tjunk = T("n_junk")
            R.Tssq = [T("n_ssq%d" % i) for i in range(4)]
            R.Trstd = T("n_rstd")
            R.Txs = [T("n_xs0"), T("n_xs1")]
            R.pT = pT
            R.k = 0
            return R

        def norm_group(R, srcs, gcol, nT, TnT):
            n = len(srcs)
            for i, (xa, Tx) in enumerate(srcs):
                P.op("act", lambda e, xa=xa, i=i: e.activation(out=R.junk[:], in_=xa, func=AF.Square,
                                                                accum_out=R.ssq[:, i:i + 1]),
                     reads=[Tx], writes=[R.Tjunk, R.Tssq[i]])
            P.op("dve", lambda e: e.tensor_scalar(out=R.rstd[:, 0:n], in0=R.ssq[:, 0:n], scalar1=1.0 / 1024,
                                                  scalar2=EPS, op0=ALU.mult, op1=ALU.add),
                 reads=R.Tssq[0:n], writes=[R.Trstd])
            P.op("act", lambda e: e.activation(out=R.rstd[:, 0:n], in_=R.rstd[:, 0:n], func=AF.Sqrt),
                 reads=[R.Trstd], writes=[R.Trstd])
            P.op("dve", lambda e: e.reciprocal(out=R.rstd[:, 0:n], in_=R.rstd[:, 0:n]),
                 reads=[R.Trstd], writes=[R.Trstd])
            for i, (xa, Tx) in enumerate(srcs):
                xs = R.xs[i % 2]
                Txs = R.Txs[i % 2]
                P.op("act", lambda e, xa=xa, xs=xs, i=i: e.activation(out=xs[:], in_=xa, func=AF.Copy,
                                                                      scale=R.rstd[:, i:i + 1]),
                     reads=[Tx, R.Trstd], writes=[Txs])
                for half in range(2):
                    pt, Tp = R.pT[R.k % len(R.pT)]
                    R.k += 1
                    for c in range(4):
                        cc = half * 4 + c
                        P.op("pe", lambda e, pt=pt, c=c, cc=cc, xs=xs: e.transpose(
                            out=pt[:, c * 128:(c + 1) * 128], in_=xs[:, cc * 128:(cc + 1) * 128],
                            identity=ident_f[:]), reads=[Txs, Tc], writes=[Tp])
                    P.op("dve", lambda e, pt=pt, half=half, i=i: e.tensor_tensor(
                        out=nT[:, half * 4:(half + 1) * 4, i * 128:(i + 1) * 128],
                        in0=pt[:, :].rearrange("p (c n) -> p c n", c=4),
                        in1=gcol[:, half * 4:(half + 1) * 4].unsqueeze(2).broadcast_to([128, 4, 128]),
                        op=ALU.mult), reads=[Tp, Tc], writes=[TnT])

        with contextlib.ExitStack() as sAC:
            QT_sb = alloc(sAC, "QT_sb", [128, 8, 2048], BF16)
            convT_sb = alloc(sAC, "convT_sb", [128, 4, 2048], BF16)
            sbT_sb = alloc(sAC, "sbT_sb", [128, 4, 2048], BF16)
            ssq_c = alloc(sAC, "ssq_c", [128, 16], F32)
            ssq_s = alloc(sAC, "ssq_s", [128, 16], F32)
            TQT, TconvT, TsbT, Tssqc, Tssqs = T("QT"), T("convT"), T("sbT"), T("ssqc"), T("ssqs")
            TkTd, Tvd = T("kT_d"), T("v_d")

            with contextlib.ExitStack() as ph:
                pT = [(palloc(ph, "pT%d" % i, [128, 512]), T("pT%d" % i)) for i in range(4)]
                pK = [(palloc(ph, "pK%d" % i, [128, 512]), T("pK%d" % i)) for i in range(2)]
                pV = [(palloc(ph, "pV%d" % i, [128, 512]), T("pV%d" % i)) for i in range(2)]
                R = make_norm_res(ph, pT)
                wkv = alloc(ph, "wkv", [128, 8, 1024], BF16)
                Twkv = T("wkv")
                P.dma(wkv[:], w_in[:, 512:1536].rearrange("(c p) n -> p c n", p=128), writes=[Twkv], qeng="pool")
                NXB = 8
                xt = [alloc(ph, "xt%d" % i, [128, 1024], F32) for i in range(NXB)]
                Txt = [T("xt%d" % i) for i in range(NXB)]
                nTb = [alloc(ph, "nT%d" % i, [128, 8, 512], BF16) for i in range(2)]
                TnTb = [T("nT0"), T("nT1")]
                kst = [alloc(ph, "kst%d" % i, [128, 4, 512], BF16) for i in range(2)]
                vst = [alloc(ph, "vst%d" % i, [128, 4, 512], BF16) for i in range(2)]
                Tkst = [T("kst0"), T("kst1")]
                Tvst = [T("vst0"), T("vst1")]
                NG = 16 if stage >= 1 else 0

                def load_group(g):
                    for tt in range(4):
                        b = (g * 4 + tt) % NXB
                        r0 = (g * 4 + tt) * 128
                        P.dma(xt[b][:], xb[r0:r0 + 128, :], writes=[Txt[b]])
                if NG:
                    load_group(0)
                for g in range(NG):
                    if g + 1 < NG:
                        load_group(g + 1)
                    nT = nTb[g % 2]
                    TnT = TnTb[g % 2]
                    srcs = [(xt[(g * 4 + tt) % NXB][:], Txt[(g * 4 + tt) % NXB]) for tt in range(4)]
                    norm_group(R, srcs, gp[:, G_MIX:G_MIX + 8], nT, TnT)
                    ks, Tks = kst[g % 2], Tkst[g % 2]
                    vs, Tvs = vst[g % 2], Tvst[g % 2]
                    for hp in range(4):
                        pk, Tpk = pK[hp % 2]
                        for dc in range(8):
                            P.op("pe", lambda e, pk=pk, dc=dc, hp=hp, nT=nT: e.matmul(
                                pk[:], lhsT=wkv[:, dc, hp * 128:(hp + 1) * 128], rhs=nT[:, dc, :],
                                start=(dc == 0), stop=(dc == 7)), reads=[Twkv, TnT], writes=[Tpk])
                        P.op("act", lambda e, pk=pk, ks=ks, hp=hp: e.activation(out=ks[:, hp, :], in_=pk[:],
                                                                              func=AF.Copy),
                             reads=[Tpk], writes=[Tks])
                    P.dma(kT_d[:, :, g * 512:(g + 1) * 512].rearrange("h p n -> p h n"), ks[:],
                          reads=[Tks], writes=[TkTd], qeng="pool")
                    precast_some(1)
                    for tt in range(4):
                        pv, Tpv = pV[tt % 2]
                        for dc in range(8):
                            P.op("pe", lambda e, pv=pv, dc=dc, tt=tt, nT=nT: e.matmul(
                                pv[:], lhsT=nT[:, dc, tt * 128:(tt + 1) * 128], rhs=wkv[:, dc, 512:1024],
                                start=(dc == 0), stop=(dc == 7)), reads=[Twkv, TnT], writes=[Tpv])
                        P.op("dve", lambda e, pv=pv, vs=vs, tt=tt: e.tensor_copy(out=vs[:, tt, :], in_=pv[:]),
                             reads=[Tpv], writes=[Tvs])
                    for hp in range(4):
                        P.dma(v_d[hp, :, 4 * g:4 * g + 4, :], vs[:, :, hp * 128:(hp + 1) * 128],
                              reads=[Tvs], writes=[Tvd], qeng="pool")
                P.barrier()

            with contextlib.ExitStack() as ph:
                pT = [(palloc(ph, "pT%d" % i, [128, 512]), T("pT%d" % i)) for i in range(2)]
                pQ = [(palloc(ph, "pQ%d" % i, [128, 512]), T("pQ%d" % i)) for i in range(2)]
                pG3 = [(palloc(ph, "pG%d" % i, [128, 512]), T("pG%d" % i)) for i in range(3)]
                pss = palloc(ph, "pss", [128, 512])
                Tpss = T("pss")
                R = make_norm_res(ph, pT)
                wq = alloc(ph, "wq", [128, 8, 512], BF16)
                wg = alloc(ph, "wg", [128, 8, 1536], BF16)
                Twq, Twg = T("wq"), T("wg")
                xt = [alloc(ph, "xt%d" % i, [128, 1024], F32) for i in range(8)]
                Txt = [T("xt%d" % i) for i in range(8)]
                xht = alloc(ph, "xht", [128, 1024], F32)
                Txht = T("xht")
                nTb = [alloc(ph, "nT%d" % i, [128, 8, 512], BF16) for i in range(2)]
                TnTb = [T("nT0"), T("nT1")]
                nhT = alloc(ph, "nhT", [128, 8, 128], BF16)
                TnhT = T("nhT")
                uh = alloc(ph, "uh", [128, 4, 8], F32)
                Tuh = T("uh")
                gch = alloc(ph, "gch", [128, 8], F32)
                Tgch = T("gch")
                gc_sb = alloc(ph, "gc_sb", [128, 512], F32)
                u_sb = alloc(ph, "u_sb", [128, 514], F32)
                acc = alloc(ph, "acc", [128, 512], F32)
                conv = alloc(ph, "conv", [128, 512], F32)
                sqc = alloc(ph, "sqc", [128, 512], F32)
                Tgc, Tu, Tacc, Tconv, Tsqc = T("gc"), T("u"), T("acc"), T("conv"), T("sqc")
                if stage >= 2:
                    P.dma(wq[:], w_in[:, 0:512].rearrange("(c p) n -> p c n", p=128), writes=[Twq], qeng="pool")
                    P.dma(wg[:], w_in[:, 1536:3072].rearrange("(c p) n -> p c n", p=128), writes=[Twg], qeng="pool")
                    P.op("pool", lambda e: e.memset(QT_sb[:], 0.0), writes=[TQT])
                    P.op("pool", lambda e: e.memset(xht[:], 0.0), writes=[Txht])
                    P.dma(xht[0:8, :], xh[:, :], writes=[Txht])

                    def load_slot(s):
                        for tt in range(4):
                            b = (s * 4 + tt) % 8
                            r0 = (s * 4 + tt) * 128
                            P.dma(xt[b][:], xq[r0:r0 + 128, :], writes=[Txt[b]])
                    load_slot(0)
                    norm_group(R, [(xht[:], Txht)], gp[:, G_MIX:G_MIX + 8], nhT, TnhT)
                    for cc in range(4):
                        pa, Tpa = pG3[0]
                        pb, Tpb = pG3[1]
                        for dc in range(8):
                            P.op("pe", lambda e, pa=pa, dc=dc, cc=cc: e.matmul(
                                pa[:, 0:8], lhsT=wg[:, dc, 512 + cc * 128:512 + (cc + 1) * 128], rhs=nhT[:, dc, 0:8],
                                start=(dc == 0), stop=(dc == 7)), reads=[Twg, TnhT], writes=[Tpa])
                        for dc in range(8):
                            P.op("pe", lambda e, pb=pb, dc=dc, cc=cc: e.matmul(
                                pb[:, 0:8], lhsT=wg[:, dc, 1024 + cc * 128:1024 + (cc + 1) * 128], rhs=nhT[:, dc, 0:8],
                                start=(dc == 0), stop=(dc == 7)), reads=[Twg, TnhT], writes=[Tpb])
                        P.op("act", lambda e, pa=pa: e.activation(out=gch[:], in_=pa[:, 0:8], func=AF.Copy),
                             reads=[Tpa], writes=[Tgch])
                        P.op("dve", lambda e, pb=pb, cc=cc: e.tensor_tensor(out=uh[:, cc, :], in0=gch[:], in1=pb[:, 0:8],
                                                                           op=ALU.mult),
                             reads=[Tgch, Tpb], writes=[Tuh])
                    gi = 0
                    for s in range(4):
                        if s + 1 < 4:
                            load_slot(s + 1)
                        nT, TnT = nTb[s % 2], TnTb[s % 2]
                        srcs = [(xt[(s * 4 + tt) % 8][:], Txt[(s * 4 + tt) % 8]) for tt in range(4)]
                        norm_group(R, srcs, gp[:, G_MIX:G_MIX + 8], nT, TnT)
                        for hp in range(4):
                            pq, Tpq = pQ[hp % 2]
                            for dc in range(8):
                                P.op("pe", lambda e, pq=pq, dc=dc, hp=hp, nT=nT: e.matmul(
                                    pq[:], lhsT=wq[:, dc, hp * 128:(hp + 1) * 128], rhs=nT[:, dc, :],
                                    start=(dc == 0), stop=(dc == 7)), reads=[Twq, TnT], writes=[Tpq])
                            for hd in range(2):
                                r0, r1 = hd * 64, hd * 64 + 64
                                P.op("act", lambda e, pq=pq, hp=hp, s=s, hd=hd, r0=r0, r1=r1: e.activation(
                                    out=QT_sb[r0:r1, 2 * hp + hd, s * 512:(s + 1) * 512], in_=pq[r0:r1, :],
                                    func=AF.Copy, scale=0.125), reads=[Tpq], writes=[TQT])
                        for cc in range(4):
                            banks = []
                            for col0 in (512 + cc * 128, 1024 + cc * 128, cc * 128):
                                pg, Tpg = pG3[gi % 3]
                                gi += 1
                                for dc in range(8):
                                    P.op("pe", lambda e, pg=pg, dc=dc, col0=col0, nT=nT: e.matmul(
                                        pg[:], lhsT=wg[:, dc, col0:col0 + 128], rhs=nT[:, dc, :],
                                        start=(dc == 0), stop=(dc == 7)), reads=[Twg, TnT], writes=[Tpg])
                                banks.append((pg, Tpg))
                            (pgc, Tpgc), (pxi, Tpxi), (pgb, Tpgb) = banks
                            P.op("act", lambda e, pgc=pgc: e.activation(out=gc_sb[:], in_=pgc[:], func=AF.Copy),
                                 reads=[Tpgc], writes=[Tgc])
                            P.op("dve", lambda e, cc=cc, s=s: e.tensor_copy(out=u_sb[:, 0:2], in_=uh[:, cc, 2 * s:2 * s + 2]),
                                 reads=[Tuh], writes=[Tu])
                            P.op("dve", lambda e, pxi=pxi: e.tensor_tensor(out=u_sb[:, 2:514], in0=gc_sb[:], in1=pxi[:],
                                                                          op=ALU.mult),
                                 reads=[Tgc, Tpxi], writes=[Tu])
                            P.op("dve", lambda e, cc=cc: e.tensor_scalar(
                                out=acc[:], in0=u_sb[:, 2:514], scalar1=gp[:, G_CW + cc * 3 + 2:G_CW + cc * 3 + 3],
                                scalar2=None, op0=ALU.mult), reads=[Tu, Tc], writes=[Tacc])
                            P.op("dve", lambda e, cc=cc: e.scalar_tensor_tensor(
                                out=acc[:], in0=u_sb[:, 1:513], scalar=gp[:, G_CW + cc * 3 + 1:G_CW + cc * 3 + 2],
                                in1=acc[:], op0=ALU.mult, op1=ALU.add), reads=[Tu, Tc, Tacc], writes=[Tacc])
                            P.op("dve", lambda e, cc=cc: e.scalar_tensor_tensor(
                                out=acc[:], in0=u_sb[:, 0:512], scalar=gp[:, G_CW + cc * 3:G_CW + cc * 3 + 1],
                                in1=acc[:], op0=ALU.mult, op1=ALU.add), reads=[Tu, Tc, Tacc], writes=[Tacc])
                            P.op("dve", lambda e, pgb=pgb: e.tensor_tensor(out=conv[:], in0=acc[:], in1=pgb[:],
                                                                          op=ALU.mult),
                                 reads=[Tacc, Tpgb], writes=[Tconv])
                            P.op("act", lambda e: e.activation(out=sqc[:], in_=conv[:], func=AF.Square),
                                 reads=[Tconv], writes=[Tsqc])
                            P.op("act", lambda e, cc=cc, s=s: e.activation(
                                out=convT_sb[:, cc, s * 512:(s + 1) * 512], in_=conv[:], func=AF.Copy,
                                scale=gp[:, G_CONV + cc:G_CONV + cc + 1]), reads=[Tconv, Tc], writes=[TconvT])
                            for tt in range(4):
                                col = (s * 4 + tt) * 4 + cc
                                P.op("pe", lambda e, tt=tt, col=col: e.matmul(
                                    pss[:, col:col + 1], lhsT=sqc[:, tt * 128:(tt + 1) * 128], rhs=ones_f[:, 0:1],
                                    start=True, stop=True), reads=[Tsqc, Tc], writes=[Tpss])
                    P.op("dve", lambda e: e.tensor_reduce(out=ssq_c[:], in_=pss[:, 0:64].rearrange("p (t c) -> p t c", c=4),
                                                          axis=AX.X, op=ALU.add), reads=[Tpss], writes=[Tssqc])
                P.barrier()

            with contextlib.ExitStack() as ph:
                pz = [(palloc(ph, "pz%d" % i, [128, 512]), T("pz%d" % i)) for i in range(3)]
                pGc = [(palloc(ph, "pGc%d" % i, [128, 512]), T("pGc%d" % i)) for i in range(2)]
                pO = [(palloc(ph, "pO%d" % i, [128, 512]), T("pO%d" % i)) for i in range(2)]
                pss = palloc(ph, "pssb", [128, 512])
                Tpss = T("pssb")
                msk = alloc(ph, "msk", [128, 16, 512], BF16)
                Tmsk = T("msk")
                KT_sb = [alloc(ph, "KT%d" % i, [128, 8192], BF16) for i in range(2)]
                V_sb = [alloc(ph, "V%d" % i, [128, 64, 128], BF16) for i in range(2)]
                TKT = [T("KT0"), T("KT1")]
                TV = [T("V0"), T("V1")]
                NB = 4
                e1 = [alloc(ph, "e1_%d" % i, [128, 512], F32) for i in range(NB)]
                sp = [alloc(ph, "sp_%d" % i, [128, 512], F32) for i in range(NB)]
                Lb = [alloc(ph, "Lb_%d" % i, [128, 512], BF16) for i in range(NB)]
                t2 = [alloc(ph, "t2_%d" % i, [128, 512], F32) for i in range(NB)]
                Ab = [alloc(ph, "Ab_%d" % i, [128, 512], BF16) for i in range(NB)]
                tmpf = [alloc(ph, "tmpf_%d" % i, [128, 512], F32) for i in range(2)]
                Te1 = [T("e1") for _ in range(NB)]
                Tsp = [T("sp") for _ in range(NB)]
                TLb = [T("Lb") for _ in range(NB)]
                Tt2 = [T("t2") for _ in range(NB)]
                TAb = [T("Ab") for _ in range(NB)]
                Ttmpf = [T("tmpf0"), T("tmpf1")]
                sqs = alloc(ph, "sqs", [128, 512], F32)
                Tsqs = T("sqs")
                if stage >= 3:
                    P.dma(msk[:], maskd[:, :, :], writes=[Tmsk])
                    pairs = [(s, hp) for s in range(4) for hp in range(4)]

                    def load_kv(i):
                        s, hp = pairs[i]
                        b = i % 2
                        nk = (4 * s + 4) * 512
                        nblk = nk // 128
                        P.dma(KT_sb[b][:, 0:nk], kT_d[hp, :, 0:nk], reads=[TkTd], writes=[TKT[b]])
                        P.dma(V_sb[b][:, 0:nblk, :], v_d[hp, :, 0:nblk, :], reads=[Tvd], writes=[TV[b]])
                        precast_some(2)

                    steps = []
                    for i, (s, hp) in enumerate(pairs):
                        nblk = (4 * s + 4) * 4
                        for blk in range(nblk - 1, -1, -1):
                            for hd in range(2):
                                steps.append(dict(i=i, s=s, hp=hp, blk=blk, hd=hd, first=(blk == nblk - 1),
                                                  last=(blk == 0), pfirst=(blk == nblk - 1 and hd == 0),
                                                  plast=(blk == 0 and hd == 1)))
                    NS = len(steps)

                    def info(n):
                        d = steps[n]
                        blk, s = d["blk"], d["s"]
                        j, kb = blk // 4, blk % 4
                        return d, (j >= 4 * s), (j - 4 * s) * 4 + kb

                    def pe_z(n):
                        d = steps[n]
                        b = d["i"] % 2
                        z, Tz = pz[n % 3]
                        KT = KT_sb[b]
                        blk, hp, hd, s = d["blk"], d["hp"], d["hd"], d["s"]
                        P.op("pe", lambda e: e.matmul(
                            z[:], lhsT=KT[:, blk * 128:(blk + 1) * 128],
                            rhs=QT_sb[:, 2 * hp + hd, s * 512:(s + 1) * 512], start=True, stop=True),
                            reads=[TKT[b], TQT], writes=[Tz])

                    def act_s1(n):
                        z, Tz = pz[n % 3]
                        k = n % NB
                        P.op("act", lambda e: e.activation(out=e1[k][:], in_=z[:], func=AF.Exp, scale=-1.0),
                             reads=[Tz], writes=[Te1[k]])
                        P.op("act", lambda e: e.activation(out=sp[k][:], in_=e1[k][:], func=AF.Ln, bias=1.0),
                             reads=[Te1[k]], writes=[Tsp[k]])

                    def dve_L(n):
                        d, masked, mi = info(n)
                        z, Tz = pz[n % 3]
                        k = n % NB
                        if not masked:
                            P.op("dve", lambda e: e.tensor_tensor(out=Lb[k][:], in0=z[:], in1=sp[k][:], op=ALU.add),
                                 reads=[Tz, Tsp[k]], writes=[TLb[k]])
                        else:
                            tf, Ttf = tmpf[n % 2], Ttmpf[n % 2]
                            P.op("dve", lambda e: e.tensor_tensor(out=tf[:], in0=z[:], in1=sp[k][:], op=ALU.add),
                                 reads=[Tz, Tsp[k]], writes=[Ttf])
                            P.op("pool", lambda e: e.tensor_tensor(out=Lb[k][:], in0=tf[:], in1=msk[:, mi, :], op=ALU.mult),
                                 reads=[Ttf, Tmsk], writes=[TLb[k]])

                    def pe_mm1(n):
                        d = steps[n]
                        k = n % NB
                        G, TG = pGc[d["hd"]]
                        first = d["first"]
                        P.op("pe", lambda e: e.matmul(G[:], lhsT=Uneg[:], rhs=Lb[k][:], start=first, stop=True, skip_group_check=True),
                             reads=[TLb[k], Tc], writes=[TG])

                    def dve_t2(n):
                        d = steps[n]
                        k = n % NB
                        G, TG = pGc[d["hd"]]
                        P.op("dve", lambda e: e.tensor_tensor(out=t2[k][:], in0=G[:], in1=sp[k][:], op=ALU.subtract),
                             reads=[TG, Tsp[k]], writes=[Tt2[k]])

                    def pe_mm2(n):
                        d = steps[n]
                        if d["last"]:
                            return
                        k = n % NB
                        G, TG = pGc[d["hd"]]
                        P.op("pe", lambda e: e.matmul(G[:], lhsT=Unegb[:], rhs=Lb[k][:], start=False, stop=True, skip_group_check=True),
                             reads=[TLb[k], Tc], writes=[TG])

                    def act_A(n):
                        d, masked, mi = info(n)
                        k = n % NB
                        P.op("act", lambda e: e.activation(out=Ab[k][:], in_=t2[k][:], func=AF.Exp),
                             reads=[Tt2[k]], writes=[TAb[k]])

                    def mask_A(n):
                        d, masked, mi = info(n)
                        k = n % NB
                        if masked:
                            P.op("dve", lambda e: e.tensor_tensor(out=Ab[k][:], in0=Ab[k][:], in1=msk[:, mi, :], op=ALU.mult),
                                 reads=[TAb[k], Tmsk], writes=[TAb[k]])

                    def pe_O(n):
                        d = steps[n]
                        b = d["i"] % 2
                        k = n % NB
                        blk, hd, s, hp = d["blk"], d["hd"], d["s"], d["hp"]
                        O, TO = pO[hd]
                        V = V_sb[b]
                        first, last = d["first"], d["last"]
                        P.op("pe", lambda e: e.matmul(O[:], lhsT=V[:, blk, :], rhs=Ab[k][:], start=first, stop=last),
                             reads=[TV[b], TAb[k]], writes=[TO])
                        if d["plast"]:
                            for hd2 in range(2):
                                r0, r1 = hd2 * 64, hd2 * 64 + 64
                                O2, TO2 = pO[hd2]
                                P.op("act", lambda e, O2=O2, r0=r0, r1=r1: e.activation(
                                    out=sqs[r0:r1, :], in_=O2[r0:r1, :], func=AF.Square), reads=[], writes=[Tsqs, TO2])
                                P.op("dve", lambda e, O2=O2, r0=r0, r1=r1: e.tensor_scalar(
                                    out=sbT_sb[r0:r1, hp, s * 512:(s + 1) * 512], in0=O2[r0:r1, :],
                                    scalar1=gp[r0:r1, G_SB + hp:G_SB + hp + 1], scalar2=None, op0=ALU.mult),
                                    reads=[Tc], writes=[TsbT, TO2])
                            for tt in range(4):
                                col = (s * 4 + tt) * 4 + hp
                                P.op("pe", lambda e, tt=tt, col=col: e.matmul(
                                    pss[:, col:col + 1], lhsT=sqs[:, tt * 128:(tt + 1) * 128], rhs=ones_f[:, 0:1],
                                    start=True, stop=True), reads=[Tsqs, Tc], writes=[Tpss])

                    load_kv(0)
                    ok = lambda m: 0 <= m < NS
                    for n in range(NS + 5):
                        if ok(n):
                            pe_z(n)
                            act_s1(n)
                        if ok(n - 5):
                            mask_A(n - 5)
                            pe_O(n - 5)
                            if steps[n - 5]["pfirst"]:
                                ni = steps[n - 5]["i"] + 1
                                if ni < len(pairs):
                                    load_kv(ni)
                        if ok(n - 1):
                            dve_L(n - 1)
                        if ok(n - 2):
                            pe_mm1(n - 2)
                            dve_t2(n - 2)
                        if ok(n - 3):
                            pe_mm2(n - 3)
                        if ok(n - 4):
                            act_A(n - 4)
                    P.op("dve", lambda e: e.tensor_reduce(out=ssq_s[:], in_=pss[:, 0:64].rearrange("p (t c) -> p t c", c=4),
                                                          axis=AX.X, op=ALU.add), reads=[Tpss], writes=[Tssqs])
                P.barrier()

            if "d_sbT" in dbg_out:
                P.dma(dbg_out["d_sbT"].rearrange("c p n -> p c n"), sbT_sb[:], reads=[TsbT], writes=[T("x")])
                P.dma(dbg_out["d_convT"].rearrange("c p n -> p c n"), convT_sb[:], reads=[TconvT], writes=[T("x")])
                P.dma(dbg_out["d_ssq"][:, 0:16], ssq_s[:], reads=[Tssqs], writes=[T("x")])
                P.dma(dbg_out["d_ssq"][:, 16:32], ssq_c[:], reads=[Tssqc], writes=[T("x")])
                P.barrier()

            with contextlib.ExitStack() as ph:
                pP = [(palloc(ph, "pP%d" % i, [128, 512]), T("pP%d" % i)) for i in range(8)]
                wo = alloc(ph, "wo", [128, 8, 1024], BF16)
                Two = T("wo")
                rs = alloc(ph, "rs", [128, 32], F32)
                Trs = T("rs")
                xt = [alloc(ph, "xt%d" % i, [128, 1024], F32) for i in range(4)]
                Txt = [T("xt%d" % i) for i in range(4)]
                ht = [alloc(ph, "ht%d" % i, [128, 1024], F32) for i in range(2)]
                Tht = [T("ht0"), T("ht1")]
                Thd = T("h_d")
                if stage >= 4:
                    P.dma(wo[:], w_out[:, :].rearrange("(c p) n -> p c n", p=128), writes=[Two], qeng="pool")
                    P.op("dve", lambda e: e.tensor_scalar(out=rs[:, 0:16], in0=ssq_s[:], scalar1=1.0 / 512, scalar2=EPS,
                                                          op0=ALU.mult, op1=ALU.add), reads=[Tssqs], writes=[Trs])
                    P.op("dve", lambda e: e.tensor_scalar(out=rs[:, 16:32], in0=ssq_c[:], scalar1=1.0 / 512, scalar2=EPS,
                                                          op0=ALU.mult, op1=ALU.add), reads=[Tssqc, Trs], writes=[Trs])
                    P.op("act", lambda e: e.activation(out=rs[:], in_=rs[:], func=AF.Sqrt), reads=[Trs], writes=[Trs])
                    P.op("dve", lambda e: e.reciprocal(out=rs[:], in_=rs[:]), reads=[Trs], writes=[Trs])
                    for t in range(16):
                        xa, Txa = xt[t % 4], Txt[t % 4]
                        P.dma(xa[:], xq[t * 128:(t + 1) * 128, :], writes=[Txa])
                        h, Th = ht[t % 2], Tht[t % 2]
                        banks = [pP[(t % 2) * 4 + i] for i in range(4)]
                        for src, (Tsrc) in ((0, TsbT), (1, TconvT)):
                            srcT = sbT_sb if src == 0 else convT_sb
                            for half in range(2):
                                pb, Tpb = banks[src * 2 + half]
                                for c in range(4):
                                    P.op("pe", lambda e, pb=pb, srcT=srcT, c=c, t=t, src=src, half=half: e.matmul(
                                        pb[:], lhsT=srcT[:, c, t * 128:(t + 1) * 128],
                                        rhs=wo[:, src * 4 + c, half * 512:(half + 1) * 512],
                                        start=(c == 0), stop=(c == 3)), reads=[Tsrc, Two], writes=[Tpb])
                        for half in range(2):
                            pb, Tpb = banks[half]
                            P.op("dve", lambda e, pb=pb, h=h, xa=xa, half=half, t=t: e.scalar_tensor_tensor(
                                out=h[:, half * 512:(half + 1) * 512], in0=pb[:], scalar=rs[:, t:t + 1],
                                in1=xa[:, half * 512:(half + 1) * 512], op0=ALU.mult, op1=ALU.add),
                                reads=[Tpb, Trs, Txa], writes=[Th])
                        for half in range(2):
                            pb, Tpb = banks[2 + half]
                            P.op("dve", lambda e, pb=pb, h=h, half=half, t=t: e.scalar_tensor_tensor(
                                out=h[:, half * 512:(half + 1) * 512], in0=pb[:], scalar=rs[:, 16 + t:17 + t],
                                in1=h[:, half * 512:(half + 1) * 512], op0=ALU.mult, op1=ALU.add),
                                reads=[Tpb, Trs, Th], writes=[Th])
                        P.dma(h_d[t * 128:(t + 1) * 128, :], h[:], reads=[Th], writes=[Thd], qeng="pool")
                P.barrier()

        if "d_h1" in dbg_out:
            with contextlib.ExitStack() as ph:
                tmp = alloc(ph, "dbgtmp", [128, 16, 1024], F32)
                Tt = T("dbgtmp")
                P.dma(tmp[:], h_d.rearrange("(t p) n -> p t n", p=128), writes=[Tt])
                P.dma(dbg_out["d_h1"].rearrange("(t p) n -> p t n", p=128), tmp[:], reads=[Tt], writes=[T("x")])
                P.barrier()

        Thd = T("h_d")
        with contextlib.ExitStack() as ph:
            pT = [(palloc(ph, "pT%d" % i, [128, 512]), T("pT%d" % i)) for i in range(2)]
            pA = [(palloc(ph, "pA%d" % i, [128, 512]), T("pA%d" % i)) for i in range(2)]
            psc = palloc(ph, "psc", [128, 512])
            Tpsc = T("psc")
            pTp = palloc(ph, "pTp", [128, 1024], BF16)
            TpTp = T("pTp")
            poT = [(palloc(ph, "poT%d" % i, [128, 512]), T("poT%d" % i)) for i in range(2)]
            R = make_norm_res(ph, pT)
            wqm = alloc(ph, "wqm", [128, 8, 1024], BF16)
            wkvm = alloc(ph, "wkvm", [128, 8, 2048], BF16)
            wom = alloc(ph, "wom", [128, 8, 1024], BF16)
            Twqm, Twkvm, Twom = T("wqm"), T("wkvm"), T("wom")
            memt = [alloc(ph, "memt%d" % i, [128, 1024], F32) for i in range(2)]
            Tmemt = [T("memt0"), T("memt1")]
            memT = alloc(ph, "memT", [128, 8, 256], BF16)
            TmemT = T("memT")
            kTm = alloc(ph, "kTm", [128, 8, 256], BF16)
            vm = alloc(ph, "vm", [128, 2, 1024], BF16)
            TkTm, Tvm = T("kTm"), T("vm")
            ht = [alloc(ph, "ht%d" % i, [128, 1024], F32) for i in range(8)]
            Tht = [T("ht%d" % i) for i in range(8)]
            n2T = [alloc(ph, "n2T%d" % i, [128, 8, 512], BF16) for i in range(2)]
            Tn2T = [T("n2T0"), T("n2T1")]
            qTm = alloc(ph, "qTm", [128, 8, 512], BF16)
            TqTm = T("qTm")
            nmx = alloc(ph, "nmx", [128, 4], F32)
            rsum = alloc(ph, "rsum", [128, 4], F32)
            Tnmx = [T("nmx%d" % i) for i in range(4)]
            Trsum = [T("rsum%d" % i) for i in range(4)]
            pexp = [alloc(ph, "pexp%d" % i, [128, 256], F32) for i in range(2)]
            pn = [alloc(ph, "pn%d" % i, [128, 256], BF16) for i in range(2)]
            pTs = [alloc(ph, "pTs%d" % i, [128, 256], BF16) for i in range(2)]
            Tpexp = [T("pexp0"), T("pexp1")]
            Tpn = [T("pn0"), T("pn1")]
            TpTs = [T("pTs0"), T("pTs1")]
            oT_sb = alloc(ph, "oT_sb", [128, 8, 128], BF16)
            ToT = T("oT_sb")
            if stage >= 5:
                P.dma(wqm[:], w_q_mem[:, :].rearrange("(c p) n -> p c n", p=128), writes=[Twqm], qeng="pool")
                P.dma(wkvm[:], w_kv_mem[:, :].rearrange("(c p) n -> p c n", p=128), writes=[Twkvm], qeng="pool")
                P.dma(wom[:], w_o_mem[:, :].rearrange("(c p) n -> p c n", p=128), writes=[Twom], qeng="pool")
                for i in range(2):
                    P.dma(memt[i][:], memb[i * 128:(i + 1) * 128, :], writes=[Tmemt[i]])

                def load_hgroup(g):
                    for tt in range(4):
                        bb = (g * 4 + tt) % 8
                        r0 = (g * 4 + tt) * 128
                        P.dma(ht[bb][:], h_d[r0:r0 + 128, :], reads=[Thd], writes=[Tht[bb]])
                load_hgroup(0)
                norm_group(R, [(memt[0][:], Tmemt[0]), (memt[1][:], Tmemt[1])], gp[:, G_MEM:G_MEM + 8], memT, TmemT)
                for c in range(8):
                    pa, Tpa = pA[c % 2]
                    for dc in range(8):
                        P.op("pe", lambda e, pa=pa, dc=dc, c=c: e.matmul(
                            pa[:, 0:256], lhsT=wkvm[:, dc, c * 128:(c + 1) * 128], rhs=memT[:, dc, :],
                            start=(dc == 0), stop=(dc == 7)), reads=[Twkvm, TmemT], writes=[Tpa])
                    P.op("act", lambda e, pa=pa, c=c: e.activation(out=kTm[:, c, :], in_=pa[:, 0:256], func=AF.Copy),
                         reads=[Tpa], writes=[TkTm])
                for mc in range(2):
                    for half in range(2):
                        pa, Tpa = pA[half]
                        for dc in range(8):
                            P.op("pe", lambda e, pa=pa, dc=dc, mc=mc, half=half: e.matmul(
                                pa[:], lhsT=memT[:, dc, mc * 128:(mc + 1) * 128],
                                rhs=wkvm[:, dc, 1024 + half * 512:1024 + (half + 1) * 512],
                                start=(dc == 0), stop=(dc == 7)), reads=[Twkvm, TmemT], writes=[Tpa])
                        P.op("act", lambda e, pa=pa, mc=mc, half=half: e.activation(
                            out=vm[:, mc, half * 512:(half + 1) * 512], in_=pa[:], func=AF.Copy),
                            reads=[Tpa], writes=[Tvm])
                hk = 0
                for g in range(4):
                    if g + 1 < 4:
                        load_hgroup(g + 1)
                    nT, TnT = n2T[g % 2], Tn2T[g % 2]
                    srcs = [(ht[(g * 4 + tt) % 8][:], Tht[(g * 4 + tt) % 8]) for tt in range(4)]
                    norm_group(R, srcs, gp[:, G_XATTN:G_XATTN + 8], nT, TnT)
                    for c in range(8):
                        pa, Tpa = pA[c % 2]
                        for dc in range(8):
                            P.op("pe", lambda e, pa=pa, dc=dc, c=c, nT=nT: e.matmul(
                                pa[:], lhsT=wqm[:, dc, c * 128:(c + 1) * 128], rhs=nT[:, dc, :],
                                start=(dc == 0), stop=(dc == 7)), reads=[Twqm, TnT], writes=[Tpa])
                        P.op("act", lambda e, pa=pa, c=c: e.activation(out=qTm[:, c, :], in_=pa[:], func=AF.Copy,
                                                                      scale=1.0 / 16), reads=[Tpa], writes=[TqTm])
                    for tt in range(4):
                        bb = (g * 4 + tt) % 8
                        h, Th = ht[bb], Tht[bb]
                        for hd in range(4):
                            k2 = hk % 2
                            hk += 1
                            for c in range(2):
                                P.op("pe", lambda e, c=c, hd=hd, tt=tt: e.matmul(
                                    psc[:, 0:256], lhsT=qTm[:, 2 * hd + c, tt * 128:(tt + 1) * 128],
                                    rhs=kTm[:, 2 * hd + c, :], start=(c == 0), stop=(c == 1)),
                                    reads=[TqTm, TkTm], writes=[Tpsc])
                            P.op("dve", lambda e, hd=hd: e.tensor_reduce(out=nmx[:, hd:hd + 1], in_=psc[:, 0:256],
                                                                        axis=AX.X, op=ALU.max, negate=True),
                                 reads=[Tpsc], writes=[Tnmx[hd]])
                            P.op("act", lambda e, hd=hd, k2=k2: e.activation(
                                out=pexp[k2][:], in_=psc[:, 0:256], func=AF.Exp, bias=nmx[:, hd:hd + 1],
                                accum_out=rsum[:, hd:hd + 1]), reads=[Tnmx[hd]], writes=[Tpexp[k2], Trsum[hd], Tpsc])
                            P.op("dve", lambda e, hd=hd: e.reciprocal(out=rsum[:, hd:hd + 1], in_=rsum[:, hd:hd + 1]),
                                 reads=[Trsum[hd]], writes=[Trsum[hd]])
                            P.op("dve", lambda e, hd=hd, k2=k2: e.tensor_scalar(
                                out=pn[k2][:], in0=pexp[k2][:], scalar1=rsum[:, hd:hd + 1], scalar2=None, op0=ALU.mult),
                                reads=[Tpexp[k2], Trsum[hd]], writes=[Tpn[k2]])
                            for mc in range(2):
                                P.op("pe", lambda e, mc=mc, k2=k2: e.transpose(
                                    out=pTp[:, mc * 128:(mc + 1) * 128], in_=pn[k2][:, mc * 128:(mc + 1) * 128],
                                    identity=ident_b[:]), reads=[Tpn[k2], Tc], writes=[TpTp])
                            P.op("act", lambda e, k2=k2: e.activation(out=pTs[k2][:], in_=pTp[:, 0:256], func=AF.Copy),
                                 reads=[TpTp], writes=[TpTs[k2]])
                            for dch in range(2):
                                ch = 2 * hd + dch
                                po, Tpo = poT[ch // 4]
                                for mc in range(2):
                                    P.op("pe", lambda e, po=po, ch=ch, mc=mc, hd=hd, dch=dch, k2=k2: e.matmul(
                                        po[:, (ch % 4) * 128:(ch % 4 + 1) * 128],
                                        lhsT=vm[:, mc, hd * 256 + dch * 128:hd * 256 + (dch + 1) * 128],
                                        rhs=pTs[k2][:, mc * 128:(mc + 1) * 128], start=(mc == 0), stop=(mc == 1)),
                                        reads=[Tvm, TpTs[k2]], writes=[Tpo])
                        for i2 in range(2):
                            po, Tpo = poT[i2]
                            P.op("dve" if i2 == 0 else "act",
                                 (lambda e, po=po, i2=i2: e.tensor_copy(
                                     out=oT_sb[:, i2 * 4:(i2 + 1) * 4, :].rearrange("p c n -> p (c n)"), in_=po[:]))
                                 if i2 == 0 else
                                 (lambda e, po=po, i2=i2: e.activation(
                                     out=oT_sb[:, i2 * 4:(i2 + 1) * 4, :].rearrange("p c n -> p (c n)"), in_=po[:],
                                     func=AF.Copy)),
                                 reads=[Tpo], writes=[ToT])
                        for half in range(2):
                            pa, Tpa = pA[half]
                            for c in range(8):
                                P.op("pe", lambda e, pa=pa, c=c, half=half: e.matmul(
                                    pa[:], lhsT=oT_sb[:, c, :], rhs=wom[:, c, half * 512:(half + 1) * 512],
                                    start=(c == 0), stop=(c == 7)), reads=[ToT, Twom], writes=[Tpa])
                            P.op("dve", lambda e, pa=pa, h=h, half=half: e.tensor_tensor(
                                out=h[:, half * 512:(half + 1) * 512], in0=pa[:], in1=h[:, half * 512:(half + 1) * 512],
                                op=ALU.add), reads=[Tpa, Th], writes=[Th])
                        r0 = (g * 4 + tt) * 128
                        P.dma(h_d[r0:r0 + 128, :], h[:], reads=[Th], writes=[Thd], qeng="pool")
            P.barrier()

        if "d_h2" in dbg_out:
            with contextlib.ExitStack() as ph:
                tmp = alloc(ph, "dbgtmp2", [128, 16, 1024], F32)
                Tt = T("dbgtmp2")
                P.dma(tmp[:], h_d.rearrange("(t p) n -> p t n", p=128), reads=[Thd], writes=[Tt])
                P.dma(dbg_out["d_h2"].rearrange("(t p) n -> p t n", p=128), tmp[:], reads=[Tt], writes=[T("x")])
                P.barrier()

        with contextlib.ExitStack() as sE:
            iota128 = alloc(sE, "iota128", [128, 128], F32)
            c16 = alloc(sE, "c16", [128, 16], F32)
            i16 = alloc(sE, "i16", [128, 16], F32)
            sE1 = contextlib.ExitStack()
            n3T = alloc(sE1, "n3T", [128, 8, 2048], BF16)
            Tn3T = T("n3T")
            IDX0 = alloc(sE1, "IDX0", [128, 16, 128], F32)
            IDX1 = alloc(sE1, "IDX1", [128, 16, 128], F32)
            GATE = alloc(sE1, "GATE", [128, 16, 128], F32)
            TIDX = [T("IDX%d" % i) for i in range(16)]
            Tci = T("peer_consts")
            with contextlib.ExitStack() as ph:
                pT = [(palloc(ph, "pT%d" % i, [128, 512]), T("pT%d" % i)) for i in range(2)]
                pA = [(palloc(ph, "pA%d" % i, [128, 512]), T("pA%d" % i)) for i in range(2)]
                pscr = palloc(ph, "pscr", [128, 2048])
                Tpscr = T("pscr")
                R = make_norm_res(ph, pT)
                wqp = alloc(ph, "wqp", [128, 8, 2048], BF16)
                skb = alloc(ph, "skb", [128, 16, 128], BF16)
                Twqp, Tskb = T("wqp"), T("skb")
                ht = [alloc(ph, "ht%d" % i, [128, 1024], F32) for i in range(8)]
                Tht = [T("ht%d" % i) for i in range(8)]
                qTp = alloc(ph, "qTp", [128, 16, 512], BF16)
                TqTp = T("qTp")
                sc_sb = alloc(ph, "sc_sb", [128, 2048], F32)
                Tsc = T("sc_sb")
                scw = alloc(ph, "scw", [128, 256], F32)
                Tscw = T("scw")
                top_s = alloc(ph, "top_s", [128, 16, 16], F32)
                top_i = alloc(ph, "top_i", [128, 16, 16], U32)
                top_if = alloc(ph, "top_if", [128, 16, 16], F32)
                Ttop = T("top")
                Ttops = [T("tops%d" % i) for i in range(16)]
                Ttops2 = [T("tops2_%d" % i) for i in range(16)]
                Ttopi = [T("topi%d" % i) for i in range(16)]
                Ttopi2 = [T("topi2_%d" % i) for i in range(16)]
                scw4 = [alloc(ph, "scw4_%d" % i, [128, 256], F32) for i in range(4)]
                Tscw4 = [T("scw4_%d" % i) for i in range(4)]
                Tbs = [T("bs%d" % i) for i in range(8)]
                Tbs2 = [T("bs2_%d" % i) for i in range(8)]
                Tbj = [T("bj%d" % i) for i in range(8)]
                Tbj2 = [T("bj2_%d" % i) for i in range(8)]
                cand = alloc(ph, "cand", [128, 8, 256], F32)
                Tcand = T("cand")
                best_s = alloc(ph, "best_s", [128, 8, 16], F32)
                best_j = alloc(ph, "best_j", [128, 8, 16], U32)
                jf = alloc(ph, "jf", [128, 8, 16], F32)
                Tbest = T("best")
                big = [alloc(ph, "big%d" % i, [128, 8, 16, 16], F32) for i in range(3)]
                Tbig = [T("big%d" % i) for i in range(3)]
                sm = [alloc(ph, "sm%d" % i, [128, 8, 16], F32) for i in range(3)]
                Tsm = [T("sm%d" % i) for i in range(3)]
                s8 = alloc(ph, "s8", [128, 8], F32)
                Ts8 = T("s8")
                if stage >= 6:
                    P.dma(wqp[:], w_query[:, :].rearrange("(c p) n -> p c n", p=128), writes=[Twqp], qeng="pool")
                    P.dma(skb[:], skT[:, :, :], writes=[Tskb], qeng="pool")
                    P.op("pool", lambda e: e.iota(iota128[:], pattern=[[1, 128]], base=0, channel_multiplier=0,
                                                  allow_small_or_imprecise_dtypes=True), writes=[Tci])
                    P.op("pool", lambda e: e.iota(c16[:], pattern=[[16, 16]], base=0, channel_multiplier=0,
                                                  allow_small_or_imprecise_dtypes=True), writes=[Tci])
                    P.op("pool", lambda e: e.iota(i16[:], pattern=[[1, 16]], base=0, channel_multiplier=0,
                                                  allow_small_or_imprecise_dtypes=True), writes=[Tci])

                    def load_hgroup(g):
                        for tt in range(4):
                            bb = (g * 4 + tt) % 8
                            r0 = (g * 4 + tt) * 128
                            P.dma(ht[bb][:], h_d[r0:r0 + 128, :], reads=[Thd], writes=[Tht[bb]])
                    load_hgroup(0)
                    B4 = [128, 8, 16, 16]
                    for g in range(4):
                        if g + 1 < 4:
                            load_hgroup(g + 1)
                        srcs = [(ht[(g * 4 + tt) % 8][:], Tht[(g * 4 + tt) % 8]) for tt in range(4)]
                        nTg = n3T[:, :, g * 512:(g + 1) * 512]
                        norm_group(R, srcs, gp[:, G_FFN:G_FFN + 8], nTg, Tn3T)
                        for c in range(16):
                            pa, Tpa = pA[c % 2]
                            for dc in range(8):
                                P.op("pe", lambda e, pa=pa, dc=dc, c=c, g=g: e.matmul(
                                    pa[:], lhsT=wqp[:, dc, c * 128:(c + 1) * 128], rhs=n3T[:, dc, g * 512:(g + 1) * 512],
                                    start=(dc == 0), stop=(dc == 7)), reads=[Twqp, Tn3T], writes=[Tpa])
                            P.op("act", lambda e, pa=pa, c=c: e.activation(out=qTp[:, c, :], in_=pa[:], func=AF.Copy),
                                 reads=[Tpa], writes=[TqTp])
                        for tt in range(4):
                            t = g * 4 + tt
                            for hc in range(16):
                                P.op("pe", lambda e, hc=hc, tt=tt: e.matmul(
                                    pscr[:, hc * 128:(hc + 1) * 128], lhsT=qTp[:, hc, tt * 128:(tt + 1) * 128],
                                    rhs=skb[:, hc, :], start=True, stop=True), reads=[TqTp, Tskb], writes=[Tpscr])
                            P.op("act", lambda e: e.activation(out=sc_sb[:], in_=pscr[:], func=AF.Copy),
                                 reads=[Tpscr], writes=[Tsc])
                            for hc0 in range(0, 16, 4):
                                grp = list(range(hc0, hc0 + 4))
                                srcs_ = {hc: sc_sb[:, hc * 128:(hc + 1) * 128] for hc in grp}
                                for hc in grp:
                                    P.op("dve", lambda e, hc=hc: e.max(out=top_s[:, hc, 0:8], in_=srcs_[hc]) if False else
                                         e.max(out=top_s[:, hc, 0:8], in_=sc_sb[:, hc * 128:(hc + 1) * 128]),
                                         reads=[Tsc], writes=[Ttops[hc]])
                                for hc in grp:
                                    P.op("dve", lambda e, hc=hc: e.max_index(
                                        out=top_i[:, hc, 0:8], in_max=top_s[:, hc, 0:8],
                                        in_values=sc_sb[:, hc * 128:(hc + 1) * 128]),
                                        reads=[Tsc, Ttops[hc]], writes=[Ttopi[hc]])
                                for hc in grp:
                                    P.op("dve", lambda e, hc=hc: e.match_replace(
                                        out=scw4[hc % 4][:, 0:128], in_to_replace=top_s[:, hc, 0:8],
                                        in_values=sc_sb[:, hc * 128:(hc + 1) * 128], imm_value=-1e30),
                                        reads=[Tsc, Ttops[hc]], writes=[Tscw4[hc % 4]])
                                for hc in grp:
                                    P.op("dve", lambda e, hc=hc: e.max(out=top_s[:, hc, 8:16], in_=scw4[hc % 4][:, 0:128]),
                                         reads=[Tscw4[hc % 4]], writes=[Ttops2[hc]])
                                for hc in grp:
                                    P.op("dve", lambda e, hc=hc: e.max_index(
                                        out=top_i[:, hc, 8:16], in_max=top_s[:, hc, 8:16], in_values=scw4[hc % 4][:, 0:128]),
                                        reads=[Tscw4[hc % 4], Ttops2[hc]], writes=[Ttopi2[hc]])
                            P.op("dve", lambda e: e.tensor_copy(out=top_if[:], in_=top_i[:]),
                                 reads=Ttops + Ttops2 + Ttopi + Ttopi2, writes=[Ttop])
                            ts4 = top_s[:, :, :].rearrange("p (h c) k -> p h c k", c=2)
                            ti4 = top_if[:, :, :].rearrange("p (h c) k -> p h c k", c=2)
                            P.op("dve", lambda e, ts4=ts4: e.tensor_tensor(
                                out=cand[:, :, :].rearrange("p h (a b) -> p h a b", b=16),
                                in0=ts4[:, :, 0, :].unsqueeze(3).broadcast_to(B4),
                                in1=ts4[:, :, 1, :].unsqueeze(2).broadcast_to(B4), op=ALU.add),
                                reads=[Ttop] + Ttops + Ttops2, writes=[Tcand])
                            for h0 in range(0, 8, 4):
                                grp = list(range(h0, h0 + 4))
                                for h8 in grp:
                                    P.op("dve", lambda e, h8=h8: e.max(out=best_s[:, h8, 0:8], in_=cand[:, h8, :]),
                                         reads=[Tcand], writes=[Tbs[h8]])
                                for h8 in grp:
                                    P.op("dve", lambda e, h8=h8: e.max_index(out=best_j[:, h8, 0:8],
                                                                            in_max=best_s[:, h8, 0:8], in_values=cand[:, h8, :]),
                                         reads=[Tcand, Tbs[h8]], writes=[Tbj[h8]])
                                for h8 in grp:
                                    P.op("dve", lambda e, h8=h8: e.match_replace(
                                        out=scw4[h8 % 4][:, 0:256], in_to_replace=best_s[:, h8, 0:8], in_values=cand[:, h8, :],
                                        imm_value=-1e30), reads=[Tcand, Tbs[h8]], writes=[Tscw4[h8 % 4]])
                                for h8 in grp:
                                    P.op("dve", lambda e, h8=h8: e.max(out=best_s[:, h8, 8:16], in_=scw4[h8 % 4][:, 0:256]),
                                         reads=[Tscw4[h8 % 4]], writes=[Tbs2[h8]])
                                for h8 in grp:
                                    P.op("dve", lambda e, h8=h8: e.max_index(out=best_j[:, h8, 8:16],
                                                                            in_max=best_s[:, h8, 8:16],
                                                                            in_values=scw4[h8 % 4][:, 0:256]),
                                         reads=[Tscw4[h8 % 4], Tbs2[h8]], writes=[Tbj2[h8]])
                            P.op("dve", lambda e: e.tensor_copy(out=jf[:], in_=best_j[:]),
                                 reads=Tbs + Tbs2 + Tbj + Tbj2, writes=[Tbest])
                            c16b = c16[:, :].unsqueeze(1).unsqueeze(1).broadcast_to(B4)
                            i16b = i16[:, :].unsqueeze(1).unsqueeze(1).broadcast_to(B4)
                            P.op("dve", lambda e, c16b=c16b: e.tensor_tensor(
                                out=big[0][:], in0=jf[:, :, :].unsqueeze(3).broadcast_to(B4), in1=c16b, op=ALU.subtract),
                                reads=[Tbest, Tci], writes=[Tbig[0]])
                            P.op("dve", lambda e: e.tensor_scalar(out=big[1][:], in0=big[0][:], scalar1=0.0, scalar2=None,
                                                                  op0=ALU.is_ge), reads=[Tbig[0]], writes=[Tbig[1]])
                            P.op("dve", lambda e: e.scalar_tensor_tensor(out=big[2][:], in0=big[0][:], scalar=16.0,
                                                                         in1=big[1][:], op0=ALU.is_lt, op1=ALU.mult),
                                 reads=[Tbig[0], Tbig[1]], writes=[Tbig[2]])
                            P.op("dve", lambda e, ti4=ti4: e.tensor_tensor(
                                out=big[0][:], in0=big[2][:], in1=ti4[:, :, 0, :].unsqueeze(2).broadcast_to(B4), op=ALU.mult),
                                reads=[Tbig[2], Ttop], writes=[Tbig[0]])
                            P.op("dve", lambda e, t=t: e.tensor_reduce(
                                out=IDX0[:, t, :].rearrange("p (h k) -> p h k", k=16), in_=big[0][:], axis=AX.X, op=ALU.add),
                                reads=[Tbig[0]], writes=[TIDX[t]])
                            P.op("dve", lambda e, i16b=i16b: e.tensor_tensor(out=big[1][:], in0=big[2][:], in1=i16b, op=ALU.mult),
                                 reads=[Tbig[2], Tci], writes=[Tbig[1]])
                            P.op("dve", lambda e: e.tensor_reduce(out=sm[0][:], in_=big[1][:], axis=AX.X, op=ALU.add),
                                 reads=[Tbig[1]], writes=[Tsm[0]])
                            P.op("dve", lambda e: e.scalar_tensor_tensor(out=sm[1][:], in0=sm[0][:], scalar=-16.0, in1=jf[:],
                                                                         op0=ALU.mult, op1=ALU.add),
                                 reads=[Tsm[0], Tbest], writes=[Tsm[1]])
                            P.op("dve", lambda e, i16b=i16b: e.tensor_tensor(
                                out=big[0][:], in0=sm[1][:, :, :].unsqueeze(3).broadcast_to(B4), in1=i16b, op=ALU.is_equal),
                                reads=[Tsm[1], Tci], writes=[Tbig[0]])
                            P.op("dve", lambda e, ti4=ti4: e.tensor_tensor(
                                out=big[1][:], in0=big[0][:], in1=ti4[:, :, 1, :].unsqueeze(2).broadcast_to(B4), op=ALU.mult),
                                reads=[Tbig[0], Ttop], writes=[Tbig[1]])
                            P.op("dve", lambda e, t=t: e.tensor_reduce(
                                out=IDX1[:, t, :].rearrange("p (h k) -> p h k", k=16), in_=big[1][:], axis=AX.X, op=ALU.add),
                                reads=[Tbig[1]], writes=[TIDX[t]])
                            P.op("dve", lambda e: e.tensor_tensor(
                                out=sm[2][:], in0=best_s[:], in1=best_s[:, :, 0:1].broadcast_to([128, 8, 16]), op=ALU.subtract),
                                reads=[Tbest], writes=[Tsm[2]])
                            P.op("act", lambda e: e.activation(out=sm[2][:], in_=sm[2][:], func=AF.Exp),
                                 reads=[Tsm[2]], writes=[Tsm[2]])
                            P.op("dve", lambda e: e.tensor_reduce(out=s8[:], in_=sm[2][:], axis=AX.X, op=ALU.add),
                                 reads=[Tsm[2]], writes=[Ts8])
                            P.op("dve", lambda e: e.reciprocal(out=s8[:], in_=s8[:]), reads=[Ts8], writes=[Ts8])
                            P.op("dve", lambda e, t=t: e.tensor_tensor(
                                out=GATE[:, t, :].rearrange("p (h k) -> p h k", k=16), in0=sm[2][:],
                                in1=s8[:, :].unsqueeze(2).broadcast_to([128, 8, 16]), op=ALU.mult),
                                reads=[Tsm[2], Ts8], writes=[TIDX[t]])
                P.barrier()

            if "d_idx" in dbg_out:
                P.dma(dbg_out["d_idx"][0].rearrange("(t p) n -> p t n", p=128), IDX0[:], reads=TIDX, writes=[T("x")])
                P.dma(dbg_out["d_idx"][1].rearrange("(t p) n -> p t n", p=128), IDX1[:], reads=TIDX, writes=[T("x")])
                P.dma(dbg_out["d_idx"][2].rearrange("(t p) n -> p t n", p=128), GATE[:], reads=TIDX, writes=[T("x")])
                P.barrier()

            precast_some(len(precast_jobs))
            Tn3d, Tidxd = T("n3_d"), T("idx_d")
            if stage >= 7:
                P.dma(n3_d[:, :, :], n3T[:], reads=[Tn3T], writes=[Tn3d], qeng="pool")
                for i3, srcI in enumerate((IDX0, IDX1, GATE)):
                    P.dma(idx_d[i3], srcI[:], reads=TIDX, writes=[Tidxd], qeng="pool")
            P.barrier()
            sE1.close()
            with contextlib.ExitStack() as ph:
                pout = [(palloc(ph, "pout%d" % i, [128, 512]), T("pout%d" % i)) for i in range(4)]
                pact = [(palloc(ph, "pact%d" % i, [128, 512]), T("pact%d" % i)) for i in range(2)]
                pG = palloc(ph, "pG", [128, 512])
                TpG = T("pG")
                ptr = palloc(ph, "ptr", [128, 512])
                Tptr = T("ptr")
                GT = [alloc(ph, "GT%d" % i, [128, 256, 128], BF16) for i in range(2)]
                TGT = [T("GT0"), T("GT1")]
                n3p = [alloc(ph, "n3p%d" % i, [128, 8, 256], BF16) for i in range(2)]
                Tn3p = [T("n3p0"), T("n3p1")]
                ip = [alloc(ph, "ip%d" % i, [128, 3, 2, 128], F32) for i in range(2)]
                Tip = [T("ip0"), T("ip1")]
                trT = [alloc(ph, "trT%d" % i, [128, 3, 128], F32) for i in range(2)]
                TtrT = [T("trT0"), T("trT1")]
                NOH = 8
                Aoh = [alloc(ph, "Aoh%d" % i, [128, 128], BF16) for i in range(NOH)]
                Boh = [alloc(ph, "Boh%d" % i, [128, 128], BF16) for i in range(NOH)]
                TAoh = [T("Aoh%d" % i) for i in range(NOH)]
                TBoh = [T("Boh%d" % i) for i in range(NOH)]
                NUB = 4
                ub = [alloc(ph, "ub%d" % i, [128, 8, 256], BF16) for i in range(NUB)]
                vb = [alloc(ph, "vb%d" % i, [128, 2, 1024], BF16) for i in range(NUB)]
                Tub = [T("ub%d" % i) for i in range(NUB)]
                Tvb = [T("vb%d" % i) for i in range(NUB)]
                ga = [alloc(ph, "ga%d" % i, [128, 256], BF16) for i in range(3)]
                coef = [alloc(ph, "coef%d" % i, [128, 256], BF16) for i in range(3)]
                Tga = [T("ga%d" % i) for i in range(3)]
                Tcoef = [T("coef%d" % i) for i in range(3)]
                hf = [alloc(ph, "hf%d" % i, [128, 1024], F32) for i in range(2)]
                Thf = [T("hf0"), T("hf1")]
                gf = alloc(ph, "gf", [128, 1024], F32)
                Tgf = T("gf")
                junk2 = alloc(ph, "junk2", [128, 1024], BF16)
                Tjunk2 = T("junk2")
                fs = alloc(ph, "fs", [128, 2], F32)
                Tfs = [T("fs0"), T("fs1")]
                To = T("out")
                NPASS = 8 if stage >= 7 else 0
                if NPASS:
                    P.dma(gf[:], gfin[:, :], writes=[Tgf])

                def load_pass(p):
                    b = p % 2
                    P.dma(n3p[b][:], n3_d[:, :, p * 256:(p + 1) * 256], reads=[Tn3d], writes=[Tn3p[b]])
                    for i3 in range(3):
                        P.dma(ip[b][:, i3, :, :], idx_d[i3, :, 2 * p:2 * p + 2, :], reads=[Tidxd], writes=[Tip[b]])

                def load_blk(bk):
                    bi_ = bk % NUB
                    c0 = bk * 2
                    P.dma(ub[bi_][:], eu_b[bk, :, :, :], reads=[Teub], writes=[Tub[bi_]])
                    P.dma(vb[bi_][:], ev_b[c0 * 128:c0 * 128 + 256, :].rearrange("(k p) d -> p k d", p=128),
                          reads=[Tevb], writes=[Tvb[bi_]])

                noh = [0]

                def gb_tr(p, tl):
                    b = p % 2
                    for i3 in range(3):
                        P.op("pe", lambda e, i3=i3: e.transpose(
                            out=ptr[:, i3 * 128:(i3 + 1) * 128], in_=ip[b][:, i3, tl, :], identity=ident_f[:]),
                            reads=[Tip[b], Tc], writes=[Tptr])
                    P.op("act", lambda e: e.activation(out=trT[tl][:, :, :].rearrange("p a n -> p (a n)"),
                                                       in_=ptr[:, 0:384], func=AF.Copy), reads=[Tptr], writes=[TtrT[tl]])

                def gb_oh(p, tok):
                    tl, tk = tok // 128, tok % 128
                    k = noh[0] % NOH
                    noh[0] += 1
                    P.op("dve", lambda e: e.tensor_scalar(
                        out=Boh[k][:], in0=iota128[:], scalar1=trT[tl][:, 1, tk:tk + 1], scalar2=trT[tl][:, 2, tk:tk + 1],
                        op0=ALU.is_equal, op1=ALU.mult), reads=[TtrT[tl], Tci], writes=[TBoh[k]])
                    P.op("dve", lambda e: e.tensor_scalar(
                        out=Aoh[k][:], in0=iota128[:], scalar1=trT[tl][:, 0, tk:tk + 1], scalar2=None,
                        op0=ALU.is_equal), reads=[TtrT[tl], Tci], writes=[TAoh[k]])
                    return k

                def gb_mm(tok, k):
                    P.op("pe", lambda e: e.matmul(pG[:, (tok % 4) * 128:(tok % 4 + 1) * 128], lhsT=Boh[k][:], rhs=Aoh[k][:],
                                                  start=True, stop=True), reads=[TBoh[k], TAoh[k]], writes=[TpG])

                def gb_evac(p, tok0):
                    b = p % 2
                    P.op("act", lambda e: e.activation(
                        out=GT[b][:, tok0:tok0 + 4, :].rearrange("p t n -> p (t n)"), in_=pG[:], func=AF.Copy),
                        reads=[TpG], writes=[TGT[b]])

                def U(c, p):
                    b = p % 2
                    bi = (c // 2) % NUB
                    pa, Tpa = pact[c % 2]
                    k3 = c % 3
                    for dc in range(8):
                        P.op("pe", lambda e, dc=dc: e.matmul(
                            pa[:, 0:256], lhsT=ub[bi][:, dc, (c % 2) * 128:(c % 2 + 1) * 128],
                            rhs=n3p[b][:, dc, :], start=(dc == 0), stop=(dc == 7)),
                            reads=[Tub[bi], Tn3p[b]], writes=[Tpa])
                    P.op("act", lambda e: e.activation(out=ga[k3][:], in_=pa[:, 0:256], func=AF.Gelu),
                         reads=[Tpa], writes=[Tga[k3]])
                    P.op("dve", lambda e: e.tensor_tensor(out=coef[k3][:], in0=ga[k3][:], in1=GT[b][:, :, c], op=ALU.mult),
                         reads=[Tga[k3], TGT[b]], writes=[Tcoef[k3]])

                def Vv(c):
                    bi = (c // 2) % NUB
                    k3 = c % 3
                    for tl in range(2):
                        for half in range(2):
                            po, Tpo = pout[tl * 2 + half]
                            P.op("pe", lambda e, po=po, tl=tl, half=half: e.matmul(
                                po[:], lhsT=coef[k3][:, tl * 128:(tl + 1) * 128],
                                rhs=vb[bi][:, c % 2, half * 512:(half + 1) * 512], start=(c == 0), stop=(c == 127)),
                                reads=[Tcoef[k3], Tvb[bi]], writes=[Tpo])

                def finish_pass(p):
                    for tl in range(2):
                        t = p * 2 + tl
                        h, Th = hf[tl], Thf[tl]
                        P.dma(h[:], h_d[t * 128:(t + 1) * 128, :], reads=[Thd], writes=[Th])
                        for half in range(2):
                            po, Tpo = pout[tl * 2 + half]
                            P.op("dve", lambda e, po=po, h=h, half=half: e.tensor_tensor(
                                out=h[:, half * 512:(half + 1) * 512], in0=po[:], in1=h[:, half * 512:(half + 1) * 512],
                                op=ALU.add), reads=[Tpo, Th], writes=[Th])
                        P.op("act", lambda e, h=h, tl=tl: e.activation(out=junk2[:], in_=h[:], func=AF.Square,
                                                                      accum_out=fs[:, tl:tl + 1]),
                             reads=[Th], writes=[Tjunk2, Tfs[tl]])
                        P.op("dve", lambda e, tl=tl: e.tensor_scalar(out=fs[:, tl:tl + 1], in0=fs[:, tl:tl + 1],
                                                                    scalar1=1.0 / 1024, scalar2=EPS, op0=ALU.mult, op1=ALU.add),
                             reads=[Tfs[tl]], writes=[Tfs[tl]])
                        P.op("act", lambda e, tl=tl: e.activation(out=fs[:, tl:tl + 1], in_=fs[:, tl:tl + 1], func=AF.Sqrt),
                             reads=[Tfs[tl]], writes=[Tfs[tl]])
                        P.op("dve", lambda e, tl=tl: e.reciprocal(out=fs[:, tl:tl + 1], in_=fs[:, tl:tl + 1]),
                             reads=[Tfs[tl]], writes=[Tfs[tl]])
                        P.op("dve", lambda e, h=h, tl=tl: e.scalar_tensor_tensor(
                            out=h[:], in0=h[:], scalar=fs[:, tl:tl + 1], in1=gf[:], op0=ALU.mult, op1=ALU.mult),
                            reads=[Th, Tfs[tl], Tgf], writes=[Th])
                        P.dma(out[t * 128:(t + 1) * 128, :], h[:], reads=[Th], writes=[To])

                if NPASS:
                    load_pass(0)
                    load_pass(1)
                    for tl in range(2):
                        gb_tr(0, tl)
                        pend = []
                        for tk in range(128):
                            tok = tl * 128 + tk
                            k = gb_oh(0, tok)
                            gb_mm(tok, k)
                            if tok % 4 == 3:
                                gb_evac(0, tok - 3)
                for p in range(NPASS):
                    nxt = p + 1 if p + 1 < NPASS else None
                    for bk in range(3):
                        load_blk(bk)
                    if nxt is not None:
                        gb_tr(nxt, 0)
                    pend = []
                    for c in range(128 + 2):
                        if nxt is not None:
                            if c >= 3 and c % 2 == 1:
                                gb_evac(nxt, (c - 3) * 2)
                            if c == 63:
                                gb_tr(nxt, 1)
                        if c < 128:
                            U(c, p)
                        if nxt is not None:
                            for tok, k in pend:
                                gb_mm(tok, k)
                            pend = []
                            if c < 128:
                                for tok in (2 * c, 2 * c + 1):
                                    pend.append((tok, gb_oh(nxt, tok)))
                        if c - 2 >= 0:
                            Vv(c - 2)
                            if (c - 2) % 2 == 1:
                                nb = (c - 2) // 2 + 3
                                if nb < 64:
                                    load_blk(nb)
                    finish_pass(p)
                    if p + 2 < NPASS:
                        load_pass(p + 2)
                P.barrier()
        P.barrier()
        P.emit()
    return nc, P


def prep_inputs(inputs):
    f32 = np.float32
    x = np.asarray(inputs["x"], f32)
    mem = np.asarray(inputs["mem"], f32)

    def cols(v):
        v = np.asarray(v, f32).reshape(-1, 128)
        return np.ascontiguousarray(v.T)

    gpack = np.zeros((128, NGP), f32)
    gpack[:, G_MIX:G_MIX + 8] = cols(inputs["g_mix"][0])
    gpack[:, G_XATTN:G_XATTN + 8] = cols(inputs["g_xattn"][0])
    gpack[:, G_MEM:G_MEM + 8] = cols(inputs["g_mem"][0])
    gpack[:, G_FFN:G_FFN + 8] = cols(inputs["g_ffn"][0])
    gpack[:, G_SB:G_SB + 4] = cols(inputs["g_sb_out"][0])
    gpack[:, G_CONV:G_CONV + 4] = cols(inputs["g_conv_out"][0])
    cw = np.asarray(inputs["conv_w"][0], f32)
    for cc in range(4):
        for k in range(3):
            gpack[:, G_CW + cc * 3 + k] = cw[k, cc * 128:(cc + 1) * 128]
    gfin = np.ascontiguousarray(np.broadcast_to(np.asarray(inputs["g_final"], f32)[None, :], (128, 1024)))
    sk = np.asarray(inputs["sub_keys"][0], f32)
    skT = np.ascontiguousarray(sk.reshape(16, 128, 128).transpose(2, 0, 1))
    euT = np.ascontiguousarray(np.asarray(inputs["expert_u"][0], f32).T)
    ev = np.ascontiguousarray(np.asarray(inputs["expert_v"][0], f32))
    shared = dict(
        gpack=gpack, gfin=gfin,
        w_in=np.ascontiguousarray(inputs["w_in"][0], dtype=f32),
        w_out=np.ascontiguousarray(inputs["w_out"][0], dtype=f32),
        w_q_mem=np.ascontiguousarray(inputs["w_q_mem"][0], dtype=f32),
        w_kv_mem=np.ascontiguousarray(inputs["w_kv_mem"][0], dtype=f32),
        w_o_mem=np.ascontiguousarray(inputs["w_o_mem"][0], dtype=f32),
        w_query=np.ascontiguousarray(inputs["w_query"][0], dtype=f32),
        skT=skT, euT=euT, ev=ev)
    in_maps = []
    kpos = np.arange(2048)
    for c in range(8):
        b, ci = c // 4, c % 4
        xqs, xhs = [], []
        for s in range(4):
            t0 = (4 * s + ci) * 512
            xqs.append(x[b, t0:t0 + 512])
            if t0 == 0:
                xhs.append(np.zeros((2, 1024), f32))
            else:
                xhs.append(x[b, t0 - 2:t0])
        qpos = ci * 512 + np.arange(512)
        m = (kpos[:, None] < qpos[None, :]).astype(f32)
        m = m.reshape(16, 128, 512).transpose(1, 0, 2)
        d = dict(shared)
        d.update(xb=np.ascontiguousarray(x[b]), xq=np.ascontiguousarray(np.concatenate(xqs, 0)),
                 xh=np.ascontiguousarray(np.concatenate(xhs, 0)), memb=np.ascontiguousarray(mem[b]),
                 mask=np.ascontiguousarray(m).astype(ml_dtypes.bfloat16))
        in_maps.append(d)
    return in_maps


def assemble(results, key="out"):
    out = np.zeros((2, 8192, 1024), np.float32)
    for c in range(8):
        b, ci = c // 4, c % 4
        o = np.asarray(results[c][key])
        for s in range(4):
            t0 = (4 * s + ci) * 512
            out[b, t0:t0 + 512] = o[s * 512:(s + 1) * 512]
    return out


def kernel(**inputs):
    in_maps = prep_inputs(inputs)
    nc, _ = build()
    res = run_bass_kernel_spmd(nc, in_maps, core_ids=list(range(8)))
    return assemble(res.results)
```
